# Optimizing a Trainium2 kernel written in Bass

```python
import jax, jax.numpy as jnp
from jax import lax
import numpy as np

D_MODEL = 1024
BATCH = 4
SEQ = 8192
DEPTH = 2

GRID_W = 64
CTX_LEN = 256
ROPE_BASE = 10000.0
EPS = 1e-6
NEG = -1e30
N_MOD = 9

D_GROUP = D_MODEL // 4
MLA_HEADS = 4
MLA_NOPE = 64
MLA_ROPE = 32
MLA_V = 64
MLA_Q_RANK = 192
MLA_KV_RANK = 128
SC_WIDTH = D_GROUP
SC_K = 3
WA_HEADS = 4
WA_KV_HEADS = 2
WA_HEAD_DIM = 64
WINDOW = 128
BLOCK = 128
CF_WIDTH = D_GROUP
CF_K = 31
D_FF = 2816

IN_SIZES = (MLA_Q_RANK, MLA_KV_RANK, MLA_ROPE, 3 * SC_WIDTH,
            WA_HEADS * WA_HEAD_DIM, 2 * WA_KV_HEADS * WA_HEAD_DIM, 2 * CF_WIDTH)
D_IN = sum(IN_SIZES)
D_MIX = MLA_HEADS * MLA_V + SC_WIDTH + WA_HEADS * WA_HEAD_DIM + CF_WIDTH

kernel_name = "hybrid_parallel_group_dit_block"


def rms_norm(x, g):
    xf = x.astype(jnp.float32)
    y = xf * lax.rsqrt(jnp.mean(xf * xf, axis=-1, keepdims=True) + EPS)
    return (y * g.astype(jnp.float32)).astype(x.dtype)


def layer_norm(x, g, b):
    xf = x.astype(jnp.float32)
    mu = jnp.mean(xf, axis=-1, keepdims=True)
    var = jnp.mean(jnp.square(xf - mu), axis=-1, keepdims=True)
    y = (xf - mu) * lax.rsqrt(var + EPS)
    return (y * g.astype(jnp.float32) + b.astype(jnp.float32)).astype(x.dtype)


def modulate(h, shift, scale):
    return h * (1 + scale) + shift


def swiglu(h, w_gate, w_up, w_down):
    return (jax.nn.silu(h @ w_gate) * (h @ w_up)) @ w_down


def split_in(z):
    idx = tuple(int(i) for i in np.cumsum(IN_SIZES)[:-1])
    return jnp.split(z, idx, axis=-1)


def dw_conv(x, w):
    k = w.shape[0]
    return lax.conv_general_dilated(
        x, w[:, None, :].astype(x.dtype), (1,), [(k // 2, k // 2)],
        dimension_numbers=('NWC', 'WIO', 'NWC'), feature_group_count=x.shape[-1])


def axial_tables(row_pos, col_pos, d_rot):
    d_ax = d_rot // 2
    inv = ROPE_BASE ** (-jnp.arange(0, d_ax, 2, dtype=jnp.float32) / d_ax)
    ar = row_pos[:, None] * inv[None, :]
    ac = col_pos[:, None] * inv[None, :]
    return (jnp.cos(ar), jnp.sin(ar), jnp.cos(ac), jnp.sin(ac))


def _rotate(x, cos, sin):
    x1, x2 = jnp.split(x, 2, axis=-1)
    cos = cos[:, None, :]
    sin = sin[:, None, :]
    return jnp.concatenate([x1 * cos - x2 * sin, x2 * cos + x1 * sin], axis=-1)


def axial_rope(x, tab):
    cr, sr, cc, sc = tab
    xr, xc = jnp.split(x, 2, axis=-1)
    return jnp.concatenate([_rotate(xr, cr, sr), _rotate(xc, cc, sc)], axis=-1).astype(x.dtype)


def mla_q(cq, g_q, w_uq, tab):
    b, l, _ = cq.shape
    q = (rms_norm(cq, g_q) @ w_uq).reshape(b, l, MLA_HEADS, MLA_NOPE + MLA_ROPE)
    q_nope, q_rope = q[..., :MLA_NOPE], q[..., MLA_NOPE:]
    if tab is not None:
        q_rope = axial_rope(q_rope, tab)
    return jnp.concatenate([q_nope, q_rope], axis=-1)


def mla_kv(ckv, kr, g_kv, w_ukv, tab):
    b, l, _ = ckv.shape
    kv = (rms_norm(ckv, g_kv) @ w_ukv).reshape(b, l, MLA_HEADS, MLA_NOPE + MLA_V)
    k_nope, v = kv[..., :MLA_NOPE], kv[..., MLA_NOPE:]
    k_rope = kr[:, :, None, :]
    if tab is not None:
        k_rope = axial_rope(k_rope, tab)
    k = jnp.concatenate([k_nope, jnp.broadcast_to(k_rope, (b, l, MLA_HEADS, MLA_ROPE))], axis=-1)
    return k, v


def dense_attend(q, k, v):
    scale = q.shape[-1] ** -0.5
    s = jnp.einsum('bqhd,bkhd->bhqk', q, k).astype(jnp.float32) * scale
    p = jax.nn.softmax(s, axis=-1).astype(v.dtype)
    o = jnp.einsum('bhqk,bkhd->bqhd', p, v)
    return o.reshape(o.shape[0], o.shape[1], -1)


def dense_attend_blocked(q, k, v):
    b, s, h, dq = q.shape
    nb = s // BLOCK
    qb = jnp.moveaxis(q.reshape(b, nb, BLOCK, h, dq), 1, 0)
    scale = dq ** -0.5

    def one(qblk):
        sc = jnp.einsum('bqhd,bkhd->bhqk', qblk, k).astype(jnp.float32) * scale
        p = jax.nn.softmax(sc, axis=-1).astype(v.dtype)
        return jnp.einsum('bhqk,bkhd->bqhd', p, v)

    o = lax.map(one, qb)
    return jnp.moveaxis(o, 0, 1).reshape(b, s, -1)


def short_conv_mix(z, w_conv):
    b_g, c_g, xin = jnp.split(z, 3, axis=-1)
    return b_g * dw_conv(c_g * xin, w_conv)


def wa_q(zq, tab):
    b, l, _ = zq.shape
    q = zq.reshape(b, l, WA_HEADS, WA_HEAD_DIM)
    return axial_rope(q, tab) if tab is not None else q


def wa_kv(zkv, tab):
    b, l, _ = zkv.shape
    k, v = jnp.split(zkv.reshape(b, l, 2 * WA_KV_HEADS, WA_HEAD_DIM), 2, axis=2)
    if tab is not None:
        k = axial_rope(k, tab)
    return k, v


def window_attend_latent(q, k, v, k_ctx, v_ctx, sink):
    b, s, hq, d = q.shape
    hkv = k.shape[2]
    g = hq // hkv
    nb = s // BLOCK
    c = k_ctx.shape[1]

    def band(t):
        tp = jnp.pad(t, ((0, 0), (BLOCK, BLOCK), (0, 0), (0, 0))).reshape(b, nb + 2, BLOCK, hkv, d)
        return jnp.concatenate([tp[:, :-2], tp[:, 1:-1], tp[:, 2:]], axis=2)

    kb, vb = band(k), band(v)
    qb = q.reshape(b, nb, BLOCK, hkv, g, d)
    scale = d ** -0.5
    s_loc = jnp.einsum('bnqhgd,bnkhd->bnhgqk', qb, kb).astype(jnp.float32) * scale
    qpos = jnp.arange(nb)[:, None, None] * BLOCK + jnp.arange(BLOCK)[None, :, None]
    kpos = (jnp.arange(nb)[:, None, None] - 1) * BLOCK + jnp.arange(3 * BLOCK)[None, None, :]
    valid = (jnp.abs(kpos - qpos) <= WINDOW) & (kpos >= 0) & (kpos < s)
    s_loc = jnp.where(valid[None, :, None, None], s_loc, NEG)
    s_ctx = jnp.einsum('bnqhgd,bchd->bnhgqc', qb, k_ctx).astype(jnp.float32) * scale
    s_sink = jnp.broadcast_to(sink.astype(jnp.float32).reshape(1, 1, hkv, g, 1, 1),
                              s_ctx.shape[:-1] + (1,))
    p = jax.nn.softmax(jnp.concatenate([s_loc, s_ctx, s_sink], axis=-1), axis=-1)
    p_loc = p[..., :3 * BLOCK].astype(v.dtype)
    p_ctx = p[..., 3 * BLOCK:3 * BLOCK + c].astype(v.dtype)
    o = (jnp.einsum('bnhgqk,bnkhd->bnqhgd', p_loc, vb)
         + jnp.einsum('bnhgqc,bchd->bnqhgd', p_ctx, v_ctx))
    return o.reshape(b, s, hq * d)


def ctx_gqa_sink(q, k, v, sink):
    b, c, hq, d = q.shape
    hkv = k.shape[2]
    g = hq // hkv
    qg = q.reshape(b, c, hkv, g, d)
    s = jnp.einsum('bqhgd,bkhd->bhgqk', qg, k).astype(jnp.float32) * (d ** -0.5)
    s_sink = jnp.broadcast_to(sink.astype(jnp.float32).reshape(1, hkv, g, 1, 1), s.shape[:-1] + (1,))
    p = jax.nn.softmax(jnp.concatenate([s, s_sink], axis=-1), axis=-1)[..., :c].astype(v.dtype)
    o = jnp.einsum('bhgqk,bkhd->bqhgd', p, v)
    return o.reshape(b, c, hq * d)


def conformer_conv_mix(z, w_conv, b_conv, g_ln, b_ln):
    a, gt = jnp.split(z, 2, axis=-1)
    u = a * jax.nn.sigmoid(gt)
    u = dw_conv(u, w_conv) + b_conv
    return jax.nn.silu(layer_norm(u, g_ln, b_ln))


def setup_inputs(seed: int = 0) -> dict:
    key = jax.random.key(seed)
    ks = iter(jax.random.split(key, 40))
    f32 = jnp.float32

    def nrm(shape, scale):
        return jax.random.normal(next(ks), shape, f32) * scale

    def gain(shape):
        return 1.0 + 0.05 * jax.random.normal(next(ks), shape, f32)

    L, D = DEPTH, D_MODEL
    return {
        "x": nrm((BATCH, SEQ, D), 1.0),
        "c": nrm((BATCH, D), 1.0),
        "ctx": nrm((BATCH, CTX_LEN, D), 1.0),
        "c_ctx": nrm((D,), 1.0),
        "w_mod": nrm((L, D, N_MOD * D), D ** -0.5),
        "b_mod": nrm((L, N_MOD * D), 0.02),
        "g_ffn1": gain((L, D)),
        "w1_gate": nrm((L, D, D_FF), D ** -0.5),
        "w1_up": nrm((L, D, D_FF), D ** -0.5),
        "w1_down": nrm((L, D_FF, D), D_FF ** -0.5),
        "g_mix": gain((L, D)),
        "w_in": nrm((L, D, D_IN), D ** -0.5),
        "g_mla_q": gain((L, MLA_Q_RANK)),
        "w_mla_uq": nrm((L, MLA_Q_RANK, MLA_HEADS * (MLA_NOPE + MLA_ROPE)), MLA_Q_RANK ** -0.5),
        "g_mla_kv": gain((L, MLA_KV_RANK)),
        "w_mla_ukv": nrm((L, MLA_KV_RANK, MLA_HEADS * (MLA_NOPE + MLA_V)), MLA_KV_RANK ** -0.5),
        "w_sc_conv": nrm((L, SC_K, SC_WIDTH), SC_K ** -0.5),
        "wa_sink": nrm((L, WA_HEADS), 0.5),
        "w_cf_conv": nrm((L, CF_K, CF_WIDTH), CF_K ** -0.5),
        "b_cf_conv": nrm((L, CF_WIDTH), 0.02),
        "g_cf_ln": gain((L, CF_WIDTH)),
        "b_cf_ln": nrm((L, CF_WIDTH), 0.02),
        "w_out": nrm((L, D_MIX, D), D_MIX ** -0.5),
        "g_ffn2": gain((L, D)),
        "w2_gate": nrm((L, D, D_FF), D ** -0.5),
        "w2_up": nrm((L, D, D_FF), D ** -0.5),
        "w2_down": nrm((L, D_FF, D), D_FF ** -0.5),
        "g_final": gain((D,)),
    }


def reference(x, c, ctx, c_ctx, w_mod, b_mod, g_ffn1, w1_gate, w1_up, w1_down, g_mix, w_in,
              g_mla_q, w_mla_uq, g_mla_kv, w_mla_ukv, w_sc_conv, wa_sink, w_cf_conv, b_cf_conv,
              g_cf_ln, b_cf_ln, w_out, g_ffn2, w2_gate, w2_up, w2_down, g_final):
    b, s, d = x.shape
    rows = s // GRID_W
    row_pos = jnp.repeat(jnp.arange(rows), GRID_W).astype(jnp.float32)
    col_pos = jnp.tile(jnp.arange(GRID_W), rows).astype(jnp.float32)
    tab_mla = axial_tables(row_pos, col_pos, MLA_ROPE)
    tab_wa = axial_tables(row_pos, col_pos, WA_HEAD_DIM)

    h_lat, h_ctx = x, ctx
    for l in range(DEPTH):
        last = l == DEPTH - 1
        m_l = (jax.nn.silu(c) @ w_mod[l] + b_mod[l]).reshape(b, N_MOD, d).transpose(1, 0, 2)[:, :, None, :]
        m_c = (jax.nn.silu(c_ctx) @ w_mod[l] + b_mod[l]).reshape(N_MOD, 1, 1, d)

        h_lat = h_lat + 0.5 * m_l[2] * swiglu(modulate(rms_norm(h_lat, g_ffn1[l]), m_l[0], m_l[1]),
                                              w1_gate[l], w1_up[l], w1_down[l])
        h_ctx = h_ctx + 0.5 * m_c[2] * swiglu(modulate(rms_norm(h_ctx, g_ffn1[l]), m_c[0], m_c[1]),
                                              w1_gate[l], w1_up[l], w1_down[l])

        z_lat = split_in(modulate(rms_norm(h_lat, g_mix[l]), m_l[3], m_l[4]) @ w_in[l])
        z_ctx = split_in(modulate(rms_norm(h_ctx, g_mix[l]), m_c[3], m_c[4]) @ w_in[l])

        q_a = mla_q(z_lat[0], g_mla_q[l], w_mla_uq[l], tab_mla)
        k_a, v_a = mla_kv(z_lat[1], z_lat[2], g_mla_kv[l], w_mla_ukv[l], tab_mla)
        kc_a, vc_a = mla_kv(z_ctx[1], z_ctx[2], g_mla_kv[l], w_mla_ukv[l], None)
        o_a = dense_attend_blocked(q_a, jnp.concatenate([k_a, kc_a], axis=1),
                                   jnp.concatenate([v_a, vc_a], axis=1))
        o_b = short_conv_mix(z_lat[3], w_sc_conv[l])
        q_w = wa_q(z_lat[4], tab_wa)
        k_w, v_w = wa_kv(z_lat[5], tab_wa)
        kc_w, vc_w = wa_kv(z_ctx[5], None)
        o_c = window_attend_latent(q_w, k_w, v_w, kc_w, vc_w, wa_sink[l])
        o_d = conformer_conv_mix(z_lat[6], w_cf_conv[l], b_cf_conv[l], g_cf_ln[l], b_cf_ln[l])
        h_lat = h_lat + m_l[5] * (jnp.concatenate([o_a, o_b, o_c, o_d], axis=-1) @ w_out[l])

        if not last:
            oc_a = dense_attend(mla_q(z_ctx[0], g_mla_q[l], w_mla_uq[l], None), kc_a, vc_a)
            oc_b = short_conv_mix(z_ctx[3], w_sc_conv[l])
            oc_c = ctx_gqa_sink(wa_q(z_ctx[4], None), kc_w, vc_w, wa_sink[l])
            oc_d = conformer_conv_mix(z_ctx[6], w_cf_conv[l], b_cf_conv[l], g_cf_ln[l], b_cf_ln[l])
            h_ctx = h_ctx + m_c[5] * (jnp.concatenate([oc_a, oc_b, oc_c, oc_d], axis=-1) @ w_out[l])
            h_ctx = h_ctx + 0.5 * m_c[8] * swiglu(modulate(rms_norm(h_ctx, g_ffn2[l]), m_c[6], m_c[7]),
                                                  w2_gate[l], w2_up[l], w2_down[l])

        h_lat = h_lat + 0.5 * m_l[8] * swiglu(modulate(rms_norm(h_lat, g_ffn2[l]), m_l[6], m_l[7]),
                                              w2_gate[l], w2_up[l], w2_down[l])

    return rms_norm(h_lat, g_final)
```

```python
import contextlib
import numpy as np
import concourse.bass as bass
import concourse.mybir as mybir
from concourse.bass_utils import run_bass_kernel_spmd

F32 = mybir.dt.float32
BF16 = mybir.dt.bfloat16
AF = mybir.ActivationFunctionType
ALU = mybir.AluOpType
ENGS = ("pe", "act", "dve", "pool", "sp")

L = 2
D = 1024
S_OWN = 4096
CTX = 256
T = 512
NT = S_OWN // T
DFF = 2816
NJ = DFF // 128
NWIN = 2560
ZW = 128 + S_OWN + 128
ZCW = 128 + CTX + 128
EPS = 1e-6
NZ = 14
ZQ, ZSCB, ZSCT, ZWAQ, ZWAK, ZWAV, ZCFU = 0, 4, 6, 8, 10, 11, 12
HALO_SLOTS = (6, 7, 10, 11, 12, 13)
NEGM = -30000.0

_VSPEC = [("gf1", L * 8), ("gmix", L * 8), ("gf2", L * 8), ("bmod", L * 72), ("gfin", 8), ("gq", L * 2),
          ("gkv", L), ("wsc", L * 6), ("wcf", L * 62), ("bcf", L * 2), ("gln", L * 2), ("bln", L * 2),
          ("sink", L * 4), ("lm", 1), ("rm", 1), ("eps", 1), ("zero", 1)]
VOFF = {}
_o = 0
for _n, _w in _VSPEC:
    VOFF[_n] = _o
    _o += _w
NV = _o


class Op:
    __slots__ = ("eng", "fn", "deps", "signal", "count", "chan", "idx")

    def __init__(self, eng, fn, chan):
        self.idx = 0
        self.eng = eng
        self.fn = fn
        self.deps = set()
        self.signal = False
        self.count = 0
        self.chan = chan


class Chan:
    def __init__(self, name):
        self.name = name
        self.sem = None
        self.n = 0
        self.last = None


class Prog:
    def __init__(self):
        self.ops = {e: [] for e in ENGS}
        self.last_w = {}
        self.readers = {}
        self.chans = []

    def chan(self, name):
        c = Chan(name)
        self.chans.append(c)
        return c

    def add(self, eng, fn, reads=(), writes=(), chan=None):
        op = Op(eng, fn, chan)
        deps = set()
        for k in reads:
            w = self.last_w.get(k)
            if w is not None:
                deps.add(w)
        for k in writes:
            w = self.last_w.get(k)
            if w is not None:
                deps.add(w)
            deps.update(self.readers.get(k, ()))
        if chan is not None:
            if chan.last is not None:
                deps.add(chan.last)
            chan.last = op
            chan.n += 1
            op.count = 16 * chan.n
            op.signal = True
        deps.discard(op)
        if eng == "pe":
            deps = {d for d in deps if not (d.eng == "pe" and d.chan is None)}
        best = {}
        for d in deps:
            k = ("c", id(d.chan)) if d.chan is not None else ("e", d.eng)
            if k not in best or best[k].idx < d.idx:
                best[k] = d
        deps = set(best.values())
        for d in deps:
            d.signal = True
        op.deps = deps
        op.idx = len(self.ops[eng]) if chan is None else chan.n
        for k in reads:
            self.readers.setdefault(k, []).append(op)
        for k in writes:
            self.last_w[k] = op
            self.readers[k] = []
        self.ops[eng].append(op)
        return op

    def barrier(self):
        lasts = []
        for e in ENGS:
            for op in reversed(self.ops[e]):
                if op.chan is None and op.fn is not None:
                    lasts.append(op)
                    break
        for c in self.chans:
            if c.last is not None:
                lasts.append(c.last)
        for e in ENGS:
            op = Op(e, None, None)
            op.deps = set(lasts)
            for d in op.deps:
                d.signal = True
            self.ops[e].append(op)
        self.last_w = {}
        self.readers = {}

    def emit(self, nc, stack):
        esem = {e: stack.enter_context(nc.semaphore("s_" + e)) for e in ENGS}
        for c in self.chans:
            if c.n > 0:
                c.sem = stack.enter_context(nc.semaphore("c_" + c.name))
        for e in ENGS:
            n = 0
            for op in self.ops[e]:
                if op.chan is None and op.signal:
                    n += 1
                    op.count = n
        block = stack.enter_context(nc.Block())

        def run(e, eng):
            waited = {}
            for op in self.ops[e]:
                need = {}
                for d in op.deps:
                    if d.chan is not None:
                        s, v = d.chan.sem, d.count
                    else:
                        s, v = esem[d.eng], d.count
                    key = id(s)
                    if key not in need or need[key][1] < v:
                        need[key] = (s, v)
                for key, (s, v) in need.items():
                    if waited.get(key, 0) < v:
                        eng.wait_ge(s, v)
                        waited[key] = v
                if op.fn is None:
                    continue
                ins = op.fn(eng)
                if op.chan is not None:
                    ins.then_inc(op.chan.sem, 16)
                elif op.signal:
                    ins.then_inc(esem[e], 1)

        @block.tensor
        def _(eng):
            run("pe", eng)

        @block.scalar
        def _(eng):
            run("act", eng)

        @block.vector
        def _(eng):
            run("dve", eng)

        @block.gpsimd
        def _(eng):
            run("pool", eng)

        @block.sync
        def _(eng):
            run("sp", eng)


def build(dbg=None):
    nc = bass.Bass("TRN2", target_bir_lowering=False)
    P = Prog()
    st = contextlib.ExitStack()

    def din(name, shape):
        return nc.dram_tensor(name, list(shape), F32, kind="ExternalInput").ap()

    def dscr(name, shape, dt):
        return nc.dram_tensor(name, list(shape), dt).ap()

    xT = din("xT", [D, S_OWN])
    ctxT = din("ctxT", [D, CTX])
    cc_in = din("cc", [128, 16])
    vecs_in = din("vecs", [128, NV])
    ropeW = din("ropeW", [256, S_OWN])
    ropeQ = din("ropeQ", [192, S_OWN])
    ropeK = din("ropeK", [64, S_OWN])
    masks_in = din("masks", [128, 4 * 256])
    ident_in = din("ident", [128, 128])
    w_mod = din("w_mod", [L, D, 9 * D])
    wsrc = {n: din(n, s) for n, s in [
        ("w1_gate", [L, D, DFF]), ("w1_up", [L, D, DFF]), ("w1_down", [L, DFF, D]),
        ("w2_gate", [L, D, DFF]), ("w2_up", [L, D, DFF]), ("w2_down", [L, DFF, D]),
        ("w_in", [L, D, 2144]), ("w_out", [L, D, D]), ("w_mla_uq", [L, 192, 384]), ("w_mla_ukv", [L, 128, 512])]}
    outT = nc.dram_tensor("outT", [D, S_OWN], F32, kind="ExternalOutput").ap()
    dbg_out = None
    if dbg:
        dbg_out = nc.dram_tensor("dbg", list(dbg), F32, kind="ExternalOutput").ap()

    hbuf = dscr("hbuf", [D, S_OWN], F32)
    hcbuf = dscr("hcbuf", [D, CTX], F32)
    zbuf = dscr("zbuf", [NZ * 128, ZW], BF16)
    zcbuf = dscr("zcbuf", [NZ * 128, ZCW], BF16)
    xin_mla = dscr("xin_mla", [160, S_OWN], BF16)
    xout_mla = dscr("xout_mla", [320, S_OWN], BF16)
    ckvc = dscr("ckvc", [160, CTX], BF16)
    xin_halo = dscr("xin_halo", [768, 256], BF16)
    xout_halo = dscr("xout_halo", [1536, 256], BF16)
    wb = {}
    for l in range(L):
        for n in ("w1_gate", "w1_up", "w2_gate", "w2_up"):
            wb[n, l] = dscr(f"b_{n}{l}", [D, DFF], BF16)
        for n in ("w1_down", "w2_down"):
            wb[n, l] = dscr(f"b_{n}{l}", [DFF, D], BF16)
        wb["w_in", l] = dscr(f"b_w_in{l}", [D, NWIN], BF16)
        wb["w_out", l] = dscr(f"b_w_out{l}", [D, D], BF16)
        wb["w_mla_uq", l] = dscr(f"b_wuq{l}", [192, 768], BF16)
        wb["w_mla_ukv", l] = dscr(f"b_wukv{l}", [128, 512], BF16)

    ARENA = 53000
    arena = st.enter_context(nc.sbuf_tensor("arena", [128, ARENA], F32))
    bump = [0]

    def alloc(cols, dt=F32, shape=None):
        words = cols if dt == F32 else (cols + 1) // 2
        a = bump[0]
        bump[0] += words
        assert bump[0] <= ARENA, ("SBUF arena overflow", bump[0])
        ap = arena[:, a:a + words]
        if dt != F32:
            ap = ap.bitcast(dt)[:, 0:cols]
        if shape is not None:
            names = " ".join(f"d{i}" for i in range(len(shape)))
            kw = {f"d{i}": s for i, s in enumerate(shape)}
            ap = ap.rearrange(f"p ({names}) -> p {names}", **kw)
        return ap

    ps = [st.enter_context(nc.psum_tensor(f"ps{i}", [128, 512], F32)) for i in range(8)]

    def PSK(i):
        return ("ps", i)

    vecs = alloc(NV)
    ccs = alloc(16)
    modv = alloc(L * 144, shape=[L, 72, 2])
    Acoef = alloc(L * 2 * 3 * 8, shape=[L, 2, 3, 8])
    HG = alloc(L * 2 * 3 * 8, shape=[L, 2, 3, 8])
    esink = alloc(L * 4, shape=[L, 4])
    ones_b = alloc(128, BF16)
    ones_f = alloc(128)
    ident_b = alloc(128, BF16)
    masks_b = alloc(4 * 256, BF16, shape=[4, 256])
    persist_end = bump[0]

    def V(name, off=0, w=1):
        o = VOFF[name] + off
        return vecs[:, o:o + w]

    chn = {}

    def CH(name):
        if name not in chn:
            chn[name] = P.chan(name)
        return chn[name]

    def dma(q, out, in_, reads, writes, ch):
        return P.add(q, lambda e: e.dma_start(out=out, in_=in_), reads=reads, writes=writes, chan=CH(ch))

    dma("sp", vecs, vecs_in, [], ["vecs"], "ld0")
    dma("sp", ccs, cc_in, [], ["ccs"], "ld1")
    dma("pool", ident_b, ident_in, [], ["ident"], "cv0")
    dma("pool", masks_b.rearrange("p a b -> p (a b)"), masks_in, [], ["masks"], "cv1")
    P.add("dve", lambda e: e.memset(ones_b, 1.0), writes=["ones_b"])
    P.add("dve", lambda e: e.memset(ones_f, 1.0), writes=["ones_f"])
    P.add("act", lambda e: e.activation(out=ccs, in_=ccs, func=AF.Silu), reads=["ccs"], writes=["ccs"])
    P.add("act", lambda e: e.activation(out=esink.rearrange("p a b -> p (a b)"), in_=V("sink", 0, L * 4), func=AF.Exp),
          reads=["vecs"], writes=["esink"])

    cvn = [0]

    def conv(out, in_, key):
        cvn[0] += 1
        dma("pool", out, in_, [], [key], f"cv{cvn[0] % 8}")

    def conv_rows(name, l, nrows, dst=None, c0=0, c1=None, d0=0):
        src = wsrc[name]
        c1 = c1 if c1 is not None else src.shape[2]
        dstap = wb[name, l] if dst is None else dst
        for r in range(0, nrows, 128):
            rr = min(128, nrows - r)
            conv(dstap[r:r + rr, d0:d0 + (c1 - c0)], src[l, r:r + rr, c0:c1], (name, l, r // 128))

    def conv_layer_ffn(l, which):
        conv_rows(f"w{which}_gate", l, D)
        conv_rows(f"w{which}_up", l, D)
        conv_rows(f"w{which}_down", l, DFF)

    def conv_layer_mix(l):
        src = wsrc["w_in"]
        dst = wb["w_in", l]
        segs = [(0, 1120, 0), (1120, 1184, 1120), (1248, 1312, 1184), (1184, 1248, 1248), (1312, 1376, 1312),
                (1376, 2144, 1376)]
        for b8, s8 in enumerate([8, 0, 24, 16]):
            segs.append((320 + s8, 320 + s8 + 8, 2144 + 8 * b8))
        for hi, hsrc in enumerate([1120, 1248, 1184, 1312, 1376, 1440]):
            for b16, s16 in enumerate([16, 0, 48, 32]):
                segs.append((hsrc + s16, hsrc + s16 + 16, 2176 + 64 * hi + 16 * b16))
        for (c0, c1, d0) in segs:
            for r in range(0, D, 512):
                conv(dst[r:r + 512, d0:d0 + (c1 - c0)], src[l, r:r + 512, c0:c1], ("w_in", l))
        conv_rows("w_out", l, D)
        srcq = wsrc["w_mla_uq"]
        dq = wb["w_mla_uq", l]
        conv(dq[0:128, 0:384], srcq[l, 0:128, :], ("w_mla_uq", l))
        conv(dq[128:192, 0:384], srcq[l, 128:192, :], ("w_mla_uq", l))
        for h in range(4):
            conv(dq[0:192, 384 + h * 96:384 + h * 96 + 64], srcq[l, :, h * 96:h * 96 + 64], ("w_mla_uq", l))
            for b8, s8 in enumerate([8, 0, 24, 16]):
                conv(dq[0:192, 384 + h * 96 + 64 + 8 * b8:384 + h * 96 + 64 + 8 * b8 + 8],
                     srcq[l, :, h * 96 + 64 + s8:h * 96 + 64 + s8 + 8], ("w_mla_uq", l))
        conv(wb["w_mla_ukv", l][:, :], wsrc["w_mla_ukv"][l, :, :], ("w_mla_ukv", l))

    conv_layer_ffn(0, 1)
    conv_layer_mix(0)
    conv_layer_ffn(0, 2)
    conv_layer_ffn(1, 1)
    conv_layer_mix(1)
    conv_layer_ffn(1, 2)

    def WK(name, l):
        if name == "w_in" or name.startswith("w_mla"):
            return [(name, l)]
        n = DFF if name.endswith("down") else D
        return [(name, l, r) for r in range((n + 127) // 128)]

    def setup_mod():
        base = bump[0]
        wm = [alloc(8 * 1024, shape=[8, 1024]) for _ in range(2)]
        for l in range(L):
            for og in range(9):
                slot = wm[(l * 9 + og) % 2]
                for kc in range(8):
                    dma("sp", slot[:, kc, :], w_mod[l, kc * 128:(kc + 1) * 128, og * 1024:(og + 1) * 1024],
                        [], [("wm", (l * 9 + og) % 2, kc)], f"wm{(l * 9 + og) % 2}")
                for oc in range(8):
                    col = (og * 8 + oc) * 2
                    for kc in range(8):
                        P.add("pe", lambda e, slot=slot, kc=kc, oc=oc, col=col: e.matmul(
                            ps[0][:, col:col + 2], lhsT=slot[:, kc, oc * 128:(oc + 1) * 128],
                            rhs=ccs[:, kc * 2:kc * 2 + 2], start=(kc == 0), stop=(kc == 7)),
                            reads=[("wm", (l * 9 + og) % 2, kc), "ccs"], writes=[PSK(0)])
            for j in range(2):
                P.add("dve", lambda e, l=l, j=j: e.tensor_tensor(
                    out=modv[:, l, :, j], in0=ps[0][:, 0:144].rearrange("p (a b) -> p a b", b=2)[:, :, j],
                    in1=V("bmod", l * 72, 72), op=ALU.add), reads=[PSK(0), "vecs"], writes=["modv"])
            for j in range(2):
                for s, gname in enumerate(("gf1", "gmix", "gf2")):
                    i_scale = 3 * s + 1
                    P.add("dve", lambda e, l=l, j=j, s=s, gname=gname, i_scale=i_scale: e.scalar_tensor_tensor(
                        out=Acoef[:, l, j, s, :], in0=modv[:, l, i_scale * 8:(i_scale + 1) * 8, j], scalar=1.0,
                        in1=V(gname, l * 8, 8), op0=ALU.add, op1=ALU.mult), reads=["modv", "vecs"], writes=["coef"])
                    i_gate = 3 * s + 2
                    P.add("dve", lambda e, l=l, j=j, s=s, i_gate=i_gate: e.tensor_scalar(
                        out=HG[:, l, j, s, :], in0=modv[:, l, i_gate * 8:(i_gate + 1) * 8, j],
                        scalar1=(1.0 if s == 1 else 0.5), scalar2=None, op0=ALU.mult), reads=["modv"], writes=["coef"])
        P.barrier()
        bump[0] = base

    setup_mod()

    def Bsh(l, j, s):
        return modv[:, l, (3 * s) * 8:(3 * s + 1) * 8, j]

    class FFNBufs:
        pass

    def alloc_common(Tm):
        B = FFNBufs()
        B.ht = alloc(8 * Tm, shape=[8, Tm])
        B.xn = alloc(8 * Tm, BF16, shape=[8, Tm])
        B.sq = [alloc(Tm, BF16) for _ in range(2)]
        B.rstd = alloc(Tm)
        B.tmp = [alloc(Tm) for _ in range(2)]
        return B

    def norm_mod(B, Tn, Avec, Bvec, tag):
        for kc in range(8):
            P.add("act", lambda e, kc=kc: e.activation(out=B.sq[kc % 2][:, 0:Tn], in_=B.ht[:, kc, 0:Tn], func=AF.Square),
                  reads=[("ht", kc)], writes=[("sq", kc % 2)])
            P.add("pe", lambda e, kc=kc: e.matmul(ps[0][:, 0:Tn], lhsT=ones_b, rhs=B.sq[kc % 2][:, 0:Tn],
                                                 start=(kc == 0), stop=(kc == 7)),
                  reads=[("sq", kc % 2), "ones_b"], writes=[PSK(0)])
        P.add("act", lambda e: e.activation(out=B.rstd[:, 0:Tn], in_=ps[0][:, 0:Tn], func=AF.Sqrt, bias=V("eps"), scale=1.0 / D),
              reads=[PSK(0), "vecs"], writes=["rstd"])
        P.add("dve", lambda e: e.reciprocal(out=B.rstd[:, 0:Tn], in_=B.rstd[:, 0:Tn]), reads=["rstd"], writes=["rstd"])
        for kc in range(8):
            P.add("dve", lambda e, kc=kc: e.tensor_tensor(out=B.tmp[kc % 2][:, 0:Tn], in0=B.ht[:, kc, 0:Tn], in1=B.rstd[:, 0:Tn],
                                                         op=ALU.mult), reads=[("ht", kc), "rstd"], writes=[("tmp", kc % 2)])
            bias = Bvec[:, kc:kc + 1] if Bvec is not None else V("zero")
            P.add("act", lambda e, kc=kc, bias=bias: e.activation(out=B.xn[:, kc, 0:Tn], in_=B.tmp[kc % 2][:, 0:Tn], func=AF.Identity,
                                                                  bias=bias, scale=Avec[:, kc:kc + 1]),
                  reads=[("tmp", kc % 2), "coef", "modv", "vecs"], writes=[("xn", kc)])

    JG = 2
    NG = (NJ + JG - 1) // JG

    def alloc_ffn(B, Tm):
        B.H = alloc(NJ * Tm, BF16, shape=[NJ, Tm])
        B.wg = [alloc(8 * JG * 128, BF16, shape=[8, JG * 128]) for _ in range(2)]
        B.wu = [alloc(8 * JG * 128, BF16, shape=[8, JG * 128]) for _ in range(2)]
        B.wd = alloc(NJ * D, BF16, shape=[NJ, D])
        B.sg = [alloc(Tm) for _ in range(2)]

    gcount = [0]

    def ffn(B, Tn, l, which, hg):
        wgd, wud, wdd = wb[f"w{which}_gate", l], wb[f"w{which}_up", l], wb[f"w{which}_down", l]
        kg, ku, kd = WK(f"w{which}_gate", l), WK(f"w{which}_up", l), WK(f"w{which}_down", l)

        def load_group(g):
            slot = (gcount[0] + g) % 2
            j0 = g * JG
            nj = min(JG, NJ - j0)
            dma("sp", B.wg[slot][:, :, 0:nj * 128], wgd[:, j0 * 128:(j0 + nj) * 128].rearrange("(k p) m -> p k m", p=128),
                kg, [("wg", slot)], f"wg{slot}")
            dma("sp", B.wu[slot][:, :, 0:nj * 128], wud[:, j0 * 128:(j0 + nj) * 128].rearrange("(k p) m -> p k m", p=128),
                ku, [("wu", slot)], f"wu{slot}")

        load_group(0)
        for g in range(NG):
            slot = (gcount[0] + g) % 2
            j0 = g * JG
            nj = min(JG, NJ - j0)
            if g + 1 < NG:
                load_group(g + 1)
            dma("sp", B.wd[:, j0:j0 + nj, :], wdd[j0 * 128:(j0 + nj) * 128, :].rearrange("(j p) m -> p j m", p=128),
                kd, [("wd", g)], "wd")
            for jj in range(nj):
                j = j0 + jj
                bg, bu = 1 + (j % 2), 3 + (j % 2)
                for kc in range(8):
                    P.add("pe", lambda e, kc=kc, jj=jj, slot=slot, bg=bg: e.matmul(
                        ps[bg][:, 0:Tn], lhsT=B.wg[slot][:, kc, jj * 128:(jj + 1) * 128], rhs=B.xn[:, kc, 0:Tn],
                        start=(kc == 0), stop=(kc == 7)), reads=[("wg", slot), ("xn", kc)], writes=[PSK(bg)])
                for kc in range(8):
                    P.add("pe", lambda e, kc=kc, jj=jj, slot=slot, bu=bu: e.matmul(
                        ps[bu][:, 0:Tn], lhsT=B.wu[slot][:, kc, jj * 128:(jj + 1) * 128], rhs=B.xn[:, kc, 0:Tn],
                        start=(kc == 0), stop=(kc == 7)), reads=[("wu", slot), ("xn", kc)], writes=[PSK(bu)])
                P.add("act", lambda e, j=j, bg=bg: e.activation(out=B.sg[j % 2][:, 0:Tn], in_=ps[bg][:, 0:Tn], func=AF.Silu),
                      reads=[PSK(bg)], writes=[("sg", j % 2)])
                P.add("dve", lambda e, j=j, bu=bu: e.tensor_tensor(out=B.H[:, j, 0:Tn], in0=ps[bu][:, 0:Tn], in1=B.sg[j % 2][:, 0:Tn],
                                                                  op=ALU.mult), reads=[PSK(bu), ("sg", j % 2)], writes=[("H", j)])
        gcount[0] += NG
        for c in range(8):
            bd = 5 + (c % 2)
            for j in range(NJ):
                P.add("pe", lambda e, c=c, j=j, bd=bd: e.matmul(ps[bd][:, 0:Tn], lhsT=B.wd[:, j, c * 128:(c + 1) * 128],
                                                               rhs=B.H[:, j, 0:Tn], start=(j == 0), stop=(j == NJ - 1)),
                      reads=[("wd", j // JG), ("H", j)], writes=[PSK(bd)])
            P.add("dve", lambda e, c=c, bd=bd: e.scalar_tensor_tensor(out=B.ht[:, c, 0:Tn], in0=ps[bd][:, 0:Tn], scalar=hg[:, c:c + 1],
                                                                     in1=B.ht[:, c, 0:Tn], op0=ALU.mult, op1=ALU.add),
                  reads=[PSK(bd), ("ht", c), "coef"], writes=[("ht", c)])

    def alloc_win(B, Tm):
        B.win = alloc(8 * NWIN, BF16, shape=[8, NWIN])
        B.wuq = alloc(2 * 768, BF16, shape=[2, 768])
        B.zst = alloc(NZ * Tm, BF16, shape=[NZ, Tm])
        P.add("dve", lambda e: e.memset(B.zst, 0.0), writes=[("zst", s_) for s_ in range(NZ)])
        B.mst = alloc(2 * Tm, BF16, shape=[2, Tm])
        B.cqn = alloc(2 * Tm, BF16, shape=[2, Tm])
        B.f1 = [alloc(Tm) for _ in range(3)]
        B.rW = alloc(2 * Tm, shape=[2, Tm])
        B.rQ = alloc(2 * Tm, shape=[2, Tm])
        B.rK = alloc(2 * Tm, shape=[2, Tm])


    rr = [0]

    def nb():
        rr[0] = rr[0] % 7 + 1
        return rr[0]

    def win_proj(B, Tn, l, rope, tok0, zdst, zcol0, mdst, mcol0, halo):
        if rope:
            dma("sp", B.rW[:, :, 0:Tn], ropeW.rearrange("(a p) c -> p a c", p=128)[:, :, tok0:tok0 + Tn], [], ["rW"], "rW")
            dma("sp", B.rQ[0:96, :, 0:Tn], ropeQ.rearrange("(a p) c -> p a c", p=96)[:, :, tok0:tok0 + Tn], [], ["rQ"], "rQ")
            dma("sp", B.rK[0:32, :, 0:Tn], ropeK.rearrange("(a p) c -> p a c", p=32)[:, :, tok0:tok0 + Tn], [], ["rK"], "rK")

        def group(M, col0):
            b = nb()
            for kc in range(8):
                P.add("pe", lambda e, kc=kc, b=b: e.matmul(ps[b][0:M, 0:Tn], lhsT=B.win[:, kc, col0:col0 + M], rhs=B.xn[:, kc, 0:Tn],
                                                          start=(kc == 0), stop=(kc == 7)),
                      reads=["win", ("xn", kc)], writes=[PSK(b)])
            return b

        fi = [0]

        def ftmp():
            fi[0] = (fi[0] + 1) % 3
            return fi[0]

        def copy_out(b, M, dst, key):
            P.add("act", lambda e: e.activation(out=dst, in_=ps[b][0:M, 0:Tn], func=AF.Copy), reads=[PSK(b)], writes=[key])

        def rope_out(bA, bB, M, tab, tabkey, dst, key):
            i0, i1 = ftmp(), ftmp()
            P.add("dve", lambda e: e.tensor_tensor(out=B.f1[i0][0:M, 0:Tn], in0=ps[bA][0:M, 0:Tn], in1=tab[0:M, 0, 0:Tn], op=ALU.mult),
                  reads=[PSK(bA), tabkey], writes=[("f1", i0)])
            P.add("dve", lambda e: e.tensor_tensor(out=B.f1[i1][0:M, 0:Tn], in0=ps[bB][0:M, 0:Tn], in1=tab[0:M, 1, 0:Tn], op=ALU.mult),
                  reads=[PSK(bB), tabkey], writes=[("f1", i1)])
            P.add("dve", lambda e: e.tensor_tensor(out=dst, in0=B.f1[i0][0:M, 0:Tn], in1=B.f1[i1][0:M, 0:Tn], op=ALU.add),
                  reads=[("f1", i0), ("f1", i1)], writes=[key])

        def rstd_from(bank_ss, n):
            P.add("act", lambda e: e.activation(out=B.rstd[:, 0:Tn], in_=ps[bank_ss][:, 0:Tn], func=AF.Sqrt, bias=V("eps"), scale=1.0 / n),
                  reads=[PSK(bank_ss), "vecs"], writes=["rstd"])
            P.add("dve", lambda e: e.reciprocal(out=B.rstd[:, 0:Tn], in_=B.rstd[:, 0:Tn]), reads=["rstd"], writes=["rstd"])

        b0 = group(128, 0)
        b1 = group(64, 128)
        P.add("act", lambda e: e.activation(out=B.sq[0][:, 0:Tn], in_=ps[b0][:, 0:Tn], func=AF.Square), reads=[PSK(b0)], writes=[("sq", 0)])
        P.add("act", lambda e: e.activation(out=B.sq[1][0:64, 0:Tn], in_=ps[b1][0:64, 0:Tn], func=AF.Square), reads=[PSK(b1)], writes=[("sq", 1)])
        P.add("pe", lambda e: e.matmul(ps[0][:, 0:Tn], lhsT=ones_b, rhs=B.sq[0][:, 0:Tn], start=True, stop=False),
              reads=[("sq", 0), "ones_b"], writes=[PSK(0)])
        P.add("pe", lambda e: e.matmul(ps[0][:, 0:Tn], lhsT=ones_b[0:64, :], rhs=B.sq[1][0:64, 0:Tn], start=False, stop=True),
              reads=[("sq", 1), "ones_b"], writes=[PSK(0)])
        rstd_from(0, 192)
        P.add("dve", lambda e: e.scalar_tensor_tensor(out=B.cqn[:, 0, 0:Tn], in0=ps[b0][:, 0:Tn], scalar=V("gq", l * 2, 1), in1=B.rstd[:, 0:Tn],
                                                     op0=ALU.mult, op1=ALU.mult), reads=[PSK(b0), "rstd", "vecs"], writes=[("cqn", 0)])
        P.add("dve", lambda e: e.scalar_tensor_tensor(out=B.cqn[0:64, 1, 0:Tn], in0=ps[b1][0:64, 0:Tn], scalar=V("gq", l * 2 + 1, 1)[0:64, :],
                                                     in1=B.rstd[0:64, 0:Tn], op0=ALU.mult, op1=ALU.mult),
              reads=[PSK(b1), "rstd", "vecs"], writes=[("cqn", 1)])
        for h in range(4):
            banks = []
            for rot in ([0, 1] if rope else [0]):
                b = nb()
                c0 = rot * 384 + h * 96
                P.add("pe", lambda e, b=b, c0=c0: e.matmul(ps[b][0:96, 0:Tn], lhsT=B.wuq[:, 0, c0:c0 + 96], rhs=B.cqn[:, 0, 0:Tn], start=True, stop=False),
                      reads=["wuq", ("cqn", 0)], writes=[PSK(b)])
                P.add("pe", lambda e, b=b, c0=c0: e.matmul(ps[b][0:96, 0:Tn], lhsT=B.wuq[0:64, 1, c0:c0 + 96], rhs=B.cqn[0:64, 1, 0:Tn], start=False, stop=True),
                      reads=["wuq", ("cqn", 1)], writes=[PSK(b)])
                banks.append(b)
            if rope:
                rope_out(banks[0], banks[1], 96, B.rQ, "rQ", B.zst[0:96, ZQ + h, 0:Tn], ("zst", ZQ + h))
            else:
                copy_out(banks[0], 96, B.zst[0:96, ZQ + h, 0:Tn], ("zst", ZQ + h))
        bkv = group(128, 192)
        P.add("act", lambda e: e.activation(out=B.sq[0][:, 0:Tn], in_=ps[bkv][:, 0:Tn], func=AF.Square), reads=[PSK(bkv)], writes=[("sq", 0)])
        P.add("pe", lambda e: e.matmul(ps[0][:, 0:Tn], lhsT=ones_b, rhs=B.sq[0][:, 0:Tn], start=True, stop=True),
              reads=[("sq", 0), "ones_b"], writes=[PSK(0)])
        rstd_from(0, 128)
        P.add("dve", lambda e: e.scalar_tensor_tensor(out=B.mst[:, 0, 0:Tn], in0=ps[bkv][:, 0:Tn], scalar=V("gkv", l, 1), in1=B.rstd[:, 0:Tn],
                                                     op0=ALU.mult, op1=ALU.mult), reads=[PSK(bkv), "rstd", "vecs"], writes=[("mst", 0)])
        bA = group(32, 320)
        if rope:
            bB = group(32, 2144)
            rope_out(bA, bB, 32, B.rK, "rK", B.mst[0:32, 1, 0:Tn], ("mst", 1))
        else:
            copy_out(bA, 32, B.mst[0:32, 1, 0:Tn], ("mst", 1))
        for c in range(2):
            b = group(128, 352 + c * 128)
            copy_out(b, 128, B.zst[:, ZSCB + c, 0:Tn], ("zst", ZSCB + c))
        for c in range(2):
            bc = group(128, 608 + c * 128)
            bx = group(128, 864 + c * 128)
            i = ftmp()
            P.add("act", lambda e, i=i, bx=bx: e.activation(out=B.f1[i][:, 0:Tn], in_=ps[bx][:, 0:Tn], func=AF.Copy), reads=[PSK(bx)], writes=[("f1", i)])
            P.add("dve", lambda e, i=i, bc=bc, c=c: e.tensor_tensor(out=B.zst[:, ZSCT + c, 0:Tn], in0=ps[bc][:, 0:Tn], in1=B.f1[i][:, 0:Tn], op=ALU.mult),
                  reads=[PSK(bc), ("f1", i)], writes=[("zst", ZSCT + c)])
        for c in range(2):
            bA = group(128, 1120 + c * 128)
            if rope:
                bB = group(128, 2176 + c * 128)
                rope_out(bA, bB, 128, B.rW, "rW", B.zst[:, ZWAQ + c, 0:Tn], ("zst", ZWAQ + c))
            else:
                copy_out(bA, 128, B.zst[:, ZWAQ + c, 0:Tn], ("zst", ZWAQ + c))
        bA = group(128, 1376)
        if rope:
            bB = group(128, 2432)
            rope_out(bA, bB, 128, B.rW, "rW", B.zst[:, ZWAK, 0:Tn], ("zst", ZWAK))
        else:
            copy_out(bA, 128, B.zst[:, ZWAK, 0:Tn], ("zst", ZWAK))
        b = group(128, 1504)
        copy_out(b, 128, B.zst[:, ZWAV, 0:Tn], ("zst", ZWAV))
        for c in range(2):
            ba = group(128, 1632 + c * 128)
            bg = group(128, 1888 + c * 128)
            i = ftmp()
            P.add("act", lambda e, i=i, bg=bg: e.activation(out=B.f1[i][:, 0:Tn], in_=ps[bg][:, 0:Tn], func=AF.Sigmoid), reads=[PSK(bg)], writes=[("f1", i)])
            P.add("dve", lambda e, i=i, ba=ba, c=c: e.tensor_tensor(out=B.zst[:, ZCFU + c, 0:Tn], in0=ps[ba][:, 0:Tn], in1=B.f1[i][:, 0:Tn], op=ALU.mult),
                  reads=[PSK(ba), ("f1", i)], writes=[("zst", ZCFU + c)])
        zk = [("zst", s) for s in range(NZ)]
        dma("sp", zdst.rearrange("(s p) c -> p s c", p=128)[:, :, zcol0:zcol0 + Tn], B.zst[:, :, 0:Tn], zk, ["zdst"], "zst")
        dma("sp", mdst[0:128, mcol0:mcol0 + Tn], B.mst[:, 0, 0:Tn], [("mst", 0)], ["mdst"], "mst0")
        dma("sp", mdst[128:160, mcol0:mcol0 + Tn], B.mst[0:32, 1, 0:Tn], [("mst", 1)], ["mdst"], "mst1")
        for (which, c0) in halo:
            xh = xin_halo.rearrange("(s p) c -> p s c", p=128)
            dma("sp", xh[:, 0:2, which * 128:(which + 1) * 128], B.zst[:, 6:8, c0:c0 + 128], zk, ["xin_halo"], "hal0")
            dma("sp", xh[:, 2:6, which * 128:(which + 1) * 128], B.zst[:, 10:14, c0:c0 + 128], zk, ["xin_halo"], "hal1")

    def load_win(B, l):
        dma("sp", B.win[:, 0:4, :], wb["w_in", l][0:512, :].rearrange("(k p) m -> p k m", p=128), WK("w_in", l), ["win"], "win0")
        dma("sp", B.win[:, 4:8, :], wb["w_in", l][512:1024, :].rearrange("(k p) m -> p k m", p=128), WK("w_in", l), ["win"], "win1")
        dma("sp", B.wuq[:, 0, :], wb["w_mla_uq", l][0:128, :], WK("w_mla_uq", l), ["wuq"], "wuq0")
        dma("sp", B.wuq[0:64, 1, :], wb["w_mla_uq", l][128:192, :], WK("w_mla_uq", l), ["wuq"], "wuq1")

    def load_h(B, src, col0, Tn):
        dma("sp", B.ht[:, :, 0:Tn], src.rearrange("(k p) c -> p k c", p=128)[:, :, col0:col0 + Tn], ["hsrc"], [("ht", k) for k in range(8)], "hld")

    def store_h(B, dst, col0, Tn, key="hdst"):
        dma("sp", dst.rearrange("(k p) c -> p k c", p=128)[:, :, col0:col0 + Tn], B.ht[:, :, 0:Tn], [("ht", k) for k in range(8)], [key], "hst")

    def tiles_lat_ctx():
        out = [(False, 0, T, t * T) for t in range(NT)]
        out.append((True, 1, CTX, 0))
        return out

    def phase_ffn(l_prev, l_next, first):
        base = bump[0]
        B = alloc_common(T)
        alloc_ffn(B, T)
        if l_next is not None:
            alloc_win(B, T)
            load_win(B, l_next)
        for (is_ctx, j, Tn, tok0) in tiles_lat_ctx():
            if first:
                load_h(B, ctxT if is_ctx else xT, tok0, Tn)
            else:
                if is_ctx and l_next is None:
                    continue
                load_h(B, hcbuf if is_ctx else hbuf, tok0, Tn)
            if l_prev is not None:
                norm_mod(B, Tn, Acoef[:, l_prev, j, 2, :], Bsh(l_prev, j, 2), "f2")
                ffn(B, Tn, l_prev, 2, HG[:, l_prev, j, 2, :])
            if l_next is not None:
                norm_mod(B, Tn, Acoef[:, l_next, j, 0, :], Bsh(l_next, j, 0), "f1")
                ffn(B, Tn, l_next, 1, HG[:, l_next, j, 0, :])
                store_h(B, hcbuf if is_ctx else hbuf, tok0, Tn)
                norm_mod(B, Tn, Acoef[:, l_next, j, 1, :], Bsh(l_next, j, 1), "mx")
                halo = []
                if not is_ctx and tok0 == 0:
                    halo.append((0, 0))
                if not is_ctx and tok0 == S_OWN - T:
                    halo.append((1, T - 128))
                win_proj(B, Tn, l_next, not is_ctx, tok0, zcbuf if is_ctx else zbuf, 128 + tok0,
                         ckvc if is_ctx else xin_mla, tok0, halo)
            else:
                norm_mod(B, Tn, V("gfin", 0, 8), None, "fin")
                for kc in range(8):
                    P.add("dve", lambda e, kc=kc, Tn=Tn: e.tensor_tensor(out=B.tmp[kc % 2][:, 0:Tn], in0=B.ht[:, kc, 0:Tn], in1=B.rstd[:, 0:Tn], op=ALU.mult),
                          reads=[("ht", kc), "rstd"], writes=[("tmp", kc % 2)])
                    P.add("dve", lambda e, kc=kc, Tn=Tn: e.tensor_scalar(out=B.ht[:, kc, 0:Tn], in0=B.tmp[kc % 2][:, 0:Tn], scalar1=V("gfin", kc, 1), scalar2=None,
                                                                 op0=ALU.mult), reads=[("tmp", kc % 2), "vecs"], writes=[("ht", kc)])
                store_h(B, outT, tok0, Tn, key="out")
        P.barrier()
        bump[0] = base

    RG = RG_OVERRIDE[0] or [[0, 1], [2, 3], [4, 5], [6, 7]]

    def exchange(l):
        base = bump[0]
        P.add("pool", lambda e: e.collective_compute("AllGather", ALU.bypass, replica_groups=RG, ins=[xin_mla.opt()], outs=[xout_mla.opt()]),
              reads=["mdst"], writes=["xout_mla"]).signal = True
        P.add("pool", lambda e: e.collective_compute("AllGather", ALU.bypass, replica_groups=RG, ins=[xin_halo.opt()], outs=[xout_halo.opt()]),
              reads=["xin_halo"], writes=["xout_halo"]).signal = True
        hl = alloc(6 * 128, BF16, shape=[6, 128])
        hr = alloc(6 * 128, BF16, shape=[6, 128])
        xo = xout_halo.rearrange("(r s p) c -> r p s c", r=2, p=128)
        dma("sp", hl, xo[0, :, :, 128:256], ["xout_halo"], ["hl"], "hl")
        dma("sp", hr, xo[1, :, :, 0:128], ["xout_halo"], ["hr"], "hr")
        P.add("dve", lambda e: e.tensor_scalar(out=hl, in0=hl, scalar1=V("lm"), scalar2=None, op0=ALU.mult), reads=["hl", "vecs"], writes=["hl"])
        P.add("dve", lambda e: e.tensor_scalar(out=hr, in0=hr, scalar1=V("rm"), scalar2=None, op0=ALU.mult), reads=["hr", "vecs"], writes=["hr"])
        zb = zbuf.rearrange("(s p) c -> p s c", p=128)
        dma("sp", zb[:, 6:8, 0:128], hl[:, 0:2, :], ["hl"], ["zdst"], "hl")
        dma("sp", zb[:, 10:14, 0:128], hl[:, 2:6, :], ["hl"], ["zdst"], "hl")
        dma("sp", zb[:, 6:8, 128 + S_OWN:ZW], hr[:, 0:2, :], ["hr"], ["zdst"], "hr")
        dma("sp", zb[:, 10:14, 128 + S_OWN:ZW], hr[:, 2:6, :], ["hr"], ["zdst"], "hr")
        P.barrier()
        bump[0] = base

    NKC = 66

    def alloc_kv():
        K = FFNBufs()
        K.KT = alloc(4 * NKC * 128, BF16, shape=[4, NKC * 128])
        K.Vx = alloc(NKC * 384, BF16, shape=[NKC, 384])
        K.wukv = alloc(512, BF16)
        K.ckt = [alloc(512, BF16) for _ in range(2)]
        return K

    def kv_build(K, l):
        dma("sp", K.wukv, wb["w_mla_ukv", l], WK("w_mla_ukv", l), ["wukv"], "wukv")
        P.add("dve", lambda e: e.memset(K.Vx, 0.0), writes=["Vx"])
        P.add("dve", lambda e: e.memset(K.Vx.rearrange("p k (a c) -> p k a c", a=2)[:, :, :, 64:65], 1.0), writes=["Vx"])
        srcs = [(xout_mla[r * 160:r * 160 + 128, t8 * 512:(t8 + 1) * 512], 512, r * S_OWN + t8 * 512) for r in range(2) for t8 in range(8)]
        srcs.append((ckvc[0:128, 0:CTX], CTX, 2 * S_OWN))
        for r in range(2):
            for h in range(4):
                dma("sp", K.KT[64:96, h, r * S_OWN:(r + 1) * S_OWN], xout_mla[r * 160 + 128:r * 160 + 160, :], ["xout_mla"], [("KTr", h)], f"ktr{h}")
        for h in range(4):
            dma("sp", K.KT[64:96, h, 2 * S_OWN:2 * S_OWN + CTX], ckvc[128:160, :], ["mdstc"], [("KTr", h)], f"ktr{h}")
        wv = K.wukv.rearrange("p (h c) -> p h c", h=4)[:, :, 64:128]
        for i, (src, n, key0) in enumerate(srcs):
            slot = i % 2
            dma("sp", K.ckt[slot][:, 0:n], src, ["xout_mla", "mdstc"], [("ckt", slot)], f"ckt{slot}")
            for h in range(4):
                b = 1 + (h % 2)
                P.add("pe", lambda e, h=h, b=b, slot=slot, n=n: e.matmul(ps[b][0:64, 0:n], lhsT=K.wukv[:, h * 128:h * 128 + 64], rhs=K.ckt[slot][:, 0:n],
                                                                        start=True, stop=True), reads=["wukv", ("ckt", slot)], writes=[PSK(b)])
                eng = "act" if h % 2 == 0 else "dve"
                if eng == "act":
                    P.add("act", lambda e, h=h, b=b, n=n, key0=key0: e.activation(out=K.KT[0:64, h, key0:key0 + n], in_=ps[b][0:64, 0:n], func=AF.Copy),
                          reads=[PSK(b)], writes=[("KTn", h)])
                else:
                    P.add("dve", lambda e, h=h, b=b, n=n, key0=key0: e.tensor_copy(out=K.KT[0:64, h, key0:key0 + n], in_=ps[b][0:64, 0:n]),
                          reads=[PSK(b)], writes=[("KTn", h)])
            for kb in range(n // 128):
                b = 3 + (kb % 2)
                kc = key0 // 128 + kb
                P.add("pe", lambda e, kb=kb, b=b, slot=slot: e.matmul(ps[b][:, 0:256], lhsT=K.ckt[slot][:, kb * 128:(kb + 1) * 128], rhs=wv,
                                                                     start=True, stop=True), reads=["wukv", ("ckt", slot)], writes=[PSK(b)])
                pv = ps[b][:, 0:256].rearrange("p (a b c) -> p a b c", a=2, b=2)
                vo = K.Vx[:, kc, :].rearrange("p (a b c) -> p a b c", a=2, b=3)
                P.add("act", lambda e, pv=pv, vo=vo: e.activation(out=vo[:, :, 0, :], in_=pv[:, :, 0, :], func=AF.Copy), reads=[PSK(b)], writes=["Vx"])
                P.add("dve", lambda e, pv=pv, vo=vo: e.tensor_copy(out=vo[:, :, 2, :], in_=pv[:, :, 1, :]), reads=[PSK(b)], writes=["Vx"])

    def alloc_mix():
        M = FFNBufs()
        M.ht = alloc(8 * T, shape=[8, T])
        M.mix = alloc(8 * T, BF16, shape=[8, T])
        M.wout = alloc(8 * D, BF16, shape=[8, D])
        M.QT = alloc(4 * T, BF16, shape=[4, T])
        M.PT = [alloc(T, BF16) for _ in range(3)]
        M.scb = alloc(2 * T, BF16, shape=[2, T])
        M.sct = alloc(2 * (T + 2), BF16, shape=[2, T + 2])
        M.waq = alloc(2 * T, BF16, shape=[2, T])
        M.wak = alloc(T + 256, BF16)
        M.wav = alloc(T + 256, BF16)
        M.cfu = alloc(2 * (T + 32), BF16, shape=[2, T + 32])
        M.Vw = alloc(6 * 384, BF16, shape=[6, 384])
        M.wakc = alloc(CTX, BF16)
        M.wavc = alloc(CTX, BF16)
        M.Vwc = alloc(2 * 384, BF16, shape=[2, 384])
        M.acc = [alloc(T) for _ in range(2)]
        M.rinv = alloc(T)
        M.bc = alloc(T)
        M.f = [alloc(T) for _ in range(3)]
        return M

    def transpose_to_triples(M, src, nblk, dst):
        p7 = ps[7][:, :].bitcast(BF16)
        for blk in range(nblk):
            o = (blk % 4) * 128
            P.add("pe", lambda e, blk=blk, o=o: e.transpose(out=p7[:, o:o + 128], in_=src[:, blk * 128:(blk + 1) * 128], identity=ident_b),
                  reads=["wavsrc", "ident"], writes=[PSK(7)])
            pv = p7[:, o:o + 128].rearrange("p (a c) -> p a c", a=2)
            vo = dst[:, blk, :].rearrange("p (a b c) -> p a b c", a=2, b=3)
            P.add("act", lambda e, pv=pv, vo=vo: e.activation(out=vo[:, :, 0, :], in_=pv, func=AF.Copy), reads=[PSK(7)], writes=["Vw"])
            P.add("dve", lambda e, pv=pv, vo=vo: e.tensor_copy(out=vo[:, :, 2, :], in_=pv), reads=[PSK(7)], writes=["Vw"])

    def init_triples(buf):
        P.add("dve", lambda e: e.memset(buf, 0.0), writes=["Vw"])
        P.add("dve", lambda e: e.memset(buf.rearrange("p k (a c) -> p k a c", a=2)[:, :, :, 64:65], 1.0), writes=["Vw"])

    def normalize(M, Tn, ob, lo, l, h, chunk, sink):
        if 'norm' in DBG_SKIP:
            return
        sp = 64 if lo else 0
        mrows = 64 if lo else 128
        r0 = 0 if lo else 64
        if sink:
            P.add("dve", lambda e: e.tensor_scalar(out=M.rinv[sp:sp + 1, 0:Tn], in0=ps[ob][sp:sp + 1, 0:Tn], scalar1=esink[sp:sp + 1, l, h:h + 1],
                                                  scalar2=None, op0=ALU.add), reads=[PSK(ob), "esink"], writes=["rinv"])
            P.add("dve", lambda e: e.reciprocal(out=M.rinv[sp:sp + 1, 0:Tn], in_=M.rinv[sp:sp + 1, 0:Tn]), reads=["rinv"], writes=["rinv"])
        else:
            P.add("dve", lambda e: e.reciprocal(out=M.rinv[sp:sp + 1, 0:Tn], in_=ps[ob][sp:sp + 1, 0:Tn]), reads=[PSK(ob)], writes=["rinv"])
        P.add("pe", lambda e: e.matmul(ps[5][0:mrows, 0:Tn], lhsT=ones_f[sp:sp + 1, 0:mrows], rhs=M.rinv[sp:sp + 1, 0:Tn], start=True, stop=True),
              reads=["rinv", "ones_f"], writes=[PSK(5)])
        P.add("act", lambda e: e.activation(out=M.bc[r0:r0 + 64, 0:Tn], in_=ps[5][r0:r0 + 64, 0:Tn], func=AF.Copy), reads=[PSK(5)], writes=["bc"])
        P.add("dve", lambda e: e.tensor_tensor(out=M.mix[r0:r0 + 64, chunk, 0:Tn], in0=ps[ob][r0:r0 + 64, 0:Tn], in1=M.bc[r0:r0 + 64, 0:Tn], op=ALU.mult),
              reads=[PSK(ob), "bc"], writes=[("mix", chunk)])

    def mixers(K, M, l, is_ctx, t):
        Tn = CTX if is_ctx else T
        tok0 = 0 if is_ctx else t * T
        j = 1 if is_ctx else 0
        zv = (zcbuf if is_ctx else zbuf).rearrange("(s p) c -> p s c", p=128)
        c0 = 128 + tok0
        dma("sp", M.QT[0:96, :, 0:Tn], zv[0:96, 0:4, c0:c0 + Tn], [], ["QT"], "lq")
        dma("sp", M.scb[:, :, 0:Tn], zv[:, 4:6, c0:c0 + Tn], [], ["scb"], "lscb")
        dma("sp", M.sct[:, :, 0:Tn + 2], zv[:, 6:8, c0 - 1:c0 + Tn + 1], [], ["sct"], "lsct")
        dma("sp", M.waq[:, :, 0:Tn], zv[:, 8:10, c0:c0 + Tn], [], ["waq"], "lwaq")
        dma("sp", M.wak[:, 0:Tn + 256], zv[:, 10, c0 - 128:c0 + Tn + 128], [], ["wak"], "lwak")
        dma("sp", M.wav[:, 0:Tn + 256], zv[:, 11, c0 - 128:c0 + Tn + 128], [], ["wavsrc"], "lwav")
        dma("sp", M.cfu[:, :, 0:Tn + 30], zv[:, 12:14, c0 - 15:c0 + Tn + 15], [], ["cfu"], "lcfu")
        hsrc = hcbuf if is_ctx else hbuf
        dma("sp", M.ht[:, :, 0:Tn], hsrc.rearrange("(k p) c -> p k c", p=128)[:, :, tok0:tok0 + Tn], ["hdst"], [("ht", k) for k in range(8)], "hld")

        kcs = [64, 65] if is_ctx else list(range(NKC))
        sc_a = 96.0 ** -0.5
        n = len(kcs)
        for h in (range(4) if 'mla' not in DBG_SKIP else []):
            ob = 3 + (h % 2)
            lo = (h % 2 == 0)
            vcol = (h // 2) * 192 + (0 if lo else 64)

            def S(i, h=h):
                kc = kcs[i]
                b = i % 3
                P.add("pe", lambda e: e.matmul(ps[b][:, 0:Tn], lhsT=K.KT[0:96, h, kc * 128:(kc + 1) * 128], rhs=M.QT[0:96, h, 0:Tn], start=True, stop=True),
                      reads=[("KTn", h), ("KTr", h), "QT"], writes=[PSK(b)])

            def E(i):
                b = i % 3
                P.add("act", lambda e: e.activation(out=M.PT[b][:, 0:Tn], in_=ps[b][:, 0:Tn], func=AF.Exp, scale=sc_a), reads=[PSK(b)], writes=[("PT", b)])

            def PV(i, ob=ob, vcol=vcol):
                kc = kcs[i]
                b = i % 3
                P.add("pe", lambda e: e.matmul(ps[ob][:, 0:Tn], lhsT=K.Vx[:, kc, vcol:vcol + 128], rhs=M.PT[b][:, 0:Tn], start=(i == 0), stop=(i == n - 1)),
                      reads=["Vx", ("PT", b)], writes=[PSK(ob)])

            S(0)
            if n > 1:
                S(1)
            for i in range(n):
                if 'mlaE' not in DBG_SKIP:
                    E(i)
                if 'mlaPV' not in DBG_SKIP:
                    PV(i)
                if i + 2 < n:
                    S(i + 2)
            normalize(M, Tn, ob, lo, l, h, h // 2, False)

        sc_w = 64.0 ** -0.5
        nqb = Tn // 128
        if not is_ctx and 'tr' not in DBG_SKIP:
            transpose_to_triples(M, M.wav, nqb + 2, M.Vw)
        for g in (range(2) if 'win' not in DBG_SKIP else []):
            pr = slice(g * 64, (g + 1) * 64)
            for qb in range(nqb):
                if is_ctx:
                    kbl = [("ctx", 0, None), ("ctx", 1, None)]
                else:
                    mP = 2 if (t == 0 and qb == 0) else 0
                    mN = 3 if (t == NT - 1 and qb == nqb - 1) else 1
                    kbl = [("loc", qb, mP), ("loc", qb + 1, None), ("loc", qb + 2, mN), ("ctx", 0, None), ("ctx", 1, None)]
                for i, (kind, blk, mi) in enumerate(kbl):
                    b, off = i // 2, (i % 2) * 256
                    ksrc = M.wakc if kind == "ctx" else M.wak
                    outv = ps[b][:, off:off + 256].rearrange("p (a c) -> p a c", a=2)
                    kl = ksrc[pr, blk * 128:(blk + 1) * 128]
                    qr = M.waq[pr, :, qb * 128:(qb + 1) * 128]
                    P.add("pe", lambda e, kl=kl, qr=qr, outv=outv, mi=mi: e.matmul(outv, lhsT=kl, rhs=qr, start=True, stop=(mi is None)),
                          reads=["wak", "wakc", "waq"], writes=[PSK(b)])
                    if mi is not None:
                        P.add("pe", lambda e, b=b, off=off, mi=mi: e.matmul(ps[b][:, off:off + 256], lhsT=ident_b, rhs=masks_b[:, mi, :], start=False, stop=True),
                              reads=["ident", "masks"], writes=[PSK(b)])
                ntile = (len(kbl) + 1) // 2
                for i in range(ntile):
                    w = 256 * min(2, len(kbl) - 2 * i)
                    P.add("act", lambda e, i=i, w=w: e.activation(out=M.PT[i][:, 0:w], in_=ps[i][:, 0:w], func=AF.Exp, scale=sc_w),
                          reads=[PSK(i)], writes=[("PT", i)])
                for c in range(2):
                    ob = 3 + c
                    for i, (kind, blk, mi) in enumerate(kbl):
                        vsrc = M.Vwc if kind == "ctx" else M.Vw
                        col = (i % 2) * 256 + c * 128
                        vl = vsrc[:, blk, g * 192 + c * 64:g * 192 + c * 64 + 128]
                        pr_ = M.PT[i // 2][:, col:col + 128]
                        oo = ps[ob][:, qb * 128:(qb + 1) * 128]
                        last = (i == len(kbl) - 1)
                        P.add("pe", lambda e, vl=vl, pr_=pr_, oo=oo, i=i, last=last: e.matmul(oo, lhsT=vl, rhs=pr_, start=(i == 0), stop=last),
                              reads=["Vw", ("PT", i // 2)], writes=[PSK(ob)])
            for c in range(2):
                normalize(M, Tn, 3 + c, c == 0, l, 2 * g + c, 4 + g, True)

        def wsc(c, k):
            return V("wsc", (l * 2 + c) * 3 + k, 1)

        def wcf(c, k):
            return V("wcf", (l * 2 + c) * 31 + k, 1)

        for c in (range(2) if 'sc' not in DBG_SKIP else []):
            acc = M.acc[c]
            P.add("dve", lambda e, c=c, acc=acc: e.tensor_scalar(out=acc[:, 0:Tn], in0=M.sct[:, c, 0:Tn], scalar1=wsc(c, 0), scalar2=None, op0=ALU.mult),
                  reads=["sct", "vecs"], writes=[("acc", c)])
            for k in (1, 2):
                P.add("dve", lambda e, c=c, k=k, acc=acc: e.scalar_tensor_tensor(out=acc[:, 0:Tn], in0=M.sct[:, c, k:k + Tn], scalar=wsc(c, k), in1=acc[:, 0:Tn],
                                                                                 op0=ALU.mult, op1=ALU.add), reads=["sct", ("acc", c), "vecs"], writes=[("acc", c)])
            P.add("dve", lambda e, c=c, acc=acc: e.tensor_tensor(out=M.mix[:, 2 + c, 0:Tn], in0=acc[:, 0:Tn], in1=M.scb[:, c, 0:Tn], op=ALU.mult),
                  reads=[("acc", c), "scb"], writes=[("mix", 2 + c)])
        for c in range(2):
            acc = M.acc[c]
            P.add("dve", lambda e, c=c, acc=acc: e.tensor_scalar(out=acc[:, 0:Tn], in0=M.cfu[:, c, 0:Tn], scalar1=wcf(c, 0), scalar2=V("bcf", l * 2 + c, 1),
                                                                 op0=ALU.mult, op1=ALU.add), reads=["cfu", "vecs", ("acc", c)], writes=[("acc", c)])
            for k in range(1, 31):
                P.add("dve", lambda e, c=c, k=k, acc=acc: e.scalar_tensor_tensor(out=acc[:, 0:Tn], in0=M.cfu[:, c, k:k + Tn], scalar=wcf(c, k), in1=acc[:, 0:Tn],
                                                                                 op0=ALU.mult, op1=ALU.add), reads=["cfu", ("acc", c), "vecs"], writes=[("acc", c)])
            P.add("dve", lambda e, c=c, acc=acc: e.tensor_tensor(out=M.f[c][:, 0:Tn], in0=acc[:, 0:Tn], in1=acc[:, 0:Tn], op=ALU.mult),
                  reads=[("acc", c)], writes=[("f", c)])
        for c in range(2):
            P.add("pe", lambda e, c=c: e.matmul(ps[6][:, 0:Tn], lhsT=ones_f, rhs=M.acc[c][:, 0:Tn], start=(c == 0), stop=(c == 1)),
                  reads=[("acc", c), "ones_f"], writes=[PSK(6)])
        for c in range(2):
            P.add("pe", lambda e, c=c: e.matmul(ps[7][:, 0:Tn], lhsT=ones_f, rhs=M.f[c][:, 0:Tn], start=(c == 0), stop=(c == 1)),
                  reads=[("f", c), "ones_f"], writes=[PSK(7)])
        P.add("act", lambda e: e.activation(out=M.f[2][:, 0:Tn], in_=ps[6][:, 0:Tn], func=AF.Identity, bias=V("zero"), scale=1.0 / 256),
              reads=[PSK(6), "vecs"], writes=[("f", 2)])
        P.add("dve", lambda e: e.tensor_tensor(out=M.f[0][:, 0:Tn], in0=M.f[2][:, 0:Tn], in1=M.f[2][:, 0:Tn], op=ALU.mult), reads=[("f", 2)], writes=[("f", 0)])
        P.add("dve", lambda e: e.scalar_tensor_tensor(out=M.f[0][:, 0:Tn], in0=ps[7][:, 0:Tn], scalar=1.0 / 256, in1=M.f[0][:, 0:Tn], op0=ALU.mult, op1=ALU.subtract),
              reads=[PSK(7), ("f", 0)], writes=[("f", 0)])
        P.add("act", lambda e: e.activation(out=M.f[0][:, 0:Tn], in_=M.f[0][:, 0:Tn], func=AF.Sqrt, bias=V("eps"), scale=1.0), reads=[("f", 0), "vecs"], writes=[("f", 0)])
        P.add("dve", lambda e: e.reciprocal(out=M.f[0][:, 0:Tn], in_=M.f[0][:, 0:Tn]), reads=[("f", 0)], writes=[("f", 0)])
        for c in range(2):
            acc = M.acc[c]
            P.add("dve", lambda e, acc=acc: e.tensor_tensor(out=acc[:, 0:Tn], in0=acc[:, 0:Tn], in1=M.f[2][:, 0:Tn], op=ALU.subtract),
                  reads=[("acc", c), ("f", 2)], writes=[("acc", c)])
            P.add("dve", lambda e, acc=acc: e.tensor_tensor(out=acc[:, 0:Tn], in0=acc[:, 0:Tn], in1=M.f[0][:, 0:Tn], op=ALU.mult),
                  reads=[("acc", c), ("f", 0)], writes=[("acc", c)])
            P.add("act", lambda e, c=c, acc=acc: e.activation(out=M.mix[:, 6 + c, 0:Tn], in_=acc[:, 0:Tn], func=AF.Silu, bias=V("bln", l * 2 + c, 1),
                                                             scale=V("gln", l * 2 + c, 1)), reads=[("acc", c), "vecs"], writes=[("mix", 6 + c)])
        G2 = HG[:, l, j, 1, :]
        for c in range(8):
            bd = 1 + (c % 2)
            for k in range(8):
                P.add("pe", lambda e, c=c, k=k, bd=bd: e.matmul(ps[bd][:, 0:Tn], lhsT=M.wout[:, k, c * 128:(c + 1) * 128], rhs=M.mix[:, k, 0:Tn],
                                                               start=(k == 0), stop=(k == 7)), reads=["wout", ("mix", k)], writes=[PSK(bd)])
            P.add("dve", lambda e, c=c, bd=bd: e.scalar_tensor_tensor(out=M.ht[:, c, 0:Tn], in0=ps[bd][:, 0:Tn], scalar=G2[:, c:c + 1], in1=M.ht[:, c, 0:Tn],
                                                                     op0=ALU.mult, op1=ALU.add), reads=[PSK(bd), ("ht", c), "coef"], writes=[("ht", c)])
        dma("sp", hsrc.rearrange("(k p) c -> p k c", p=128)[:, :, tok0:tok0 + Tn], M.ht[:, :, 0:Tn], [("ht", k) for k in range(8)], ["hdst"], "hst")

    def phase_mix(l):
        base = bump[0]
        K = alloc_kv()
        kv_build(K, l)
        M = alloc_mix()
        dma("sp", M.wout, wb["w_out", l].rearrange("(k p) m -> p k m", p=128), WK("w_out", l), ["wout"], "wout")
        zc = zcbuf.rearrange("(s p) c -> p s c", p=128)
        dma("sp", M.wakc, zc[:, 10, 128:128 + CTX], [], ["wakc"], "lwakc")
        dma("sp", M.wavc, zc[:, 11, 128:128 + CTX], [], ["wavsrc"], "lwavc")
        init_triples(M.Vw)
        init_triples(M.Vwc)
        transpose_to_triples(M, M.wavc, 2, M.Vwc)
        for t in range(min(NT, DBG_MIXT[0])):
            mixers(K, M, l, False, t)
        if l == 0 and DBG_MIXT[0] >= NT:
            mixers(K, M, l, True, 0)
        P.barrier()
        bump[0] = base

    zt = alloc(NZ * 128, BF16, shape=[NZ, 128])
    P.add("dve", lambda e: e.memset(zt, 0.0), writes=["zt"])
    zcv = zcbuf.rearrange("(s p) c -> p s c", p=128)
    dma("sp", zcv[:, :, 0:128], zt, ["zt"], ["zc0"], "zc0")
    dma("sp", zcv[:, :, 128 + CTX:ZCW], zt, ["zt"], ["zc1"], "zc1")
    P.barrier()
    bump[0] = persist_end

    stage = STAGE[0]
    phase_ffn(None, 0, True)
    if stage >= 1.2:
        exchange(0)
    if stage >= 1.5:
        phase_mix(0)
    if stage >= 2.5:
        phase_ffn(0, 1, False)
    if stage >= 2.7:
        exchange(1)
        phase_mix(1)
    if stage >= 3:
        phase_ffn(1, None, False)
    if stage < 3:
        base = bump[0]
        Bd = alloc_common(T)
        for t in range(NT):
            load_h(Bd, hbuf, t * T, T)
            store_h(Bd, outT, t * T, T, key="out")
        if 'dumpc' in DBG_SKIP:
            load_h(Bd, hcbuf, 0, CTX)
            store_h(Bd, outT, 0, CTX, key="out")
        P.barrier()
        bump[0] = base
    P.barrier()
    P.emit(nc, st)
    st.close()
    return nc


STAGE = [3]
DBG_MIXT = [99]
DBG_SKIP = set()
RG_OVERRIDE = [None]


def _rope_tables(half):
    pos = half * S_OWN + np.arange(S_OWN)
    row = (pos // 64).astype(np.float32)
    col = (pos % 64).astype(np.float32)

    def tab(d_rot):
        d_ax = d_rot // 2
        inv = (10000.0 ** (-np.arange(0, d_ax, 2, dtype=np.float32) / d_ax)).astype(np.float32)
        ar = row[:, None] * inv[None, :]
        ac = col[:, None] * inv[None, :]
        cr, sr, cc_, sc_ = np.cos(ar), np.sin(ar), np.cos(ac), np.sin(ac)
        C = np.concatenate([cr, cr, cc_, cc_], axis=1).T.astype(np.float32)
        Sg = np.concatenate([-sr, sr, -sc_, sc_], axis=1).T.astype(np.float32)
        return C, Sg

    Cw, Sw = tab(64)
    ropeW = np.concatenate([np.tile(Cw, (2, 1)), np.tile(Sw, (2, 1))], axis=0)
    Cm, Sm = tab(32)
    ropeK = np.concatenate([Cm, Sm], axis=0)
    Cq = np.concatenate([np.ones((64, S_OWN), np.float32), Cm], axis=0)
    Sq = np.concatenate([np.zeros((64, S_OWN), np.float32), Sm], axis=0)
    ropeQ = np.concatenate([Cq, Sq], axis=0)
    return np.ascontiguousarray(ropeW), np.ascontiguousarray(ropeQ), np.ascontiguousarray(ropeK)


def _masks(half):
    kp = np.arange(128)[:, None]
    qp = np.arange(128)[None, :]
    mP = np.where(qp <= kp, 0.0, NEGM).astype(np.float32)
    mN = np.where(kp <= qp, 0.0, NEGM).astype(np.float32)
    neg = np.full((128, 128), NEGM, np.float32)
    kinds = [mP, mN, mP if half == 1 else neg, mN if half == 0 else neg]
    m = np.stack([np.concatenate([k, k], axis=1) for k in kinds], axis=1)
    return np.ascontiguousarray(m.reshape(128, 4 * 256))


def _fm(v):
    v = np.asarray(v, np.float32)
    lead = v.shape[:-1]
    n = v.shape[-1] // 128
    return np.moveaxis(v.reshape(lead + (n, 128)), -1, 0)


def _pack_vecs(inp, half):
    vec = np.zeros((128, NV), np.float32)

    def put(name, arr):
        arr = np.asarray(arr, np.float32).reshape(128, -1)
        w = dict(_VSPEC)[name]
        assert arr.shape[1] == w, (name, arr.shape, w)
        vec[:, VOFF[name]:VOFF[name] + w] = arr

    put("gf1", _fm(inp["g_ffn1"]))
    put("gmix", _fm(inp["g_mix"]))
    put("gf2", _fm(inp["g_ffn2"]))
    put("bmod", _fm(inp["b_mod"]))
    put("gfin", _fm(inp["g_final"]))
    gq = np.zeros((L, 256), np.float32)
    gq[:, :192] = inp["g_mla_q"]
    put("gq", _fm(gq))
    put("gkv", _fm(inp["g_mla_kv"]))
    wsc = np.asarray(inp["w_sc_conv"], np.float32)
    put("wsc", np.transpose(wsc.reshape(L, 3, 2, 128), (3, 0, 2, 1)))
    wcf = np.asarray(inp["w_cf_conv"], np.float32)
    put("wcf", np.transpose(wcf.reshape(L, 31, 2, 128), (3, 0, 2, 1)))
    put("bcf", _fm(inp["b_cf_conv"]))
    put("gln", _fm(inp["g_cf_ln"]))
    put("bln", _fm(inp["b_cf_ln"]))
    put("sink", np.broadcast_to(np.asarray(inp["wa_sink"], np.float32).reshape(1, L * 4), (128, L * 4)))
    put("lm", np.full((128, 1), 1.0 if half == 1 else 0.0, np.float32))
    put("rm", np.full((128, 1), 1.0 if half == 0 else 0.0, np.float32))
    put("eps", np.full((128, 1), EPS, np.float32))
    put("zero", np.zeros((128, 1), np.float32))
    return vec


_NC_CACHE = {}


def kernel(**inputs):
    inp = {k: np.asarray(v) for k, v in inputs.items()}
    x = inp["x"].astype(np.float32, copy=False)
    ctx = inp["ctx"].astype(np.float32, copy=False)
    Bn = x.shape[0]
    key = STAGE[0]
    if key not in _NC_CACHE:
        _NC_CACHE[key] = build()
    nc = _NC_CACHE[key]
    shared = {n: np.ascontiguousarray(inp[n], dtype=np.float32) for n in
              ("w_mod", "w1_gate", "w1_up", "w1_down", "w2_gate", "w2_up", "w2_down", "w_in", "w_out", "w_mla_uq", "w_mla_ukv")}
    ident = np.eye(128, dtype=np.float32)
    in_maps = []
    for core in range(8):
        b, half = core // 2, core % 2
        rW, rQ, rK = _rope_tables(half)
        cc = np.stack([_fm(inp["c"][b]), _fm(inp["c_ctx"])], axis=-1).reshape(128, 16)
        m = {"xT": np.ascontiguousarray(x[b, half * S_OWN:(half + 1) * S_OWN, :].T),
             "ctxT": np.ascontiguousarray(ctx[b].T),
             "cc": np.ascontiguousarray(cc, dtype=np.float32),
             "vecs": _pack_vecs(inp, half),
             "ropeW": rW, "ropeQ": rQ, "ropeK": rK,
             "masks": _masks(half), "ident": ident}
        m.update(shared)
        in_maps.append(m)
    res = run_bass_kernel_spmd(nc, in_maps, core_ids=list(range(8)))
    out = np.empty((Bn, 2 * S_OWN, D), np.float32)
    for core in range(8):
        b, half = core // 2, core % 2
        out[b, half * S_OWN:(half + 1) * S_OWN, :] = np.asarray(res.results[core]["outT"]).T
    return out
```

```python
import contextlib
import numpy as np
import concourse.bass as bass
import concourse.mybir as mybir
from concourse.bass_utils import run_bass_kernel_spmd

F32 = mybir.dt.float32
BF16 = mybir.dt.bfloat16
AF = mybir.ActivationFunctionType
ALU = mybir.AluOpType
ENGS = ("pe", "act", "dve", "pool", "sp")

L = 2
D = 1024
S_OWN = 4096
CTX = 256
T = 512
NT = S_OWN // T
DFF = 2816
NJ = DFF // 128
NWIN = 2560
ZW = 128 + S_OWN + 128
ZCW = 128 + CTX + 128
EPS = 1e-6
NZ = 14
ZQ, ZSCB, ZSCT, ZWAQ, ZWAK, ZWAV, ZCFU = 0, 4, 6, 8, 10, 11, 12
HALO_SLOTS = (6, 7, 10, 11, 12, 13)
NEGM = -30000.0

_VSPEC = [("gf1", L * 8), ("gmix", L * 8), ("gf2", L * 8), ("bmod", L * 72), ("gfin", 8), ("gq", L * 2),
          ("gkv", L), ("wsc", L * 6), ("wcf", L * 62), ("bcf", L * 2), ("gln", L * 2), ("bln", L * 2),
          ("sink", L * 4), ("lm", 1), ("rm", 1), ("eps", 1), ("zero", 1)]
VOFF = {}
_o = 0
for _n, _w in _VSPEC:
    VOFF[_n] = _o
    _o += _w
NV = _o


class Op:
    __slots__ = ("eng", "fn", "deps", "signal", "count", "chan", "idx")

    def __init__(self, eng, fn, chan):
        self.idx = 0
        self.eng = eng
        self.fn = fn
        self.deps = set()
        self.signal = False
        self.count = 0
        self.chan = chan


class Chan:
    def __init__(self, name):
        self.name = name
        self.sem = None
        self.n = 0
        self.last = None


class Prog:
    def __init__(self):
        self.ops = {e: [] for e in ENGS}
        self.last_w = {}
        self.readers = {}
        self.chans = []

    def chan(self, name):
        c = Chan(name)
        self.chans.append(c)
        return c

    def add(self, eng, fn, reads=(), writes=(), chan=None):
        op = Op(eng, fn, chan)
        deps = set()
        for k in reads:
            w = self.last_w.get(k)
            if w is not None:
                deps.add(w)
        for k in writes:
            w = self.last_w.get(k)
            if w is not None:
                deps.add(w)
            deps.update(self.readers.get(k, ()))
        if chan is not None:
            if chan.last is not None:
                deps.add(chan.last)
            chan.last = op
            chan.n += 1
            op.count = 16 * chan.n
            op.signal = True
        deps.discard(op)
        if eng == "pe":
            deps = {d for d in deps if not (d.eng == "pe" and d.chan is None)}
        best = {}
        for d in deps:
            k = ("c", id(d.chan)) if d.chan is not None else ("e", d.eng)
            if k not in best or best[k].idx < d.idx:
                best[k] = d
        deps = set(best.values())
        for d in deps:
            d.signal = True
        op.deps = deps
        op.idx = len(self.ops[eng]) if chan is None else chan.n
        for k in reads:
            self.readers.setdefault(k, []).append(op)
        for k in writes:
            self.last_w[k] = op
            self.readers[k] = []
        self.ops[eng].append(op)
        return op

    def barrier(self):
        lasts = []
        for e in ENGS:
            for op in reversed(self.ops[e]):
                if op.chan is None and op.fn is not None:
                    lasts.append(op)
                    break
        for c in self.chans:
            if c.last is not None:
                lasts.append(c.last)
        for e in ENGS:
            op = Op(e, None, None)
            op.deps = set(lasts)
            for d in op.deps:
                d.signal = True
            self.ops[e].append(op)
        self.last_w = {}
        self.readers = {}

    def emit(self, nc, stack):
        esem = {e: stack.enter_context(nc.semaphore("s_" + e)) for e in ENGS}
        for c in self.chans:
            if c.n > 0:
                c.sem = stack.enter_context(nc.semaphore("c_" + c.name))
        for e in ENGS:
            n = 0
            for op in self.ops[e]:
                if op.chan is None and op.signal:
                    n += 1
                    op.count = n
        block = stack.enter_context(nc.Block())

        def run(e, eng):
            waited = {}
            for op in self.ops[e]:
                need = {}
                for d in op.deps:
                    if d.chan is not None:
                        s, v = d.chan.sem, d.count
                    else:
                        s, v = esem[d.eng], d.count
                    key = id(s)
                    if key not in need or need[key][1] < v:
                        need[key] = (s, v)
                for key, (s, v) in need.items():
                    if waited.get(key, 0) < v:
                        eng.wait_ge(s, v)
                        waited[key] = v
                if op.fn is None:
                    continue
                ins = op.fn(eng)
                if op.chan is not None:
                    ins.then_inc(op.chan.sem, 16)
                elif op.signal:
                    ins.then_inc(esem[e], 1)

        @block.tensor
        def _(eng):
            run("pe", eng)

        @block.scalar
        def _(eng):
            run("act", eng)

        @block.vector
        def _(eng):
            run("dve", eng)

        @block.gpsimd
        def _(eng):
            run("pool", eng)

        @block.sync
        def _(eng):
            run("sp", eng)


def build(dbg=None):
    nc = bass.Bass("TRN2", target_bir_lowering=False)
    P = Prog()
    st = contextlib.ExitStack()

    def din(name, shape):
        return nc.dram_tensor(name, list(shape), F32, kind="ExternalInput").ap()

    def dscr(name, shape, dt):
        return nc.dram_tensor(name, list(shape), dt).ap()

    xT = din("xT", [D, S_OWN])
    ctxT = din("ctxT", [D, CTX])
    cc_in = din("cc", [128, 16])
    vecs_in = din("vecs", [128, NV])
    ropeW = din("ropeW", [256, S_OWN])
    ropeQ = din("ropeQ", [192, S_OWN])
    ropeK = din("ropeK", [64, S_OWN])
    masks_in = din("masks", [128, 4 * 256])
    ident_in = din("ident", [128, 128])
    w_mod = din("w_mod", [L, D, 9 * D])
    wsrc = {n: din(n, s) for n, s in [
        ("w1_gate", [L, D, DFF]), ("w1_up", [L, D, DFF]), ("w1_down", [L, DFF, D]),
        ("w2_gate", [L, D, DFF]), ("w2_up", [L, D, DFF]), ("w2_down", [L, DFF, D]),
        ("w_in", [L, D, 2144]), ("w_out", [L, D, D]), ("w_mla_uq", [L, 192, 384]), ("w_mla_ukv", [L, 128, 512])]}
    outT = nc.dram_tensor("outT", [D, S_OWN], F32, kind="ExternalOutput").ap()
    dbg_out = None
    if dbg:
        dbg_out = nc.dram_tensor("dbg", list(dbg), F32, kind="ExternalOutput").ap()

    hbuf = dscr("hbuf", [D, S_OWN], F32)
    hcbuf = dscr("hcbuf", [D, CTX], F32)
    zbuf = dscr("zbuf", [NZ * 128, ZW], BF16)
    zcbuf = dscr("zcbuf", [NZ * 128, ZCW], BF16)
    xin_mla = dscr("xin_mla", [160, S_OWN], BF16)
    xout_mla = dscr("xout_mla", [320, S_OWN], BF16)
    ckvc = dscr("ckvc", [160, CTX], BF16)
    xin_halo = dscr("xin_halo", [768, 256], BF16)
    xout_halo = dscr("xout_halo", [1536, 256], BF16)
    wb = {}
    for l in range(L):
        for n in ("w1_gate", "w1_up", "w2_gate", "w2_up"):
            wb[n, l] = dscr(f"b_{n}{l}", [D, DFF], BF16)
        for n in ("w1_down", "w2_down"):
            wb[n, l] = dscr(f"b_{n}{l}", [DFF, D], BF16)
        wb["w_in", l] = dscr(f"b_w_in{l}", [D, NWIN], BF16)
        wb["w_out", l] = dscr(f"b_w_out{l}", [D, D], BF16)
        wb["w_mla_uq", l] = dscr(f"b_wuq{l}", [192, 768], BF16)
        wb["w_mla_ukv", l] = dscr(f"b_wukv{l}", [128, 512], BF16)

    ARENA = 53000
    arena = st.enter_context(nc.sbuf_tensor("arena", [128, ARENA], F32))
    bump = [0]

    def alloc(cols, dt=F32, shape=None):
        words = cols if dt == F32 else (cols + 1) // 2
        a = bump[0]
        bump[0] += words
        assert bump[0] <= ARENA, ("SBUF arena overflow", bump[0])
        ap = arena[:, a:a + words]
        if dt != F32:
            ap = ap.bitcast(dt)[:, 0:cols]
        if shape is not None:
            names = " ".join(f"d{i}" for i in range(len(shape)))
            kw = {f"d{i}": s for i, s in enumerate(shape)}
            ap = ap.rearrange(f"p ({names}) -> p {names}", **kw)
        return ap

    psw = [st.enter_context(nc.psum_tensor(f"psw{i}", [128, 1024], F32)) for i in range(4)]
    ps = []
    for i in range(4):
        ps.append(psw[i][:, 0:512])
        ps.append(psw[i][:, 512:1024])

    def PSK(i):
        return ("ps", i)

    vecs = alloc(NV)
    ccs = alloc(16)
    modv = alloc(L * 144, shape=[L, 72, 2])
    Acoef = alloc(L * 2 * 3 * 8, shape=[L, 2, 3, 8])
    HG = alloc(L * 2 * 3 * 8, shape=[L, 2, 3, 8])
    esink = alloc(L * 4, shape=[L, 4])
    ones_b = alloc(128, BF16)
    ones_f = alloc(128)
    ident_b = alloc(128, BF16)
    masks_b = alloc(4 * 256, BF16, shape=[4, 256])
    persist_end = bump[0]

    def V(name, off=0, w=1):
        o = VOFF[name] + off
        return vecs[:, o:o + w]

    chn = {}

    def CH(name):
        if name not in chn:
            chn[name] = P.chan(name)
        return chn[name]

    def dma(q, out, in_, reads, writes, ch):
        return P.add(q, lambda e: e.dma_start(out=out, in_=in_), reads=reads, writes=writes, chan=CH(ch))

    dma("sp", vecs, vecs_in, [], ["vecs"], "ld0")
    dma("sp", ccs, cc_in, [], ["ccs"], "ld1")
    dma("pool", ident_b, ident_in, [], ["ident"], "cv0")
    dma("pool", masks_b.rearrange("p a b -> p (a b)"), masks_in, [], ["masks"], "cv1")
    P.add("dve", lambda e: e.memset(ones_b, 1.0), writes=["ones_b"])
    P.add("dve", lambda e: e.memset(ones_f, 1.0), writes=["ones_f"])
    P.add("act", lambda e: e.activation(out=ccs, in_=ccs, func=AF.Silu), reads=["ccs"], writes=["ccs"])
    P.add("act", lambda e: e.activation(out=esink.rearrange("p a b -> p (a b)"), in_=V("sink", 0, L * 4), func=AF.Exp),
          reads=["vecs"], writes=["esink"])

    cvn = [0]

    def conv(out, in_, key):
        cvn[0] += 1
        dma("pool", out, in_, [], [key], f"cv{cvn[0] % 8}")

    def conv_rows(name, l, nrows, dst=None, c0=0, c1=None, d0=0):
        src = wsrc[name]
        c1 = c1 if c1 is not None else src.shape[2]
        dstap = wb[name, l] if dst is None else dst
        for r in range(0, nrows, 128):
            rr = min(128, nrows - r)
            conv(dstap[r:r + rr, d0:d0 + (c1 - c0)], src[l, r:r + rr, c0:c1], (name, l, r // 128))

    def conv_layer_ffn(l, which):
        conv_rows(f"w{which}_gate", l, D)
        conv_rows(f"w{which}_up", l, D)
        conv_rows(f"w{which}_down", l, DFF)

    def conv_layer_mix(l):
        src = wsrc["w_in"]
        dst = wb["w_in", l]
        segs = [(0, 1120, 0), (1120, 1184, 1120), (1248, 1312, 1184), (1184, 1248, 1248), (1312, 1376, 1312),
                (1376, 2144, 1376)]
        for b8, s8 in enumerate([8, 0, 24, 16]):
            segs.append((320 + s8, 320 + s8 + 8, 2144 + 8 * b8))
        for hi, hsrc in enumerate([1120, 1248, 1184, 1312, 1376, 1440]):
            for b16, s16 in enumerate([16, 0, 48, 32]):
                segs.append((hsrc + s16, hsrc + s16 + 16, 2176 + 64 * hi + 16 * b16))
        for (c0, c1, d0) in segs:
            for r in range(0, D, 512):
                conv(dst[r:r + 512, d0:d0 + (c1 - c0)], src[l, r:r + 512, c0:c1], ("w_in", l))
        conv_rows("w_out", l, D)
        srcq = wsrc["w_mla_uq"]
        dq = wb["w_mla_uq", l]
        conv(dq[0:128, 0:384], srcq[l, 0:128, :], ("w_mla_uq", l))
        conv(dq[128:192, 0:384], srcq[l, 128:192, :], ("w_mla_uq", l))
        for h in range(4):
            conv(dq[0:192, 384 + h * 96:384 + h * 96 + 64], srcq[l, :, h * 96:h * 96 + 64], ("w_mla_uq", l))
            for b8, s8 in enumerate([8, 0, 24, 16]):
                conv(dq[0:192, 384 + h * 96 + 64 + 8 * b8:384 + h * 96 + 64 + 8 * b8 + 8],
                     srcq[l, :, h * 96 + 64 + s8:h * 96 + 64 + s8 + 8], ("w_mla_uq", l))
        conv(wb["w_mla_ukv", l][:, :], wsrc["w_mla_ukv"][l, :, :], ("w_mla_ukv", l))

    conv_layer_ffn(0, 1)
    conv_layer_mix(0)
    conv_layer_ffn(0, 2)
    conv_layer_ffn(1, 1)
    conv_layer_mix(1)
    conv_layer_ffn(1, 2)

    def WK(name, l):
        if name == "w_in" or name.startswith("w_mla"):
            return [(name, l)]
        n = DFF if name.endswith("down") else D
        return [(name, l, r) for r in range((n + 127) // 128)]

    def setup_mod():
        base = bump[0]
        wm = [alloc(8 * 1024, shape=[8, 1024]) for _ in range(2)]
        for l in range(L):
            for og in range(9):
                slot = wm[(l * 9 + og) % 2]
                for kc in range(8):
                    dma("sp", slot[:, kc, :], w_mod[l, kc * 128:(kc + 1) * 128, og * 1024:(og + 1) * 1024],
                        [], [("wm", (l * 9 + og) % 2, kc)], f"wm{(l * 9 + og) % 2}")
                for oc in range(8):
                    col = (og * 8 + oc) * 2
                    for kc in range(8):
                        P.add("pe", lambda e, slot=slot, kc=kc, oc=oc, col=col: e.matmul(
                            ps[0][:, col:col + 2], lhsT=slot[:, kc, oc * 128:(oc + 1) * 128],
                            rhs=ccs[:, kc * 2:kc * 2 + 2], start=(kc == 0), stop=(kc == 7)),
                            reads=[("wm", (l * 9 + og) % 2, kc), "ccs"], writes=[PSK(0)])
            for j in range(2):
                P.add("dve", lambda e, l=l, j=j: e.tensor_tensor(
                    out=modv[:, l, :, j], in0=ps[0][:, 0:144].rearrange("p (a b) -> p a b", b=2)[:, :, j],
                    in1=V("bmod", l * 72, 72), op=ALU.add), reads=[PSK(0), "vecs"], writes=["modv"])
            for j in range(2):
                for s, gname in enumerate(("gf1", "gmix", "gf2")):
                    i_scale = 3 * s + 1
                    P.add("dve", lambda e, l=l, j=j, s=s, gname=gname, i_scale=i_scale: e.scalar_tensor_tensor(
                        out=Acoef[:, l, j, s, :], in0=modv[:, l, i_scale * 8:(i_scale + 1) * 8, j], scalar=1.0,
                        in1=V(gname, l * 8, 8), op0=ALU.add, op1=ALU.mult), reads=["modv", "vecs"], writes=["coef"])
                    i_gate = 3 * s + 2
                    P.add("dve", lambda e, l=l, j=j, s=s, i_gate=i_gate: e.tensor_scalar(
                        out=HG[:, l, j, s, :], in0=modv[:, l, i_gate * 8:(i_gate + 1) * 8, j],
                        scalar1=(1.0 if s == 1 else 0.5), scalar2=None, op0=ALU.mult), reads=["modv"], writes=["coef"])
        P.barrier()
        bump[0] = base

    setup_mod()

    def Bsh(l, j, s):
        return modv[:, l, (3 * s) * 8:(3 * s + 1) * 8, j]

    class FFNBufs:
        pass

    def alloc_common(Tm):
        B = FFNBufs()
        B.ht = alloc(8 * Tm, shape=[8, Tm])
        B.xn = alloc(8 * Tm, BF16, shape=[8, Tm])
        B.sq = [alloc(Tm, BF16) for _ in range(2)]
        B.rstd = alloc(Tm)
        B.tmp = [alloc(Tm) for _ in range(2)]
        return B

    def norm_mod(B, Tn, Avec, Bvec, tag):
        for kc in range(8):
            P.add("act", lambda e, kc=kc: e.activation(out=B.sq[kc % 2][:, 0:Tn], in_=B.ht[:, kc, 0:Tn], func=AF.Square),
                  reads=[("ht", kc)], writes=[("sq", kc % 2)])
            P.add("pe", lambda e, kc=kc: e.matmul(ps[0][:, 0:Tn], lhsT=ones_b, rhs=B.sq[kc % 2][:, 0:Tn],
                                                 start=(kc == 0), stop=(kc == 7)),
                  reads=[("sq", kc % 2), "ones_b"], writes=[PSK(0)])
        P.add("act", lambda e: e.activation(out=B.rstd[:, 0:Tn], in_=ps[0][:, 0:Tn], func=AF.Sqrt, bias=V("eps"), scale=1.0 / D),
              reads=[PSK(0), "vecs"], writes=["rstd"])
        P.add("dve", lambda e: e.reciprocal(out=B.rstd[:, 0:Tn], in_=B.rstd[:, 0:Tn]), reads=["rstd"], writes=["rstd"])
        for kc in range(8):
            P.add("dve", lambda e, kc=kc: e.tensor_tensor(out=B.tmp[kc % 2][:, 0:Tn], in0=B.ht[:, kc, 0:Tn], in1=B.rstd[:, 0:Tn],
                                                         op=ALU.mult), reads=[("ht", kc), "rstd"], writes=[("tmp", kc % 2)])
            bias = Bvec[:, kc:kc + 1] if Bvec is not None else V("zero")
            P.add("act", lambda e, kc=kc, bias=bias: e.activation(out=B.xn[:, kc, 0:Tn], in_=B.tmp[kc % 2][:, 0:Tn], func=AF.Identity,
                                                                  bias=bias, scale=Avec[:, kc:kc + 1]),
                  reads=[("tmp", kc % 2), "coef", "modv", "vecs"], writes=[("xn", kc)])

    JG = 2
    NG = (NJ + JG - 1) // JG

    def alloc_ffn(B, Tm):
        B.H = alloc(NJ * Tm, BF16, shape=[NJ, Tm])
        B.wg = [alloc(8 * JG * 128, BF16, shape=[8, JG * 128]) for _ in range(2)]
        B.wu = [alloc(8 * JG * 128, BF16, shape=[8, JG * 128]) for _ in range(2)]
        B.wd = alloc(NJ * D, BF16, shape=[NJ, D])
        B.sg = [alloc(Tm) for _ in range(2)]

    gcount = [0]

    def ffn(B, Tn, l, which, hg):
        wgd, wud, wdd = wb[f"w{which}_gate", l], wb[f"w{which}_up", l], wb[f"w{which}_down", l]
        kg, ku, kd = WK(f"w{which}_gate", l), WK(f"w{which}_up", l), WK(f"w{which}_down", l)

        def load_group(g):
            slot = (gcount[0] + g) % 2
            j0 = g * JG
            nj = min(JG, NJ - j0)
            dma("sp", B.wg[slot][:, :, 0:nj * 128], wgd[:, j0 * 128:(j0 + nj) * 128].rearrange("(k p) m -> p k m", p=128),
                kg, [("wg", slot)], f"wg{slot}")
            dma("sp", B.wu[slot][:, :, 0:nj * 128], wud[:, j0 * 128:(j0 + nj) * 128].rearrange("(k p) m -> p k m", p=128),
                ku, [("wu", slot)], f"wu{slot}")

        load_group(0)
        for g in range(NG):
            slot = (gcount[0] + g) % 2
            j0 = g * JG
            nj = min(JG, NJ - j0)
            if g + 1 < NG:
                load_group(g + 1)
            dma("sp", B.wd[:, j0:j0 + nj, :], wdd[j0 * 128:(j0 + nj) * 128, :].rearrange("(j p) m -> p j m", p=128),
                kd, [("wd", g)], "wd")
            for jj in range(nj):
                j = j0 + jj
                bg, bu = 1 + (j % 2), 3 + (j % 2)
                for kc in range(8):
                    P.add("pe", lambda e, kc=kc, jj=jj, slot=slot, bg=bg: e.matmul(
                        ps[bg][:, 0:Tn], lhsT=B.wg[slot][:, kc, jj * 128:(jj + 1) * 128], rhs=B.xn[:, kc, 0:Tn],
                        start=(kc == 0), stop=(kc == 7)), reads=[("wg", slot), ("xn", kc)], writes=[PSK(bg)])
                for kc in range(8):
                    P.add("pe", lambda e, kc=kc, jj=jj, slot=slot, bu=bu: e.matmul(
                        ps[bu][:, 0:Tn], lhsT=B.wu[slot][:, kc, jj * 128:(jj + 1) * 128], rhs=B.xn[:, kc, 0:Tn],
                        start=(kc == 0), stop=(kc == 7)), reads=[("wu", slot), ("xn", kc)], writes=[PSK(bu)])
                P.add("act", lambda e, j=j, bg=bg: e.activation(out=B.sg[j % 2][:, 0:Tn], in_=ps[bg][:, 0:Tn], func=AF.Silu),
                      reads=[PSK(bg)], writes=[("sg", j % 2)])
                P.add("dve", lambda e, j=j, bu=bu: e.tensor_tensor(out=B.H[:, j, 0:Tn], in0=ps[bu][:, 0:Tn], in1=B.sg[j % 2][:, 0:Tn],
                                                                  op=ALU.mult), reads=[PSK(bu), ("sg", j % 2)], writes=[("H", j)])
        gcount[0] += NG
        for c in range(8):
            bd = 5 + (c % 2)
            for j in range(NJ):
                P.add("pe", lambda e, c=c, j=j, bd=bd: e.matmul(ps[bd][:, 0:Tn], lhsT=B.wd[:, j, c * 128:(c + 1) * 128],
                                                               rhs=B.H[:, j, 0:Tn], start=(j == 0), stop=(j == NJ - 1)),
                      reads=[("wd", j // JG), ("H", j)], writes=[PSK(bd)])
            P.add("dve", lambda e, c=c, bd=bd: e.scalar_tensor_tensor(out=B.ht[:, c, 0:Tn], in0=ps[bd][:, 0:Tn], scalar=hg[:, c:c + 1],
                                                                     in1=B.ht[:, c, 0:Tn], op0=ALU.mult, op1=ALU.add),
                  reads=[PSK(bd), ("ht", c), "coef"], writes=[("ht", c)])

    def alloc_win(B, Tm):
        B.win = alloc(8 * NWIN, BF16, shape=[8, NWIN])
        B.wuq = alloc(2 * 768, BF16, shape=[2, 768])
        B.zst = alloc(NZ * Tm, BF16, shape=[NZ, Tm])
        P.add("dve", lambda e: e.memset(B.zst, 0.0), writes=[("zst", s_) for s_ in range(NZ)])
        B.mst = alloc(2 * Tm, BF16, shape=[2, Tm])
        B.cqn = alloc(2 * Tm, BF16, shape=[2, Tm])
        B.f1 = [alloc(Tm) for _ in range(3)]
        B.rW = alloc(2 * Tm, shape=[2, Tm])
        B.rQ = alloc(2 * Tm, shape=[2, Tm])
        B.rK = alloc(2 * Tm, shape=[2, Tm])


    rr = [0]

    def nb():
        rr[0] = rr[0] % 7 + 1
        return rr[0]

    def win_proj(B, Tn, l, rope, tok0, zdst, zcol0, mdst, mcol0, halo):
        if rope:
            dma("sp", B.rW[:, :, 0:Tn], ropeW.rearrange("(a p) c -> p a c", p=128)[:, :, tok0:tok0 + Tn], [], ["rW"], "rW")
            dma("sp", B.rQ[0:96, :, 0:Tn], ropeQ.rearrange("(a p) c -> p a c", p=96)[:, :, tok0:tok0 + Tn], [], ["rQ"], "rQ")
            dma("sp", B.rK[0:32, :, 0:Tn], ropeK.rearrange("(a p) c -> p a c", p=32)[:, :, tok0:tok0 + Tn], [], ["rK"], "rK")

        def group(M, col0):
            b = nb()
            for kc in range(8):
                P.add("pe", lambda e, kc=kc, b=b: e.matmul(ps[b][0:M, 0:Tn], lhsT=B.win[:, kc, col0:col0 + M], rhs=B.xn[:, kc, 0:Tn],
                                                          start=(kc == 0), stop=(kc == 7)),
                      reads=["win", ("xn", kc)], writes=[PSK(b)])
            return b

        fi = [0]

        def ftmp():
            fi[0] = (fi[0] + 1) % 3
            return fi[0]

        def copy_out(b, M, dst, key):
            P.add("act", lambda e: e.activation(out=dst, in_=ps[b][0:M, 0:Tn], func=AF.Copy), reads=[PSK(b)], writes=[key])

        def rope_out(bA, bB, M, tab, tabkey, dst, key):
            i0, i1 = ftmp(), ftmp()
            P.add("dve", lambda e: e.tensor_tensor(out=B.f1[i0][0:M, 0:Tn], in0=ps[bA][0:M, 0:Tn], in1=tab[0:M, 0, 0:Tn], op=ALU.mult),
                  reads=[PSK(bA), tabkey], writes=[("f1", i0)])
            P.add("dve", lambda e: e.tensor_tensor(out=B.f1[i1][0:M, 0:Tn], in0=ps[bB][0:M, 0:Tn], in1=tab[0:M, 1, 0:Tn], op=ALU.mult),
                  reads=[PSK(bB), tabkey], writes=[("f1", i1)])
            P.add("dve", lambda e: e.tensor_tensor(out=dst, in0=B.f1[i0][0:M, 0:Tn], in1=B.f1[i1][0:M, 0:Tn], op=ALU.add),
                  reads=[("f1", i0), ("f1", i1)], writes=[key])

        def rstd_from(bank_ss, n):
            P.add("act", lambda e: e.activation(out=B.rstd[:, 0:Tn], in_=ps[bank_ss][:, 0:Tn], func=AF.Sqrt, bias=V("eps"), scale=1.0 / n),
                  reads=[PSK(bank_ss), "vecs"], writes=["rstd"])
            P.add("dve", lambda e: e.reciprocal(out=B.rstd[:, 0:Tn], in_=B.rstd[:, 0:Tn]), reads=["rstd"], writes=["rstd"])

        b0 = group(128, 0)
        b1 = group(64, 128)
        P.add("act", lambda e: e.activation(out=B.sq[0][:, 0:Tn], in_=ps[b0][:, 0:Tn], func=AF.Square), reads=[PSK(b0)], writes=[("sq", 0)])
        P.add("act", lambda e: e.activation(out=B.sq[1][0:64, 0:Tn], in_=ps[b1][0:64, 0:Tn], func=AF.Square), reads=[PSK(b1)], writes=[("sq", 1)])
        P.add("pe", lambda e: e.matmul(ps[0][:, 0:Tn], lhsT=ones_b, rhs=B.sq[0][:, 0:Tn], start=True, stop=False),
              reads=[("sq", 0), "ones_b"], writes=[PSK(0)])
        P.add("pe", lambda e: e.matmul(ps[0][:, 0:Tn], lhsT=ones_b[0:64, :], rhs=B.sq[1][0:64, 0:Tn], start=False, stop=True),
              reads=[("sq", 1), "ones_b"], writes=[PSK(0)])
        rstd_from(0, 192)
        P.add("dve", lambda e: e.scalar_tensor_tensor(out=B.cqn[:, 0, 0:Tn], in0=ps[b0][:, 0:Tn], scalar=V("gq", l * 2, 1), in1=B.rstd[:, 0:Tn],
                                                     op0=ALU.mult, op1=ALU.mult), reads=[PSK(b0), "rstd", "vecs"], writes=[("cqn", 0)])
        P.add("dve", lambda e: e.scalar_tensor_tensor(out=B.cqn[0:64, 1, 0:Tn], in0=ps[b1][0:64, 0:Tn], scalar=V("gq", l * 2 + 1, 1)[0:64, :],
                                                     in1=B.rstd[0:64, 0:Tn], op0=ALU.mult, op1=ALU.mult),
              reads=[PSK(b1), "rstd", "vecs"], writes=[("cqn", 1)])
        for h in range(4):
            banks = []
            for rot in ([0, 1] if rope else [0]):
                b = nb()
                c0 = rot * 384 + h * 96
                P.add("pe", lambda e, b=b, c0=c0: e.matmul(ps[b][0:96, 0:Tn], lhsT=B.wuq[:, 0, c0:c0 + 96], rhs=B.cqn[:, 0, 0:Tn], start=True, stop=False),
                      reads=["wuq", ("cqn", 0)], writes=[PSK(b)])
                P.add("pe", lambda e, b=b, c0=c0: e.matmul(ps[b][0:96, 0:Tn], lhsT=B.wuq[0:64, 1, c0:c0 + 96], rhs=B.cqn[0:64, 1, 0:Tn], start=False, stop=True),
                      reads=["wuq", ("cqn", 1)], writes=[PSK(b)])
                banks.append(b)
            if rope:
                rope_out(banks[0], banks[1], 96, B.rQ, "rQ", B.zst[0:96, ZQ + h, 0:Tn], ("zst", ZQ + h))
            else:
                copy_out(banks[0], 96, B.zst[0:96, ZQ + h, 0:Tn], ("zst", ZQ + h))
        bkv = group(128, 192)
        P.add("act", lambda e: e.activation(out=B.sq[0][:, 0:Tn], in_=ps[bkv][:, 0:Tn], func=AF.Square), reads=[PSK(bkv)], writes=[("sq", 0)])
        P.add("pe", lambda e: e.matmul(ps[0][:, 0:Tn], lhsT=ones_b, rhs=B.sq[0][:, 0:Tn], start=True, stop=True),
              reads=[("sq", 0), "ones_b"], writes=[PSK(0)])
        rstd_from(0, 128)
        P.add("dve", lambda e: e.scalar_tensor_tensor(out=B.mst[:, 0, 0:Tn], in0=ps[bkv][:, 0:Tn], scalar=V("gkv", l, 1), in1=B.rstd[:, 0:Tn],
                                                     op0=ALU.mult, op1=ALU.mult), reads=[PSK(bkv), "rstd", "vecs"], writes=[("mst", 0)])
        bA = group(32, 320)
        if rope:
            bB = group(32, 2144)
            rope_out(bA, bB, 32, B.rK, "rK", B.mst[0:32, 1, 0:Tn], ("mst", 1))
        else:
            copy_out(bA, 32, B.mst[0:32, 1, 0:Tn], ("mst", 1))
        for c in range(2):
            b = group(128, 352 + c * 128)
            copy_out(b, 128, B.zst[:, ZSCB + c, 0:Tn], ("zst", ZSCB + c))
        for c in range(2):
            bc = group(128, 608 + c * 128)
            bx = group(128, 864 + c * 128)
            i = ftmp()
            P.add("act", lambda e, i=i, bx=bx: e.activation(out=B.f1[i][:, 0:Tn], in_=ps[bx][:, 0:Tn], func=AF.Copy), reads=[PSK(bx)], writes=[("f1", i)])
            P.add("dve", lambda e, i=i, bc=bc, c=c: e.tensor_tensor(out=B.zst[:, ZSCT + c, 0:Tn], in0=ps[bc][:, 0:Tn], in1=B.f1[i][:, 0:Tn], op=ALU.mult),
                  reads=[PSK(bc), ("f1", i)], writes=[("zst", ZSCT + c)])
        for c in range(2):
            bA = group(128, 1120 + c * 128)
            if rope:
                bB = group(128, 2176 + c * 128)
                rope_out(bA, bB, 128, B.rW, "rW", B.zst[:, ZWAQ + c, 0:Tn], ("zst", ZWAQ + c))
            else:
                copy_out(bA, 128, B.zst[:, ZWAQ + c, 0:Tn], ("zst", ZWAQ + c))
        bA = group(128, 1376)
        if rope:
            bB = group(128, 2432)
            rope_out(bA, bB, 128, B.rW, "rW", B.zst[:, ZWAK, 0:Tn], ("zst", ZWAK))
        else:
            copy_out(bA, 128, B.zst[:, ZWAK, 0:Tn], ("zst", ZWAK))
        b = group(128, 1504)
        copy_out(b, 128, B.zst[:, ZWAV, 0:Tn], ("zst", ZWAV))
        for c in range(2):
            ba = group(128, 1632 + c * 128)
            bg = group(128, 1888 + c * 128)
            i = ftmp()
            P.add("act", lambda e, i=i, bg=bg: e.activation(out=B.f1[i][:, 0:Tn], in_=ps[bg][:, 0:Tn], func=AF.Sigmoid), reads=[PSK(bg)], writes=[("f1", i)])
            P.add("dve", lambda e, i=i, ba=ba, c=c: e.tensor_tensor(out=B.zst[:, ZCFU + c, 0:Tn], in0=ps[ba][:, 0:Tn], in1=B.f1[i][:, 0:Tn], op=ALU.mult),
                  reads=[PSK(ba), ("f1", i)], writes=[("zst", ZCFU + c)])
        zk = [("zst", s) for s in range(NZ)]
        dma("sp", zdst.rearrange("(s p) c -> p s c", p=128)[:, :, zcol0:zcol0 + Tn], B.zst[:, :, 0:Tn], zk, ["zdst"], "zst")
        dma("sp", mdst[0:128, mcol0:mcol0 + Tn], B.mst[:, 0, 0:Tn], [("mst", 0)], ["mdst"], "mst0")
        dma("sp", mdst[128:160, mcol0:mcol0 + Tn], B.mst[0:32, 1, 0:Tn], [("mst", 1)], ["mdst"], "mst1")
        for (which, c0) in halo:
            xh = xin_halo.rearrange("(s p) c -> p s c", p=128)
            dma("sp", xh[:, 0:2, which * 128:(which + 1) * 128], B.zst[:, 6:8, c0:c0 + 128], zk, ["xin_halo"], "hal0")
            dma("sp", xh[:, 2:6, which * 128:(which + 1) * 128], B.zst[:, 10:14, c0:c0 + 128], zk, ["xin_halo"], "hal1")

    def load_win(B, l):
        dma("sp", B.win[:, 0:4, :], wb["w_in", l][0:512, :].rearrange("(k p) m -> p k m", p=128), WK("w_in", l), ["win"], "win0")
        dma("sp", B.win[:, 4:8, :], wb["w_in", l][512:1024, :].rearrange("(k p) m -> p k m", p=128), WK("w_in", l), ["win"], "win1")
        dma("sp", B.wuq[:, 0, :], wb["w_mla_uq", l][0:128, :], WK("w_mla_uq", l), ["wuq"], "wuq0")
        dma("sp", B.wuq[0:64, 1, :], wb["w_mla_uq", l][128:192, :], WK("w_mla_uq", l), ["wuq"], "wuq1")

    def load_h(B, src, col0, Tn):
        dma("sp", B.ht[:, :, 0:Tn], src.rearrange("(k p) c -> p k c", p=128)[:, :, col0:col0 + Tn], ["hsrc"], [("ht", k) for k in range(8)], "hld")

    def store_h(B, dst, col0, Tn, key="hdst"):
        dma("sp", dst.rearrange("(k p) c -> p k c", p=128)[:, :, col0:col0 + Tn], B.ht[:, :, 0:Tn], [("ht", k) for k in range(8)], [key], "hst")

    def tiles_lat_ctx():
        out = [(False, 0, T, t * T) for t in range(NT)]
        out.append((True, 1, CTX, 0))
        return out

    def phase_ffn(l_prev, l_next, first):
        base = bump[0]
        B = alloc_common(T)
        alloc_ffn(B, T)
        if l_next is not None:
            alloc_win(B, T)
            load_win(B, l_next)
        for (is_ctx, j, Tn, tok0) in tiles_lat_ctx():
            if first:
                load_h(B, ctxT if is_ctx else xT, tok0, Tn)
            else:
                if is_ctx and l_next is None:
                    continue
                load_h(B, hcbuf if is_ctx else hbuf, tok0, Tn)
            if l_prev is not None:
                norm_mod(B, Tn, Acoef[:, l_prev, j, 2, :], Bsh(l_prev, j, 2), "f2")
                ffn(B, Tn, l_prev, 2, HG[:, l_prev, j, 2, :])
            if l_next is not None:
                norm_mod(B, Tn, Acoef[:, l_next, j, 0, :], Bsh(l_next, j, 0), "f1")
                ffn(B, Tn, l_next, 1, HG[:, l_next, j, 0, :])
                store_h(B, hcbuf if is_ctx else hbuf, tok0, Tn)
                norm_mod(B, Tn, Acoef[:, l_next, j, 1, :], Bsh(l_next, j, 1), "mx")
                halo = []
                if not is_ctx and tok0 == 0:
                    halo.append((0, 0))
                if not is_ctx and tok0 == S_OWN - T:
                    halo.append((1, T - 128))
                win_proj(B, Tn, l_next, not is_ctx, tok0, zcbuf if is_ctx else zbuf, 128 + tok0,
                         ckvc if is_ctx else xin_mla, tok0, halo)
            else:
                norm_mod(B, Tn, V("gfin", 0, 8), None, "fin")
                for kc in range(8):
                    P.add("dve", lambda e, kc=kc, Tn=Tn: e.tensor_tensor(out=B.tmp[kc % 2][:, 0:Tn], in0=B.ht[:, kc, 0:Tn], in1=B.rstd[:, 0:Tn], op=ALU.mult),
                          reads=[("ht", kc), "rstd"], writes=[("tmp", kc % 2)])
                    P.add("dve", lambda e, kc=kc, Tn=Tn: e.tensor_scalar(out=B.ht[:, kc, 0:Tn], in0=B.tmp[kc % 2][:, 0:Tn], scalar1=V("gfin", kc, 1), scalar2=None,
                                                                 op0=ALU.mult), reads=[("tmp", kc % 2), "vecs"], writes=[("ht", kc)])
                store_h(B, outT, tok0, Tn, key="out")
        P.barrier()
        bump[0] = base

    RG = RG_OVERRIDE[0] or [[0, 1], [2, 3], [4, 5], [6, 7]]

    def exchange(l):
        base = bump[0]
        P.add("pool", lambda e: e.collective_compute("AllGather", ALU.bypass, replica_groups=RG, ins=[xin_mla.opt()], outs=[xout_mla.opt()]),
              reads=["mdst"], writes=["xout_mla"]).signal = True
        P.add("pool", lambda e: e.collective_compute("AllGather", ALU.bypass, replica_groups=RG, ins=[xin_halo.opt()], outs=[xout_halo.opt()]),
              reads=["xin_halo"], writes=["xout_halo"]).signal = True
        hl = alloc(6 * 128, BF16, shape=[6, 128])
        hr = alloc(6 * 128, BF16, shape=[6, 128])
        xo = xout_halo.rearrange("(r s p) c -> r p s c", r=2, p=128)
        dma("sp", hl, xo[0, :, :, 128:256], ["xout_halo"], ["hl"], "hl")
        dma("sp", hr, xo[1, :, :, 0:128], ["xout_halo"], ["hr"], "hr")
        P.add("dve", lambda e: e.tensor_scalar(out=hl, in0=hl, scalar1=V("lm"), scalar2=None, op0=ALU.mult), reads=["hl", "vecs"], writes=["hl"])
        P.add("dve", lambda e: e.tensor_scalar(out=hr, in0=hr, scalar1=V("rm"), scalar2=None, op0=ALU.mult), reads=["hr", "vecs"], writes=["hr"])
        zb = zbuf.rearrange("(s p) c -> p s c", p=128)
        dma("sp", zb[:, 6:8, 0:128], hl[:, 0:2, :], ["hl"], ["zdst"], "hl")
        dma("sp", zb[:, 10:14, 0:128], hl[:, 2:6, :], ["hl"], ["zdst"], "hl")
        dma("sp", zb[:, 6:8, 128 + S_OWN:ZW], hr[:, 0:2, :], ["hr"], ["zdst"], "hr")
        dma("sp", zb[:, 10:14, 128 + S_OWN:ZW], hr[:, 2:6, :], ["hr"], ["zdst"], "hr")
        P.barrier()
        bump[0] = base

    NKC = 66

    def alloc_kv():
        K = FFNBufs()
        K.KT = alloc(4 * NKC * 128, BF16, shape=[4, NKC * 128])
        K.Vx = alloc(NKC * 384, BF16, shape=[NKC, 384])
        K.wukv = alloc(512, BF16)
        K.ckt = [alloc(512, BF16) for _ in range(2)]
        return K

    def kv_build(K, l):
        dma("sp", K.wukv, wb["w_mla_ukv", l], WK("w_mla_ukv", l), ["wukv"], "wukv")
        P.add("dve", lambda e: e.memset(K.Vx, 0.0), writes=["Vx"])
        P.add("dve", lambda e: e.memset(K.Vx.rearrange("p k (a c) -> p k a c", a=2)[:, :, :, 64:65], 1.0), writes=["Vx"])
        srcs = [(xout_mla[r * 160:r * 160 + 128, t8 * 512:(t8 + 1) * 512], 512, r * S_OWN + t8 * 512) for r in range(2) for t8 in range(8)]
        srcs.append((ckvc[0:128, 0:CTX], CTX, 2 * S_OWN))
        for r in range(2):
            for h in range(4):
                dma("sp", K.KT[64:96, h, r * S_OWN:(r + 1) * S_OWN], xout_mla[r * 160 + 128:r * 160 + 160, :], ["xout_mla"], [("KTr", h)], f"ktr{h}")
        for h in range(4):
            dma("sp", K.KT[64:96, h, 2 * S_OWN:2 * S_OWN + CTX], ckvc[128:160, :], ["mdstc"], [("KTr", h)], f"ktr{h}")
        wv = K.wukv.rearrange("p (h c) -> p h c", h=4)[:, :, 64:128]
        for i, (src, n, key0) in enumerate(srcs):
            slot = i % 2
            dma("sp", K.ckt[slot][:, 0:n], src, ["xout_mla", "mdstc"], [("ckt", slot)], f"ckt{slot}")
            for h in range(4):
                b = 1 + (h % 2)
                P.add("pe", lambda e, h=h, b=b, slot=slot, n=n: e.matmul(ps[b][0:64, 0:n], lhsT=K.wukv[:, h * 128:h * 128 + 64], rhs=K.ckt[slot][:, 0:n],
                                                                        start=True, stop=True), reads=["wukv", ("ckt", slot)], writes=[PSK(b)])
                eng = "act" if h % 2 == 0 else "dve"
                if eng == "act":
                    P.add("act", lambda e, h=h, b=b, n=n, key0=key0: e.activation(out=K.KT[0:64, h, key0:key0 + n], in_=ps[b][0:64, 0:n], func=AF.Copy),
                          reads=[PSK(b)], writes=[("KTn", h)])
                else:
                    P.add("dve", lambda e, h=h, b=b, n=n, key0=key0: e.tensor_copy(out=K.KT[0:64, h, key0:key0 + n], in_=ps[b][0:64, 0:n]),
                          reads=[PSK(b)], writes=[("KTn", h)])
            for kb in range(n // 128):
                b = 3 + (kb % 2)
                kc = key0 // 128 + kb
                P.add("pe", lambda e, kb=kb, b=b, slot=slot: e.matmul(ps[b][:, 0:256], lhsT=K.ckt[slot][:, kb * 128:(kb + 1) * 128], rhs=wv,
                                                                     start=True, stop=True), reads=["wukv", ("ckt", slot)], writes=[PSK(b)])
                pv = ps[b][:, 0:256].rearrange("p (a b c) -> p a b c", a=2, b=2)
                vo = K.Vx[:, kc, :].rearrange("p (a b c) -> p a b c", a=2, b=3)
                P.add("act", lambda e, pv=pv, vo=vo: e.activation(out=vo[:, :, 0, :], in_=pv[:, :, 0, :], func=AF.Copy), reads=[PSK(b)], writes=["Vx"])
                P.add("dve", lambda e, pv=pv, vo=vo: e.tensor_copy(out=vo[:, :, 2, :], in_=pv[:, :, 1, :]), reads=[PSK(b)], writes=["Vx"])

    def alloc_mix():
        M = FFNBufs()
        M.ht = alloc(8 * T, shape=[8, T])
        M.mix = alloc(8 * T, BF16, shape=[8, T])
        M.wout = alloc(8 * D, BF16, shape=[8, D])
        M.QT = alloc(4 * T, BF16, shape=[4, T])
        M.PT = [alloc(2 * T, BF16) for _ in range(3)]
        M.scb = alloc(2 * T, BF16, shape=[2, T])
        M.sct = alloc(2 * (T + 2), BF16, shape=[2, T + 2])
        M.waq = alloc(2 * T, BF16, shape=[2, T])
        M.wak = alloc(T + 256, BF16)
        M.wav = alloc(T + 256, BF16)
        M.cfu = alloc(2 * (T + 32), BF16, shape=[2, T + 32])
        M.Vw = alloc(6 * 384, BF16, shape=[6, 384])
        M.wakc = alloc(CTX, BF16)
        M.wavc = alloc(CTX, BF16)
        M.Vwc = alloc(2 * 384, BF16, shape=[2, 384])
        M.acc = [alloc(T) for _ in range(2)]
        M.rinv = alloc(T)
        M.bc = alloc(T)
        M.f = [alloc(T) for _ in range(3)]
        return M

    def transpose_to_triples(M, src, nblk, dst):
        p7 = ps[7][:, :].bitcast(BF16)
        for blk in range(nblk):
            o = (blk % 4) * 128
            P.add("pe", lambda e, blk=blk, o=o: e.transpose(out=p7[:, o:o + 128], in_=src[:, blk * 128:(blk + 1) * 128], identity=ident_b),
                  reads=["wavsrc", "ident"], writes=[PSK(7)])
            pv = p7[:, o:o + 128].rearrange("p (a c) -> p a c", a=2)
            vo = dst[:, blk, :].rearrange("p (a b c) -> p a b c", a=2, b=3)
            P.add("act", lambda e, pv=pv, vo=vo: e.activation(out=vo[:, :, 0, :], in_=pv, func=AF.Copy), reads=[PSK(7)], writes=["Vw"])
            P.add("dve", lambda e, pv=pv, vo=vo: e.tensor_copy(out=vo[:, :, 2, :], in_=pv), reads=[PSK(7)], writes=["Vw"])

    def init_triples(buf):
        P.add("dve", lambda e: e.memset(buf, 0.0), writes=["Vw"])
        P.add("dve", lambda e: e.memset(buf.rearrange("p k (a c) -> p k a c", a=2)[:, :, :, 64:65], 1.0), writes=["Vw"])

    def normalize(M, Tn, ob, lo, l, h, chunk, sink):
        if 'norm' in DBG_SKIP:
            return
        sp = 64 if lo else 0
        mrows = 64 if lo else 128
        r0 = 0 if lo else 64
        if sink:
            P.add("dve", lambda e: e.tensor_scalar(out=M.rinv[sp:sp + 1, 0:Tn], in0=ps[ob][sp:sp + 1, 0:Tn], scalar1=esink[sp:sp + 1, l, h:h + 1],
                                                  scalar2=None, op0=ALU.add), reads=[PSK(ob), "esink"], writes=["rinv"])
            P.add("dve", lambda e: e.reciprocal(out=M.rinv[sp:sp + 1, 0:Tn], in_=M.rinv[sp:sp + 1, 0:Tn]), reads=["rinv"], writes=["rinv"])
        else:
            P.add("dve", lambda e: e.reciprocal(out=M.rinv[sp:sp + 1, 0:Tn], in_=ps[ob][sp:sp + 1, 0:Tn]), reads=[PSK(ob)], writes=["rinv"])
        P.add("pe", lambda e: e.matmul(ps[5][0:mrows, 0:Tn], lhsT=ones_f[sp:sp + 1, 0:mrows], rhs=M.rinv[sp:sp + 1, 0:Tn], start=True, stop=True),
              reads=["rinv", "ones_f"], writes=[PSK(5)])
        P.add("act", lambda e: e.activation(out=M.bc[r0:r0 + 64, 0:Tn], in_=ps[5][r0:r0 + 64, 0:Tn], func=AF.Copy), reads=[PSK(5)], writes=["bc"])
        P.add("dve", lambda e: e.tensor_tensor(out=M.mix[r0:r0 + 64, chunk, 0:Tn], in0=ps[ob][r0:r0 + 64, 0:Tn], in1=M.bc[r0:r0 + 64, 0:Tn], op=ALU.mult),
              reads=[PSK(ob), "bc"], writes=[("mix", chunk)])

    def mixers(K, M, l, is_ctx, t):
        Tn = CTX if is_ctx else T
        tok0 = 0 if is_ctx else t * T
        j = 1 if is_ctx else 0
        zv = (zcbuf if is_ctx else zbuf).rearrange("(s p) c -> p s c", p=128)
        c0 = 128 + tok0
        dma("sp", M.QT[0:96, :, 0:Tn], zv[0:96, 0:4, c0:c0 + Tn], [], ["QT"], "lq")
        dma("sp", M.scb[:, :, 0:Tn], zv[:, 4:6, c0:c0 + Tn], [], ["scb"], "lscb")
        dma("sp", M.sct[:, :, 0:Tn + 2], zv[:, 6:8, c0 - 1:c0 + Tn + 1], [], ["sct"], "lsct")
        dma("sp", M.waq[:, :, 0:Tn], zv[:, 8:10, c0:c0 + Tn], [], ["waq"], "lwaq")
        dma("sp", M.wak[:, 0:Tn + 256], zv[:, 10, c0 - 128:c0 + Tn + 128], [], ["wak"], "lwak")
        dma("sp", M.wav[:, 0:Tn + 256], zv[:, 11, c0 - 128:c0 + Tn + 128], [], ["wavsrc"], "lwav")
        dma("sp", M.cfu[:, :, 0:Tn + 30], zv[:, 12:14, c0 - 15:c0 + Tn + 15], [], ["cfu"], "lcfu")
        hsrc = hcbuf if is_ctx else hbuf
        dma("sp", M.ht[:, :, 0:Tn], hsrc.rearrange("(k p) c -> p k c", p=128)[:, :, tok0:tok0 + Tn], ["hdst"], [("ht", k) for k in range(8)], "hld")

        kcs = [64, 65] if is_ctx else list(range(NKC))
        sc_a = 96.0 ** -0.5
        n = len(kcs)
        for h in (range(4) if 'mla' not in DBG_SKIP else []):
            ob = 4
            lo = (h % 2 == 0)
            vcol = (h // 2) * 192 + (0 if lo else 64)
            npair = n // 2
            SBK = [0, 1, 3]

            def S(p, h=h):
                w = SBK[p % 3]
                for half in range(2):
                    kc = kcs[2 * p + half]
                    P.add("pe", lambda e, kc=kc, half=half, w=w: e.matmul(psw[w][:, half * 512:half * 512 + Tn], lhsT=K.KT[0:96, h, kc * 128:(kc + 1) * 128],
                                                                         rhs=M.QT[0:96, h, 0:Tn], start=True, stop=True),
                          reads=[("KTn", h), ("KTr", h), "QT"], writes=[PSK(2 * w + half)])

            def E(p):
                w = SBK[p % 3]
                src = psw[w][:, :].rearrange("p (a c) -> p a c", a=2)[:, :, 0:Tn]
                dst = M.PT[p % 3].rearrange("p (a c) -> p a c", a=2)[:, :, 0:Tn]
                P.add("act", lambda e: e.activation(out=dst, in_=src, func=AF.Exp, scale=sc_a), reads=[PSK(2 * w), PSK(2 * w + 1)], writes=[("PT", p % 3)])

            def PV(p, ob=ob, vcol=vcol):
                for half in range(2):
                    kc = kcs[2 * p + half]
                    first = (p == 0 and half == 0)
                    last = (p == npair - 1 and half == 1)
                    P.add("pe", lambda e, kc=kc, half=half, first=first, last=last: e.matmul(
                        ps[ob][:, 0:Tn], lhsT=K.Vx[:, kc, vcol:vcol + 128], rhs=M.PT[p % 3][:, half * 512:half * 512 + Tn], start=first, stop=last),
                        reads=["Vx", ("PT", p % 3)], writes=[PSK(ob)])

            for p in range(min(3, npair)):
                S(p)
            for p in range(npair):
                E(p)
                PV(p)
                if p + 3 < npair:
                    S(p + 3)
            normalize(M, Tn, ob, lo, l, h, h // 2, False)

        sc_w = 64.0 ** -0.5
        nqb = Tn // 128
        if not is_ctx and 'tr' not in DBG_SKIP:
            transpose_to_triples(M, M.wav, nqb + 2, M.Vw)
        for g in (range(2) if 'win' not in DBG_SKIP else []):
            pr = slice(g * 64, (g + 1) * 64)
            for qb in range(nqb):
                if is_ctx:
                    kbl = [("ctx", 0, None), ("ctx", 1, None)]
                else:
                    mP = 2 if (t == 0 and qb == 0) else 0
                    mN = 3 if (t == NT - 1 and qb == nqb - 1) else 1
                    kbl = [("loc", qb, mP), ("loc", qb + 1, None), ("loc", qb + 2, mN), ("ctx", 0, None), ("ctx", 1, None)]
                for i, (kind, blk, mi) in enumerate(kbl):
                    b, off = i // 2, (i % 2) * 256
                    ksrc = M.wakc if kind == "ctx" else M.wak
                    outv = ps[b][:, off:off + 256].rearrange("p (a c) -> p a c", a=2)
                    kl = ksrc[pr, blk * 128:(blk + 1) * 128]
                    qr = M.waq[pr, :, qb * 128:(qb + 1) * 128]
                    P.add("pe", lambda e, kl=kl, qr=qr, outv=outv, mi=mi: e.matmul(outv, lhsT=kl, rhs=qr, start=True, stop=(mi is None)),
                          reads=["wak", "wakc", "waq"], writes=[PSK(b)])
                    if mi is not None:
                        P.add("pe", lambda e, b=b, off=off, mi=mi: e.matmul(ps[b][:, off:off + 256], lhsT=ident_b, rhs=masks_b[:, mi, :], start=False, stop=True),
                              reads=["ident", "masks"], writes=[PSK(b)])
                ntile = (len(kbl) + 1) // 2
                for i in range(ntile):
                    w = 256 * min(2, len(kbl) - 2 * i)
                    P.add("act", lambda e, i=i, w=w: e.activation(out=M.PT[i][:, 0:w], in_=ps[i][:, 0:w], func=AF.Exp, scale=sc_w),
                          reads=[PSK(i)], writes=[("PT", i)])
                for c in range(2):
                    ob = 3 + c
                    for i, (kind, blk, mi) in enumerate(kbl):
                        vsrc = M.Vwc if kind == "ctx" else M.Vw
                        col = (i % 2) * 256 + c * 128
                        vl = vsrc[:, blk, g * 192 + c * 64:g * 192 + c * 64 + 128]
                        pr_ = M.PT[i // 2][:, col:col + 128]
                        oo = ps[ob][:, qb * 128:(qb + 1) * 128]
                        last = (i == len(kbl) - 1)
                        P.add("pe", lambda e, vl=vl, pr_=pr_, oo=oo, i=i, last=last: e.matmul(oo, lhsT=vl, rhs=pr_, start=(i == 0), stop=last),
                              reads=["Vw", ("PT", i // 2)], writes=[PSK(ob)])
            for c in range(2):
                normalize(M, Tn, 3 + c, c == 0, l, 2 * g + c, 4 + g, True)

        def wsc(c, k):
            return V("wsc", (l * 2 + c) * 3 + k, 1)

        def wcf(c, k):
            return V("wcf", (l * 2 + c) * 31 + k, 1)

        for c in (range(2) if 'sc' not in DBG_SKIP else []):
            acc = M.acc[c]
            P.add("dve", lambda e, c=c, acc=acc: e.tensor_scalar(out=acc[:, 0:Tn], in0=M.sct[:, c, 0:Tn], scalar1=wsc(c, 0), scalar2=None, op0=ALU.mult),
                  reads=["sct", "vecs"], writes=[("acc", c)])
            for k in (1, 2):
                P.add("dve", lambda e, c=c, k=k, acc=acc: e.scalar_tensor_tensor(out=acc[:, 0:Tn], in0=M.sct[:, c, k:k + Tn], scalar=wsc(c, k), in1=acc[:, 0:Tn],
                                                                                 op0=ALU.mult, op1=ALU.add), reads=["sct", ("acc", c), "vecs"], writes=[("acc", c)])
            P.add("dve", lambda e, c=c, acc=acc: e.tensor_tensor(out=M.mix[:, 2 + c, 0:Tn], in0=acc[:, 0:Tn], in1=M.scb[:, c, 0:Tn], op=ALU.mult),
                  reads=[("acc", c), "scb"], writes=[("mix", 2 + c)])
        for c in range(2):
            acc = M.acc[c]
            P.add("dve", lambda e, c=c, acc=acc: e.tensor_scalar(out=acc[:, 0:Tn], in0=M.cfu[:, c, 0:Tn], scalar1=wcf(c, 0), scalar2=V("bcf", l * 2 + c, 1),
                                                                 op0=ALU.mult, op1=ALU.add), reads=["cfu", "vecs", ("acc", c)], writes=[("acc", c)])
            for k in range(1, 31):
                P.add("dve", lambda e, c=c, k=k, acc=acc: e.scalar_tensor_tensor(out=acc[:, 0:Tn], in0=M.cfu[:, c, k:k + Tn], scalar=wcf(c, k), in1=acc[:, 0:Tn],
                                                                                 op0=ALU.mult, op1=ALU.add), reads=["cfu", ("acc", c), "vecs"], writes=[("acc", c)])
            P.add("dve", lambda e, c=c, acc=acc: e.tensor_tensor(out=M.f[c][:, 0:Tn], in0=acc[:, 0:Tn], in1=acc[:, 0:Tn], op=ALU.mult),
                  reads=[("acc", c)], writes=[("f", c)])
        for c in range(2):
            P.add("pe", lambda e, c=c: e.matmul(ps[6][:, 0:Tn], lhsT=ones_f, rhs=M.acc[c][:, 0:Tn], start=(c == 0), stop=(c == 1)),
                  reads=[("acc", c), "ones_f"], writes=[PSK(6)])
        for c in range(2):
            P.add("pe", lambda e, c=c: e.matmul(ps[7][:, 0:Tn], lhsT=ones_f, rhs=M.f[c][:, 0:Tn], start=(c == 0), stop=(c == 1)),
                  reads=[("f", c), "ones_f"], writes=[PSK(7)])
        P.add("act", lambda e: e.activation(out=M.f[2][:, 0:Tn], in_=ps[6][:, 0:Tn], func=AF.Identity, bias=V("zero"), scale=1.0 / 256),
              reads=[PSK(6), "vecs"], writes=[("f", 2)])
        P.add("dve", lambda e: e.tensor_tensor(out=M.f[0][:, 0:Tn], in0=M.f[2][:, 0:Tn], in1=M.f[2][:, 0:Tn], op=ALU.mult), reads=[("f", 2)], writes=[("f", 0)])
        P.add("dve", lambda e: e.scalar_tensor_tensor(out=M.f[0][:, 0:Tn], in0=ps[7][:, 0:Tn], scalar=1.0 / 256, in1=M.f[0][:, 0:Tn], op0=ALU.mult, op1=ALU.subtract),
              reads=[PSK(7), ("f", 0)], writes=[("f", 0)])
        P.add("act", lambda e: e.activation(out=M.f[0][:, 0:Tn], in_=M.f[0][:, 0:Tn], func=AF.Sqrt, bias=V("eps"), scale=1.0), reads=[("f", 0), "vecs"], writes=[("f", 0)])
        P.add("dve", lambda e: e.reciprocal(out=M.f[0][:, 0:Tn], in_=M.f[0][:, 0:Tn]), reads=[("f", 0)], writes=[("f", 0)])
        for c in range(2):
            acc = M.acc[c]
            P.add("dve", lambda e, acc=acc: e.tensor_tensor(out=acc[:, 0:Tn], in0=acc[:, 0:Tn], in1=M.f[2][:, 0:Tn], op=ALU.subtract),
                  reads=[("acc", c), ("f", 2)], writes=[("acc", c)])
            P.add("dve", lambda e, acc=acc: e.tensor_tensor(out=acc[:, 0:Tn], in0=acc[:, 0:Tn], in1=M.f[0][:, 0:Tn], op=ALU.mult),
                  reads=[("acc", c), ("f", 0)], writes=[("acc", c)])
            P.add("act", lambda e, c=c, acc=acc: e.activation(out=M.mix[:, 6 + c, 0:Tn], in_=acc[:, 0:Tn], func=AF.Silu, bias=V("bln", l * 2 + c, 1),
                                                             scale=V("gln", l * 2 + c, 1)), reads=[("acc", c), "vecs"], writes=[("mix", 6 + c)])
        G2 = HG[:, l, j, 1, :]
        for c in range(8):
            bd = 1 + (c % 2)
            for k in range(8):
                P.add("pe", lambda e, c=c, k=k, bd=bd: e.matmul(ps[bd][:, 0:Tn], lhsT=M.wout[:, k, c * 128:(c + 1) * 128], rhs=M.mix[:, k, 0:Tn],
                                                               start=(k == 0), stop=(k == 7)), reads=["wout", ("mix", k)], writes=[PSK(bd)])
            P.add("dve", lambda e, c=c, bd=bd: e.scalar_tensor_tensor(out=M.ht[:, c, 0:Tn], in0=ps[bd][:, 0:Tn], scalar=G2[:, c:c + 1], in1=M.ht[:, c, 0:Tn],
                                                                     op0=ALU.mult, op1=ALU.add), reads=[PSK(bd), ("ht", c), "coef"], writes=[("ht", c)])
        dma("sp", hsrc.rearrange("(k p) c -> p k c", p=128)[:, :, tok0:tok0 + Tn], M.ht[:, :, 0:Tn], [("ht", k) for k in range(8)], ["hdst"], "hst")

    def phase_mix(l):
        base = bump[0]
        K = alloc_kv()
        kv_build(K, l)
        M = alloc_mix()
        dma("sp", M.wout, wb["w_out", l].rearrange("(k p) m -> p k m", p=128), WK("w_out", l), ["wout"], "wout")
        zc = zcbuf.rearrange("(s p) c -> p s c", p=128)
        dma("sp", M.wakc, zc[:, 10, 128:128 + CTX], [], ["wakc"], "lwakc")
        dma("sp", M.wavc, zc[:, 11, 128:128 + CTX], [], ["wavsrc"], "lwavc")
        init_triples(M.Vw)
        init_triples(M.Vwc)
        transpose_to_triples(M, M.wavc, 2, M.Vwc)
        for t in range(min(NT, DBG_MIXT[0])):
            mixers(K, M, l, False, t)
        if l == 0 and DBG_MIXT[0] >= NT:
            mixers(K, M, l, True, 0)
        P.barrier()
        bump[0] = base

    zt = alloc(NZ * 128, BF16, shape=[NZ, 128])
    P.add("dve", lambda e: e.memset(zt, 0.0), writes=["zt"])
    zcv = zcbuf.rearrange("(s p) c -> p s c", p=128)
    dma("sp", zcv[:, :, 0:128], zt, ["zt"], ["zc0"], "zc0")
    dma("sp", zcv[:, :, 128 + CTX:ZCW], zt, ["zt"], ["zc1"], "zc1")
    P.barrier()
    bump[0] = persist_end

    stage = STAGE[0]
    phase_ffn(None, 0, True)
    if stage >= 1.2:
        exchange(0)
    if stage >= 1.5:
        phase_mix(0)
    if stage >= 2.5:
        phase_ffn(0, 1, False)
    if stage >= 2.7:
        exchange(1)
        phase_mix(1)
    if stage >= 3:
        phase_ffn(1, None, False)
    if stage < 3:
        base = bump[0]
        Bd = alloc_common(T)
        for t in range(NT):
            load_h(Bd, hbuf, t * T, T)
            store_h(Bd, outT, t * T, T, key="out")
        if 'dumpc' in DBG_SKIP:
            load_h(Bd, hcbuf, 0, CTX)
            store_h(Bd, outT, 0, CTX, key="out")
        P.barrier()
        bump[0] = base
    P.barrier()
    P.emit(nc, st)
    st.close()
    return nc


STAGE = [3]
DBG_MIXT = [99]
DBG_SKIP = set()
RG_OVERRIDE = [None]


def _rope_tables(half):
    pos = half * S_OWN + np.arange(S_OWN)
    row = (pos // 64).astype(np.float32)
    col = (pos % 64).astype(np.float32)

    def tab(d_rot):
        d_ax = d_rot // 2
        inv = (10000.0 ** (-np.arange(0, d_ax, 2, dtype=np.float32) / d_ax)).astype(np.float32)
        ar = row[:, None] * inv[None, :]
        ac = col[:, None] * inv[None, :]
        cr, sr, cc_, sc_ = np.cos(ar), np.sin(ar), np.cos(ac), np.sin(ac)
        C = np.concatenate([cr, cr, cc_, cc_], axis=1).T.astype(np.float32)
        Sg = np.concatenate([-sr, sr, -sc_, sc_], axis=1).T.astype(np.float32)
        return C, Sg

    Cw, Sw = tab(64)
    ropeW = np.concatenate([np.tile(Cw, (2, 1)), np.tile(Sw, (2, 1))], axis=0)
    Cm, Sm = tab(32)
    ropeK = np.concatenate([Cm, Sm], axis=0)
    Cq = np.concatenate([np.ones((64, S_OWN), np.float32), Cm], axis=0)
    Sq = np.concatenate([np.zeros((64, S_OWN), np.float32), Sm], axis=0)
    ropeQ = np.concatenate([Cq, Sq], axis=0)
    return np.ascontiguousarray(ropeW), np.ascontiguousarray(ropeQ), np.ascontiguousarray(ropeK)


def _masks(half):
    kp = np.arange(128)[:, None]
    qp = np.arange(128)[None, :]
    mP = np.where(qp <= kp, 0.0, NEGM).astype(np.float32)
    mN = np.where(kp <= qp, 0.0, NEGM).astype(np.float32)
    neg = np.full((128, 128), NEGM, np.float32)
    kinds = [mP, mN, mP if half == 1 else neg, mN if half == 0 else neg]
    m = np.stack([np.concatenate([k, k], axis=1) for k in kinds], axis=1)
    return np.ascontiguousarray(m.reshape(128, 4 * 256))


def _fm(v):
    v = np.asarray(v, np.float32)
    lead = v.shape[:-1]
    n = v.shape[-1] // 128
    return np.moveaxis(v.reshape(lead + (n, 128)), -1, 0)


def _pack_vecs(inp, half):
    vec = np.zeros((128, NV), np.float32)

    def put(name, arr):
        arr = np.asarray(arr, np.float32).reshape(128, -1)
        w = dict(_VSPEC)[name]
        assert arr.shape[1] == w, (name, arr.shape, w)
        vec[:, VOFF[name]:VOFF[name] + w] = arr

    put("gf1", _fm(inp["g_ffn1"]))
    put("gmix", _fm(inp["g_mix"]))
    put("gf2", _fm(inp["g_ffn2"]))
    put("bmod", _fm(inp["b_mod"]))
    put("gfin", _fm(inp["g_final"]))
    gq = np.zeros((L, 256), np.float32)
    gq[:, :192] = inp["g_mla_q"]
    put("gq", _fm(gq))
    put("gkv", _fm(inp["g_mla_kv"]))
    wsc = np.asarray(inp["w_sc_conv"], np.float32)
    put("wsc", np.transpose(wsc.reshape(L, 3, 2, 128), (3, 0, 2, 1)))
    wcf = np.asarray(inp["w_cf_conv"], np.float32)
    put("wcf", np.transpose(wcf.reshape(L, 31, 2, 128), (3, 0, 2, 1)))
    put("bcf", _fm(inp["b_cf_conv"]))
    put("gln", _fm(inp["g_cf_ln"]))
    put("bln", _fm(inp["b_cf_ln"]))
    put("sink", np.broadcast_to(np.asarray(inp["wa_sink"], np.float32).reshape(1, L * 4), (128, L * 4)))
    put("lm", np.full((128, 1), 1.0 if half == 1 else 0.0, np.float32))
    put("rm", np.full((128, 1), 1.0 if half == 0 else 0.0, np.float32))
    put("eps", np.full((128, 1), EPS, np.float32))
    put("zero", np.zeros((128, 1), np.float32))
    return vec


_NC_CACHE = {}


def kernel(**inputs):
    inp = {k: np.asarray(v) for k, v in inputs.items()}
    x = inp["x"].astype(np.float32, copy=False)
    ctx = inp["ctx"].astype(np.float32, copy=False)
    Bn = x.shape[0]
    key = STAGE[0]
    if key not in _NC_CACHE:
        _NC_CACHE[key] = build()
    nc = _NC_CACHE[key]
    shared = {n: np.ascontiguousarray(inp[n], dtype=np.float32) for n in
              ("w_mod", "w1_gate", "w1_up", "w1_down", "w2_gate", "w2_up", "w2_down", "w_in", "w_out", "w_mla_uq", "w_mla_ukv")}
    ident = np.eye(128, dtype=np.float32)
    in_maps = []
    for core in range(8):
        b, half = core // 2, core % 2
        rW, rQ, rK = _rope_tables(half)
        cc = np.stack([_fm(inp["c"][b]), _fm(inp["c_ctx"])], axis=-1).reshape(128, 16)
        m = {"xT": np.ascontiguousarray(x[b, half * S_OWN:(half + 1) * S_OWN, :].T),
             "ctxT": np.ascontiguousarray(ctx[b].T),
             "cc": np.ascontiguousarray(cc, dtype=np.float32),
             "vecs": _pack_vecs(inp, half),
             "ropeW": rW, "ropeQ": rQ, "ropeK": rK,
             "masks": _masks(half), "ident": ident}
        m.update(shared)
        in_maps.append(m)
    res = run_bass_kernel_spmd(nc, in_maps, core_ids=list(range(8)))
    out = np.empty((Bn, 2 * S_OWN, D), np.float32)
    for core in range(8):
        b, half = core // 2, core % 2
        out[b, half * S_OWN:(half + 1) * S_OWN, :] = np.asarray(res.results[core]["outT"]).T
    return out
```

```python
import contextlib
import numpy as np
import concourse.bass as bass
import concourse.mybir as mybir
from concourse.bass_utils import run_bass_kernel_spmd

F32 = mybir.dt.float32
BF16 = mybir.dt.bfloat16
AF = mybir.ActivationFunctionType
ALU = mybir.AluOpType
ENGS = ("pe", "act", "dve", "pool", "sp")

L = 2
D = 1024
S_OWN = 4096
CTX = 256
T = 512
NT = S_OWN // T
DFF = 2816
NJ = DFF // 128
NWIN = 2560
ZW = 128 + S_OWN + 128
ZCW = 128 + CTX + 128
EPS = 1e-6
NZ = 14
ZQ, ZSCB, ZSCT, ZWAQ, ZWAK, ZWAV, ZCFU = 0, 4, 6, 8, 10, 11, 12
HALO_SLOTS = (6, 7, 10, 11, 12, 13)
NEGM = -30000.0

_VSPEC = [("gf1", L * 8), ("gmix", L * 8), ("gf2", L * 8), ("bmod", L * 72), ("gfin", 8), ("gq", L * 2),
          ("gkv", L), ("wsc", L * 6), ("wcf", L * 62), ("bcf", L * 2), ("gln", L * 2), ("bln", L * 2),
          ("sink", L * 4), ("lm", 1), ("rm", 1), ("eps", 1), ("zero", 1), ("bsel", 4)]
VOFF = {}
_o = 0
for _n, _w in _VSPEC:
    VOFF[_n] = _o
    _o += _w
NV = _o


class Op:
    __slots__ = ("eng", "fn", "deps", "signal", "count", "chan", "idx")

    def __init__(self, eng, fn, chan):
        self.idx = 0
        self.eng = eng
        self.fn = fn
        self.deps = set()
        self.signal = False
        self.count = 0
        self.chan = chan


class Chan:
    def __init__(self, name):
        self.name = name
        self.sem = None
        self.n = 0
        self.last = None


class Prog:
    def __init__(self):
        self.ops = {e: [] for e in ENGS}
        self.last_w = {}
        self.readers = {}
        self.chans = []

    def chan(self, name):
        c = Chan(name)
        self.chans.append(c)
        return c

    def add(self, eng, fn, reads=(), writes=(), chan=None):
        op = Op(eng, fn, chan)
        deps = set()
        for k in reads:
            w = self.last_w.get(k)
            if w is not None:
                deps.add(w)
        for k in writes:
            w = self.last_w.get(k)
            if w is not None:
                deps.add(w)
            deps.update(self.readers.get(k, ()))
        if chan is not None:
            if chan.last is not None:
                deps.add(chan.last)
            chan.last = op
            chan.n += 1
            op.count = 16 * chan.n
            op.signal = True
        deps.discard(op)
        if eng == "pe":
            deps = {d for d in deps if not (d.eng == "pe" and d.chan is None)}
        best = {}
        for d in deps:
            k = ("c", id(d.chan)) if d.chan is not None else ("e", d.eng)
            if k not in best or best[k].idx < d.idx:
                best[k] = d
        deps = set(best.values())
        for d in deps:
            d.signal = True
        op.deps = deps
        op.idx = len(self.ops[eng]) if chan is None else chan.n
        for k in reads:
            self.readers.setdefault(k, []).append(op)
        for k in writes:
            self.last_w[k] = op
            self.readers[k] = []
        self.ops[eng].append(op)
        return op

    def barrier(self):
        lasts = []
        for e in ENGS:
            for op in reversed(self.ops[e]):
                if op.chan is None and op.fn is not None:
                    lasts.append(op)
                    break
        for c in self.chans:
            if c.last is not None:
                lasts.append(c.last)
        for e in ENGS:
            op = Op(e, None, None)
            op.deps = set(lasts)
            for d in op.deps:
                d.signal = True
            self.ops[e].append(op)
        self.last_w = {}
        self.readers = {}

    def emit(self, nc, stack):
        esem = {e: stack.enter_context(nc.semaphore("s_" + e)) for e in ENGS}
        for c in self.chans:
            if c.n > 0:
                c.sem = stack.enter_context(nc.semaphore("c_" + c.name))
        for e in ENGS:
            n = 0
            for op in self.ops[e]:
                if op.chan is None and op.signal:
                    n += 1
                    op.count = n
        block = stack.enter_context(nc.Block())

        def run(e, eng):
            waited = {}
            for op in self.ops[e]:
                need = {}
                for d in op.deps:
                    if d.chan is not None:
                        s, v = d.chan.sem, d.count
                    else:
                        s, v = esem[d.eng], d.count
                    key = id(s)
                    if key not in need or need[key][1] < v:
                        need[key] = (s, v)
                for key, (s, v) in need.items():
                    if waited.get(key, 0) < v:
                        eng.wait_ge(s, v)
                        waited[key] = v
                if op.fn is None:
                    continue
                ins = op.fn(eng)
                if op.chan is not None:
                    ins.then_inc(op.chan.sem, 16)
                elif op.signal:
                    ins.then_inc(esem[e], 1)

        @block.tensor
        def _(eng):
            run("pe", eng)

        @block.scalar
        def _(eng):
            run("act", eng)

        @block.vector
        def _(eng):
            run("dve", eng)

        @block.gpsimd
        def _(eng):
            run("pool", eng)

        @block.sync
        def _(eng):
            run("sp", eng)


def build(dbg=None):
    nc = bass.Bass("TRN2", target_bir_lowering=False)
    P = Prog()
    st = contextlib.ExitStack()

    def din(name, shape):
        return nc.dram_tensor(name, list(shape), F32, kind="ExternalInput").ap()

    def dscr(name, shape, dt):
        return nc.dram_tensor(name, list(shape), dt).ap()

    xT = din("xT", [D, S_OWN])
    ctxT = din("ctxT", [D, CTX])
    cc_in = din("cc", [128, 40])
    vecs_in = din("vecs", [128, NV])
    ropeW = din("ropeW", [256, S_OWN])
    ropeQ = din("ropeQ", [192, S_OWN])
    ropeK = din("ropeK", [64, S_OWN])
    masks_in = din("masks", [128, 4 * 256])
    ident_in = din("ident", [128, 128])
    w_mod = din("w_mod", [L, D, 9 * D // MOD_RANKS[0]])
    wsrc = {n: din(n, s) for n, s in [
        ("w1_gate", [L, D, DFF]), ("w1_up", [L, D, DFF]), ("w1_down", [L, DFF, D]),
        ("w2_gate", [L, D, DFF]), ("w2_up", [L, D, DFF]), ("w2_down", [L, DFF, D]),
        ("w_in", [L, D, 2144]), ("w_out", [L, D, D]), ("w_mla_uq", [L, 192, 384]), ("w_mla_ukv", [L, 128, 512])]}
    outT = nc.dram_tensor("outT", [D, S_OWN], F32, kind="ExternalOutput").ap()
    dbg_out = None
    if dbg:
        dbg_out = nc.dram_tensor("dbg", list(dbg), F32, kind="ExternalOutput").ap()

    hbuf = dscr("hbuf", [D, S_OWN], F32)
    hcbuf = dscr("hcbuf", [D, CTX], F32)
    zbuf = dscr("zbuf", [NZ * 128, ZW], BF16)
    zcbuf = dscr("zcbuf", [NZ * 128, ZCW], BF16)
    xin_mla = dscr("xin_mla", [160, S_OWN], BF16)
    xout_mla = dscr("xout_mla", [320, S_OWN], BF16)
    ckvc = dscr("ckvc", [160, CTX], BF16)
    xin_halo = dscr("xin_halo", [768, 256], BF16)
    xout_halo = dscr("xout_halo", [1536, 256], BF16)
    wb = {}
    for l in range(L):
        for n in ("w1_gate", "w1_up", "w2_gate", "w2_up"):
            wb[n, l] = dscr(f"b_{n}{l}", [D, DFF], BF16)
        for n in ("w1_down", "w2_down"):
            wb[n, l] = dscr(f"b_{n}{l}", [DFF, D], BF16)
        wb["w_in", l] = dscr(f"b_w_in{l}", [D, NWIN], BF16)
        wb["w_out", l] = dscr(f"b_w_out{l}", [D, D], BF16)
        wb["w_mla_uq", l] = dscr(f"b_wuq{l}", [192, 768], BF16)
        wb["w_mla_ukv", l] = dscr(f"b_wukv{l}", [128, 512], BF16)

    ARENA = 53000
    arena = st.enter_context(nc.sbuf_tensor("arena", [128, ARENA], F32))
    bump = [0]

    def alloc(cols, dt=F32, shape=None):
        words = cols if dt == F32 else (cols + 1) // 2
        a = bump[0]
        bump[0] += words
        assert bump[0] <= ARENA, ("SBUF arena overflow", bump[0])
        ap = arena[:, a:a + words]
        if dt != F32:
            ap = ap.bitcast(dt)[:, 0:cols]
        if shape is not None:
            names = " ".join(f"d{i}" for i in range(len(shape)))
            kw = {f"d{i}": s for i, s in enumerate(shape)}
            ap = ap.rearrange(f"p ({names}) -> p {names}", **kw)
        return ap

    psw = [st.enter_context(nc.psum_tensor(f"psw{i}", [128, 1024], F32)) for i in range(4)]
    ps = []
    for i in range(4):
        ps.append(psw[i][:, 0:512])
        ps.append(psw[i][:, 512:1024])

    def PSK(i):
        return ("ps", i)

    vecs = alloc(NV)
    ccs = alloc(40)
    modv = alloc(L * 144, shape=[L, 72, 2])
    Acoef = alloc(L * 2 * 3 * 8, shape=[L, 2, 3, 8])
    HG = alloc(L * 2 * 3 * 8, shape=[L, 2, 3, 8])
    esink = alloc(L * 4, shape=[L, 4])
    ones_b = alloc(128, BF16)
    ones_f = alloc(128)
    ident_b = alloc(128, BF16)
    masks_b = alloc(4 * 256, BF16, shape=[4, 256])
    persist_end = bump[0]

    def V(name, off=0, w=1):
        o = VOFF[name] + off
        return vecs[:, o:o + w]

    chn = {}

    def CH(name):
        if name not in chn:
            chn[name] = P.chan(name)
        return chn[name]

    def dma(q, out, in_, reads, writes, ch):
        return P.add(q, lambda e: e.dma_start(out=out, in_=in_), reads=reads, writes=writes, chan=CH(ch))

    dma("sp", vecs, vecs_in, [], ["vecs"], "ld0")
    dma("sp", ccs, cc_in, [], ["ccs"], "ld1")
    dma("pool", ident_b, ident_in, [], ["ident"], "cv0")
    dma("pool", masks_b.rearrange("p a b -> p (a b)"), masks_in, [], ["masks"], "cv1")
    P.add("dve", lambda e: e.memset(ones_b, 1.0), writes=["ones_b"])
    P.add("dve", lambda e: e.memset(ones_f, 1.0), writes=["ones_f"])
    P.add("act", lambda e: e.activation(out=ccs, in_=ccs, func=AF.Silu), reads=["ccs"], writes=["ccs"])
    P.add("act", lambda e: e.activation(out=esink.rearrange("p a b -> p (a b)"), in_=V("sink", 0, L * 4), func=AF.Exp),
          reads=["vecs"], writes=["esink"])

    cvn = [0]

    def conv(out, in_, key):
        cvn[0] += 1
        dma("pool", out, in_, [], [key], f"cv{cvn[0] % 8}")

    def conv_rows(name, l, nrows, dst=None, c0=0, c1=None, d0=0):
        src = wsrc[name]
        c1 = c1 if c1 is not None else src.shape[2]
        dstap = wb[name, l] if dst is None else dst
        for r in range(0, nrows, 128):
            rr = min(128, nrows - r)
            conv(dstap[r:r + rr, d0:d0 + (c1 - c0)], src[l, r:r + rr, c0:c1], (name, l, r // 128))

    def conv_layer_ffn(l, which):
        conv_rows(f"w{which}_gate", l, D)
        conv_rows(f"w{which}_up", l, D)
        conv_rows(f"w{which}_down", l, DFF)

    def conv_layer_mix(l):
        src = wsrc["w_in"]
        dst = wb["w_in", l]
        segs = [(0, 1120, 0), (1120, 1184, 1120), (1248, 1312, 1184), (1184, 1248, 1248), (1312, 1376, 1312),
                (1376, 2144, 1376)]
        for b8, s8 in enumerate([8, 0, 24, 16]):
            segs.append((320 + s8, 320 + s8 + 8, 2144 + 8 * b8))
        for hi, hsrc in enumerate([1120, 1248, 1184, 1312, 1376, 1440]):
            for b16, s16 in enumerate([16, 0, 48, 32]):
                segs.append((hsrc + s16, hsrc + s16 + 16, 2176 + 64 * hi + 16 * b16))
        for (c0, c1, d0) in segs:
            for r in range(0, D, 512):
                conv(dst[r:r + 512, d0:d0 + (c1 - c0)], src[l, r:r + 512, c0:c1], ("w_in", l))
        conv_rows("w_out", l, D)
        srcq = wsrc["w_mla_uq"]
        dq = wb["w_mla_uq", l]
        conv(dq[0:128, 0:384], srcq[l, 0:128, :], ("w_mla_uq", l))
        conv(dq[128:192, 0:384], srcq[l, 128:192, :], ("w_mla_uq", l))
        for h in range(4):
            conv(dq[0:192, 384 + h * 96:384 + h * 96 + 64], srcq[l, :, h * 96:h * 96 + 64], ("w_mla_uq", l))
            for b8, s8 in enumerate([8, 0, 24, 16]):
                conv(dq[0:192, 384 + h * 96 + 64 + 8 * b8:384 + h * 96 + 64 + 8 * b8 + 8],
                     srcq[l, :, h * 96 + 64 + s8:h * 96 + 64 + s8 + 8], ("w_mla_uq", l))
        conv(wb["w_mla_ukv", l][:, :], wsrc["w_mla_ukv"][l, :, :], ("w_mla_ukv", l))

    conv_layer_ffn(0, 1)
    conv_layer_mix(0)
    conv_layer_ffn(0, 2)
    conv_layer_ffn(1, 1)
    conv_layer_mix(1)
    conv_layer_ffn(1, 2)

    def WK(name, l):
        if name == "w_in" or name.startswith("w_mla"):
            return [(name, l)]
        n = DFF if name.endswith("down") else D
        return [(name, l, r) for r in range((n + 127) // 128)]

    def setup_mod():
        base = bump[0]
        nsh = MOD_RANKS[0]
        nq = 72 // nsh
        slab = alloc(8 * nq * 128, shape=[8, nq * 128])
        part = alloc(L * nq * 5)
        modall = alloc(nsh * L * nq * 5, shape=[nsh, L * nq, 5])
        modx_in = dscr("modx_in", [128, L * nq * 5], F32)
        modx_out = dscr("modx_out", [nsh * 128, L * nq * 5], F32)
        for l in range(L):
            for kc in range(8):
                dma("sp", slab[:, kc, :], w_mod[l, kc * 128:(kc + 1) * 128, :], [], [("wm", kc)], f"wm{kc % 2}")
            for oc in range(nq):
                col = (l * nq + oc) * 5
                for kc in range(8):
                    P.add("pe", lambda e, kc=kc, oc=oc, col=col: e.matmul(
                        ps[0][:, col:col + 5], lhsT=slab[:, kc, oc * 128:(oc + 1) * 128],
                        rhs=ccs[:, kc * 5:kc * 5 + 5], start=(kc == 0), stop=(kc == 7)),
                        reads=[("wm", kc), "ccs"], writes=[PSK(0)])
        P.add("act", lambda e: e.activation(out=part, in_=ps[0][:, 0:L * nq * 5], func=AF.Copy), reads=[PSK(0)], writes=["part"])
        dma("sp", modx_in, part, ["part"], ["modx_in"], "ld0")
        rg = MOD_GROUPS[0]
        P.add("pool", lambda e: e.collective_compute("AllGather", ALU.bypass, replica_groups=rg, ins=[modx_in.opt()], outs=[modx_out.opt()]),
              reads=["modx_in"], writes=["modx_out"]).signal = True
        dma("sp", modall.rearrange("p r q f -> p r (q f)"), modx_out.rearrange("(r p) c -> p r c", p=128), ["modx_out"], ["modall"], "ld1")
        for l in range(L):
            mv = modv[:, l, :, :].rearrange("p (r q) j -> p r q j", r=nsh)
            src = modall[:, :, l * nq:(l + 1) * nq, :]
            P.add("dve", lambda e, mv=mv, src=src: e.tensor_copy(out=mv[:, :, :, 1], in_=src[:, :, :, 4]), reads=["modall"], writes=["modv"])
            P.add("dve", lambda e, mv=mv, src=src: e.tensor_scalar(out=mv[:, :, :, 0], in0=src[:, :, :, 0], scalar1=V("bsel", 0, 1), scalar2=None, op0=ALU.mult),
                  reads=["modall", "vecs"], writes=["modv"])
            for bb in range(1, 4):
                P.add("dve", lambda e, mv=mv, src=src, bb=bb: e.scalar_tensor_tensor(out=mv[:, :, :, 0], in0=src[:, :, :, bb], scalar=V("bsel", bb, 1),
                                                                                   in1=mv[:, :, :, 0], op0=ALU.mult, op1=ALU.add),
                      reads=["modall", "vecs", "modv"], writes=["modv"])
            for j in range(2):
                P.add("dve", lambda e, l=l, j=j: e.tensor_tensor(out=modv[:, l, :, j], in0=modv[:, l, :, j], in1=V("bmod", l * 72, 72), op=ALU.add),
                      reads=["modv", "vecs"], writes=["modv"])
            for j in range(2):
                for s_, gname in enumerate(("gf1", "gmix", "gf2")):
                    i_scale = 3 * s_ + 1
                    P.add("dve", lambda e, l=l, j=j, s_=s_, gname=gname, i_scale=i_scale: e.scalar_tensor_tensor(
                        out=Acoef[:, l, j, s_, :], in0=modv[:, l, i_scale * 8:(i_scale + 1) * 8, j], scalar=1.0,
                        in1=V(gname, l * 8, 8), op0=ALU.add, op1=ALU.mult), reads=["modv", "vecs"], writes=["coef"])
                    i_gate = 3 * s_ + 2
                    P.add("dve", lambda e, l=l, j=j, s_=s_, i_gate=i_gate: e.tensor_scalar(
                        out=HG[:, l, j, s_, :], in0=modv[:, l, i_gate * 8:(i_gate + 1) * 8, j],
                        scalar1=(1.0 if s_ == 1 else 0.5), scalar2=None, op0=ALU.mult), reads=["modv"], writes=["coef"])
        P.barrier()
        bump[0] = base

    setup_mod()

    def Bsh(l, j, s):
        return modv[:, l, (3 * s) * 8:(3 * s + 1) * 8, j]

    class FFNBufs:
        pass

    def alloc_common(Tm):
        B = FFNBufs()
        B.ht = alloc(8 * Tm, shape=[8, Tm])
        B.xn = alloc(8 * Tm, BF16, shape=[8, Tm])
        B.sq = [alloc(Tm, BF16) for _ in range(2)]
        B.rstd = alloc(Tm)
        B.tmp = [alloc(Tm) for _ in range(2)]
        return B

    def norm_mod(B, Tn, Avec, Bvec, tag):
        for kc in range(8):
            P.add("act", lambda e, kc=kc: e.activation(out=B.sq[kc % 2][:, 0:Tn], in_=B.ht[:, kc, 0:Tn], func=AF.Square),
                  reads=[("ht", kc)], writes=[("sq", kc % 2)])
            P.add("pe", lambda e, kc=kc: e.matmul(ps[0][:, 0:Tn], lhsT=ones_b, rhs=B.sq[kc % 2][:, 0:Tn],
                                                 start=(kc == 0), stop=(kc == 7)),
                  reads=[("sq", kc % 2), "ones_b"], writes=[PSK(0)])
        P.add("act", lambda e: e.activation(out=B.rstd[:, 0:Tn], in_=ps[0][:, 0:Tn], func=AF.Sqrt, bias=V("eps"), scale=1.0 / D),
              reads=[PSK(0), "vecs"], writes=["rstd"])
        P.add("dve", lambda e: e.reciprocal(out=B.rstd[:, 0:Tn], in_=B.rstd[:, 0:Tn]), reads=["rstd"], writes=["rstd"])
        for kc in range(8):
            P.add("dve", lambda e, kc=kc: e.tensor_tensor(out=B.tmp[kc % 2][:, 0:Tn], in0=B.ht[:, kc, 0:Tn], in1=B.rstd[:, 0:Tn],
                                                         op=ALU.mult), reads=[("ht", kc), "rstd"], writes=[("tmp", kc % 2)])
            bias = Bvec[:, kc:kc + 1] if Bvec is not None else V("zero")
            P.add("act", lambda e, kc=kc, bias=bias: e.activation(out=B.xn[:, kc, 0:Tn], in_=B.tmp[kc % 2][:, 0:Tn], func=AF.Identity,
                                                                  bias=bias, scale=Avec[:, kc:kc + 1]),
                  reads=[("tmp", kc % 2), "coef", "modv", "vecs"], writes=[("xn", kc)])

    JG = 2
    NG = (NJ + JG - 1) // JG

    def alloc_ffn(B, Tm):
        B.H = alloc(NJ * Tm, BF16, shape=[NJ, Tm])
        B.wg = [alloc(8 * JG * 128, BF16, shape=[8, JG * 128]) for _ in range(2)]
        B.wu = [alloc(8 * JG * 128, BF16, shape=[8, JG * 128]) for _ in range(2)]
        B.wd = alloc(NJ * D, BF16, shape=[NJ, D])
        B.sg = [alloc(Tm) for _ in range(2)]

    gcount = [0]

    def ffn(B, Tn, l, which, hg):
        wgd, wud, wdd = wb[f"w{which}_gate", l], wb[f"w{which}_up", l], wb[f"w{which}_down", l]
        kg, ku, kd = WK(f"w{which}_gate", l), WK(f"w{which}_up", l), WK(f"w{which}_down", l)

        def load_group(g):
            slot = (gcount[0] + g) % 2
            j0 = g * JG
            nj = min(JG, NJ - j0)
            dma("sp", B.wg[slot][:, :, 0:nj * 128], wgd[:, j0 * 128:(j0 + nj) * 128].rearrange("(k p) m -> p k m", p=128),
                kg, [("wg", slot)], f"wg{slot}")
            dma("sp", B.wu[slot][:, :, 0:nj * 128], wud[:, j0 * 128:(j0 + nj) * 128].rearrange("(k p) m -> p k m", p=128),
                ku, [("wu", slot)], f"wu{slot}")

        load_group(0)
        for g in range(NG):
            slot = (gcount[0] + g) % 2
            j0 = g * JG
            nj = min(JG, NJ - j0)
            if g + 1 < NG:
                load_group(g + 1)
            dma("sp", B.wd[:, j0:j0 + nj, :], wdd[j0 * 128:(j0 + nj) * 128, :].rearrange("(j p) m -> p j m", p=128),
                kd, [("wd", g)], "wd")
            for jj in range(nj):
                j = j0 + jj
                bg, bu = 1 + (j % 2), 3 + (j % 2)
                for kc in range(8):
                    P.add("pe", lambda e, kc=kc, jj=jj, slot=slot, bg=bg: e.matmul(
                        ps[bg][:, 0:Tn], lhsT=B.wg[slot][:, kc, jj * 128:(jj + 1) * 128], rhs=B.xn[:, kc, 0:Tn],
                        start=(kc == 0), stop=(kc == 7)), reads=[("wg", slot), ("xn", kc)], writes=[PSK(bg)])
                for kc in range(8):
                    P.add("pe", lambda e, kc=kc, jj=jj, slot=slot, bu=bu: e.matmul(
                        ps[bu][:, 0:Tn], lhsT=B.wu[slot][:, kc, jj * 128:(jj + 1) * 128], rhs=B.xn[:, kc, 0:Tn],
                        start=(kc == 0), stop=(kc == 7)), reads=[("wu", slot), ("xn", kc)], writes=[PSK(bu)])
                P.add("act", lambda e, j=j, bg=bg: e.activation(out=B.sg[j % 2][:, 0:Tn], in_=ps[bg][:, 0:Tn], func=AF.Silu),
                      reads=[PSK(bg)], writes=[("sg", j % 2)])
                P.add("dve", lambda e, j=j, bu=bu: e.tensor_tensor(out=B.H[:, j, 0:Tn], in0=ps[bu][:, 0:Tn], in1=B.sg[j % 2][:, 0:Tn],
                                                                  op=ALU.mult), reads=[PSK(bu), ("sg", j % 2)], writes=[("H", j)])
        gcount[0] += NG
        for c in range(8):
            bd = 5 + (c % 2)
            for j in range(NJ):
                P.add("pe", lambda e, c=c, j=j, bd=bd: e.matmul(ps[bd][:, 0:Tn], lhsT=B.wd[:, j, c * 128:(c + 1) * 128],
                                                               rhs=B.H[:, j, 0:Tn], start=(j == 0), stop=(j == NJ - 1)),
                      reads=[("wd", j // JG), ("H", j)], writes=[PSK(bd)])
            P.add("dve", lambda e, c=c, bd=bd: e.scalar_tensor_tensor(out=B.ht[:, c, 0:Tn], in0=ps[bd][:, 0:Tn], scalar=hg[:, c:c + 1],
                                                                     in1=B.ht[:, c, 0:Tn], op0=ALU.mult, op1=ALU.add),
                  reads=[PSK(bd), ("ht", c), "coef"], writes=[("ht", c)])

    def alloc_win(B, Tm):
        B.win = alloc(8 * NWIN, BF16, shape=[8, NWIN])
        B.wuq = alloc(2 * 768, BF16, shape=[2, 768])
        B.zst = alloc(NZ * Tm, BF16, shape=[NZ, Tm])
        P.add("dve", lambda e: e.memset(B.zst, 0.0), writes=[("zst", s_) for s_ in range(NZ)])
        B.mst = alloc(2 * Tm, BF16, shape=[2, Tm])
        B.cqn = alloc(2 * Tm, BF16, shape=[2, Tm])
        B.f1 = [alloc(Tm) for _ in range(3)]
        B.rW = alloc(2 * Tm, shape=[2, Tm])
        B.rQ = alloc(2 * Tm, shape=[2, Tm])
        B.rK = alloc(2 * Tm, shape=[2, Tm])


    rr = [0]

    def nb():
        rr[0] = rr[0] % 7 + 1
        return rr[0]

    def win_proj(B, Tn, l, rope, tok0, zdst, zcol0, mdst, mcol0, halo):
        if rope:
            dma("sp", B.rW[:, :, 0:Tn], ropeW.rearrange("(a p) c -> p a c", p=128)[:, :, tok0:tok0 + Tn], [], ["rW"], "rW")
            dma("sp", B.rQ[0:96, :, 0:Tn], ropeQ.rearrange("(a p) c -> p a c", p=96)[:, :, tok0:tok0 + Tn], [], ["rQ"], "rQ")
            dma("sp", B.rK[0:32, :, 0:Tn], ropeK.rearrange("(a p) c -> p a c", p=32)[:, :, tok0:tok0 + Tn], [], ["rK"], "rK")

        def group(M, col0):
            b = nb()
            for kc in range(8):
                P.add("pe", lambda e, kc=kc, b=b: e.matmul(ps[b][0:M, 0:Tn], lhsT=B.win[:, kc, col0:col0 + M], rhs=B.xn[:, kc, 0:Tn],
                                                          start=(kc == 0), stop=(kc == 7)),
                      reads=["win", ("xn", kc)], writes=[PSK(b)])
            return b

        fi = [0]

        def ftmp():
            fi[0] = (fi[0] + 1) % 3
            return fi[0]

        def copy_out(b, M, dst, key):
            P.add("act", lambda e: e.activation(out=dst, in_=ps[b][0:M, 0:Tn], func=AF.Copy), reads=[PSK(b)], writes=[key])

        def rope_out(bA, bB, M, tab, tabkey, dst, key):
            i0, i1 = ftmp(), ftmp()
            P.add("dve", lambda e: e.tensor_tensor(out=B.f1[i0][0:M, 0:Tn], in0=ps[bA][0:M, 0:Tn], in1=tab[0:M, 0, 0:Tn], op=ALU.mult),
                  reads=[PSK(bA), tabkey], writes=[("f1", i0)])
            P.add("dve", lambda e: e.tensor_tensor(out=B.f1[i1][0:M, 0:Tn], in0=ps[bB][0:M, 0:Tn], in1=tab[0:M, 1, 0:Tn], op=ALU.mult),
                  reads=[PSK(bB), tabkey], writes=[("f1", i1)])
            P.add("dve", lambda e: e.tensor_tensor(out=dst, in0=B.f1[i0][0:M, 0:Tn], in1=B.f1[i1][0:M, 0:Tn], op=ALU.add),
                  reads=[("f1", i0), ("f1", i1)], writes=[key])

        def rstd_from(bank_ss, n):
            P.add("act", lambda e: e.activation(out=B.rstd[:, 0:Tn], in_=ps[bank_ss][:, 0:Tn], func=AF.Sqrt, bias=V("eps"), scale=1.0 / n),
                  reads=[PSK(bank_ss), "vecs"], writes=["rstd"])
            P.add("dve", lambda e: e.reciprocal(out=B.rstd[:, 0:Tn], in_=B.rstd[:, 0:Tn]), reads=["rstd"], writes=["rstd"])

        b0 = group(128, 0)
        b1 = group(64, 128)
        P.add("act", lambda e: e.activation(out=B.sq[0][:, 0:Tn], in_=ps[b0][:, 0:Tn], func=AF.Square), reads=[PSK(b0)], writes=[("sq", 0)])
        P.add("act", lambda e: e.activation(out=B.sq[1][0:64, 0:Tn], in_=ps[b1][0:64, 0:Tn], func=AF.Square), reads=[PSK(b1)], writes=[("sq", 1)])
        P.add("pe", lambda e: e.matmul(ps[0][:, 0:Tn], lhsT=ones_b, rhs=B.sq[0][:, 0:Tn], start=True, stop=False),
              reads=[("sq", 0), "ones_b"], writes=[PSK(0)])
        P.add("pe", lambda e: e.matmul(ps[0][:, 0:Tn], lhsT=ones_b[0:64, :], rhs=B.sq[1][0:64, 0:Tn], start=False, stop=True),
              reads=[("sq", 1), "ones_b"], writes=[PSK(0)])
        rstd_from(0, 192)
        P.add("dve", lambda e: e.scalar_tensor_tensor(out=B.cqn[:, 0, 0:Tn], in0=ps[b0][:, 0:Tn], scalar=V("gq", l * 2, 1), in1=B.rstd[:, 0:Tn],
                                                     op0=ALU.mult, op1=ALU.mult), reads=[PSK(b0), "rstd", "vecs"], writes=[("cqn", 0)])
        P.add("dve", lambda e: e.scalar_tensor_tensor(out=B.cqn[0:64, 1, 0:Tn], in0=ps[b1][0:64, 0:Tn], scalar=V("gq", l * 2 + 1, 1)[0:64, :],
                                                     in1=B.rstd[0:64, 0:Tn], op0=ALU.mult, op1=ALU.mult),
              reads=[PSK(b1), "rstd", "vecs"], writes=[("cqn", 1)])
        for h in range(4):
            banks = []
            for rot in ([0, 1] if rope else [0]):
                b = nb()
                c0 = rot * 384 + h * 96
                P.add("pe", lambda e, b=b, c0=c0: e.matmul(ps[b][0:96, 0:Tn], lhsT=B.wuq[:, 0, c0:c0 + 96], rhs=B.cqn[:, 0, 0:Tn], start=True, stop=False),
                      reads=["wuq", ("cqn", 0)], writes=[PSK(b)])
                P.add("pe", lambda e, b=b, c0=c0: e.matmul(ps[b][0:96, 0:Tn], lhsT=B.wuq[0:64, 1, c0:c0 + 96], rhs=B.cqn[0:64, 1, 0:Tn], start=False, stop=True),
                      reads=["wuq", ("cqn", 1)], writes=[PSK(b)])
                banks.append(b)
            if rope:
                rope_out(banks[0], banks[1], 96, B.rQ, "rQ", B.zst[0:96, ZQ + h, 0:Tn], ("zst", ZQ + h))
            else:
                copy_out(banks[0], 96, B.zst[0:96, ZQ + h, 0:Tn], ("zst", ZQ + h))
        bkv = group(128, 192)
        P.add("act", lambda e: e.activation(out=B.sq[0][:, 0:Tn], in_=ps[bkv][:, 0:Tn], func=AF.Square), reads=[PSK(bkv)], writes=[("sq", 0)])
        P.add("pe", lambda e: e.matmul(ps[0][:, 0:Tn], lhsT=ones_b, rhs=B.sq[0][:, 0:Tn], start=True, stop=True),
              reads=[("sq", 0), "ones_b"], writes=[PSK(0)])
        rstd_from(0, 128)
        P.add("dve", lambda e: e.scalar_tensor_tensor(out=B.mst[:, 0, 0:Tn], in0=ps[bkv][:, 0:Tn], scalar=V("gkv", l, 1), in1=B.rstd[:, 0:Tn],
                                                     op0=ALU.mult, op1=ALU.mult), reads=[PSK(bkv), "rstd", "vecs"], writes=[("mst", 0)])
        bA = group(32, 320)
        if rope:
            bB = group(32, 2144)
            rope_out(bA, bB, 32, B.rK, "rK", B.mst[0:32, 1, 0:Tn], ("mst", 1))
        else:
            copy_out(bA, 32, B.mst[0:32, 1, 0:Tn], ("mst", 1))
        for c in range(2):
            b = group(128, 352 + c * 128)
            copy_out(b, 128, B.zst[:, ZSCB + c, 0:Tn], ("zst", ZSCB + c))
        for c in range(2):
            bc = group(128, 608 + c * 128)
            bx = group(128, 864 + c * 128)
            i = ftmp()
            P.add("act", lambda e, i=i, bx=bx: e.activation(out=B.f1[i][:, 0:Tn], in_=ps[bx][:, 0:Tn], func=AF.Copy), reads=[PSK(bx)], writes=[("f1", i)])
            P.add("dve", lambda e, i=i, bc=bc, c=c: e.tensor_tensor(out=B.zst[:, ZSCT + c, 0:Tn], in0=ps[bc][:, 0:Tn], in1=B.f1[i][:, 0:Tn], op=ALU.mult),
                  reads=[PSK(bc), ("f1", i)], writes=[("zst", ZSCT + c)])
        for c in range(2):
            bA = group(128, 1120 + c * 128)
            if rope:
                bB = group(128, 2176 + c * 128)
                rope_out(bA, bB, 128, B.rW, "rW", B.zst[:, ZWAQ + c, 0:Tn], ("zst", ZWAQ + c))
            else:
                copy_out(bA, 128, B.zst[:, ZWAQ + c, 0:Tn], ("zst", ZWAQ + c))
        bA = group(128, 1376)
        if rope:
            bB = group(128, 2432)
            rope_out(bA, bB, 128, B.rW, "rW", B.zst[:, ZWAK, 0:Tn], ("zst", ZWAK))
        else:
            copy_out(bA, 128, B.zst[:, ZWAK, 0:Tn], ("zst", ZWAK))
        b = group(128, 1504)
        copy_out(b, 128, B.zst[:, ZWAV, 0:Tn], ("zst", ZWAV))
        for c in range(2):
            ba = group(128, 1632 + c * 128)
            bg = group(128, 1888 + c * 128)
            i = ftmp()
            P.add("act", lambda e, i=i, bg=bg: e.activation(out=B.f1[i][:, 0:Tn], in_=ps[bg][:, 0:Tn], func=AF.Sigmoid), reads=[PSK(bg)], writes=[("f1", i)])
            P.add("dve", lambda e, i=i, ba=ba, c=c: e.tensor_tensor(out=B.zst[:, ZCFU + c, 0:Tn], in0=ps[ba][:, 0:Tn], in1=B.f1[i][:, 0:Tn], op=ALU.mult),
                  reads=[PSK(ba), ("f1", i)], writes=[("zst", ZCFU + c)])
        zk = [("zst", s) for s in range(NZ)]
        dma("sp", zdst.rearrange("(s p) c -> p s c", p=128)[:, :, zcol0:zcol0 + Tn], B.zst[:, :, 0:Tn], zk, ["zdst"], "zst")
        dma("sp", mdst[0:128, mcol0:mcol0 + Tn], B.mst[:, 0, 0:Tn], [("mst", 0)], ["mdst"], "mst0")
        dma("sp", mdst[128:160, mcol0:mcol0 + Tn], B.mst[0:32, 1, 0:Tn], [("mst", 1)], ["mdst"], "mst1")
        for (which, c0) in halo:
            xh = xin_halo.rearrange("(s p) c -> p s c", p=128)
            dma("sp", xh[:, 0:2, which * 128:(which + 1) * 128], B.zst[:, 6:8, c0:c0 + 128], zk, ["xin_halo"], "hal0")
            dma("sp", xh[:, 2:6, which * 128:(which + 1) * 128], B.zst[:, 10:14, c0:c0 + 128], zk, ["xin_halo"], "hal1")

    def load_win(B, l):
        dma("sp", B.win[:, 0:4, :], wb["w_in", l][0:512, :].rearrange("(k p) m -> p k m", p=128), WK("w_in", l), ["win"], "win0")
        dma("sp", B.win[:, 4:8, :], wb["w_in", l][512:1024, :].rearrange("(k p) m -> p k m", p=128), WK("w_in", l), ["win"], "win1")
        dma("sp", B.wuq[:, 0, :], wb["w_mla_uq", l][0:128, :], WK("w_mla_uq", l), ["wuq"], "wuq0")
        dma("sp", B.wuq[0:64, 1, :], wb["w_mla_uq", l][128:192, :], WK("w_mla_uq", l), ["wuq"], "wuq1")

    def load_h(B, src, col0, Tn):
        dma("sp", B.ht[:, :, 0:Tn], src.rearrange("(k p) c -> p k c", p=128)[:, :, col0:col0 + Tn], ["hsrc"], [("ht", k) for k in range(8)], "hld")

    def store_h(B, dst, col0, Tn, key="hdst"):
        dma("sp", dst.rearrange("(k p) c -> p k c", p=128)[:, :, col0:col0 + Tn], B.ht[:, :, 0:Tn], [("ht", k) for k in range(8)], [key], "hst")

    def tiles_lat_ctx():
        out = [(False, 0, T, t * T) for t in range(NT)]
        out.append((True, 1, CTX, 0))
        return out

    def phase_ffn(l_prev, l_next, first):
        base = bump[0]
        B = alloc_common(T)
        alloc_ffn(B, T)
        if l_next is not None:
            alloc_win(B, T)
            load_win(B, l_next)
        for (is_ctx, j, Tn, tok0) in tiles_lat_ctx():
            if first:
                load_h(B, ctxT if is_ctx else xT, tok0, Tn)
            else:
                if is_ctx and l_next is None:
                    continue
                load_h(B, hcbuf if is_ctx else hbuf, tok0, Tn)
            if l_prev is not None:
                norm_mod(B, Tn, Acoef[:, l_prev, j, 2, :], Bsh(l_prev, j, 2), "f2")
                ffn(B, Tn, l_prev, 2, HG[:, l_prev, j, 2, :])
            if l_next is not None:
                norm_mod(B, Tn, Acoef[:, l_next, j, 0, :], Bsh(l_next, j, 0), "f1")
                ffn(B, Tn, l_next, 1, HG[:, l_next, j, 0, :])
                store_h(B, hcbuf if is_ctx else hbuf, tok0, Tn)
                norm_mod(B, Tn, Acoef[:, l_next, j, 1, :], Bsh(l_next, j, 1), "mx")
                halo = []
                if not is_ctx and tok0 == 0:
                    halo.append((0, 0))
                if not is_ctx and tok0 == S_OWN - T:
                    halo.append((1, T - 128))
                win_proj(B, Tn, l_next, not is_ctx, tok0, zcbuf if is_ctx else zbuf, 128 + tok0,
                         ckvc if is_ctx else xin_mla, tok0, halo)
            else:
                norm_mod(B, Tn, V("gfin", 0, 8), None, "fin")
                for kc in range(8):
                    P.add("dve", lambda e, kc=kc, Tn=Tn: e.tensor_tensor(out=B.tmp[kc % 2][:, 0:Tn], in0=B.ht[:, kc, 0:Tn], in1=B.rstd[:, 0:Tn], op=ALU.mult),
                          reads=[("ht", kc), "rstd"], writes=[("tmp", kc % 2)])
                    P.add("dve", lambda e, kc=kc, Tn=Tn: e.tensor_scalar(out=B.ht[:, kc, 0:Tn], in0=B.tmp[kc % 2][:, 0:Tn], scalar1=V("gfin", kc, 1), scalar2=None,
                                                                 op0=ALU.mult), reads=[("tmp", kc % 2), "vecs"], writes=[("ht", kc)])
                store_h(B, outT, tok0, Tn, key="out")
        P.barrier()
        bump[0] = base

    RG = RG_OVERRIDE[0] or [[0, 1], [2, 3], [4, 5], [6, 7]]

    def exchange(l):
        base = bump[0]
        P.add("pool", lambda e: e.collective_compute("AllGather", ALU.bypass, replica_groups=RG, ins=[xin_mla.opt()], outs=[xout_mla.opt()]),
              reads=["mdst"], writes=["xout_mla"]).signal = True
        P.add("pool", lambda e: e.collective_compute("AllGather", ALU.bypass, replica_groups=RG, ins=[xin_halo.opt()], outs=[xout_halo.opt()]),
              reads=["xin_halo"], writes=["xout_halo"]).signal = True
        hl = alloc(6 * 128, BF16, shape=[6, 128])
        hr = alloc(6 * 128, BF16, shape=[6, 128])
        xo = xout_halo.rearrange("(r s p) c -> r p s c", r=2, p=128)
        dma("sp", hl, xo[0, :, :, 128:256], ["xout_halo"], ["hl"], "hl")
        dma("sp", hr, xo[1, :, :, 0:128], ["xout_halo"], ["hr"], "hr")
        P.add("dve", lambda e: e.tensor_scalar(out=hl, in0=hl, scalar1=V("lm"), scalar2=None, op0=ALU.mult), reads=["hl", "vecs"], writes=["hl"])
        P.add("dve", lambda e: e.tensor_scalar(out=hr, in0=hr, scalar1=V("rm"), scalar2=None, op0=ALU.mult), reads=["hr", "vecs"], writes=["hr"])
        zb = zbuf.rearrange("(s p) c -> p s c", p=128)
        dma("sp", zb[:, 6:8, 0:128], hl[:, 0:2, :], ["hl"], ["zdst"], "hl")
        dma("sp", zb[:, 10:14, 0:128], hl[:, 2:6, :], ["hl"], ["zdst"], "hl")
        dma("sp", zb[:, 6:8, 128 + S_OWN:ZW], hr[:, 0:2, :], ["hr"], ["zdst"], "hr")
        dma("sp", zb[:, 10:14, 128 + S_OWN:ZW], hr[:, 2:6, :], ["hr"], ["zdst"], "hr")
        P.barrier()
        bump[0] = base

    NKC = 66

    def alloc_kv():
        K = FFNBufs()
        K.KT = alloc(4 * NKC * 128, BF16, shape=[4, NKC * 128])
        K.Vx = alloc(NKC * 384, BF16, shape=[NKC, 384])
        K.wukv = alloc(512, BF16)
        K.ckt = [alloc(512, BF16) for _ in range(2)]
        return K

    def kv_build(K, l):
        dma("sp", K.wukv, wb["w_mla_ukv", l], WK("w_mla_ukv", l), ["wukv"], "wukv")
        P.add("dve", lambda e: e.memset(K.Vx, 0.0), writes=["Vx"])
        P.add("dve", lambda e: e.memset(K.Vx.rearrange("p k (a c) -> p k a c", a=2)[:, :, :, 64:65], 1.0), writes=["Vx"])
        srcs = [(xout_mla[r * 160:r * 160 + 128, t8 * 512:(t8 + 1) * 512], 512, r * S_OWN + t8 * 512) for r in range(2) for t8 in range(8)]
        srcs.append((ckvc[0:128, 0:CTX], CTX, 2 * S_OWN))
        for r in range(2):
            for h in range(4):
                dma("sp", K.KT[64:96, h, r * S_OWN:(r + 1) * S_OWN], xout_mla[r * 160 + 128:r * 160 + 160, :], ["xout_mla"], [("KTr", h)], f"ktr{h}")
        for h in range(4):
            dma("sp", K.KT[64:96, h, 2 * S_OWN:2 * S_OWN + CTX], ckvc[128:160, :], ["mdstc"], [("KTr", h)], f"ktr{h}")
        wv = K.wukv.rearrange("p (h c) -> p h c", h=4)[:, :, 64:128]
        for i, (src, n, key0) in enumerate(srcs):
            slot = i % 2
            dma("sp", K.ckt[slot][:, 0:n], src, ["xout_mla", "mdstc"], [("ckt", slot)], f"ckt{slot}")
            for h in range(4):
                b = 1 + (h % 2)
                P.add("pe", lambda e, h=h, b=b, slot=slot, n=n: e.matmul(ps[b][0:64, 0:n], lhsT=K.wukv[:, h * 128:h * 128 + 64], rhs=K.ckt[slot][:, 0:n],
                                                                        start=True, stop=True), reads=["wukv", ("ckt", slot)], writes=[PSK(b)])
                eng = "act" if h % 2 == 0 else "dve"
                if eng == "act":
                    P.add("act", lambda e, h=h, b=b, n=n, key0=key0: e.activation(out=K.KT[0:64, h, key0:key0 + n], in_=ps[b][0:64, 0:n], func=AF.Copy),
                          reads=[PSK(b)], writes=[("KTn", h)])
                else:
                    P.add("dve", lambda e, h=h, b=b, n=n, key0=key0: e.tensor_copy(out=K.KT[0:64, h, key0:key0 + n], in_=ps[b][0:64, 0:n]),
                          reads=[PSK(b)], writes=[("KTn", h)])
            for kb in range(n // 128):
                b = 3 + (kb % 2)
                kc = key0 // 128 + kb
                P.add("pe", lambda e, kb=kb, b=b, slot=slot: e.matmul(ps[b][:, 0:256], lhsT=K.ckt[slot][:, kb * 128:(kb + 1) * 128], rhs=wv,
                                                                     start=True, stop=True), reads=["wukv", ("ckt", slot)], writes=[PSK(b)])
                pv = ps[b][:, 0:256].rearrange("p (a b c) -> p a b c", a=2, b=2)
                vo = K.Vx[:, kc, :].rearrange("p (a b c) -> p a b c", a=2, b=3)
                P.add("act", lambda e, pv=pv, vo=vo: e.activation(out=vo[:, :, 0, :], in_=pv[:, :, 0, :], func=AF.Copy), reads=[PSK(b)], writes=["Vx"])
                P.add("dve", lambda e, pv=pv, vo=vo: e.tensor_copy(out=vo[:, :, 2, :], in_=pv[:, :, 1, :]), reads=[PSK(b)], writes=["Vx"])

    def alloc_mix():
        M = FFNBufs()
        M.ht = alloc(8 * T, shape=[8, T])
        M.mix = alloc(8 * T, BF16, shape=[8, T])
        M.wout = alloc(8 * D, BF16, shape=[8, D])
        M.QT = alloc(4 * T, BF16, shape=[4, T])
        M.PT = [alloc(2 * T, BF16) for _ in range(3)]
        M.scb = alloc(2 * T, BF16, shape=[2, T])
        M.sct = alloc(2 * (T + 2), BF16, shape=[2, T + 2])
        M.waq = alloc(2 * T, BF16, shape=[2, T])
        M.wak = alloc(T + 256, BF16)
        M.wav = alloc(T + 256, BF16)
        M.cfu = alloc(2 * (T + 32), BF16, shape=[2, T + 32])
        M.Vw = alloc(6 * 384, BF16, shape=[6, 384])
        M.wakc = alloc(CTX, BF16)
        M.Vwc = alloc(2 * 384, BF16, shape=[2, 384])
        M.acc = [alloc(T) for _ in range(2)]
        M.rinv = alloc(T)
        M.bc = alloc(T)
        M.wavc = M.bc.bitcast(BF16)[:, 0:CTX]
        M.f = [alloc(T) for _ in range(3)]
        return M

    def transpose_to_triples(M, src, nblk, dst):
        p7 = ps[7][:, :].bitcast(BF16)
        for blk in range(nblk):
            o = (blk % 4) * 128
            P.add("pe", lambda e, blk=blk, o=o: e.transpose(out=p7[:, o:o + 128], in_=src[:, blk * 128:(blk + 1) * 128], identity=ident_b),
                  reads=["wavsrc", "ident"], writes=[PSK(7)])
            pv = p7[:, o:o + 128].rearrange("p (a c) -> p a c", a=2)
            vo = dst[:, blk, :].rearrange("p (a b c) -> p a b c", a=2, b=3)
            P.add("act", lambda e, pv=pv, vo=vo: e.activation(out=vo[:, :, 0, :], in_=pv, func=AF.Copy), reads=[PSK(7)], writes=["Vw"])
            P.add("dve", lambda e, pv=pv, vo=vo: e.tensor_copy(out=vo[:, :, 2, :], in_=pv), reads=[PSK(7)], writes=["Vw"])

    def init_triples(buf):
        P.add("dve", lambda e: e.memset(buf, 0.0), writes=["Vw"])
        P.add("dve", lambda e: e.memset(buf.rearrange("p k (a c) -> p k a c", a=2)[:, :, :, 64:65], 1.0), writes=["Vw"])

    def normalize(M, Tn, ob, lo, l, h, chunk, sink):
        if 'norm' in DBG_SKIP:
            return
        sp = 64 if lo else 0
        mrows = 64 if lo else 128
        r0 = 0 if lo else 64
        if sink:
            P.add("dve", lambda e: e.tensor_scalar(out=M.rinv[sp:sp + 1, 0:Tn], in0=ps[ob][sp:sp + 1, 0:Tn], scalar1=esink[sp:sp + 1, l, h:h + 1],
                                                  scalar2=None, op0=ALU.add), reads=[PSK(ob), "esink"], writes=["rinv"])
            P.add("dve", lambda e: e.reciprocal(out=M.rinv[sp:sp + 1, 0:Tn], in_=M.rinv[sp:sp + 1, 0:Tn]), reads=["rinv"], writes=["rinv"])
        else:
            P.add("dve", lambda e: e.reciprocal(out=M.rinv[sp:sp + 1, 0:Tn], in_=ps[ob][sp:sp + 1, 0:Tn]), reads=[PSK(ob)], writes=["rinv"])
        P.add("pe", lambda e: e.matmul(ps[5][0:mrows, 0:Tn], lhsT=ones_f[sp:sp + 1, 0:mrows], rhs=M.rinv[sp:sp + 1, 0:Tn], start=True, stop=True),
              reads=["rinv", "ones_f"], writes=[PSK(5)])
        P.add("act", lambda e: e.activation(out=M.bc[r0:r0 + 64, 0:Tn], in_=ps[5][r0:r0 + 64, 0:Tn], func=AF.Copy), reads=[PSK(5)], writes=["bc"])
        P.add("dve", lambda e: e.tensor_tensor(out=M.mix[r0:r0 + 64, chunk, 0:Tn], in0=ps[ob][r0:r0 + 64, 0:Tn], in1=M.bc[r0:r0 + 64, 0:Tn], op=ALU.mult),
              reads=[PSK(ob), "bc"], writes=[("mix", chunk)])

    def mixers(K, M, l, is_ctx, t):
        Tn = CTX if is_ctx else T
        tok0 = 0 if is_ctx else t * T
        j = 1 if is_ctx else 0
        zv = (zcbuf if is_ctx else zbuf).rearrange("(s p) c -> p s c", p=128)
        c0 = 128 + tok0
        dma("sp", M.QT[0:96, :, 0:Tn], zv[0:96, 0:4, c0:c0 + Tn], [], ["QT"], "lq")
        dma("sp", M.scb[:, :, 0:Tn], zv[:, 4:6, c0:c0 + Tn], [], ["scb"], "lscb")
        dma("sp", M.sct[:, :, 0:Tn + 2], zv[:, 6:8, c0 - 1:c0 + Tn + 1], [], ["sct"], "lsct")
        dma("sp", M.waq[:, :, 0:Tn], zv[:, 8:10, c0:c0 + Tn], [], ["waq"], "lwaq")
        dma("sp", M.wak[:, 0:Tn + 256], zv[:, 10, c0 - 128:c0 + Tn + 128], [], ["wak"], "lwak")
        dma("sp", M.wav[:, 0:Tn + 256], zv[:, 11, c0 - 128:c0 + Tn + 128], [], ["wavsrc"], "lwav")
        dma("sp", M.cfu[:, :, 0:Tn + 30], zv[:, 12:14, c0 - 15:c0 + Tn + 15], [], ["cfu"], "lcfu")
        hsrc = hcbuf if is_ctx else hbuf
        dma("sp", M.ht[:, :, 0:Tn], hsrc.rearrange("(k p) c -> p k c", p=128)[:, :, tok0:tok0 + Tn], ["hdst"], [("ht", k) for k in range(8)], "hld")

        conv_ops = []

        def cadd(*a_, **k_):
            conv_ops.append((a_, k_))

        def wsc(c, k):
            return V("wsc", (l * 2 + c) * 3 + k, 1)

        def wcf(c, k):
            return V("wcf", (l * 2 + c) * 31 + k, 1)

        for c in (range(2) if 'sc' not in DBG_SKIP else []):
            acc = M.acc[c]
            cadd("dve", lambda e, c=c, acc=acc: e.tensor_scalar(out=acc[:, 0:Tn], in0=M.sct[:, c, 0:Tn], scalar1=wsc(c, 0), scalar2=None, op0=ALU.mult),
                  reads=["sct", "vecs"], writes=[("acc", c)])
            for k in (1, 2):
                cadd("dve", lambda e, c=c, k=k, acc=acc: e.scalar_tensor_tensor(out=acc[:, 0:Tn], in0=M.sct[:, c, k:k + Tn], scalar=wsc(c, k), in1=acc[:, 0:Tn],
                                                                                 op0=ALU.mult, op1=ALU.add), reads=["sct", ("acc", c), "vecs"], writes=[("acc", c)])
            cadd("dve", lambda e, c=c, acc=acc: e.tensor_tensor(out=M.mix[:, 2 + c, 0:Tn], in0=acc[:, 0:Tn], in1=M.scb[:, c, 0:Tn], op=ALU.mult),
                  reads=[("acc", c), "scb"], writes=[("mix", 2 + c)])
        for c in range(2):
            acc = M.acc[c]
            cadd("dve", lambda e, c=c, acc=acc: e.tensor_scalar(out=acc[:, 0:Tn], in0=M.cfu[:, c, 0:Tn], scalar1=wcf(c, 0), scalar2=V("bcf", l * 2 + c, 1),
                                                                 op0=ALU.mult, op1=ALU.add), reads=["cfu", "vecs", ("acc", c)], writes=[("acc", c)])
            for k in range(1, 31):
                cadd("dve", lambda e, c=c, k=k, acc=acc: e.scalar_tensor_tensor(out=acc[:, 0:Tn], in0=M.cfu[:, c, k:k + Tn], scalar=wcf(c, k), in1=acc[:, 0:Tn],
                                                                                 op0=ALU.mult, op1=ALU.add), reads=["cfu", ("acc", c), "vecs"], writes=[("acc", c)])
            cadd("dve", lambda e, c=c, acc=acc: e.tensor_tensor(out=M.f[c][:, 0:Tn], in0=acc[:, 0:Tn], in1=acc[:, 0:Tn], op=ALU.mult),
                  reads=[("acc", c)], writes=[("f", c)])

        kcs = [64, 65] if is_ctx else list(range(NKC))
        sc_a = 96.0 ** -0.5
        n = len(kcs)
        for h in (range(4) if 'mla' not in DBG_SKIP else []):
            ob = 4
            lo = (h % 2 == 0)
            vcol = (h // 2) * 192 + (0 if lo else 64)
            npair = n // 2
            SBK = [0, 1, 3]

            def S(p, h=h):
                w = SBK[p % 3]
                for half in range(2):
                    kc = kcs[2 * p + half]
                    P.add("pe", lambda e, kc=kc, half=half, w=w: e.matmul(psw[w][:, half * 512:half * 512 + Tn], lhsT=K.KT[0:96, h, kc * 128:(kc + 1) * 128],
                                                                         rhs=M.QT[0:96, h, 0:Tn], start=True, stop=True),
                          reads=[("KTn", h), ("KTr", h), "QT"], writes=[PSK(2 * w + half)])

            def E(p):
                w = SBK[p % 3]
                src = psw[w][:, :].rearrange("p (a c) -> p a c", a=2)[:, :, 0:Tn]
                dst = M.PT[p % 3].rearrange("p (a c) -> p a c", a=2)[:, :, 0:Tn]
                P.add("act", lambda e: e.activation(out=dst, in_=src, func=AF.Exp, scale=sc_a), reads=[PSK(2 * w), PSK(2 * w + 1)], writes=[("PT", p % 3)])

            def PV(p, ob=ob, vcol=vcol):
                for half in range(2):
                    kc = kcs[2 * p + half]
                    first = (p == 0 and half == 0)
                    last = (p == npair - 1 and half == 1)
                    P.add("pe", lambda e, kc=kc, half=half, first=first, last=last: e.matmul(
                        ps[ob][:, 0:Tn], lhsT=K.Vx[:, kc, vcol:vcol + 128], rhs=M.PT[p % 3][:, half * 512:half * 512 + Tn], start=first, stop=last),
                        reads=["Vx", ("PT", p % 3)], writes=[PSK(ob)])

            for p in range(min(3, npair)):
                S(p)
            for p in range(npair):
                E(p)
                PV(p)
                if p + 3 < npair:
                    S(p + 3)
            normalize(M, Tn, ob, lo, l, h, h // 2, False)
            per = (len(conv_ops) + 3) // 4
            for (a_, k_) in conv_ops[h * per:(h + 1) * per]:
                P.add(*a_, **k_)

        if 'mla' in DBG_SKIP:
            for (a_, k_) in conv_ops:
                P.add(*a_, **k_)
        for c in range(2):
            P.add("pe", lambda e, c=c: e.matmul(ps[6][:, 0:Tn], lhsT=ones_f, rhs=M.acc[c][:, 0:Tn], start=(c == 0), stop=(c == 1)),
                  reads=[("acc", c), "ones_f"], writes=[PSK(6)])
        for c in range(2):
            P.add("pe", lambda e, c=c: e.matmul(ps[7][:, 0:Tn], lhsT=ones_f, rhs=M.f[c][:, 0:Tn], start=(c == 0), stop=(c == 1)),
                  reads=[("f", c), "ones_f"], writes=[PSK(7)])
        P.add("act", lambda e: e.activation(out=M.f[2][:, 0:Tn], in_=ps[6][:, 0:Tn], func=AF.Identity, bias=V("zero"), scale=1.0 / 256),
              reads=[PSK(6), "vecs"], writes=[("f", 2)])
        P.add("dve", lambda e: e.tensor_tensor(out=M.f[0][:, 0:Tn], in0=M.f[2][:, 0:Tn], in1=M.f[2][:, 0:Tn], op=ALU.mult), reads=[("f", 2)], writes=[("f", 0)])
        P.add("dve", lambda e: e.scalar_tensor_tensor(out=M.f[0][:, 0:Tn], in0=ps[7][:, 0:Tn], scalar=1.0 / 256, in1=M.f[0][:, 0:Tn], op0=ALU.mult, op1=ALU.subtract),
              reads=[PSK(7), ("f", 0)], writes=[("f", 0)])
        P.add("act", lambda e: e.activation(out=M.f[0][:, 0:Tn], in_=M.f[0][:, 0:Tn], func=AF.Sqrt, bias=V("eps"), scale=1.0), reads=[("f", 0), "vecs"], writes=[("f", 0)])
        P.add("dve", lambda e: e.reciprocal(out=M.f[0][:, 0:Tn], in_=M.f[0][:, 0:Tn]), reads=[("f", 0)], writes=[("f", 0)])
        for c in range(2):
            acc = M.acc[c]
            P.add("dve", lambda e, acc=acc: e.tensor_tensor(out=acc[:, 0:Tn], in0=acc[:, 0:Tn], in1=M.f[2][:, 0:Tn], op=ALU.subtract),
                  reads=[("acc", c), ("f", 2)], writes=[("acc", c)])
            P.add("dve", lambda e, acc=acc: e.tensor_tensor(out=acc[:, 0:Tn], in0=acc[:, 0:Tn], in1=M.f[0][:, 0:Tn], op=ALU.mult),
                  reads=[("acc", c), ("f", 0)], writes=[("acc", c)])
            P.add("act", lambda e, c=c, acc=acc: e.activation(out=M.mix[:, 6 + c, 0:Tn], in_=acc[:, 0:Tn], func=AF.Silu, bias=V("bln", l * 2 + c, 1),
                                                             scale=V("gln", l * 2 + c, 1)), reads=[("acc", c), "vecs"], writes=[("mix", 6 + c)])

        sc_w = 64.0 ** -0.5
        nqb = Tn // 128
        if not is_ctx and 'tr' not in DBG_SKIP:
            transpose_to_triples(M, M.wav, nqb + 2, M.Vw)
        for g in (range(2) if 'win' not in DBG_SKIP else []):
            pr = slice(g * 64, (g + 1) * 64)
            for qb in range(nqb):
                if is_ctx:
                    kbl = [("ctx", 0, None), ("ctx", 1, None)]
                else:
                    mP = 2 if (t == 0 and qb == 0) else 0
                    mN = 3 if (t == NT - 1 and qb == nqb - 1) else 1
                    kbl = [("loc", qb, mP), ("loc", qb + 1, None), ("loc", qb + 2, mN), ("ctx", 0, None), ("ctx", 1, None)]
                for i, (kind, blk, mi) in enumerate(kbl):
                    b, off = i // 2, (i % 2) * 256
                    ksrc = M.wakc if kind == "ctx" else M.wak
                    outv = ps[b][:, off:off + 256].rearrange("p (a c) -> p a c", a=2)
                    kl = ksrc[pr, blk * 128:(blk + 1) * 128]
                    qr = M.waq[pr, :, qb * 128:(qb + 1) * 128]
                    P.add("pe", lambda e, kl=kl, qr=qr, outv=outv, mi=mi: e.matmul(outv, lhsT=kl, rhs=qr, start=True, stop=(mi is None)),
                          reads=["wak", "wakc", "waq"], writes=[PSK(b)])
                    if mi is not None:
                        P.add("pe", lambda e, b=b, off=off, mi=mi: e.matmul(ps[b][:, off:off + 256], lhsT=ident_b, rhs=masks_b[:, mi, :], start=False, stop=True),
                              reads=["ident", "masks"], writes=[PSK(b)])
                ntile = (len(kbl) + 1) // 2
                for i in range(ntile):
                    w = 256 * min(2, len(kbl) - 2 * i)
                    P.add("act", lambda e, i=i, w=w: e.activation(out=M.PT[i][:, 0:w], in_=ps[i][:, 0:w], func=AF.Exp, scale=sc_w),
                          reads=[PSK(i)], writes=[("PT", i)])
                for c in range(2):
                    ob = 3 + c
                    for i, (kind, blk, mi) in enumerate(kbl):
                        vsrc = M.Vwc if kind == "ctx" else M.Vw
                        col = (i % 2) * 256 + c * 128
                        vl = vsrc[:, blk, g * 192 + c * 64:g * 192 + c * 64 + 128]
                        pr_ = M.PT[i // 2][:, col:col + 128]
                        oo = ps[ob][:, qb * 128:(qb + 1) * 128]
                        last = (i == len(kbl) - 1)
                        P.add("pe", lambda e, vl=vl, pr_=pr_, oo=oo, i=i, last=last: e.matmul(oo, lhsT=vl, rhs=pr_, start=(i == 0), stop=last),
                              reads=["Vw", ("PT", i // 2)], writes=[PSK(ob)])
            for c in range(2):
                normalize(M, Tn, 3 + c, c == 0, l, 2 * g + c, 4 + g, True)

        G2 = HG[:, l, j, 1, :]
        for c in range(8):
            bd = 1 + (c % 2)
            for k in range(8):
                P.add("pe", lambda e, c=c, k=k, bd=bd: e.matmul(ps[bd][:, 0:Tn], lhsT=M.wout[:, k, c * 128:(c + 1) * 128], rhs=M.mix[:, k, 0:Tn],
                                                               start=(k == 0), stop=(k == 7)), reads=["wout", ("mix", k)], writes=[PSK(bd)])
            P.add("dve", lambda e, c=c, bd=bd: e.scalar_tensor_tensor(out=M.ht[:, c, 0:Tn], in0=ps[bd][:, 0:Tn], scalar=G2[:, c:c + 1], in1=M.ht[:, c, 0:Tn],
                                                                     op0=ALU.mult, op1=ALU.add), reads=[PSK(bd), ("ht", c), "coef"], writes=[("ht", c)])
        dma("sp", hsrc.rearrange("(k p) c -> p k c", p=128)[:, :, tok0:tok0 + Tn], M.ht[:, :, 0:Tn], [("ht", k) for k in range(8)], ["hdst"], "hst")

    def phase_mix(l):
        base = bump[0]
        K = alloc_kv()
        kv_build(K, l)
        M = alloc_mix()
        dma("sp", M.wout, wb["w_out", l].rearrange("(k p) m -> p k m", p=128), WK("w_out", l), ["wout"], "wout")
        zc = zcbuf.rearrange("(s p) c -> p s c", p=128)
        dma("sp", M.wakc, zc[:, 10, 128:128 + CTX], [], ["wakc"], "lwakc")
        dma("sp", M.wavc, zc[:, 11, 128:128 + CTX], [], ["wavsrc"], "lwavc")
        init_triples(M.Vw)
        init_triples(M.Vwc)
        transpose_to_triples(M, M.wavc, 2, M.Vwc)
        for t in range(min(NT, DBG_MIXT[0])):
            mixers(K, M, l, False, t)
        if l == 0 and DBG_MIXT[0] >= NT:
            mixers(K, M, l, True, 0)
        P.barrier()
        bump[0] = base

    zt = alloc(NZ * 128, BF16, shape=[NZ, 128])
    P.add("dve", lambda e: e.memset(zt, 0.0), writes=["zt"])
    zcv = zcbuf.rearrange("(s p) c -> p s c", p=128)
    dma("sp", zcv[:, :, 0:128], zt, ["zt"], ["zc0"], "zc0")
    dma("sp", zcv[:, :, 128 + CTX:ZCW], zt, ["zt"], ["zc1"], "zc1")
    P.barrier()
    bump[0] = persist_end

    stage = STAGE[0]
    phase_ffn(None, 0, True)
    if stage >= 1.2:
        exchange(0)
    if stage >= 1.5:
        phase_mix(0)
    if stage >= 2.5:
        phase_ffn(0, 1, False)
    if stage >= 2.7:
        exchange(1)
        phase_mix(1)
    if stage >= 3:
        phase_ffn(1, None, False)
    if stage < 3:
        base = bump[0]
        Bd = alloc_common(T)
        for t in range(NT):
            load_h(Bd, hbuf, t * T, T)
            store_h(Bd, outT, t * T, T, key="out")
        if 'dumpc' in DBG_SKIP:
            load_h(Bd, hcbuf, 0, CTX)
            store_h(Bd, outT, 0, CTX, key="out")
        P.barrier()
        bump[0] = base
    P.barrier()
    P.emit(nc, st)
    st.close()
    return nc


STAGE = [3]
DBG_MIXT = [99]
DBG_SKIP = set()
RG_OVERRIDE = [None]
MOD_RANKS = [2]
MOD_GROUPS = [[[0, 1], [2, 3], [4, 5], [6, 7]]]


def _rope_tables(half):
    pos = half * S_OWN + np.arange(S_OWN)
    row = (pos // 64).astype(np.float32)
    col = (pos % 64).astype(np.float32)

    def tab(d_rot):
        d_ax = d_rot // 2
        inv = (10000.0 ** (-np.arange(0, d_ax, 2, dtype=np.float32) / d_ax)).astype(np.float32)
        ar = row[:, None] * inv[None, :]
        ac = col[:, None] * inv[None, :]
        cr, sr, cc_, sc_ = np.cos(ar), np.sin(ar), np.cos(ac), np.sin(ac)
        C = np.concatenate([cr, cr, cc_, cc_], axis=1).T.astype(np.float32)
        Sg = np.concatenate([-sr, sr, -sc_, sc_], axis=1).T.astype(np.float32)
        return C, Sg

    Cw, Sw = tab(64)
    ropeW = np.concatenate([np.tile(Cw, (2, 1)), np.tile(Sw, (2, 1))], axis=0)
    Cm, Sm = tab(32)
    ropeK = np.concatenate([Cm, Sm], axis=0)
    Cq = np.concatenate([np.ones((64, S_OWN), np.float32), Cm], axis=0)
    Sq = np.concatenate([np.zeros((64, S_OWN), np.float32), Sm], axis=0)
    ropeQ = np.concatenate([Cq, Sq], axis=0)
    return np.ascontiguousarray(ropeW), np.ascontiguousarray(ropeQ), np.ascontiguousarray(ropeK)


def _masks(half):
    kp = np.arange(128)[:, None]
    qp = np.arange(128)[None, :]
    mP = np.where(qp <= kp, 0.0, NEGM).astype(np.float32)
    mN = np.where(kp <= qp, 0.0, NEGM).astype(np.float32)
    neg = np.full((128, 128), NEGM, np.float32)
    kinds = [mP, mN, mP if half == 1 else neg, mN if half == 0 else neg]
    m = np.stack([np.concatenate([k, k], axis=1) for k in kinds], axis=1)
    return np.ascontiguousarray(m.reshape(128, 4 * 256))


def _fm(v):
    v = np.asarray(v, np.float32)
    lead = v.shape[:-1]
    n = v.shape[-1] // 128
    return np.moveaxis(v.reshape(lead + (n, 128)), -1, 0)


def _pack_vecs(inp, half, b):
    vec = np.zeros((128, NV), np.float32)

    def put(name, arr):
        arr = np.asarray(arr, np.float32).reshape(128, -1)
        w = dict(_VSPEC)[name]
        assert arr.shape[1] == w, (name, arr.shape, w)
        vec[:, VOFF[name]:VOFF[name] + w] = arr

    put("gf1", _fm(inp["g_ffn1"]))
    put("gmix", _fm(inp["g_mix"]))
    put("gf2", _fm(inp["g_ffn2"]))
    put("bmod", _fm(inp["b_mod"]))
    put("gfin", _fm(inp["g_final"]))
    gq = np.zeros((L, 256), np.float32)
    gq[:, :192] = inp["g_mla_q"]
    put("gq", _fm(gq))
    put("gkv", _fm(inp["g_mla_kv"]))
    wsc = np.asarray(inp["w_sc_conv"], np.float32)
    put("wsc", np.transpose(wsc.reshape(L, 3, 2, 128), (3, 0, 2, 1)))
    wcf = np.asarray(inp["w_cf_conv"], np.float32)
    put("wcf", np.transpose(wcf.reshape(L, 31, 2, 128), (3, 0, 2, 1)))
    put("bcf", _fm(inp["b_cf_conv"]))
    put("gln", _fm(inp["g_cf_ln"]))
    put("bln", _fm(inp["b_cf_ln"]))
    put("sink", np.broadcast_to(np.asarray(inp["wa_sink"], np.float32).reshape(1, L * 4), (128, L * 4)))
    put("lm", np.full((128, 1), 1.0 if half == 1 else 0.0, np.float32))
    put("rm", np.full((128, 1), 1.0 if half == 0 else 0.0, np.float32))
    put("eps", np.full((128, 1), EPS, np.float32))
    put("zero", np.zeros((128, 1), np.float32))
    bsel = np.zeros((128, 4), np.float32)
    bsel[:, b] = 1.0
    put("bsel", bsel)
    return vec


_NC_CACHE = {}


def kernel(**inputs):
    inp = {k: np.asarray(v) for k, v in inputs.items()}
    x = inp["x"].astype(np.float32, copy=False)
    ctx = inp["ctx"].astype(np.float32, copy=False)
    Bn = x.shape[0]
    key = STAGE[0]
    if key not in _NC_CACHE:
        _NC_CACHE[key] = build()
    nc = _NC_CACHE[key]
    shared = {n: np.ascontiguousarray(inp[n], dtype=np.float32) for n in
              ("w1_gate", "w1_up", "w1_down", "w2_gate", "w2_up", "w2_down", "w_in", "w_out", "w_mla_uq", "w_mla_ukv")}
    ident = np.eye(128, dtype=np.float32)
    in_maps = []
    for core in range(8):
        b, half = core // 2, core % 2
        rW, rQ, rK = _rope_tables(half)
        cc = np.stack([_fm(inp["c"][bb]) for bb in range(4)] + [_fm(inp["c_ctx"])], axis=-1).reshape(128, 40)
        nsh = MOD_RANKS[0]
        rk = core % nsh
        wsl = 9 * D // nsh
        m = {"xT": np.ascontiguousarray(x[b, half * S_OWN:(half + 1) * S_OWN, :].T),
             "ctxT": np.ascontiguousarray(ctx[b].T),
             "cc": np.ascontiguousarray(cc, dtype=np.float32),
             "vecs": _pack_vecs(inp, half, b),
             "ropeW": rW, "ropeQ": rQ, "ropeK": rK,
             "masks": _masks(half), "ident": ident}
        m["w_mod"] = np.ascontiguousarray(inp["w_mod"][:, :, rk * wsl:(rk + 1) * wsl], dtype=np.float32)
        m.update(shared)
        in_maps.append(m)
    res = run_bass_kernel_spmd(nc, in_maps, core_ids=list(range(8)))
    out = np.empty((Bn, 2 * S_OWN, D), np.float32)
    for core in range(8):
        b, half = core // 2, core % 2
        out[b, half * S_OWN:(half + 1) * S_OWN, :] = np.asarray(res.results[core]["outT"]).T
    return out
```

```python
import contextlib
import numpy as np
import concourse.bass as bass
import concourse.mybir as mybir
from concourse.bass_utils import run_bass_kernel_spmd

F32 = mybir.dt.float32
BF16 = mybir.dt.bfloat16
AF = mybir.ActivationFunctionType
ALU = mybir.AluOpType
ENGS = ("pe", "act", "dve", "pool", "sp")

L = 2
D = 1024
S_OWN = 4096
CTX = 256
T = 512
NT = S_OWN // T
DFF = 2816
NJ = DFF // 128
NWIN = 2560
ZW = 128 + S_OWN + 128
ZCW = 128 + CTX + 128
EPS = 1e-6
NZ = 14
ZQ, ZSCB, ZSCT, ZWAQ, ZWAK, ZWAV, ZCFU = 0, 4, 6, 8, 10, 11, 12
HALO_SLOTS = (6, 7, 10, 11, 12, 13)
NEGM = -30000.0

_VSPEC = [("gf1", L * 8), ("gmix", L * 8), ("gf2", L * 8), ("bmod", L * 72), ("gfin", 8), ("gq", L * 2),
          ("gkv", L), ("wsc", L * 6), ("wcf", L * 62), ("bcf", L * 2), ("gln", L * 2), ("bln", L * 2),
          ("sink", L * 4), ("lm", 1), ("rm", 1), ("eps", 1), ("zero", 1), ("bsel", 4)]
VOFF = {}
_o = 0
for _n, _w in _VSPEC:
    VOFF[_n] = _o
    _o += _w
NV = _o


class Op:
    __slots__ = ("eng", "fn", "deps", "signal", "count", "chan", "idx")

    def __init__(self, eng, fn, chan):
        self.idx = 0
        self.eng = eng
        self.fn = fn
        self.deps = set()
        self.signal = False
        self.count = 0
        self.chan = chan


class Chan:
    def __init__(self, name):
        self.name = name
        self.nobar = False
        self.sem = None
        self.n = 0
        self.last = None


class Prog:
    def __init__(self):
        self.ops = {e: [] for e in ENGS}
        self.last_w = {}
        self.readers = {}
        self.chans = []
        self.sticky = set()

    def chan(self, name):
        c = Chan(name)
        self.chans.append(c)
        return c

    def add(self, eng, fn, reads=(), writes=(), chan=None):
        op = Op(eng, fn, chan)
        deps = set()
        for k in reads:
            w = self.last_w.get(k)
            if w is not None:
                deps.add(w)
        for k in writes:
            w = self.last_w.get(k)
            if w is not None:
                deps.add(w)
            deps.update(self.readers.get(k, ()))
        if chan is not None:
            if chan.last is not None:
                deps.add(chan.last)
            chan.last = op
            chan.n += 1
            op.count = 16 * chan.n
            op.signal = True
        deps.discard(op)
        if eng == "pe":
            deps = {d for d in deps if not (d.eng == "pe" and d.chan is None)}
        best = {}
        for d in deps:
            k = ("c", id(d.chan)) if d.chan is not None else ("e", d.eng)
            if k not in best or best[k].idx < d.idx:
                best[k] = d
        deps = set(best.values())
        for d in deps:
            d.signal = True
        op.deps = deps
        op.idx = len(self.ops[eng]) if chan is None else chan.n
        for k in reads:
            self.readers.setdefault(k, []).append(op)
        for k in writes:
            self.last_w[k] = op
            self.readers[k] = []
        self.ops[eng].append(op)
        return op

    def barrier(self, final=False):
        lasts = []
        for e in ENGS:
            for op in reversed(self.ops[e]):
                if op.chan is None and op.fn is not None:
                    lasts.append(op)
                    break
        for c in self.chans:
            if c.last is not None and (final or not c.nobar):
                lasts.append(c.last)
        for e in ENGS:
            op = Op(e, None, None)
            op.deps = set(lasts)
            for d in op.deps:
                d.signal = True
            self.ops[e].append(op)
        self.last_w = {k: v for k, v in self.last_w.items() if k in self.sticky}
        self.readers = {}

    def emit(self, nc, stack):
        esem = {e: stack.enter_context(nc.semaphore("s_" + e)) for e in ENGS}
        for c in self.chans:
            if c.n > 0:
                c.sem = stack.enter_context(nc.semaphore("c_" + c.name))
        for e in ENGS:
            n = 0
            for op in self.ops[e]:
                if op.chan is None and op.signal:
                    n += 1
                    op.count = n
        block = stack.enter_context(nc.Block())

        def run(e, eng):
            waited = {}
            for op in self.ops[e]:
                need = {}
                for d in op.deps:
                    if d.chan is not None:
                        s, v = d.chan.sem, d.count
                    else:
                        s, v = esem[d.eng], d.count
                    key = id(s)
                    if key not in need or need[key][1] < v:
                        need[key] = (s, v)
                for key, (s, v) in need.items():
                    if waited.get(key, 0) < v:
                        eng.wait_ge(s, v)
                        waited[key] = v
                if op.fn is None:
                    continue
                ins = op.fn(eng)
                if op.chan is not None:
                    ins.then_inc(op.chan.sem, 16)
                elif op.signal:
                    ins.then_inc(esem[e], 1)

        @block.tensor
        def _(eng):
            run("pe", eng)

        @block.scalar
        def _(eng):
            run("act", eng)

        @block.vector
        def _(eng):
            run("dve", eng)

        @block.gpsimd
        def _(eng):
            run("pool", eng)

        @block.sync
        def _(eng):
            run("sp", eng)


def build(dbg=None):
    nc = bass.Bass("TRN2", target_bir_lowering=False)
    P = Prog()
    st = contextlib.ExitStack()

    def din(name, shape):
        return nc.dram_tensor(name, list(shape), F32, kind="ExternalInput").ap()

    def dscr(name, shape, dt):
        return nc.dram_tensor(name, list(shape), dt).ap()

    xT = din("xT", [D, S_OWN])
    ctxT = din("ctxT", [D, CTX])
    cc_in = din("cc", [128, 40])
    vecs_in = din("vecs", [128, NV])
    ropeW = din("ropeW", [256, S_OWN])
    ropeQ = din("ropeQ", [192, S_OWN])
    ropeK = din("ropeK", [64, S_OWN])
    masks_in = din("masks", [128, 4 * 256])
    ident_in = din("ident", [128, 128])
    w_mod = din("w_mod", [L, D, 9 * D // MOD_RANKS[0]])
    wsrc = {n: din(n, s) for n, s in [
        ("w1_gate", [L, D, DFF]), ("w1_up", [L, D, DFF]), ("w1_down", [L, DFF, D]),
        ("w2_gate", [L, D, DFF]), ("w2_up", [L, D, DFF]), ("w2_down", [L, DFF, D]),
        ("w_in", [L, D, 2144]), ("w_out", [L, D, D]), ("w_mla_uq", [L, 192, 384]), ("w_mla_ukv", [L, 128, 512])]}
    outT = nc.dram_tensor("outT", [D, S_OWN], F32, kind="ExternalOutput").ap()
    dbg_out = None
    if dbg:
        dbg_out = nc.dram_tensor("dbg", list(dbg), F32, kind="ExternalOutput").ap()

    hbuf = dscr("hbuf", [D, S_OWN], F32)
    hcbuf = dscr("hcbuf", [D, CTX], F32)
    zbuf = dscr("zbuf", [NZ * 128, ZW], BF16)
    zcbuf = dscr("zcbuf", [NZ * 128, ZCW], BF16)
    xin_mla = dscr("xin_mla", [160, S_OWN], BF16)
    xout_mla = dscr("xout_mla", [320, S_OWN], BF16)
    ckvc = dscr("ckvc", [160, CTX], BF16)
    xin_halo = dscr("xin_halo", [768, 256], BF16)
    xout_halo = dscr("xout_halo", [1536, 256], BF16)
    wb = {}
    for l in range(L):
        for n in ("w1_gate", "w1_up", "w2_gate", "w2_up"):
            wb[n, l] = dscr(f"b_{n}{l}", [D, DFF], BF16)
        for n in ("w1_down", "w2_down"):
            wb[n, l] = dscr(f"b_{n}{l}", [DFF, D], BF16)
        wb["w_in", l] = dscr(f"b_w_in{l}", [D, NWIN], BF16)
        wb["w_out", l] = dscr(f"b_w_out{l}", [D, D], BF16)
        wb["w_mla_uq", l] = dscr(f"b_wuq{l}", [192, 768], BF16)
        wb["w_mla_ukv", l] = dscr(f"b_wukv{l}", [128, 512], BF16)

    ARENA = 53000
    arena = st.enter_context(nc.sbuf_tensor("arena", [128, ARENA], F32))
    bump = [0]

    def alloc(cols, dt=F32, shape=None):
        words = cols if dt == F32 else (cols + 1) // 2
        a = bump[0]
        bump[0] += words
        assert bump[0] <= ARENA, ("SBUF arena overflow", bump[0])
        ap = arena[:, a:a + words]
        if dt != F32:
            ap = ap.bitcast(dt)[:, 0:cols]
        if shape is not None:
            names = " ".join(f"d{i}" for i in range(len(shape)))
            kw = {f"d{i}": s for i, s in enumerate(shape)}
            ap = ap.rearrange(f"p ({names}) -> p {names}", **kw)
        return ap

    psw = [st.enter_context(nc.psum_tensor(f"psw{i}", [128, 1024], F32)) for i in range(4)]
    ps = []
    for i in range(4):
        ps.append(psw[i][:, 0:512])
        ps.append(psw[i][:, 512:1024])

    def PSK(i):
        return ("ps", i)

    vecs = alloc(NV)
    ccs = alloc(40)
    modv = alloc(L * 144, shape=[L, 72, 2])
    Acoef = alloc(L * 2 * 3 * 8, shape=[L, 2, 3, 8])
    HG = alloc(L * 2 * 3 * 8, shape=[L, 2, 3, 8])
    esink = alloc(L * 4, shape=[L, 4])
    ones_b = alloc(128, BF16)
    ones_f = alloc(128)
    ident_b = alloc(128, BF16)
    masks_b = alloc(4 * 256, BF16, shape=[4, 256])
    persist_end = bump[0]

    def V(name, off=0, w=1):
        o = VOFF[name] + off
        return vecs[:, o:o + w]

    chn = {}

    def CH(name):
        if name not in chn:
            chn[name] = P.chan(name)
        return chn[name]

    def dma(q, out, in_, reads, writes, ch):
        return P.add(q, lambda e: e.dma_start(out=out, in_=in_), reads=reads, writes=writes, chan=CH(ch))

    dma("sp", vecs, vecs_in, [], ["vecs"], "ld0")
    dma("sp", ccs, cc_in, [], ["ccs"], "ld1")
    P.sticky.update(["ident", "masks"])
    dma("pool", ident_b, ident_in, [], ["ident"], "cv0")
    dma("pool", masks_b.rearrange("p a b -> p (a b)"), masks_in, [], ["masks"], "cv1")
    P.add("dve", lambda e: e.memset(ones_b, 1.0), writes=["ones_b"])
    P.add("dve", lambda e: e.memset(ones_f, 1.0), writes=["ones_f"])
    P.add("act", lambda e: e.activation(out=ccs, in_=ccs, func=AF.Silu), reads=["ccs"], writes=["ccs"])
    P.add("act", lambda e: e.activation(out=esink.rearrange("p a b -> p (a b)"), in_=V("sink", 0, L * 4), func=AF.Exp),
          reads=["vecs"], writes=["esink"])

    cvn = [0]
    CVKEYS = {}

    def conv(out, in_, key):
        cvn[0] += 1
        key = (key, cvn[0])
        CVKEYS.setdefault(key[0], []).append(key)
        P.sticky.add(key)
        op = dma("pool", out, in_, [], [key], f"cv{cvn[0] % 8}")
        op.chan.nobar = True

    def conv_rows(name, l, nrows, dst=None, c0=0, c1=None, d0=0):
        src = wsrc[name]
        c1 = c1 if c1 is not None else src.shape[2]
        dstap = wb[name, l] if dst is None else dst
        for r in range(0, nrows, 128):
            rr = min(128, nrows - r)
            conv(dstap[r:r + rr, d0:d0 + (c1 - c0)], src[l, r:r + rr, c0:c1], (name, l, r // 128))

    def conv_layer_ffn(l, which):
        conv_rows(f"w{which}_gate", l, D)
        conv_rows(f"w{which}_up", l, D)
        conv_rows(f"w{which}_down", l, DFF)

    def conv_layer_mix(l):
        src = wsrc["w_in"]
        dst = wb["w_in", l]
        segs = [(0, 1120, 0), (1120, 1184, 1120), (1248, 1312, 1184), (1184, 1248, 1248), (1312, 1376, 1312),
                (1376, 2144, 1376)]
        for b8, s8 in enumerate([8, 0, 24, 16]):
            segs.append((320 + s8, 320 + s8 + 8, 2144 + 8 * b8))
        for hi, hsrc in enumerate([1120, 1248, 1184, 1312, 1376, 1440]):
            for b16, s16 in enumerate([16, 0, 48, 32]):
                segs.append((hsrc + s16, hsrc + s16 + 16, 2176 + 64 * hi + 16 * b16))
        for (c0, c1, d0) in segs:
            for r in range(0, D, 512):
                conv(dst[r:r + 512, d0:d0 + (c1 - c0)], src[l, r:r + 512, c0:c1], ("w_in", l))
        conv_rows("w_out", l, D)
        srcq = wsrc["w_mla_uq"]
        dq = wb["w_mla_uq", l]
        conv(dq[0:128, 0:384], srcq[l, 0:128, :], ("w_mla_uq", l))
        conv(dq[128:192, 0:384], srcq[l, 128:192, :], ("w_mla_uq", l))
        for h in range(4):
            conv(dq[0:192, 384 + h * 96:384 + h * 96 + 64], srcq[l, :, h * 96:h * 96 + 64], ("w_mla_uq", l))
            for b8, s8 in enumerate([8, 0, 24, 16]):
                conv(dq[0:192, 384 + h * 96 + 64 + 8 * b8:384 + h * 96 + 64 + 8 * b8 + 8],
                     srcq[l, :, h * 96 + 64 + s8:h * 96 + 64 + s8 + 8], ("w_mla_uq", l))
        conv(wb["w_mla_ukv", l][:, :], wsrc["w_mla_ukv"][l, :, :], ("w_mla_ukv", l))

    conv_layer_ffn(0, 1)

    def WK(name, l):
        if name == "w_in" or name.startswith("w_mla"):
            base = [(name, l)]
        else:
            n = DFF if name.endswith("down") else D
            base = [(name, l, r) for r in range((n + 127) // 128)]
        out = []
        for bkey in base:
            out.extend(CVKEYS.get(bkey, []))
        return out

    def setup_mod():
        base = bump[0]
        nsh = MOD_RANKS[0]
        nq = 72 // nsh
        slab = alloc(8 * nq * 128, shape=[8, nq * 128])
        part = alloc(L * nq * 5)
        modall = alloc(nsh * L * nq * 5, shape=[nsh, L * nq, 5])
        modx_in = dscr("modx_in", [128, L * nq * 5], F32)
        modx_out = dscr("modx_out", [nsh * 128, L * nq * 5], F32)
        for l in range(L):
            for kc in range(8):
                dma("sp", slab[:, kc, :], w_mod[l, kc * 128:(kc + 1) * 128, :], [], [("wm", kc)], f"wm{kc % 2}")
            for oc in range(nq):
                col = (l * nq + oc) * 5
                for kc in range(8):
                    P.add("pe", lambda e, kc=kc, oc=oc, col=col: e.matmul(
                        ps[0][:, col:col + 5], lhsT=slab[:, kc, oc * 128:(oc + 1) * 128],
                        rhs=ccs[:, kc * 5:kc * 5 + 5], start=(kc == 0), stop=(kc == 7)),
                        reads=[("wm", kc), "ccs"], writes=[PSK(0)])
        P.add("act", lambda e: e.activation(out=part, in_=ps[0][:, 0:L * nq * 5], func=AF.Copy), reads=[PSK(0)], writes=["part"])
        dma("sp", modx_in, part, ["part"], ["modx_in"], "ld0")
        rg = MOD_GROUPS[0]
        P.add("pool", lambda e: e.collective_compute("AllGather", ALU.bypass, replica_groups=rg, ins=[modx_in.opt()], outs=[modx_out.opt()]),
              reads=["modx_in"], writes=["modx_out"]).signal = True
        dma("sp", modall.rearrange("p r q f -> p r (q f)"), modx_out.rearrange("(r p) c -> p r c", p=128), ["modx_out"], ["modall"], "ld1")
        for l in range(L):
            mv = modv[:, l, :, :].rearrange("p (r q) j -> p r q j", r=nsh)
            src = modall[:, :, l * nq:(l + 1) * nq, :]
            P.add("dve", lambda e, mv=mv, src=src: e.tensor_copy(out=mv[:, :, :, 1], in_=src[:, :, :, 4]), reads=["modall"], writes=["modv"])
            P.add("dve", lambda e, mv=mv, src=src: e.tensor_scalar(out=mv[:, :, :, 0], in0=src[:, :, :, 0], scalar1=V("bsel", 0, 1), scalar2=None, op0=ALU.mult),
                  reads=["modall", "vecs"], writes=["modv"])
            for bb in range(1, 4):
                P.add("dve", lambda e, mv=mv, src=src, bb=bb: e.scalar_tensor_tensor(out=mv[:, :, :, 0], in0=src[:, :, :, bb], scalar=V("bsel", bb, 1),
                                                                                   in1=mv[:, :, :, 0], op0=ALU.mult, op1=ALU.add),
                      reads=["modall", "vecs", "modv"], writes=["modv"])
            for j in range(2):
                P.add("dve", lambda e, l=l, j=j: e.tensor_tensor(out=modv[:, l, :, j], in0=modv[:, l, :, j], in1=V("bmod", l * 72, 72), op=ALU.add),
                      reads=["modv", "vecs"], writes=["modv"])
            for j in range(2):
                for s_, gname in enumerate(("gf1", "gmix", "gf2")):
                    i_scale = 3 * s_ + 1
                    P.add("dve", lambda e, l=l, j=j, s_=s_, gname=gname, i_scale=i_scale: e.scalar_tensor_tensor(
                        out=Acoef[:, l, j, s_, :], in0=modv[:, l, i_scale * 8:(i_scale + 1) * 8, j], scalar=1.0,
                        in1=V(gname, l * 8, 8), op0=ALU.add, op1=ALU.mult), reads=["modv", "vecs"], writes=["coef"])
                    i_gate = 3 * s_ + 2
                    P.add("dve", lambda e, l=l, j=j, s_=s_, i_gate=i_gate: e.tensor_scalar(
                        out=HG[:, l, j, s_, :], in0=modv[:, l, i_gate * 8:(i_gate + 1) * 8, j],
                        scalar1=(1.0 if s_ == 1 else 0.5), scalar2=None, op0=ALU.mult), reads=["modv"], writes=["coef"])
        P.barrier()
        bump[0] = base

    setup_mod()
    conv_layer_mix(0)
    conv_layer_ffn(0, 2)
    conv_layer_ffn(1, 1)
    conv_layer_mix(1)
    conv_layer_ffn(1, 2)

    def Bsh(l, j, s):
        return modv[:, l, (3 * s) * 8:(3 * s + 1) * 8, j]

    class FFNBufs:
        pass

    def alloc_common(Tm):
        B = FFNBufs()
        B.ht = alloc(8 * Tm, shape=[8, Tm])
        B.xn = alloc(8 * Tm, BF16, shape=[8, Tm])
        B.sq = [alloc(Tm, BF16) for _ in range(2)]
        B.rstd = alloc(Tm)
        B.tmp = [alloc(Tm) for _ in range(2)]
        return B

    def norm_mod(B, Tn, Avec, Bvec, tag):
        for kc in range(8):
            P.add("act", lambda e, kc=kc: e.activation(out=B.sq[kc % 2][:, 0:Tn], in_=B.ht[:, kc, 0:Tn], func=AF.Square),
                  reads=[("ht", kc)], writes=[("sq", kc % 2)])
            P.add("pe", lambda e, kc=kc: e.matmul(ps[0][:, 0:Tn], lhsT=ones_b, rhs=B.sq[kc % 2][:, 0:Tn],
                                                 start=(kc == 0), stop=(kc == 7)),
                  reads=[("sq", kc % 2), "ones_b"], writes=[PSK(0)])
        P.add("act", lambda e: e.activation(out=B.rstd[:, 0:Tn], in_=ps[0][:, 0:Tn], func=AF.Sqrt, bias=V("eps"), scale=1.0 / D),
              reads=[PSK(0), "vecs"], writes=["rstd"])
        P.add("dve", lambda e: e.reciprocal(out=B.rstd[:, 0:Tn], in_=B.rstd[:, 0:Tn]), reads=["rstd"], writes=["rstd"])
        for kc in range(8):
            P.add("dve", lambda e, kc=kc: e.tensor_tensor(out=B.tmp[kc % 2][:, 0:Tn], in0=B.ht[:, kc, 0:Tn], in1=B.rstd[:, 0:Tn],
                                                         op=ALU.mult), reads=[("ht", kc), "rstd"], writes=[("tmp", kc % 2)])
            bias = Bvec[:, kc:kc + 1] if Bvec is not None else V("zero")
            P.add("act", lambda e, kc=kc, bias=bias: e.activation(out=B.xn[:, kc, 0:Tn], in_=B.tmp[kc % 2][:, 0:Tn], func=AF.Identity,
                                                                  bias=bias, scale=Avec[:, kc:kc + 1]),
                  reads=[("tmp", kc % 2), "coef", "modv", "vecs"], writes=[("xn", kc)])

    JG = 2
    NG = (NJ + JG - 1) // JG

    def alloc_ffn(B, Tm):
        B.H = alloc(NJ * Tm, BF16, shape=[NJ, Tm])
        B.wg = [alloc(8 * JG * 128, BF16, shape=[8, JG * 128]) for _ in range(2)]
        B.wu = [alloc(8 * JG * 128, BF16, shape=[8, JG * 128]) for _ in range(2)]
        B.wd = alloc(NJ * D, BF16, shape=[NJ, D])
        B.sg = [alloc(Tm) for _ in range(2)]

    gcount = [0]

    def ffn(B, Tn, l, which, hg):
        wgd, wud, wdd = wb[f"w{which}_gate", l], wb[f"w{which}_up", l], wb[f"w{which}_down", l]
        kg, ku, kd = WK(f"w{which}_gate", l), WK(f"w{which}_up", l), WK(f"w{which}_down", l)

        def load_group(g):
            slot = (gcount[0] + g) % 2
            j0 = g * JG
            nj = min(JG, NJ - j0)
            dma("sp", B.wg[slot][:, :, 0:nj * 128], wgd[:, j0 * 128:(j0 + nj) * 128].rearrange("(k p) m -> p k m", p=128),
                kg, [("wg", slot)], f"wg{slot}")
            dma("sp", B.wu[slot][:, :, 0:nj * 128], wud[:, j0 * 128:(j0 + nj) * 128].rearrange("(k p) m -> p k m", p=128),
                ku, [("wu", slot)], f"wu{slot}")

        load_group(0)
        for g in range(NG):
            slot = (gcount[0] + g) % 2
            j0 = g * JG
            nj = min(JG, NJ - j0)
            if g + 1 < NG:
                load_group(g + 1)
            dma("sp", B.wd[:, j0:j0 + nj, :], wdd[j0 * 128:(j0 + nj) * 128, :].rearrange("(j p) m -> p j m", p=128),
                kd, [("wd", g)], "wd")
            for jj in range(nj):
                j = j0 + jj
                bg, bu = 1 + (j % 2), 3 + (j % 2)
                for kc in range(8):
                    P.add("pe", lambda e, kc=kc, jj=jj, slot=slot, bg=bg: e.matmul(
                        ps[bg][:, 0:Tn], lhsT=B.wg[slot][:, kc, jj * 128:(jj + 1) * 128], rhs=B.xn[:, kc, 0:Tn],
                        start=(kc == 0), stop=(kc == 7)), reads=[("wg", slot), ("xn", kc)], writes=[PSK(bg)])
                for kc in range(8):
                    P.add("pe", lambda e, kc=kc, jj=jj, slot=slot, bu=bu: e.matmul(
                        ps[bu][:, 0:Tn], lhsT=B.wu[slot][:, kc, jj * 128:(jj + 1) * 128], rhs=B.xn[:, kc, 0:Tn],
                        start=(kc == 0), stop=(kc == 7)), reads=[("wu", slot), ("xn", kc)], writes=[PSK(bu)])
                P.add("act", lambda e, j=j, bg=bg: e.activation(out=B.sg[j % 2][:, 0:Tn], in_=ps[bg][:, 0:Tn], func=AF.Silu),
                      reads=[PSK(bg)], writes=[("sg", j % 2)])
                P.add("dve", lambda e, j=j, bu=bu: e.tensor_tensor(out=B.H[:, j, 0:Tn], in0=ps[bu][:, 0:Tn], in1=B.sg[j % 2][:, 0:Tn],
                                                                  op=ALU.mult), reads=[PSK(bu), ("sg", j % 2)], writes=[("H", j)])
        gcount[0] += NG
        for c in range(8):
            bd = 5 + (c % 2)
            for j in range(NJ):
                P.add("pe", lambda e, c=c, j=j, bd=bd: e.matmul(ps[bd][:, 0:Tn], lhsT=B.wd[:, j, c * 128:(c + 1) * 128],
                                                               rhs=B.H[:, j, 0:Tn], start=(j == 0), stop=(j == NJ - 1)),
                      reads=[("wd", j // JG), ("H", j)], writes=[PSK(bd)])
            P.add("dve", lambda e, c=c, bd=bd: e.scalar_tensor_tensor(out=B.ht[:, c, 0:Tn], in0=ps[bd][:, 0:Tn], scalar=hg[:, c:c + 1],
                                                                     in1=B.ht[:, c, 0:Tn], op0=ALU.mult, op1=ALU.add),
                  reads=[PSK(bd), ("ht", c), "coef"], writes=[("ht", c)])

    def alloc_win(B, Tm):
        B.win = alloc(8 * NWIN, BF16, shape=[8, NWIN])
        B.wuq = alloc(2 * 768, BF16, shape=[2, 768])
        B.zst = alloc(NZ * Tm, BF16, shape=[NZ, Tm])
        P.add("dve", lambda e: e.memset(B.zst, 0.0), writes=[("zst", s_) for s_ in range(NZ)])
        B.mst = alloc(2 * Tm, BF16, shape=[2, Tm])
        B.cqn = alloc(2 * Tm, BF16, shape=[2, Tm])
        B.f1 = [alloc(Tm) for _ in range(3)]
        B.rW = alloc(2 * Tm, shape=[2, Tm])
        B.rQ = alloc(2 * Tm, shape=[2, Tm])
        B.rK = alloc(2 * Tm, shape=[2, Tm])


    rr = [0]

    def nb():
        rr[0] = rr[0] % 7 + 1
        return rr[0]

    def win_proj(B, Tn, l, rope, tok0, zdst, zcol0, mdst, mcol0, halo):
        if rope:
            dma("sp", B.rW[:, :, 0:Tn], ropeW.rearrange("(a p) c -> p a c", p=128)[:, :, tok0:tok0 + Tn], [], ["rW"], "rW")
            dma("sp", B.rQ[0:96, :, 0:Tn], ropeQ.rearrange("(a p) c -> p a c", p=96)[:, :, tok0:tok0 + Tn], [], ["rQ"], "rQ")
            dma("sp", B.rK[0:32, :, 0:Tn], ropeK.rearrange("(a p) c -> p a c", p=32)[:, :, tok0:tok0 + Tn], [], ["rK"], "rK")

        def group(M, col0):
            b = nb()
            for kc in range(8):
                P.add("pe", lambda e, kc=kc, b=b: e.matmul(ps[b][0:M, 0:Tn], lhsT=B.win[:, kc, col0:col0 + M], rhs=B.xn[:, kc, 0:Tn],
                                                          start=(kc == 0), stop=(kc == 7)),
                      reads=["win", ("xn", kc)], writes=[PSK(b)])
            return b

        fi = [0]

        def ftmp():
            fi[0] = (fi[0] + 1) % 3
            return fi[0]

        def copy_out(b, M, dst, key):
            P.add("act", lambda e: e.activation(out=dst, in_=ps[b][0:M, 0:Tn], func=AF.Copy), reads=[PSK(b)], writes=[key])

        def rope_out(bA, bB, M, tab, tabkey, dst, key):
            i0, i1 = ftmp(), ftmp()
            P.add("dve", lambda e: e.tensor_tensor(out=B.f1[i0][0:M, 0:Tn], in0=ps[bA][0:M, 0:Tn], in1=tab[0:M, 0, 0:Tn], op=ALU.mult),
                  reads=[PSK(bA), tabkey], writes=[("f1", i0)])
            P.add("dve", lambda e: e.tensor_tensor(out=B.f1[i1][0:M, 0:Tn], in0=ps[bB][0:M, 0:Tn], in1=tab[0:M, 1, 0:Tn], op=ALU.mult),
                  reads=[PSK(bB), tabkey], writes=[("f1", i1)])
            P.add("dve", lambda e: e.tensor_tensor(out=dst, in0=B.f1[i0][0:M, 0:Tn], in1=B.f1[i1][0:M, 0:Tn], op=ALU.add),
                  reads=[("f1", i0), ("f1", i1)], writes=[key])

        def rstd_from(bank_ss, n):
            P.add("act", lambda e: e.activation(out=B.rstd[:, 0:Tn], in_=ps[bank_ss][:, 0:Tn], func=AF.Sqrt, bias=V("eps"), scale=1.0 / n),
                  reads=[PSK(bank_ss), "vecs"], writes=["rstd"])
            P.add("dve", lambda e: e.reciprocal(out=B.rstd[:, 0:Tn], in_=B.rstd[:, 0:Tn]), reads=["rstd"], writes=["rstd"])

        b0 = group(128, 0)
        b1 = group(64, 128)
        P.add("act", lambda e: e.activation(out=B.sq[0][:, 0:Tn], in_=ps[b0][:, 0:Tn], func=AF.Square), reads=[PSK(b0)], writes=[("sq", 0)])
        P.add("act", lambda e: e.activation(out=B.sq[1][0:64, 0:Tn], in_=ps[b1][0:64, 0:Tn], func=AF.Square), reads=[PSK(b1)], writes=[("sq", 1)])
        P.add("pe", lambda e: e.matmul(ps[0][:, 0:Tn], lhsT=ones_b, rhs=B.sq[0][:, 0:Tn], start=True, stop=False),
              reads=[("sq", 0), "ones_b"], writes=[PSK(0)])
        P.add("pe", lambda e: e.matmul(ps[0][:, 0:Tn], lhsT=ones_b[0:64, :], rhs=B.sq[1][0:64, 0:Tn], start=False, stop=True),
              reads=[("sq", 1), "ones_b"], writes=[PSK(0)])
        rstd_from(0, 192)
        P.add("dve", lambda e: e.scalar_tensor_tensor(out=B.cqn[:, 0, 0:Tn], in0=ps[b0][:, 0:Tn], scalar=V("gq", l * 2, 1), in1=B.rstd[:, 0:Tn],
                                                     op0=ALU.mult, op1=ALU.mult), reads=[PSK(b0), "rstd", "vecs"], writes=[("cqn", 0)])
        P.add("dve", lambda e: e.scalar_tensor_tensor(out=B.cqn[0:64, 1, 0:Tn], in0=ps[b1][0:64, 0:Tn], scalar=V("gq", l * 2 + 1, 1)[0:64, :],
                                                     in1=B.rstd[0:64, 0:Tn], op0=ALU.mult, op1=ALU.mult),
              reads=[PSK(b1), "rstd", "vecs"], writes=[("cqn", 1)])
        for h in range(4):
            banks = []
            for rot in ([0, 1] if rope else [0]):
                b = nb()
                c0 = rot * 384 + h * 96
                P.add("pe", lambda e, b=b, c0=c0: e.matmul(ps[b][0:96, 0:Tn], lhsT=B.wuq[:, 0, c0:c0 + 96], rhs=B.cqn[:, 0, 0:Tn], start=True, stop=False),
                      reads=["wuq", ("cqn", 0)], writes=[PSK(b)])
                P.add("pe", lambda e, b=b, c0=c0: e.matmul(ps[b][0:96, 0:Tn], lhsT=B.wuq[0:64, 1, c0:c0 + 96], rhs=B.cqn[0:64, 1, 0:Tn], start=False, stop=True),
                      reads=["wuq", ("cqn", 1)], writes=[PSK(b)])
                banks.append(b)
            if rope:
                rope_out(banks[0], banks[1], 96, B.rQ, "rQ", B.zst[0:96, ZQ + h, 0:Tn], ("zst", ZQ + h))
            else:
                copy_out(banks[0], 96, B.zst[0:96, ZQ + h, 0:Tn], ("zst", ZQ + h))
        bkv = group(128, 192)
        P.add("act", lambda e: e.activation(out=B.sq[0][:, 0:Tn], in_=ps[bkv][:, 0:Tn], func=AF.Square), reads=[PSK(bkv)], writes=[("sq", 0)])
        P.add("pe", lambda e: e.matmul(ps[0][:, 0:Tn], lhsT=ones_b, rhs=B.sq[0][:, 0:Tn], start=True, stop=True),
              reads=[("sq", 0), "ones_b"], writes=[PSK(0)])
        rstd_from(0, 128)
        P.add("dve", lambda e: e.scalar_tensor_tensor(out=B.mst[:, 0, 0:Tn], in0=ps[bkv][:, 0:Tn], scalar=V("gkv", l, 1), in1=B.rstd[:, 0:Tn],
                                                     op0=ALU.mult, op1=ALU.mult), reads=[PSK(bkv), "rstd", "vecs"], writes=[("mst", 0)])
        bA = group(32, 320)
        if rope:
            bB = group(32, 2144)
            rope_out(bA, bB, 32, B.rK, "rK", B.mst[0:32, 1, 0:Tn], ("mst", 1))
        else:
            copy_out(bA, 32, B.mst[0:32, 1, 0:Tn], ("mst", 1))
        for c in range(2):
            b = group(128, 352 + c * 128)
            copy_out(b, 128, B.zst[:, ZSCB + c, 0:Tn], ("zst", ZSCB + c))
        for c in range(2):
            bc = group(128, 608 + c * 128)
            bx = group(128, 864 + c * 128)
            i = ftmp()
            P.add("act", lambda e, i=i, bx=bx: e.activation(out=B.f1[i][:, 0:Tn], in_=ps[bx][:, 0:Tn], func=AF.Copy), reads=[PSK(bx)], writes=[("f1", i)])
            P.add("dve", lambda e, i=i, bc=bc, c=c: e.tensor_tensor(out=B.zst[:, ZSCT + c, 0:Tn], in0=ps[bc][:, 0:Tn], in1=B.f1[i][:, 0:Tn], op=ALU.mult),
                  reads=[PSK(bc), ("f1", i)], writes=[("zst", ZSCT + c)])
        for c in range(2):
            bA = group(128, 1120 + c * 128)
            if rope:
                bB = group(128, 2176 + c * 128)
                rope_out(bA, bB, 128, B.rW, "rW", B.zst[:, ZWAQ + c, 0:Tn], ("zst", ZWAQ + c))
            else:
                copy_out(bA, 128, B.zst[:, ZWAQ + c, 0:Tn], ("zst", ZWAQ + c))
        bA = group(128, 1376)
        if rope:
            bB = group(128, 2432)
            rope_out(bA, bB, 128, B.rW, "rW", B.zst[:, ZWAK, 0:Tn], ("zst", ZWAK))
        else:
            copy_out(bA, 128, B.zst[:, ZWAK, 0:Tn], ("zst", ZWAK))
        b = group(128, 1504)
        copy_out(b, 128, B.zst[:, ZWAV, 0:Tn], ("zst", ZWAV))
        for c in range(2):
            ba = group(128, 1632 + c * 128)
            bg = group(128, 1888 + c * 128)
            i = ftmp()
            P.add("act", lambda e, i=i, bg=bg: e.activation(out=B.f1[i][:, 0:Tn], in_=ps[bg][:, 0:Tn], func=AF.Sigmoid), reads=[PSK(bg)], writes=[("f1", i)])
            P.add("dve", lambda e, i=i, ba=ba, c=c: e.tensor_tensor(out=B.zst[:, ZCFU + c, 0:Tn], in0=ps[ba][:, 0:Tn], in1=B.f1[i][:, 0:Tn], op=ALU.mult),
                  reads=[PSK(ba), ("f1", i)], writes=[("zst", ZCFU + c)])
        zk = [("zst", s) for s in range(NZ)]
        dma("sp", zdst.rearrange("(s p) c -> p s c", p=128)[:, :, zcol0:zcol0 + Tn], B.zst[:, :, 0:Tn], zk, ["zdst"], "zst")
        dma("sp", mdst[0:128, mcol0:mcol0 + Tn], B.mst[:, 0, 0:Tn], [("mst", 0)], ["mdst"], "mst0")
        dma("sp", mdst[128:160, mcol0:mcol0 + Tn], B.mst[0:32, 1, 0:Tn], [("mst", 1)], ["mdst"], "mst1")
        for (which, c0) in halo:
            xh = xin_halo.rearrange("(s p) c -> p s c", p=128)
            dma("sp", xh[:, 0:2, which * 128:(which + 1) * 128], B.zst[:, 6:8, c0:c0 + 128], zk, ["xin_halo"], "hal0")
            dma("sp", xh[:, 2:6, which * 128:(which + 1) * 128], B.zst[:, 10:14, c0:c0 + 128], zk, ["xin_halo"], "hal1")

    def load_win(B, l):
        dma("sp", B.win[:, 0:4, :], wb["w_in", l][0:512, :].rearrange("(k p) m -> p k m", p=128), WK("w_in", l), ["win"], "win0")
        dma("sp", B.win[:, 4:8, :], wb["w_in", l][512:1024, :].rearrange("(k p) m -> p k m", p=128), WK("w_in", l), ["win"], "win1")
        dma("sp", B.wuq[:, 0, :], wb["w_mla_uq", l][0:128, :], WK("w_mla_uq", l), ["wuq"], "wuq0")
        dma("sp", B.wuq[0:64, 1, :], wb["w_mla_uq", l][128:192, :], WK("w_mla_uq", l), ["wuq"], "wuq1")

    def load_h(B, src, col0, Tn):
        dma("sp", B.ht[:, :, 0:Tn], src.rearrange("(k p) c -> p k c", p=128)[:, :, col0:col0 + Tn], ["hsrc"], [("ht", k) for k in range(8)], "hld")

    def store_h(B, dst, col0, Tn, key="hdst"):
        dma("sp", dst.rearrange("(k p) c -> p k c", p=128)[:, :, col0:col0 + Tn], B.ht[:, :, 0:Tn], [("ht", k) for k in range(8)], [key], "hst")

    def tiles_lat_ctx():
        out = [(False, 0, T, t * T) for t in range(NT)]
        out.append((True, 1, CTX, 0))
        return out

    def phase_ffn(l_prev, l_next, first):
        base = bump[0]
        B = alloc_common(T)
        alloc_ffn(B, T)
        if l_next is not None:
            alloc_win(B, T)
            load_win(B, l_next)
        for (is_ctx, j, Tn, tok0) in tiles_lat_ctx():
            if first:
                load_h(B, ctxT if is_ctx else xT, tok0, Tn)
            else:
                if is_ctx and l_next is None:
                    continue
                load_h(B, hcbuf if is_ctx else hbuf, tok0, Tn)
            if l_prev is not None:
                norm_mod(B, Tn, Acoef[:, l_prev, j, 2, :], Bsh(l_prev, j, 2), "f2")
                ffn(B, Tn, l_prev, 2, HG[:, l_prev, j, 2, :])
            if l_next is not None:
                norm_mod(B, Tn, Acoef[:, l_next, j, 0, :], Bsh(l_next, j, 0), "f1")
                ffn(B, Tn, l_next, 1, HG[:, l_next, j, 0, :])
                store_h(B, hcbuf if is_ctx else hbuf, tok0, Tn)
                norm_mod(B, Tn, Acoef[:, l_next, j, 1, :], Bsh(l_next, j, 1), "mx")
                halo = []
                if not is_ctx and tok0 == 0:
                    halo.append((0, 0))
                if not is_ctx and tok0 == S_OWN - T:
                    halo.append((1, T - 128))
                win_proj(B, Tn, l_next, not is_ctx, tok0, zcbuf if is_ctx else zbuf, 128 + tok0,
                         ckvc if is_ctx else xin_mla, tok0, halo)
            else:
                norm_mod(B, Tn, V("gfin", 0, 8), None, "fin")
                for kc in range(8):
                    P.add("dve", lambda e, kc=kc, Tn=Tn: e.tensor_tensor(out=B.tmp[kc % 2][:, 0:Tn], in0=B.ht[:, kc, 0:Tn], in1=B.rstd[:, 0:Tn], op=ALU.mult),
                          reads=[("ht", kc), "rstd"], writes=[("tmp", kc % 2)])
                    P.add("dve", lambda e, kc=kc, Tn=Tn: e.tensor_scalar(out=B.ht[:, kc, 0:Tn], in0=B.tmp[kc % 2][:, 0:Tn], scalar1=V("gfin", kc, 1), scalar2=None,
                                                                 op0=ALU.mult), reads=[("tmp", kc % 2), "vecs"], writes=[("ht", kc)])
                store_h(B, outT, tok0, Tn, key="out")
        P.barrier()
        bump[0] = base

    RG = RG_OVERRIDE[0] or [[0, 1], [2, 3], [4, 5], [6, 7]]

    def exchange(l):
        base = bump[0]
        P.add("pool", lambda e: e.collective_compute("AllGather", ALU.bypass, replica_groups=RG, ins=[xin_mla.opt()], outs=[xout_mla.opt()]),
              reads=["mdst"], writes=["xout_mla"]).signal = True
        P.add("pool", lambda e: e.collective_compute("AllGather", ALU.bypass, replica_groups=RG, ins=[xin_halo.opt()], outs=[xout_halo.opt()]),
              reads=["xin_halo"], writes=["xout_halo"]).signal = True
        hl = alloc(6 * 128, BF16, shape=[6, 128])
        hr = alloc(6 * 128, BF16, shape=[6, 128])
        xo = xout_halo.rearrange("(r s p) c -> r p s c", r=2, p=128)
        dma("sp", hl, xo[0, :, :, 128:256], ["xout_halo"], ["hl"], "hl")
        dma("sp", hr, xo[1, :, :, 0:128], ["xout_halo"], ["hr"], "hr")
        P.add("dve", lambda e: e.tensor_scalar(out=hl, in0=hl, scalar1=V("lm"), scalar2=None, op0=ALU.mult), reads=["hl", "vecs"], writes=["hl"])
        P.add("dve", lambda e: e.tensor_scalar(out=hr, in0=hr, scalar1=V("rm"), scalar2=None, op0=ALU.mult), reads=["hr", "vecs"], writes=["hr"])
        zb = zbuf.rearrange("(s p) c -> p s c", p=128)
        dma("sp", zb[:, 6:8, 0:128], hl[:, 0:2, :], ["hl"], ["zdst"], "hl")
        dma("sp", zb[:, 10:14, 0:128], hl[:, 2:6, :], ["hl"], ["zdst"], "hl")
        dma("sp", zb[:, 6:8, 128 + S_OWN:ZW], hr[:, 0:2, :], ["hr"], ["zdst"], "hr")
        dma("sp", zb[:, 10:14, 128 + S_OWN:ZW], hr[:, 2:6, :], ["hr"], ["zdst"], "hr")
        P.barrier()
        bump[0] = base

    NKC = 66

    def alloc_kv():
        K = FFNBufs()
        K.KT = alloc(4 * NKC * 128, BF16, shape=[4, NKC * 128])
        K.Vx = alloc(NKC * 384, BF16, shape=[NKC, 384])
        K.wukv = alloc(512, BF16)
        K.ckt = [alloc(512, BF16) for _ in range(2)]
        return K

    def kv_build(K, l):
        dma("sp", K.wukv, wb["w_mla_ukv", l], WK("w_mla_ukv", l), ["wukv"], "wukv")
        P.add("dve", lambda e: e.memset(K.Vx, 0.0), writes=["Vx"])
        P.add("dve", lambda e: e.memset(K.Vx.rearrange("p k (a c) -> p k a c", a=2)[:, :, :, 64:65], 1.0), writes=["Vx"])
        srcs = [(xout_mla[r * 160:r * 160 + 128, t8 * 512:(t8 + 1) * 512], 512, r * S_OWN + t8 * 512) for r in range(2) for t8 in range(8)]
        srcs.append((ckvc[0:128, 0:CTX], CTX, 2 * S_OWN))
        for r in range(2):
            for h in range(4):
                dma("sp", K.KT[64:96, h, r * S_OWN:(r + 1) * S_OWN], xout_mla[r * 160 + 128:r * 160 + 160, :], ["xout_mla"], [("KTr", h)], f"ktr{h}")
        for h in range(4):
            dma("sp", K.KT[64:96, h, 2 * S_OWN:2 * S_OWN + CTX], ckvc[128:160, :], ["mdstc"], [("KTr", h)], f"ktr{h}")
        wv = K.wukv.rearrange("p (h c) -> p h c", h=4)[:, :, 64:128]
        for i, (src, n, key0) in enumerate(srcs):
            slot = i % 2
            dma("sp", K.ckt[slot][:, 0:n], src, ["xout_mla", "mdstc"], [("ckt", slot)], f"ckt{slot}")
            for h in range(4):
                b = 1 + (h % 2)
                P.add("pe", lambda e, h=h, b=b, slot=slot, n=n: e.matmul(ps[b][0:64, 0:n], lhsT=K.wukv[:, h * 128:h * 128 + 64], rhs=K.ckt[slot][:, 0:n],
                                                                        start=True, stop=True), reads=["wukv", ("ckt", slot)], writes=[PSK(b)])
                eng = "act" if h % 2 == 0 else "dve"
                if eng == "act":
                    P.add("act", lambda e, h=h, b=b, n=n, key0=key0: e.activation(out=K.KT[0:64, h, key0:key0 + n], in_=ps[b][0:64, 0:n], func=AF.Copy),
                          reads=[PSK(b)], writes=[("KTn", h)])
                else:
                    P.add("dve", lambda e, h=h, b=b, n=n, key0=key0: e.tensor_copy(out=K.KT[0:64, h, key0:key0 + n], in_=ps[b][0:64, 0:n]),
                          reads=[PSK(b)], writes=[("KTn", h)])
            for kb in range(n // 128):
                b = 3 + (kb % 2)
                kc = key0 // 128 + kb
                P.add("pe", lambda e, kb=kb, b=b, slot=slot: e.matmul(ps[b][:, 0:256], lhsT=K.ckt[slot][:, kb * 128:(kb + 1) * 128], rhs=wv,
                                                                     start=True, stop=True), reads=["wukv", ("ckt", slot)], writes=[PSK(b)])
                pv = ps[b][:, 0:256].rearrange("p (a b c) -> p a b c", a=2, b=2)
                vo = K.Vx[:, kc, :].rearrange("p (a b c) -> p a b c", a=2, b=3)
                P.add("act", lambda e, pv=pv, vo=vo: e.activation(out=vo[:, :, 0, :], in_=pv[:, :, 0, :], func=AF.Copy), reads=[PSK(b)], writes=["Vx"])
                P.add("dve", lambda e, pv=pv, vo=vo: e.tensor_copy(out=vo[:, :, 2, :], in_=pv[:, :, 1, :]), reads=[PSK(b)], writes=["Vx"])

    def alloc_mix():
        M = FFNBufs()
        M.ht = alloc(8 * T, shape=[8, T])
        M.mix = alloc(8 * T, BF16, shape=[8, T])
        M.wout = alloc(8 * D, BF16, shape=[8, D])
        M.QT = alloc(4 * T, BF16, shape=[4, T])
        M.PT = [alloc(2 * T, BF16) for _ in range(3)]
        M.scb = alloc(2 * T, BF16, shape=[2, T])
        M.sct = alloc(2 * (T + 2), BF16, shape=[2, T + 2])
        M.waq = alloc(2 * T, BF16, shape=[2, T])
        M.wak = alloc(T + 256, BF16)
        M.wav = alloc(T + 256, BF16)
        M.cfu = alloc(2 * (T + 32), BF16, shape=[2, T + 32])
        M.Vw = alloc(6 * 384, BF16, shape=[6, 384])
        M.wakc = alloc(CTX, BF16)
        M.Vwc = alloc(2 * 384, BF16, shape=[2, 384])
        M.acc = [alloc(T) for _ in range(2)]
        M.rinv = alloc(T)
        M.bc = alloc(T)
        M.wavc = M.bc.bitcast(BF16)[:, 0:CTX]
        M.f = [alloc(T) for _ in range(3)]
        return M

    def transpose_to_triples(M, src, nblk, dst):
        p7 = ps[7][:, :].bitcast(BF16)
        for blk in range(nblk):
            o = (blk % 4) * 128
            P.add("pe", lambda e, blk=blk, o=o: e.transpose(out=p7[:, o:o + 128], in_=src[:, blk * 128:(blk + 1) * 128], identity=ident_b),
                  reads=["wavsrc", "ident"], writes=[PSK(7)])
            pv = p7[:, o:o + 128].rearrange("p (a c) -> p a c", a=2)
            vo = dst[:, blk, :].rearrange("p (a b c) -> p a b c", a=2, b=3)
            P.add("act", lambda e, pv=pv, vo=vo: e.activation(out=vo[:, :, 0, :], in_=pv, func=AF.Copy), reads=[PSK(7)], writes=["Vw"])
            P.add("dve", lambda e, pv=pv, vo=vo: e.tensor_copy(out=vo[:, :, 2, :], in_=pv), reads=[PSK(7)], writes=["Vw"])

    def init_triples(buf):
        P.add("dve", lambda e: e.memset(buf, 0.0), writes=["Vw"])
        P.add("dve", lambda e: e.memset(buf.rearrange("p k (a c) -> p k a c", a=2)[:, :, :, 64:65], 1.0), writes=["Vw"])

    def normalize(M, Tn, ob, lo, l, h, chunk, sink, bcb=5):
        if 'norm' in DBG_SKIP:
            return
        sp = 64 if lo else 0
        mrows = 64 if lo else 128
        r0 = 0 if lo else 64
        if sink:
            P.add("dve", lambda e: e.tensor_scalar(out=M.rinv[sp:sp + 1, 0:Tn], in0=ps[ob][sp:sp + 1, 0:Tn], scalar1=esink[sp:sp + 1, l, h:h + 1],
                                                  scalar2=None, op0=ALU.add), reads=[PSK(ob), "esink"], writes=["rinv"])
            P.add("dve", lambda e: e.reciprocal(out=M.rinv[sp:sp + 1, 0:Tn], in_=M.rinv[sp:sp + 1, 0:Tn]), reads=["rinv"], writes=["rinv"])
        else:
            P.add("dve", lambda e: e.reciprocal(out=M.rinv[sp:sp + 1, 0:Tn], in_=ps[ob][sp:sp + 1, 0:Tn]), reads=[PSK(ob)], writes=["rinv"])
        P.add("pe", lambda e: e.matmul(ps[bcb][0:mrows, 0:Tn], lhsT=ones_f[sp:sp + 1, 0:mrows], rhs=M.rinv[sp:sp + 1, 0:Tn], start=True, stop=True),
              reads=["rinv", "ones_f"], writes=[PSK(bcb)])
        P.add("act", lambda e: e.activation(out=M.bc[r0:r0 + 64, 0:Tn], in_=ps[bcb][r0:r0 + 64, 0:Tn], func=AF.Copy), reads=[PSK(bcb)], writes=["bc"])
        P.add("dve", lambda e: e.tensor_tensor(out=M.mix[r0:r0 + 64, chunk, 0:Tn], in0=ps[ob][r0:r0 + 64, 0:Tn], in1=M.bc[r0:r0 + 64, 0:Tn], op=ALU.mult),
              reads=[PSK(ob), "bc"], writes=[("mix", chunk)])

    def mixers(K, M, l, is_ctx, t):
        Tn = CTX if is_ctx else T
        tok0 = 0 if is_ctx else t * T
        j = 1 if is_ctx else 0
        zv = (zcbuf if is_ctx else zbuf).rearrange("(s p) c -> p s c", p=128)
        c0 = 128 + tok0
        dma("sp", M.QT[0:96, :, 0:Tn], zv[0:96, 0:4, c0:c0 + Tn], [], ["QT"], "lq")
        dma("sp", M.scb[:, :, 0:Tn], zv[:, 4:6, c0:c0 + Tn], [], ["scb"], "lscb")
        dma("sp", M.sct[:, :, 0:Tn + 2], zv[:, 6:8, c0 - 1:c0 + Tn + 1], [], ["sct"], "lsct")
        dma("sp", M.waq[:, :, 0:Tn], zv[:, 8:10, c0:c0 + Tn], [], ["waq"], "lwaq")
        dma("sp", M.wak[:, 0:Tn + 256], zv[:, 10, c0 - 128:c0 + Tn + 128], [], ["wak"], "lwak")
        dma("sp", M.wav[:, 0:Tn + 256], zv[:, 11, c0 - 128:c0 + Tn + 128], [], ["wavsrc"], "lwav")
        dma("sp", M.cfu[:, :, 0:Tn + 30], zv[:, 12:14, c0 - 15:c0 + Tn + 15], [], ["cfu"], "lcfu")
        hsrc = hcbuf if is_ctx else hbuf
        dma("sp", M.ht[:, :, 0:Tn], hsrc.rearrange("(k p) c -> p k c", p=128)[:, :, tok0:tok0 + Tn], ["hdst"], [("ht", k) for k in range(8)], "hld")

        conv_ops = []

        def cadd(*a_, **k_):
            conv_ops.append((a_, k_))

        def wsc(c, k):
            return V("wsc", (l * 2 + c) * 3 + k, 1)

        def wcf(c, k):
            return V("wcf", (l * 2 + c) * 31 + k, 1)

        for c in (range(2) if 'sc' not in DBG_SKIP else []):
            acc = M.acc[c]
            cadd("dve", lambda e, c=c, acc=acc: e.tensor_scalar(out=acc[:, 0:Tn], in0=M.sct[:, c, 0:Tn], scalar1=wsc(c, 0), scalar2=None, op0=ALU.mult),
                  reads=["sct", "vecs"], writes=[("acc", c)])
            for k in (1, 2):
                cadd("dve", lambda e, c=c, k=k, acc=acc: e.scalar_tensor_tensor(out=acc[:, 0:Tn], in0=M.sct[:, c, k:k + Tn], scalar=wsc(c, k), in1=acc[:, 0:Tn],
                                                                                 op0=ALU.mult, op1=ALU.add), reads=["sct", ("acc", c), "vecs"], writes=[("acc", c)])
            cadd("dve", lambda e, c=c, acc=acc: e.tensor_tensor(out=M.mix[:, 2 + c, 0:Tn], in0=acc[:, 0:Tn], in1=M.scb[:, c, 0:Tn], op=ALU.mult),
                  reads=[("acc", c), "scb"], writes=[("mix", 2 + c)])
        for c in range(2):
            acc = M.acc[c]
            cadd("dve", lambda e, c=c, acc=acc: e.tensor_scalar(out=acc[:, 0:Tn], in0=M.cfu[:, c, 0:Tn], scalar1=wcf(c, 0), scalar2=V("bcf", l * 2 + c, 1),
                                                                 op0=ALU.mult, op1=ALU.add), reads=["cfu", "vecs", ("acc", c)], writes=[("acc", c)])
            for k in range(1, 31):
                cadd("dve", lambda e, c=c, k=k, acc=acc: e.scalar_tensor_tensor(out=acc[:, 0:Tn], in0=M.cfu[:, c, k:k + Tn], scalar=wcf(c, k), in1=acc[:, 0:Tn],
                                                                                 op0=ALU.mult, op1=ALU.add), reads=["cfu", ("acc", c), "vecs"], writes=[("acc", c)])
            cadd("dve", lambda e, c=c, acc=acc: e.tensor_tensor(out=M.f[c][:, 0:Tn], in0=acc[:, 0:Tn], in1=acc[:, 0:Tn], op=ALU.mult),
                  reads=[("acc", c)], writes=[("f", c)])

        kcs = [64, 65] if is_ctx else list(range(NKC))
        sc_a = 96.0 ** -0.5
        n = len(kcs)
        for h in (range(4) if 'mla' not in DBG_SKIP else []):
            ob = 4
            lo = (h % 2 == 0)
            vcol = (h // 2) * 192 + (0 if lo else 64)
            npair = n // 2
            SBK = [0, 1, 3]

            def S(p, h=h):
                w = SBK[p % 3]
                for half in range(2):
                    kc = kcs[2 * p + half]
                    P.add("pe", lambda e, kc=kc, half=half, w=w: e.matmul(psw[w][:, half * 512:half * 512 + Tn], lhsT=K.KT[0:96, h, kc * 128:(kc + 1) * 128],
                                                                         rhs=M.QT[0:96, h, 0:Tn], start=True, stop=True),
                          reads=[("KTn", h), ("KTr", h), "QT"], writes=[PSK(2 * w + half)])

            def E(p):
                w = SBK[p % 3]
                src = psw[w][:, :].rearrange("p (a c) -> p a c", a=2)[:, :, 0:Tn]
                dst = M.PT[p % 3].rearrange("p (a c) -> p a c", a=2)[:, :, 0:Tn]
                P.add("act", lambda e: e.activation(out=dst, in_=src, func=AF.Exp, scale=sc_a), reads=[PSK(2 * w), PSK(2 * w + 1)], writes=[("PT", p % 3)])

            def PV(p, ob=ob, vcol=vcol):
                for half in range(2):
                    kc = kcs[2 * p + half]
                    first = (p == 0 and half == 0)
                    last = (p == npair - 1 and half == 1)
                    P.add("pe", lambda e, kc=kc, half=half, first=first, last=last: e.matmul(
                        ps[ob][:, 0:Tn], lhsT=K.Vx[:, kc, vcol:vcol + 128], rhs=M.PT[p % 3][:, half * 512:half * 512 + Tn], start=first, stop=last),
                        reads=["Vx", ("PT", p % 3)], writes=[PSK(ob)])

            for p in range(min(3, npair)):
                S(p)
            for p in range(npair):
                E(p)
                PV(p)
                if p + 3 < npair:
                    S(p + 3)
            normalize(M, Tn, ob, lo, l, h, h // 2, False)
            per = (len(conv_ops) + 3) // 4
            for (a_, k_) in conv_ops[h * per:(h + 1) * per]:
                P.add(*a_, **k_)

        if 'mla' in DBG_SKIP:
            for (a_, k_) in conv_ops:
                P.add(*a_, **k_)
        for c in range(2):
            P.add("pe", lambda e, c=c: e.matmul(ps[6][:, 0:Tn], lhsT=ones_f, rhs=M.acc[c][:, 0:Tn], start=(c == 0), stop=(c == 1)),
                  reads=[("acc", c), "ones_f"], writes=[PSK(6)])
        for c in range(2):
            P.add("pe", lambda e, c=c: e.matmul(ps[7][:, 0:Tn], lhsT=ones_f, rhs=M.f[c][:, 0:Tn], start=(c == 0), stop=(c == 1)),
                  reads=[("f", c), "ones_f"], writes=[PSK(7)])
        P.add("act", lambda e: e.activation(out=M.f[2][:, 0:Tn], in_=ps[6][:, 0:Tn], func=AF.Identity, bias=V("zero"), scale=1.0 / 256),
              reads=[PSK(6), "vecs"], writes=[("f", 2)])
        P.add("dve", lambda e: e.tensor_tensor(out=M.f[0][:, 0:Tn], in0=M.f[2][:, 0:Tn], in1=M.f[2][:, 0:Tn], op=ALU.mult), reads=[("f", 2)], writes=[("f", 0)])
        P.add("dve", lambda e: e.scalar_tensor_tensor(out=M.f[0][:, 0:Tn], in0=ps[7][:, 0:Tn], scalar=1.0 / 256, in1=M.f[0][:, 0:Tn], op0=ALU.mult, op1=ALU.subtract),
              reads=[PSK(7), ("f", 0)], writes=[("f", 0)])
        P.add("act", lambda e: e.activation(out=M.f[0][:, 0:Tn], in_=M.f[0][:, 0:Tn], func=AF.Sqrt, bias=V("eps"), scale=1.0), reads=[("f", 0), "vecs"], writes=[("f", 0)])
        P.add("dve", lambda e: e.reciprocal(out=M.f[0][:, 0:Tn], in_=M.f[0][:, 0:Tn]), reads=[("f", 0)], writes=[("f", 0)])
        for c in range(2):
            acc = M.acc[c]
            P.add("dve", lambda e, acc=acc: e.tensor_tensor(out=acc[:, 0:Tn], in0=acc[:, 0:Tn], in1=M.f[2][:, 0:Tn], op=ALU.subtract),
                  reads=[("acc", c), ("f", 2)], writes=[("acc", c)])
            P.add("dve", lambda e, acc=acc: e.tensor_tensor(out=acc[:, 0:Tn], in0=acc[:, 0:Tn], in1=M.f[0][:, 0:Tn], op=ALU.mult),
                  reads=[("acc", c), ("f", 0)], writes=[("acc", c)])
            P.add("act", lambda e, c=c, acc=acc: e.activation(out=M.mix[:, 6 + c, 0:Tn], in_=acc[:, 0:Tn], func=AF.Silu, bias=V("bln", l * 2 + c, 1),
                                                             scale=V("gln", l * 2 + c, 1)), reads=[("acc", c), "vecs"], writes=[("mix", 6 + c)])

        sc_w = 64.0 ** -0.5
        nqb = Tn // 128
        if not is_ctx and 'tr' not in DBG_SKIP:
            transpose_to_triples(M, M.wav, nqb + 2, M.Vw)
        for g in (range(2) if 'win' not in DBG_SKIP else []):
            pr = slice(g * 64, (g + 1) * 64)
            for qb in range(nqb):
                setb = (g * nqb + qb) % 2
                sbanks = [0, 1, 2] if setb == 0 else [3, 6, 7]
                po = setb * 512
                if is_ctx:
                    kbl = [("ctx", 0, None), ("ctx", 1, None)]
                else:
                    mP = 2 if (t == 0 and qb == 0) else 0
                    mN = 3 if (t == NT - 1 and qb == nqb - 1) else 1
                    kbl = [("loc", qb, mP), ("loc", qb + 1, None), ("loc", qb + 2, mN), ("ctx", 0, None), ("ctx", 1, None)]
                for i, (kind, blk, mi) in enumerate(kbl):
                    b, off = sbanks[i // 2], (i % 2) * 256
                    ksrc = M.wakc if kind == "ctx" else M.wak
                    outv = ps[b][:, off:off + 256].rearrange("p (a c) -> p a c", a=2)
                    kl = ksrc[pr, blk * 128:(blk + 1) * 128]
                    qr = M.waq[pr, :, qb * 128:(qb + 1) * 128]
                    P.add("pe", lambda e, kl=kl, qr=qr, outv=outv, mi=mi: e.matmul(outv, lhsT=kl, rhs=qr, start=True, stop=(mi is None)),
                          reads=["wak", "wakc", "waq"], writes=[PSK(b)])
                    if mi is not None:
                        mo = ps[b][:, off:off + 256]
                        mr = masks_b[:, mi, :]
                        P.add("pe", lambda e, mo=mo, mr=mr: e.matmul(mo, lhsT=ident_b, rhs=mr, start=False, stop=True),
                              reads=["ident", "masks"], writes=[PSK(b)])
                ntile = (len(kbl) + 1) // 2
                for i in range(ntile):
                    w = 256 * min(2, len(kbl) - 2 * i)
                    eo = M.PT[i][:, po:po + w]
                    ei = ps[sbanks[i]][:, 0:w]
                    P.add("act", lambda e, eo=eo, ei=ei: e.activation(out=eo, in_=ei, func=AF.Exp, scale=sc_w),
                          reads=[PSK(sbanks[i])], writes=[("PTw", i, setb)])
                for c in range(2):
                    ob = 4 + c
                    for i, (kind, blk, mi) in enumerate(kbl):
                        vsrc = M.Vwc if kind == "ctx" else M.Vw
                        col = po + (i % 2) * 256 + c * 128
                        vl = vsrc[:, blk, g * 192 + c * 64:g * 192 + c * 64 + 128]
                        pr_ = M.PT[i // 2][:, col:col + 128]
                        oo = ps[ob][:, qb * 128:(qb + 1) * 128]
                        last = (i == len(kbl) - 1)
                        P.add("pe", lambda e, vl=vl, pr_=pr_, oo=oo, i=i, last=last: e.matmul(oo, lhsT=vl, rhs=pr_, start=(i == 0), stop=last),
                              reads=["Vw", ("PTw", i // 2, setb)], writes=[PSK(ob)])
            for c in range(2):
                normalize(M, Tn, 4 + c, c == 0, l, 2 * g + c, 4 + g, True, bcb=2)

        G2 = HG[:, l, j, 1, :]
        for c in range(8):
            bd = 1 + (c % 2)
            for k in range(8):
                P.add("pe", lambda e, c=c, k=k, bd=bd: e.matmul(ps[bd][:, 0:Tn], lhsT=M.wout[:, k, c * 128:(c + 1) * 128], rhs=M.mix[:, k, 0:Tn],
                                                               start=(k == 0), stop=(k == 7)), reads=["wout", ("mix", k)], writes=[PSK(bd)])
            P.add("dve", lambda e, c=c, bd=bd: e.scalar_tensor_tensor(out=M.ht[:, c, 0:Tn], in0=ps[bd][:, 0:Tn], scalar=G2[:, c:c + 1], in1=M.ht[:, c, 0:Tn],
                                                                     op0=ALU.mult, op1=ALU.add), reads=[PSK(bd), ("ht", c), "coef"], writes=[("ht", c)])
        dma("sp", hsrc.rearrange("(k p) c -> p k c", p=128)[:, :, tok0:tok0 + Tn], M.ht[:, :, 0:Tn], [("ht", k) for k in range(8)], ["hdst"], "hst")

    def phase_mix(l):
        base = bump[0]
        K = alloc_kv()
        kv_build(K, l)
        M = alloc_mix()
        dma("sp", M.wout, wb["w_out", l].rearrange("(k p) m -> p k m", p=128), WK("w_out", l), ["wout"], "wout")
        zc = zcbuf.rearrange("(s p) c -> p s c", p=128)
        dma("sp", M.wakc, zc[:, 10, 128:128 + CTX], [], ["wakc"], "lwakc")
        dma("sp", M.wavc, zc[:, 11, 128:128 + CTX], [], ["wavsrc"], "lwavc")
        init_triples(M.Vw)
        init_triples(M.Vwc)
        transpose_to_triples(M, M.wavc, 2, M.Vwc)
        for t in range(min(NT, DBG_MIXT[0])):
            mixers(K, M, l, False, t)
        if l == 0 and DBG_MIXT[0] >= NT:
            mixers(K, M, l, True, 0)
        P.barrier()
        bump[0] = base

    zt = alloc(NZ * 128, BF16, shape=[NZ, 128])
    P.add("dve", lambda e: e.memset(zt, 0.0), writes=["zt"])
    zcv = zcbuf.rearrange("(s p) c -> p s c", p=128)
    dma("sp", zcv[:, :, 0:128], zt, ["zt"], ["zc0"], "zc0")
    dma("sp", zcv[:, :, 128 + CTX:ZCW], zt, ["zt"], ["zc1"], "zc1")
    P.barrier()
    bump[0] = persist_end

    stage = STAGE[0]
    phase_ffn(None, 0, True)
    if stage >= 1.2:
        exchange(0)
    if stage >= 1.5:
        phase_mix(0)
    if stage >= 2.5:
        phase_ffn(0, 1, False)
    if stage >= 2.7:
        exchange(1)
        phase_mix(1)
    if stage >= 3:
        phase_ffn(1, None, False)
    if stage < 3:
        base = bump[0]
        Bd = alloc_common(T)
        for t in range(NT):
            load_h(Bd, hbuf, t * T, T)
            store_h(Bd, outT, t * T, T, key="out")
        if 'dumpc' in DBG_SKIP:
            load_h(Bd, hcbuf, 0, CTX)
            store_h(Bd, outT, 0, CTX, key="out")
        P.barrier()
        bump[0] = base
    P.barrier(final=True)
    P.emit(nc, st)
    st.close()
    return nc


STAGE = [3]
DBG_MIXT = [99]
DBG_SKIP = set()
RG_OVERRIDE = [None]
MOD_RANKS = [2]
MOD_GROUPS = [[[0, 1], [2, 3], [4, 5], [6, 7]]]


def _rope_tables(half):
    pos = half * S_OWN + np.arange(S_OWN)
    row = (pos // 64).astype(np.float32)
    col = (pos % 64).astype(np.float32)

    def tab(d_rot):
        d_ax = d_rot // 2
        inv = (10000.0 ** (-np.arange(0, d_ax, 2, dtype=np.float32) / d_ax)).astype(np.float32)
        ar = row[:, None] * inv[None, :]
        ac = col[:, None] * inv[None, :]
        cr, sr, cc_, sc_ = np.cos(ar), np.sin(ar), np.cos(ac), np.sin(ac)
        C = np.concatenate([cr, cr, cc_, cc_], axis=1).T.astype(np.float32)
        Sg = np.concatenate([-sr, sr, -sc_, sc_], axis=1).T.astype(np.float32)
        return C, Sg

    Cw, Sw = tab(64)
    ropeW = np.concatenate([np.tile(Cw, (2, 1)), np.tile(Sw, (2, 1))], axis=0)
    Cm, Sm = tab(32)
    ropeK = np.concatenate([Cm, Sm], axis=0)
    Cq = np.concatenate([np.ones((64, S_OWN), np.float32), Cm], axis=0)
    Sq = np.concatenate([np.zeros((64, S_OWN), np.float32), Sm], axis=0)
    ropeQ = np.concatenate([Cq, Sq], axis=0)
    return np.ascontiguousarray(ropeW), np.ascontiguousarray(ropeQ), np.ascontiguousarray(ropeK)


def _masks(half):
    kp = np.arange(128)[:, None]
    qp = np.arange(128)[None, :]
    mP = np.where(qp <= kp, 0.0, NEGM).astype(np.float32)
    mN = np.where(kp <= qp, 0.0, NEGM).astype(np.float32)
    neg = np.full((128, 128), NEGM, np.float32)
    kinds = [mP, mN, mP if half == 1 else neg, mN if half == 0 else neg]
    m = np.stack([np.concatenate([k, k], axis=1) for k in kinds], axis=1)
    return np.ascontiguousarray(m.reshape(128, 4 * 256))


def _fm(v):
    v = np.asarray(v, np.float32)
    lead = v.shape[:-1]
    n = v.shape[-1] // 128
    return np.moveaxis(v.reshape(lead + (n, 128)), -1, 0)


def _pack_vecs(inp, half, b):
    vec = np.zeros((128, NV), np.float32)

    def put(name, arr):
        arr = np.asarray(arr, np.float32).reshape(128, -1)
        w = dict(_VSPEC)[name]
        assert arr.shape[1] == w, (name, arr.shape, w)
        vec[:, VOFF[name]:VOFF[name] + w] = arr

    put("gf1", _fm(inp["g_ffn1"]))
    put("gmix", _fm(inp["g_mix"]))
    put("gf2", _fm(inp["g_ffn2"]))
    put("bmod", _fm(inp["b_mod"]))
    put("gfin", _fm(inp["g_final"]))
    gq = np.zeros((L, 256), np.float32)
    gq[:, :192] = inp["g_mla_q"]
    put("gq", _fm(gq))
    put("gkv", _fm(inp["g_mla_kv"]))
    wsc = np.asarray(inp["w_sc_conv"], np.float32)
    put("wsc", np.transpose(wsc.reshape(L, 3, 2, 128), (3, 0, 2, 1)))
    wcf = np.asarray(inp["w_cf_conv"], np.float32)
    put("wcf", np.transpose(wcf.reshape(L, 31, 2, 128), (3, 0, 2, 1)))
    put("bcf", _fm(inp["b_cf_conv"]))
    put("gln", _fm(inp["g_cf_ln"]))
    put("bln", _fm(inp["b_cf_ln"]))
    put("sink", np.broadcast_to(np.asarray(inp["wa_sink"], np.float32).reshape(1, L * 4), (128, L * 4)))
    put("lm", np.full((128, 1), 1.0 if half == 1 else 0.0, np.float32))
    put("rm", np.full((128, 1), 1.0 if half == 0 else 0.0, np.float32))
    put("eps", np.full((128, 1), EPS, np.float32))
    put("zero", np.zeros((128, 1), np.float32))
    bsel = np.zeros((128, 4), np.float32)
    bsel[:, b] = 1.0
    put("bsel", bsel)
    return vec


_NC_CACHE = {}


def kernel(**inputs):
    inp = {k: np.asarray(v) for k, v in inputs.items()}
    x = inp["x"].astype(np.float32, copy=False)
    ctx = inp["ctx"].astype(np.float32, copy=False)
    Bn = x.shape[0]
    key = STAGE[0]
    if key not in _NC_CACHE:
        _NC_CACHE[key] = build()
    nc = _NC_CACHE[key]
    shared = {n: np.ascontiguousarray(inp[n], dtype=np.float32) for n in
              ("w1_gate", "w1_up", "w1_down", "w2_gate", "w2_up", "w2_down", "w_in", "w_out", "w_mla_uq", "w_mla_ukv")}
    ident = np.eye(128, dtype=np.float32)
    in_maps = []
    for core in range(8):
        b, half = core // 2, core % 2
        rW, rQ, rK = _rope_tables(half)
        cc = np.stack([_fm(inp["c"][bb]) for bb in range(4)] + [_fm(inp["c_ctx"])], axis=-1).reshape(128, 40)
        nsh = MOD_RANKS[0]
        rk = core % nsh
        wsl = 9 * D // nsh
        m = {"xT": np.ascontiguousarray(x[b, half * S_OWN:(half + 1) * S_OWN, :].T),
             "ctxT": np.ascontiguousarray(ctx[b].T),
             "cc": np.ascontiguousarray(cc, dtype=np.float32),
             "vecs": _pack_vecs(inp, half, b),
             "ropeW": rW, "ropeQ": rQ, "ropeK": rK,
             "masks": _masks(half), "ident": ident}
        m["w_mod"] = np.ascontiguousarray(inp["w_mod"][:, :, rk * wsl:(rk + 1) * wsl], dtype=np.float32)
        m.update(shared)
        in_maps.append(m)
    res = run_bass_kernel_spmd(nc, in_maps, core_ids=list(range(8)))
    out = np.empty((Bn, 2 * S_OWN, D), np.float32)
    for core in range(8):
        b, half = core // 2, core % 2
        out[b, half * S_OWN:(half + 1) * S_OWN, :] = np.asarray(res.results[core]["outT"]).T
    return out
```

```python
import contextlib
import numpy as np
import concourse.bass as bass
import concourse.mybir as mybir
from concourse.bass_utils import run_bass_kernel_spmd

F32 = mybir.dt.float32
BF16 = mybir.dt.bfloat16
AF = mybir.ActivationFunctionType
ALU = mybir.AluOpType
ENGS = ("pe", "act", "dve", "pool", "sp")

L = 2
D = 1024
S_OWN = 4096
CTX = 256
T = 512
NT = S_OWN // T
DFF = 2816
NJ = DFF // 128
NWIN = 2560
ZW = 128 + S_OWN + 128
ZCW = 128 + CTX + 128
EPS = 1e-6
NZ = 14
ZQ, ZSCB, ZSCT, ZWAQ, ZWAK, ZWAV, ZCFU = 0, 4, 6, 8, 10, 11, 12
HALO_SLOTS = (6, 7, 10, 11, 12, 13)
NEGM = -30000.0

_VSPEC = [("gf1", L * 8), ("gmix", L * 8), ("gf2", L * 8), ("bmod", L * 72), ("gfin", 8), ("gq", L * 2),
          ("gkv", L), ("wsc", L * 6), ("wcf", L * 62), ("bcf", L * 2), ("gln", L * 2), ("bln", L * 2),
          ("sink", L * 4), ("lm", 1), ("rm", 1), ("eps", 1), ("zero", 1), ("bsel", 4)]
VOFF = {}
_o = 0
for _n, _w in _VSPEC:
    VOFF[_n] = _o
    _o += _w
NV = _o


class Op:
    __slots__ = ("eng", "fn", "deps", "signal", "count", "chan", "idx")

    def __init__(self, eng, fn, chan):
        self.idx = 0
        self.eng = eng
        self.fn = fn
        self.deps = set()
        self.signal = False
        self.count = 0
        self.chan = chan


class Chan:
    def __init__(self, name):
        self.name = name
        self.nobar = False
        self.sem = None
        self.n = 0
        self.last = None


class Prog:
    def __init__(self):
        self.ops = {e: [] for e in ENGS}
        self.last_w = {}
        self.readers = {}
        self.chans = []
        self.sticky = set()

    def chan(self, name):
        c = Chan(name)
        self.chans.append(c)
        return c

    def add(self, eng, fn, reads=(), writes=(), chan=None):
        op = Op(eng, fn, chan)
        deps = set()
        for k in reads:
            w = self.last_w.get(k)
            if w is not None:
                deps.add(w)
        for k in writes:
            w = self.last_w.get(k)
            if w is not None:
                deps.add(w)
            deps.update(self.readers.get(k, ()))
        if chan is not None:
            if chan.last is not None:
                deps.add(chan.last)
            chan.last = op
            chan.n += 1
            op.count = 16 * chan.n
            op.signal = True
        deps.discard(op)
        if eng == "pe":
            deps = {d for d in deps if not (d.eng == "pe" and d.chan is None)}
        best = {}
        for d in deps:
            k = ("c", id(d.chan)) if d.chan is not None else ("e", d.eng)
            if k not in best or best[k].idx < d.idx:
                best[k] = d
        deps = set(best.values())
        for d in deps:
            d.signal = True
        op.deps = deps
        op.idx = len(self.ops[eng]) if chan is None else chan.n
        for k in reads:
            self.readers.setdefault(k, []).append(op)
        for k in writes:
            self.last_w[k] = op
            self.readers[k] = []
        self.ops[eng].append(op)
        return op

    def barrier(self, final=False):
        lasts = []
        for e in ENGS:
            for op in reversed(self.ops[e]):
                if op.chan is None and op.fn is not None:
                    lasts.append(op)
                    break
        for c in self.chans:
            if c.last is not None and (final or not c.nobar):
                lasts.append(c.last)
        for e in ENGS:
            op = Op(e, None, None)
            op.deps = set(lasts)
            for d in op.deps:
                d.signal = True
            self.ops[e].append(op)
        self.last_w = {k: v for k, v in self.last_w.items() if k in self.sticky}
        self.readers = {}

    def emit(self, nc, stack):
        esem = {e: stack.enter_context(nc.semaphore("s_" + e)) for e in ENGS}
        for c in self.chans:
            if c.n > 0:
                c.sem = stack.enter_context(nc.semaphore("c_" + c.name))
        for e in ENGS:
            n = 0
            for op in self.ops[e]:
                if op.chan is None and op.signal:
                    n += 1
                    op.count = n
        block = stack.enter_context(nc.Block())

        def run(e, eng):
            waited = {}
            for op in self.ops[e]:
                need = {}
                for d in op.deps:
                    if d.chan is not None:
                        s, v = d.chan.sem, d.count
                    else:
                        s, v = esem[d.eng], d.count
                    key = id(s)
                    if key not in need or need[key][1] < v:
                        need[key] = (s, v)
                for key, (s, v) in need.items():
                    if waited.get(key, 0) < v:
                        eng.wait_ge(s, v)
                        waited[key] = v
                if op.fn is None:
                    continue
                ins = op.fn(eng)
                if op.chan is not None:
                    ins.then_inc(op.chan.sem, 16)
                elif op.signal:
                    ins.then_inc(esem[e], 1)

        @block.tensor
        def _(eng):
            run("pe", eng)

        @block.scalar
        def _(eng):
            run("act", eng)

        @block.vector
        def _(eng):
            run("dve", eng)

        @block.gpsimd
        def _(eng):
            run("pool", eng)

        @block.sync
        def _(eng):
            run("sp", eng)


def build(dbg=None):
    nc = bass.Bass("TRN2", target_bir_lowering=False)
    P = Prog()
    st = contextlib.ExitStack()

    def din(name, shape):
        return nc.dram_tensor(name, list(shape), F32, kind="ExternalInput").ap()

    def dscr(name, shape, dt):
        return nc.dram_tensor(name, list(shape), dt).ap()

    xT = din("xT", [D, S_OWN])
    ctxT = din("ctxT", [D, CTX])
    cc_in = din("cc", [128, 40])
    vecs_in = din("vecs", [128, NV])
    ropeW = din("ropeW", [256, S_OWN])
    ropeQ = din("ropeQ", [192, S_OWN])
    ropeK = din("ropeK", [64, S_OWN])
    masks_in = din("masks", [128, 4 * 256])
    ident_in = din("ident", [128, 128])
    w_mod = din("w_mod", [L, D, 9 * D // MOD_RANKS[0]])
    wsrc = {n: din(n, s) for n, s in [
        ("w1_gate", [L, D, DFF]), ("w1_up", [L, D, DFF]), ("w1_down", [L, DFF, D]),
        ("w2_gate", [L, D, DFF]), ("w2_up", [L, D, DFF]), ("w2_down", [L, DFF, D]),
        ("w_in", [L, D, 2144]), ("w_out", [L, D, D]), ("w_mla_uq", [L, 192, 384]), ("w_mla_ukv", [L, 128, 512])]}
    outT = nc.dram_tensor("outT", [D, S_OWN], F32, kind="ExternalOutput").ap()
    dbg_out = None
    if dbg:
        dbg_out = nc.dram_tensor("dbg", list(dbg), F32, kind="ExternalOutput").ap()

    hbuf = dscr("hbuf", [D, S_OWN], F32)
    hcbuf = dscr("hcbuf", [D, CTX], F32)
    zbuf = dscr("zbuf", [NZ * 128, ZW], BF16)
    zcbuf = dscr("zcbuf", [NZ * 128, ZCW], BF16)
    xin_mla = dscr("xin_mla", [160, S_OWN], BF16)
    xout_mla = dscr("xout_mla", [320, S_OWN], BF16)
    ckvc = dscr("ckvc", [160, CTX], BF16)
    xin_halo = dscr("xin_halo", [768, 256], BF16)
    xout_halo = dscr("xout_halo", [1536, 256], BF16)
    wb = {}
    for l in range(L):
        for n in ("w1_gate", "w1_up", "w2_gate", "w2_up"):
            wb[n, l] = dscr(f"b_{n}{l}", [D, DFF], BF16)
        for n in ("w1_down", "w2_down"):
            wb[n, l] = dscr(f"b_{n}{l}", [DFF, D], BF16)
        wb["w_in", l] = dscr(f"b_w_in{l}", [D, NWIN], BF16)
        wb["w_out", l] = dscr(f"b_w_out{l}", [D, D], BF16)
        wb["w_mla_uq", l] = dscr(f"b_wuq{l}", [192, 768], BF16)
        wb["w_mla_ukv", l] = dscr(f"b_wukv{l}", [128, 512], BF16)

    ARENA = 53000
    arena = st.enter_context(nc.sbuf_tensor("arena", [128, ARENA], F32))
    bump = [0]

    def alloc(cols, dt=F32, shape=None):
        words = cols if dt == F32 else (cols + 1) // 2
        a = bump[0]
        bump[0] += words
        assert bump[0] <= ARENA, ("SBUF arena overflow", bump[0])
        ap = arena[:, a:a + words]
        if dt != F32:
            ap = ap.bitcast(dt)[:, 0:cols]
        if shape is not None:
            names = " ".join(f"d{i}" for i in range(len(shape)))
            kw = {f"d{i}": s for i, s in enumerate(shape)}
            ap = ap.rearrange(f"p ({names}) -> p {names}", **kw)
        return ap

    psw = [st.enter_context(nc.psum_tensor(f"psw{i}", [128, 1024], F32)) for i in range(4)]
    ps = []
    for i in range(4):
        ps.append(psw[i][:, 0:512])
        ps.append(psw[i][:, 512:1024])

    def PSK(i):
        return ("ps", i)

    vecs = alloc(NV)
    ccs = alloc(40)
    modv = alloc(L * 144, shape=[L, 72, 2])
    Acoef = alloc(L * 2 * 3 * 8, shape=[L, 2, 3, 8])
    HG = alloc(L * 2 * 3 * 8, shape=[L, 2, 3, 8])
    esink = alloc(L * 4, shape=[L, 4])
    ones_b = alloc(128, BF16)
    ones_f = alloc(128)
    ident_b = alloc(128, BF16)
    masks_b = alloc(4 * 256, BF16, shape=[4, 256])
    persist_end = bump[0]

    def V(name, off=0, w=1):
        o = VOFF[name] + off
        return vecs[:, o:o + w]

    chn = {}

    def CH(name):
        if name not in chn:
            chn[name] = P.chan(name)
        return chn[name]

    def dma(q, out, in_, reads, writes, ch):
        return P.add(q, lambda e: e.dma_start(out=out, in_=in_), reads=reads, writes=writes, chan=CH(ch))

    dma("sp", vecs, vecs_in, [], ["vecs"], "ld0")
    dma("sp", ccs, cc_in, [], ["ccs"], "ld1")
    P.sticky.update(["ident", "masks"])
    dma("pool", ident_b, ident_in, [], ["ident"], "cv0")
    dma("pool", masks_b.rearrange("p a b -> p (a b)"), masks_in, [], ["masks"], "cv1")
    P.add("dve", lambda e: e.memset(ones_b, 1.0), writes=["ones_b"])
    P.add("dve", lambda e: e.memset(ones_f, 1.0), writes=["ones_f"])
    P.add("act", lambda e: e.activation(out=ccs, in_=ccs, func=AF.Silu), reads=["ccs"], writes=["ccs"])
    P.add("act", lambda e: e.activation(out=esink.rearrange("p a b -> p (a b)"), in_=V("sink", 0, L * 4), func=AF.Exp),
          reads=["vecs"], writes=["esink"])

    cvn = [0]
    CVKEYS = {}

    def conv(out, in_, key):
        cvn[0] += 1
        key = (key, cvn[0])
        CVKEYS.setdefault(key[0], []).append(key)
        P.sticky.add(key)
        op = dma("pool", out, in_, [], [key], f"cv{cvn[0] % 8}")
        op.chan.nobar = True

    def conv_rows(name, l, nrows, dst=None, c0=0, c1=None, d0=0):
        src = wsrc[name]
        c1 = c1 if c1 is not None else src.shape[2]
        dstap = wb[name, l] if dst is None else dst
        for r in range(0, nrows, 128):
            rr = min(128, nrows - r)
            conv(dstap[r:r + rr, d0:d0 + (c1 - c0)], src[l, r:r + rr, c0:c1], (name, l, r // 128))

    def conv_layer_ffn(l, which):
        conv_rows(f"w{which}_gate", l, D)
        conv_rows(f"w{which}_up", l, D)
        conv_rows(f"w{which}_down", l, DFF)

    def conv_layer_mix(l):
        src = wsrc["w_in"]
        dst = wb["w_in", l]
        segs = [(0, 1120, 0), (1120, 1184, 1120), (1248, 1312, 1184), (1184, 1248, 1248), (1312, 1376, 1312),
                (1376, 2144, 1376)]
        for b8, s8 in enumerate([8, 0, 24, 16]):
            segs.append((320 + s8, 320 + s8 + 8, 2144 + 8 * b8))
        for hi, hsrc in enumerate([1120, 1248, 1184, 1312, 1376, 1440]):
            for b16, s16 in enumerate([16, 0, 48, 32]):
                segs.append((hsrc + s16, hsrc + s16 + 16, 2176 + 64 * hi + 16 * b16))
        for (c0, c1, d0) in segs:
            for r in range(0, D, 512):
                conv(dst[r:r + 512, d0:d0 + (c1 - c0)], src[l, r:r + 512, c0:c1], ("w_in", l))
        conv_rows("w_out", l, D)
        srcq = wsrc["w_mla_uq"]
        dq = wb["w_mla_uq", l]
        conv(dq[0:128, 0:384], srcq[l, 0:128, :], ("w_mla_uq", l))
        conv(dq[128:192, 0:384], srcq[l, 128:192, :], ("w_mla_uq", l))
        for h in range(4):
            conv(dq[0:192, 384 + h * 96:384 + h * 96 + 64], srcq[l, :, h * 96:h * 96 + 64], ("w_mla_uq", l))
            for b8, s8 in enumerate([8, 0, 24, 16]):
                conv(dq[0:192, 384 + h * 96 + 64 + 8 * b8:384 + h * 96 + 64 + 8 * b8 + 8],
                     srcq[l, :, h * 96 + 64 + s8:h * 96 + 64 + s8 + 8], ("w_mla_uq", l))
        conv(wb["w_mla_ukv", l][:, :], wsrc["w_mla_ukv"][l, :, :], ("w_mla_ukv", l))

    conv_rows("w1_gate", 0, D)
    conv_rows("w1_up", 0, D)

    def WK(name, l):
        if name == "w_in" or name.startswith("w_mla"):
            base = [(name, l)]
        else:
            n = DFF if name.endswith("down") else D
            base = [(name, l, r) for r in range((n + 127) // 128)]
        out = []
        for bkey in base:
            out.extend(CVKEYS.get(bkey, []))
        return out

    def setup_mod():
        base = bump[0]
        nsh = MOD_RANKS[0]
        nq = 72 // nsh
        slab = alloc(8 * nq * 128, shape=[8, nq * 128])
        part = alloc(L * nq * 5)
        modall = alloc(nsh * L * nq * 5, shape=[nsh, L * nq, 5])
        modx_in = dscr("modx_in", [128, L * nq * 5], F32)
        modx_out = dscr("modx_out", [nsh * 128, L * nq * 5], F32)
        for l in range(L):
            for kc in range(8):
                dma("sp", slab[:, kc, :], w_mod[l, kc * 128:(kc + 1) * 128, :], [], [("wm", kc)], f"wm{kc % 2}")
            for oc in range(nq):
                col = (l * nq + oc) * 5
                for kc in range(8):
                    P.add("pe", lambda e, kc=kc, oc=oc, col=col: e.matmul(
                        ps[0][:, col:col + 5], lhsT=slab[:, kc, oc * 128:(oc + 1) * 128],
                        rhs=ccs[:, kc * 5:kc * 5 + 5], start=(kc == 0), stop=(kc == 7)),
                        reads=[("wm", kc), "ccs"], writes=[PSK(0)])
        P.add("act", lambda e: e.activation(out=part, in_=ps[0][:, 0:L * nq * 5], func=AF.Copy), reads=[PSK(0)], writes=["part"])
        dma("sp", modx_in, part, ["part"], ["modx_in"], "ld0")
        rg = MOD_GROUPS[0]
        P.add("pool", lambda e: e.collective_compute("AllGather", ALU.bypass, replica_groups=rg, ins=[modx_in.opt()], outs=[modx_out.opt()]),
              reads=["modx_in"], writes=["modx_out"]).signal = True
        dma("sp", modall.rearrange("p r q f -> p r (q f)"), modx_out.rearrange("(r p) c -> p r c", p=128), ["modx_out"], ["modall"], "ld1")
        for l in range(L):
            mv = modv[:, l, :, :].rearrange("p (r q) j -> p r q j", r=nsh)
            src = modall[:, :, l * nq:(l + 1) * nq, :]
            P.add("dve", lambda e, mv=mv, src=src: e.tensor_copy(out=mv[:, :, :, 1], in_=src[:, :, :, 4]), reads=["modall"], writes=["modv"])
            P.add("dve", lambda e, mv=mv, src=src: e.tensor_scalar(out=mv[:, :, :, 0], in0=src[:, :, :, 0], scalar1=V("bsel", 0, 1), scalar2=None, op0=ALU.mult),
                  reads=["modall", "vecs"], writes=["modv"])
            for bb in range(1, 4):
                P.add("dve", lambda e, mv=mv, src=src, bb=bb: e.scalar_tensor_tensor(out=mv[:, :, :, 0], in0=src[:, :, :, bb], scalar=V("bsel", bb, 1),
                                                                                   in1=mv[:, :, :, 0], op0=ALU.mult, op1=ALU.add),
                      reads=["modall", "vecs", "modv"], writes=["modv"])
            for j in range(2):
                P.add("dve", lambda e, l=l, j=j: e.tensor_tensor(out=modv[:, l, :, j], in0=modv[:, l, :, j], in1=V("bmod", l * 72, 72), op=ALU.add),
                      reads=["modv", "vecs"], writes=["modv"])
            for j in range(2):
                for s_, gname in enumerate(("gf1", "gmix", "gf2")):
                    i_scale = 3 * s_ + 1
                    P.add("dve", lambda e, l=l, j=j, s_=s_, gname=gname, i_scale=i_scale: e.scalar_tensor_tensor(
                        out=Acoef[:, l, j, s_, :], in0=modv[:, l, i_scale * 8:(i_scale + 1) * 8, j], scalar=1.0,
                        in1=V(gname, l * 8, 8), op0=ALU.add, op1=ALU.mult), reads=["modv", "vecs"], writes=["coef"])
                    i_gate = 3 * s_ + 2
                    P.add("dve", lambda e, l=l, j=j, s_=s_, i_gate=i_gate: e.tensor_scalar(
                        out=HG[:, l, j, s_, :], in0=modv[:, l, i_gate * 8:(i_gate + 1) * 8, j],
                        scalar1=(1.0 if s_ == 1 else 0.5), scalar2=None, op0=ALU.mult), reads=["modv"], writes=["coef"])
        P.barrier()
        bump[0] = base

    setup_mod()
    conv_rows("w1_down", 0, DFF)
    conv_layer_mix(0)
    conv_layer_ffn(0, 2)
    conv_layer_ffn(1, 1)
    conv_layer_mix(1)
    conv_layer_ffn(1, 2)

    def Bsh(l, j, s):
        return modv[:, l, (3 * s) * 8:(3 * s + 1) * 8, j]

    class FFNBufs:
        pass

    def alloc_common(Tm):
        B = FFNBufs()
        B.ht = alloc(8 * Tm, shape=[8, Tm])
        B.xn = alloc(8 * Tm, BF16, shape=[8, Tm])
        B.sq = [alloc(Tm, BF16) for _ in range(2)]
        B.rstd = alloc(Tm)
        B.tmp = [alloc(Tm) for _ in range(2)]
        return B

    def norm_mod(B, Tn, Avec, Bvec, tag):
        for kc in range(8):
            P.add("act", lambda e, kc=kc: e.activation(out=B.sq[kc % 2][:, 0:Tn], in_=B.ht[:, kc, 0:Tn], func=AF.Square),
                  reads=[("ht", kc)], writes=[("sq", kc % 2)])
            P.add("pe", lambda e, kc=kc: e.matmul(ps[0][:, 0:Tn], lhsT=ones_b, rhs=B.sq[kc % 2][:, 0:Tn],
                                                 start=(kc == 0), stop=(kc == 7)),
                  reads=[("sq", kc % 2), "ones_b"], writes=[PSK(0)])
        P.add("act", lambda e: e.activation(out=B.rstd[:, 0:Tn], in_=ps[0][:, 0:Tn], func=AF.Sqrt, bias=V("eps"), scale=1.0 / D),
              reads=[PSK(0), "vecs"], writes=["rstd"])
        P.add("dve", lambda e: e.reciprocal(out=B.rstd[:, 0:Tn], in_=B.rstd[:, 0:Tn]), reads=["rstd"], writes=["rstd"])
        for kc in range(8):
            P.add("dve", lambda e, kc=kc: e.tensor_tensor(out=B.tmp[kc % 2][:, 0:Tn], in0=B.ht[:, kc, 0:Tn], in1=B.rstd[:, 0:Tn],
                                                         op=ALU.mult), reads=[("ht", kc), "rstd"], writes=[("tmp", kc % 2)])
            bias = Bvec[:, kc:kc + 1] if Bvec is not None else V("zero")
            P.add("act", lambda e, kc=kc, bias=bias: e.activation(out=B.xn[:, kc, 0:Tn], in_=B.tmp[kc % 2][:, 0:Tn], func=AF.Identity,
                                                                  bias=bias, scale=Avec[:, kc:kc + 1]),
                  reads=[("tmp", kc % 2), "coef", "modv", "vecs"], writes=[("xn", kc)])

    JG = 2
    NG = (NJ + JG - 1) // JG

    def alloc_ffn(B, Tm):
        B.H = alloc(NJ * Tm, BF16, shape=[NJ, Tm])
        B.wg = [alloc(8 * JG * 128, BF16, shape=[8, JG * 128]) for _ in range(2)]
        B.wu = [alloc(8 * JG * 128, BF16, shape=[8, JG * 128]) for _ in range(2)]
        B.wd = alloc(NJ * D, BF16, shape=[NJ, D])
        B.sg = [alloc(Tm) for _ in range(2)]

    gcount = [0]

    def ffn(B, Tn, l, which, hg):
        wgd, wud, wdd = wb[f"w{which}_gate", l], wb[f"w{which}_up", l], wb[f"w{which}_down", l]
        kg, ku, kd = WK(f"w{which}_gate", l), WK(f"w{which}_up", l), WK(f"w{which}_down", l)

        def load_group(g):
            slot = (gcount[0] + g) % 2
            j0 = g * JG
            nj = min(JG, NJ - j0)
            dma("sp", B.wg[slot][:, :, 0:nj * 128], wgd[:, j0 * 128:(j0 + nj) * 128].rearrange("(k p) m -> p k m", p=128),
                kg, [("wg", slot)], f"wg{slot}")
            dma("sp", B.wu[slot][:, :, 0:nj * 128], wud[:, j0 * 128:(j0 + nj) * 128].rearrange("(k p) m -> p k m", p=128),
                ku, [("wu", slot)], f"wu{slot}")

        load_group(0)
        for g in range(NG):
            slot = (gcount[0] + g) % 2
            j0 = g * JG
            nj = min(JG, NJ - j0)
            if g + 1 < NG:
                load_group(g + 1)
            dma("sp", B.wd[:, j0:j0 + nj, :], wdd[j0 * 128:(j0 + nj) * 128, :].rearrange("(j p) m -> p j m", p=128),
                kd, [("wd", g)], "wd")
            for jj in range(nj):
                j = j0 + jj
                bg, bu = 1 + (j % 2), 3 + (j % 2)
                for kc in range(8):
                    P.add("pe", lambda e, kc=kc, jj=jj, slot=slot, bg=bg: e.matmul(
                        ps[bg][:, 0:Tn], lhsT=B.wg[slot][:, kc, jj * 128:(jj + 1) * 128], rhs=B.xn[:, kc, 0:Tn],
                        start=(kc == 0), stop=(kc == 7)), reads=[("wg", slot), ("xn", kc)], writes=[PSK(bg)])
                for kc in range(8):
                    P.add("pe", lambda e, kc=kc, jj=jj, slot=slot, bu=bu: e.matmul(
                        ps[bu][:, 0:Tn], lhsT=B.wu[slot][:, kc, jj * 128:(jj + 1) * 128], rhs=B.xn[:, kc, 0:Tn],
                        start=(kc == 0), stop=(kc == 7)), reads=[("wu", slot), ("xn", kc)], writes=[PSK(bu)])
                P.add("act", lambda e, j=j, bg=bg: e.activation(out=B.sg[j % 2][:, 0:Tn], in_=ps[bg][:, 0:Tn], func=AF.Silu),
                      reads=[PSK(bg)], writes=[("sg", j % 2)])
                P.add("dve", lambda e, j=j, bu=bu: e.tensor_tensor(out=B.H[:, j, 0:Tn], in0=ps[bu][:, 0:Tn], in1=B.sg[j % 2][:, 0:Tn],
                                                                  op=ALU.mult), reads=[PSK(bu), ("sg", j % 2)], writes=[("H", j)])
        gcount[0] += NG
        for c in range(8):
            bd = 5 + (c % 2)
            for j in range(NJ):
                P.add("pe", lambda e, c=c, j=j, bd=bd: e.matmul(ps[bd][:, 0:Tn], lhsT=B.wd[:, j, c * 128:(c + 1) * 128],
                                                               rhs=B.H[:, j, 0:Tn], start=(j == 0), stop=(j == NJ - 1)),
                      reads=[("wd", j // JG), ("H", j)], writes=[PSK(bd)])
            P.add("dve", lambda e, c=c, bd=bd: e.scalar_tensor_tensor(out=B.ht[:, c, 0:Tn], in0=ps[bd][:, 0:Tn], scalar=hg[:, c:c + 1],
                                                                     in1=B.ht[:, c, 0:Tn], op0=ALU.mult, op1=ALU.add),
                  reads=[PSK(bd), ("ht", c), "coef"], writes=[("ht", c)])

    def alloc_win(B, Tm):
        B.win = alloc(8 * NWIN, BF16, shape=[8, NWIN])
        B.wuq = alloc(2 * 768, BF16, shape=[2, 768])
        B.zst = alloc(NZ * Tm, BF16, shape=[NZ, Tm])
        P.add("dve", lambda e: e.memset(B.zst, 0.0), writes=[("zst", s_) for s_ in range(NZ)])
        B.mst = alloc(2 * Tm, BF16, shape=[2, Tm])
        B.cqn = alloc(2 * Tm, BF16, shape=[2, Tm])
        B.f1 = [alloc(Tm) for _ in range(3)]
        B.rW = alloc(2 * Tm, shape=[2, Tm])
        B.rQ = alloc(2 * Tm, shape=[2, Tm])
        B.rK = alloc(2 * Tm, shape=[2, Tm])


    rr = [0]

    def nb():
        rr[0] = rr[0] % 7 + 1
        return rr[0]

    def win_proj(B, Tn, l, rope, tok0, zdst, zcol0, mdst, mcol0, halo):
        if rope:
            dma("sp", B.rW[:, :, 0:Tn], ropeW.rearrange("(a p) c -> p a c", p=128)[:, :, tok0:tok0 + Tn], [], ["rW"], "rW")
            dma("sp", B.rQ[0:96, :, 0:Tn], ropeQ.rearrange("(a p) c -> p a c", p=96)[:, :, tok0:tok0 + Tn], [], ["rQ"], "rQ")
            dma("sp", B.rK[0:32, :, 0:Tn], ropeK.rearrange("(a p) c -> p a c", p=32)[:, :, tok0:tok0 + Tn], [], ["rK"], "rK")

        def group(M, col0):
            b = nb()
            for kc in range(8):
                P.add("pe", lambda e, kc=kc, b=b: e.matmul(ps[b][0:M, 0:Tn], lhsT=B.win[:, kc, col0:col0 + M], rhs=B.xn[:, kc, 0:Tn],
                                                          start=(kc == 0), stop=(kc == 7)),
                      reads=["win", ("xn", kc)], writes=[PSK(b)])
            return b

        fi = [0]

        def ftmp():
            fi[0] = (fi[0] + 1) % 3
            return fi[0]

        def copy_out(b, M, dst, key):
            P.add("act", lambda e: e.activation(out=dst, in_=ps[b][0:M, 0:Tn], func=AF.Copy), reads=[PSK(b)], writes=[key])

        def rope_out(bA, bB, M, tab, tabkey, dst, key):
            i0, i1 = ftmp(), ftmp()
            P.add("dve", lambda e: e.tensor_tensor(out=B.f1[i0][0:M, 0:Tn], in0=ps[bA][0:M, 0:Tn], in1=tab[0:M, 0, 0:Tn], op=ALU.mult),
                  reads=[PSK(bA), tabkey], writes=[("f1", i0)])
            P.add("dve", lambda e: e.tensor_tensor(out=B.f1[i1][0:M, 0:Tn], in0=ps[bB][0:M, 0:Tn], in1=tab[0:M, 1, 0:Tn], op=ALU.mult),
                  reads=[PSK(bB), tabkey], writes=[("f1", i1)])
            P.add("dve", lambda e: e.tensor_tensor(out=dst, in0=B.f1[i0][0:M, 0:Tn], in1=B.f1[i1][0:M, 0:Tn], op=ALU.add),
                  reads=[("f1", i0), ("f1", i1)], writes=[key])

        def rstd_from(bank_ss, n):
            P.add("act", lambda e: e.activation(out=B.rstd[:, 0:Tn], in_=ps[bank_ss][:, 0:Tn], func=AF.Sqrt, bias=V("eps"), scale=1.0 / n),
                  reads=[PSK(bank_ss), "vecs"], writes=["rstd"])
            P.add("dve", lambda e: e.reciprocal(out=B.rstd[:, 0:Tn], in_=B.rstd[:, 0:Tn]), reads=["rstd"], writes=["rstd"])

        b0 = group(128, 0)
        b1 = group(64, 128)
        P.add("act", lambda e: e.activation(out=B.sq[0][:, 0:Tn], in_=ps[b0][:, 0:Tn], func=AF.Square), reads=[PSK(b0)], writes=[("sq", 0)])
        P.add("act", lambda e: e.activation(out=B.sq[1][0:64, 0:Tn], in_=ps[b1][0:64, 0:Tn], func=AF.Square), reads=[PSK(b1)], writes=[("sq", 1)])
        P.add("pe", lambda e: e.matmul(ps[0][:, 0:Tn], lhsT=ones_b, rhs=B.sq[0][:, 0:Tn], start=True, stop=False),
              reads=[("sq", 0), "ones_b"], writes=[PSK(0)])
        P.add("pe", lambda e: e.matmul(ps[0][:, 0:Tn], lhsT=ones_b[0:64, :], rhs=B.sq[1][0:64, 0:Tn], start=False, stop=True),
              reads=[("sq", 1), "ones_b"], writes=[PSK(0)])
        rstd_from(0, 192)
        P.add("dve", lambda e: e.scalar_tensor_tensor(out=B.cqn[:, 0, 0:Tn], in0=ps[b0][:, 0:Tn], scalar=V("gq", l * 2, 1), in1=B.rstd[:, 0:Tn],
                                                     op0=ALU.mult, op1=ALU.mult), reads=[PSK(b0), "rstd", "vecs"], writes=[("cqn", 0)])
        P.add("dve", lambda e: e.scalar_tensor_tensor(out=B.cqn[0:64, 1, 0:Tn], in0=ps[b1][0:64, 0:Tn], scalar=V("gq", l * 2 + 1, 1)[0:64, :],
                                                     in1=B.rstd[0:64, 0:Tn], op0=ALU.mult, op1=ALU.mult),
              reads=[PSK(b1), "rstd", "vecs"], writes=[("cqn", 1)])
        for h in range(4):
            banks = []
            for rot in ([0, 1] if rope else [0]):
                b = nb()
                c0 = rot * 384 + h * 96
                P.add("pe", lambda e, b=b, c0=c0: e.matmul(ps[b][0:96, 0:Tn], lhsT=B.wuq[:, 0, c0:c0 + 96], rhs=B.cqn[:, 0, 0:Tn], start=True, stop=False),
                      reads=["wuq", ("cqn", 0)], writes=[PSK(b)])
                P.add("pe", lambda e, b=b, c0=c0: e.matmul(ps[b][0:96, 0:Tn], lhsT=B.wuq[0:64, 1, c0:c0 + 96], rhs=B.cqn[0:64, 1, 0:Tn], start=False, stop=True),
                      reads=["wuq", ("cqn", 1)], writes=[PSK(b)])
                banks.append(b)
            if rope:
                rope_out(banks[0], banks[1], 96, B.rQ, "rQ", B.zst[0:96, ZQ + h, 0:Tn], ("zst", ZQ + h))
            else:
                copy_out(banks[0], 96, B.zst[0:96, ZQ + h, 0:Tn], ("zst", ZQ + h))
        bkv = group(128, 192)
        P.add("act", lambda e: e.activation(out=B.sq[0][:, 0:Tn], in_=ps[bkv][:, 0:Tn], func=AF.Square), reads=[PSK(bkv)], writes=[("sq", 0)])
        P.add("pe", lambda e: e.matmul(ps[0][:, 0:Tn], lhsT=ones_b, rhs=B.sq[0][:, 0:Tn], start=True, stop=True),
              reads=[("sq", 0), "ones_b"], writes=[PSK(0)])
        rstd_from(0, 128)
        P.add("dve", lambda e: e.scalar_tensor_tensor(out=B.mst[:, 0, 0:Tn], in0=ps[bkv][:, 0:Tn], scalar=V("gkv", l, 1), in1=B.rstd[:, 0:Tn],
                                                     op0=ALU.mult, op1=ALU.mult), reads=[PSK(bkv), "rstd", "vecs"], writes=[("mst", 0)])
        bA = group(32, 320)
        if rope:
            bB = group(32, 2144)
            rope_out(bA, bB, 32, B.rK, "rK", B.mst[0:32, 1, 0:Tn], ("mst", 1))
        else:
            copy_out(bA, 32, B.mst[0:32, 1, 0:Tn], ("mst", 1))
        for c in range(2):
            b = group(128, 352 + c * 128)
            copy_out(b, 128, B.zst[:, ZSCB + c, 0:Tn], ("zst", ZSCB + c))
        for c in range(2):
            bc = group(128, 608 + c * 128)
            bx = group(128, 864 + c * 128)
            i = ftmp()
            P.add("act", lambda e, i=i, bx=bx: e.activation(out=B.f1[i][:, 0:Tn], in_=ps[bx][:, 0:Tn], func=AF.Copy), reads=[PSK(bx)], writes=[("f1", i)])
            P.add("dve", lambda e, i=i, bc=bc, c=c: e.tensor_tensor(out=B.zst[:, ZSCT + c, 0:Tn], in0=ps[bc][:, 0:Tn], in1=B.f1[i][:, 0:Tn], op=ALU.mult),
                  reads=[PSK(bc), ("f1", i)], writes=[("zst", ZSCT + c)])
        for c in range(2):
            bA = group(128, 1120 + c * 128)
            if rope:
                bB = group(128, 2176 + c * 128)
                rope_out(bA, bB, 128, B.rW, "rW", B.zst[:, ZWAQ + c, 0:Tn], ("zst", ZWAQ + c))
            else:
                copy_out(bA, 128, B.zst[:, ZWAQ + c, 0:Tn], ("zst", ZWAQ + c))
        bA = group(128, 1376)
        if rope:
            bB = group(128, 2432)
            rope_out(bA, bB, 128, B.rW, "rW", B.zst[:, ZWAK, 0:Tn], ("zst", ZWAK))
        else:
            copy_out(bA, 128, B.zst[:, ZWAK, 0:Tn], ("zst", ZWAK))
        b = group(128, 1504)
        copy_out(b, 128, B.zst[:, ZWAV, 0:Tn], ("zst", ZWAV))
        for c in range(2):
            ba = group(128, 1632 + c * 128)
            bg = group(128, 1888 + c * 128)
            i = ftmp()
            P.add("act", lambda e, i=i, bg=bg: e.activation(out=B.f1[i][:, 0:Tn], in_=ps[bg][:, 0:Tn], func=AF.Sigmoid), reads=[PSK(bg)], writes=[("f1", i)])
            P.add("dve", lambda e, i=i, ba=ba, c=c: e.tensor_tensor(out=B.zst[:, ZCFU + c, 0:Tn], in0=ps[ba][:, 0:Tn], in1=B.f1[i][:, 0:Tn], op=ALU.mult),
                  reads=[PSK(ba), ("f1", i)], writes=[("zst", ZCFU + c)])
        zk = [("zst", s) for s in range(NZ)]
        dma("sp", zdst.rearrange("(s p) c -> p s c", p=128)[:, :, zcol0:zcol0 + Tn], B.zst[:, :, 0:Tn], zk, ["zdst"], "zst")
        dma("sp", mdst[0:128, mcol0:mcol0 + Tn], B.mst[:, 0, 0:Tn], [("mst", 0)], ["mdst"], "mst0")
        dma("sp", mdst[128:160, mcol0:mcol0 + Tn], B.mst[0:32, 1, 0:Tn], [("mst", 1)], ["mdst"], "mst1")
        for (which, c0) in halo:
            xh = xin_halo.rearrange("(s p) c -> p s c", p=128)
            dma("sp", xh[:, 0:2, which * 128:(which + 1) * 128], B.zst[:, 6:8, c0:c0 + 128], zk, ["xin_halo"], "hal0")
            dma("sp", xh[:, 2:6, which * 128:(which + 1) * 128], B.zst[:, 10:14, c0:c0 + 128], zk, ["xin_halo"], "hal1")

    def load_win(B, l):
        dma("sp", B.win[:, 0:4, :], wb["w_in", l][0:512, :].rearrange("(k p) m -> p k m", p=128), WK("w_in", l), ["win"], "win0")
        dma("sp", B.win[:, 4:8, :], wb["w_in", l][512:1024, :].rearrange("(k p) m -> p k m", p=128), WK("w_in", l), ["win"], "win1")
        dma("sp", B.wuq[:, 0, :], wb["w_mla_uq", l][0:128, :], WK("w_mla_uq", l), ["wuq"], "wuq0")
        dma("sp", B.wuq[0:64, 1, :], wb["w_mla_uq", l][128:192, :], WK("w_mla_uq", l), ["wuq"], "wuq1")

    def load_h(B, src, col0, Tn):
        dma("sp", B.ht[:, :, 0:Tn], src.rearrange("(k p) c -> p k c", p=128)[:, :, col0:col0 + Tn], ["hsrc"], [("ht", k) for k in range(8)], "hld")

    def store_h(B, dst, col0, Tn, key="hdst"):
        dma("sp", dst.rearrange("(k p) c -> p k c", p=128)[:, :, col0:col0 + Tn], B.ht[:, :, 0:Tn], [("ht", k) for k in range(8)], [key], "hst")

    def tiles_lat_ctx():
        out = [(False, 0, T, t * T) for t in range(NT)]
        out.append((True, 1, CTX, 0))
        return out

    def phase_ffn(l_prev, l_next, first):
        base = bump[0]
        B = alloc_common(T)
        alloc_ffn(B, T)
        if l_next is not None:
            alloc_win(B, T)
            load_win(B, l_next)
        for (is_ctx, j, Tn, tok0) in tiles_lat_ctx():
            if first:
                load_h(B, ctxT if is_ctx else xT, tok0, Tn)
            else:
                if is_ctx and l_next is None:
                    continue
                load_h(B, hcbuf if is_ctx else hbuf, tok0, Tn)
            if l_prev is not None:
                norm_mod(B, Tn, Acoef[:, l_prev, j, 2, :], Bsh(l_prev, j, 2), "f2")
                ffn(B, Tn, l_prev, 2, HG[:, l_prev, j, 2, :])
            if l_next is not None:
                norm_mod(B, Tn, Acoef[:, l_next, j, 0, :], Bsh(l_next, j, 0), "f1")
                ffn(B, Tn, l_next, 1, HG[:, l_next, j, 0, :])
                store_h(B, hcbuf if is_ctx else hbuf, tok0, Tn)
                norm_mod(B, Tn, Acoef[:, l_next, j, 1, :], Bsh(l_next, j, 1), "mx")
                halo = []
                if not is_ctx and tok0 == 0:
                    halo.append((0, 0))
                if not is_ctx and tok0 == S_OWN - T:
                    halo.append((1, T - 128))
                win_proj(B, Tn, l_next, not is_ctx, tok0, zcbuf if is_ctx else zbuf, 128 + tok0,
                         ckvc if is_ctx else xin_mla, tok0, halo)
            else:
                norm_mod(B, Tn, V("gfin", 0, 8), None, "fin")
                for kc in range(8):
                    P.add("dve", lambda e, kc=kc, Tn=Tn: e.tensor_tensor(out=B.tmp[kc % 2][:, 0:Tn], in0=B.ht[:, kc, 0:Tn], in1=B.rstd[:, 0:Tn], op=ALU.mult),
                          reads=[("ht", kc), "rstd"], writes=[("tmp", kc % 2)])
                    P.add("dve", lambda e, kc=kc, Tn=Tn: e.tensor_scalar(out=B.ht[:, kc, 0:Tn], in0=B.tmp[kc % 2][:, 0:Tn], scalar1=V("gfin", kc, 1), scalar2=None,
                                                                 op0=ALU.mult), reads=[("tmp", kc % 2), "vecs"], writes=[("ht", kc)])
                store_h(B, outT, tok0, Tn, key="out")
        P.barrier()
        bump[0] = base

    RG = RG_OVERRIDE[0] or [[0, 1], [2, 3], [4, 5], [6, 7]]

    def exchange(l):
        base = bump[0]
        P.add("pool", lambda e: e.collective_compute("AllGather", ALU.bypass, replica_groups=RG, ins=[xin_mla.opt()], outs=[xout_mla.opt()]),
              reads=["mdst"], writes=["xout_mla"]).signal = True
        P.add("pool", lambda e: e.collective_compute("AllGather", ALU.bypass, replica_groups=RG, ins=[xin_halo.opt()], outs=[xout_halo.opt()]),
              reads=["xin_halo"], writes=["xout_halo"]).signal = True
        hl = alloc(6 * 128, BF16, shape=[6, 128])
        hr = alloc(6 * 128, BF16, shape=[6, 128])
        xo = xout_halo.rearrange("(r s p) c -> r p s c", r=2, p=128)
        dma("sp", hl, xo[0, :, :, 128:256], ["xout_halo"], ["hl"], "hl")
        dma("sp", hr, xo[1, :, :, 0:128], ["xout_halo"], ["hr"], "hr")
        P.add("dve", lambda e: e.tensor_scalar(out=hl, in0=hl, scalar1=V("lm"), scalar2=None, op0=ALU.mult), reads=["hl", "vecs"], writes=["hl"])
        P.add("dve", lambda e: e.tensor_scalar(out=hr, in0=hr, scalar1=V("rm"), scalar2=None, op0=ALU.mult), reads=["hr", "vecs"], writes=["hr"])
        zb = zbuf.rearrange("(s p) c -> p s c", p=128)
        dma("sp", zb[:, 6:8, 0:128], hl[:, 0:2, :], ["hl"], ["zdst"], "hl")
        dma("sp", zb[:, 10:14, 0:128], hl[:, 2:6, :], ["hl"], ["zdst"], "hl")
        dma("sp", zb[:, 6:8, 128 + S_OWN:ZW], hr[:, 0:2, :], ["hr"], ["zdst"], "hr")
        dma("sp", zb[:, 10:14, 128 + S_OWN:ZW], hr[:, 2:6, :], ["hr"], ["zdst"], "hr")
        P.barrier()
        bump[0] = base

    NKC = 66

    def alloc_kv():
        K = FFNBufs()
        K.KT = alloc(4 * NKC * 128, BF16, shape=[4, NKC * 128])
        K.Vx = alloc(NKC * 384, BF16, shape=[NKC, 384])
        K.wukv = alloc(512, BF16)
        K.ckt = [alloc(512, BF16) for _ in range(2)]
        return K

    def kv_build(K, l):
        dma("sp", K.wukv, wb["w_mla_ukv", l], WK("w_mla_ukv", l), ["wukv"], "wukv")
        P.add("dve", lambda e: e.memset(K.Vx, 0.0), writes=["Vx"])
        P.add("dve", lambda e: e.memset(K.Vx.rearrange("p k (a c) -> p k a c", a=2)[:, :, :, 64:65], 1.0), writes=["Vx"])
        srcs = [(xout_mla[r * 160:r * 160 + 128, t8 * 512:(t8 + 1) * 512], 512, r * S_OWN + t8 * 512) for r in range(2) for t8 in range(8)]
        srcs.append((ckvc[0:128, 0:CTX], CTX, 2 * S_OWN))
        for r in range(2):
            for h in range(4):
                dma("sp", K.KT[64:96, h, r * S_OWN:(r + 1) * S_OWN], xout_mla[r * 160 + 128:r * 160 + 160, :], ["xout_mla"], [("KTr", h)], f"ktr{h}")
        for h in range(4):
            dma("sp", K.KT[64:96, h, 2 * S_OWN:2 * S_OWN + CTX], ckvc[128:160, :], ["mdstc"], [("KTr", h)], f"ktr{h}")
        wv = K.wukv.rearrange("p (h c) -> p h c", h=4)[:, :, 64:128]
        for i, (src, n, key0) in enumerate(srcs):
            slot = i % 2
            dma("sp", K.ckt[slot][:, 0:n], src, ["xout_mla", "mdstc"], [("ckt", slot)], f"ckt{slot}")
            for h in range(4):
                b = 1 + (h % 2)
                P.add("pe", lambda e, h=h, b=b, slot=slot, n=n: e.matmul(ps[b][0:64, 0:n], lhsT=K.wukv[:, h * 128:h * 128 + 64], rhs=K.ckt[slot][:, 0:n],
                                                                        start=True, stop=True), reads=["wukv", ("ckt", slot)], writes=[PSK(b)])
                eng = "act" if h % 2 == 0 else "dve"
                if eng == "act":
                    P.add("act", lambda e, h=h, b=b, n=n, key0=key0: e.activation(out=K.KT[0:64, h, key0:key0 + n], in_=ps[b][0:64, 0:n], func=AF.Copy),
                          reads=[PSK(b)], writes=[("KTn", h)])
                else:
                    P.add("dve", lambda e, h=h, b=b, n=n, key0=key0: e.tensor_copy(out=K.KT[0:64, h, key0:key0 + n], in_=ps[b][0:64, 0:n]),
                          reads=[PSK(b)], writes=[("KTn", h)])
            for kb in range(n // 128):
                b = 3 + (kb % 2)
                kc = key0 // 128 + kb
                P.add("pe", lambda e, kb=kb, b=b, slot=slot: e.matmul(ps[b][:, 0:256], lhsT=K.ckt[slot][:, kb * 128:(kb + 1) * 128], rhs=wv,
                                                                     start=True, stop=True), reads=["wukv", ("ckt", slot)], writes=[PSK(b)])
                pv = ps[b][:, 0:256].rearrange("p (a b c) -> p a b c", a=2, b=2)
                vo = K.Vx[:, kc, :].rearrange("p (a b c) -> p a b c", a=2, b=3)
                P.add("act", lambda e, pv=pv, vo=vo: e.activation(out=vo[:, :, 0, :], in_=pv[:, :, 0, :], func=AF.Copy), reads=[PSK(b)], writes=["Vx"])
                P.add("dve", lambda e, pv=pv, vo=vo: e.tensor_copy(out=vo[:, :, 2, :], in_=pv[:, :, 1, :]), reads=[PSK(b)], writes=["Vx"])

    def alloc_mix():
        M = FFNBufs()
        M.ht = alloc(8 * T, shape=[8, T])
        M.mix = alloc(8 * T, BF16, shape=[8, T])
        M.wout = alloc(8 * D, BF16, shape=[8, D])
        M.QT = alloc(4 * T, BF16, shape=[4, T])
        M.PT = [alloc(2 * T, BF16) for _ in range(3)]
        M.scb = alloc(2 * T, BF16, shape=[2, T])
        M.sct = alloc(2 * (T + 2), BF16, shape=[2, T + 2])
        M.waq = alloc(2 * T, BF16, shape=[2, T])
        M.wak = alloc(T + 256, BF16)
        M.wav = alloc(T + 256, BF16)
        M.cfu = alloc(2 * (T + 32), BF16, shape=[2, T + 32])
        M.Vw = alloc(6 * 384, BF16, shape=[6, 384])
        M.wakc = alloc(CTX, BF16)
        M.Vwc = alloc(2 * 384, BF16, shape=[2, 384])
        M.acc = [alloc(T) for _ in range(2)]
        M.rinv = alloc(T)
        M.bc = alloc(T)
        M.wavc = M.bc.bitcast(BF16)[:, 0:CTX]
        M.f = [alloc(T) for _ in range(3)]
        return M

    def transpose_to_triples(M, src, nblk, dst):
        p7 = ps[7][:, :].bitcast(BF16)
        for blk in range(nblk):
            o = (blk % 4) * 128
            P.add("pe", lambda e, blk=blk, o=o: e.transpose(out=p7[:, o:o + 128], in_=src[:, blk * 128:(blk + 1) * 128], identity=ident_b),
                  reads=["wavsrc", "ident"], writes=[PSK(7)])
            pv = p7[:, o:o + 128].rearrange("p (a c) -> p a c", a=2)
            vo = dst[:, blk, :].rearrange("p (a b c) -> p a b c", a=2, b=3)
            P.add("act", lambda e, pv=pv, vo=vo: e.activation(out=vo[:, :, 0, :], in_=pv, func=AF.Copy), reads=[PSK(7)], writes=["Vw"])
            P.add("dve", lambda e, pv=pv, vo=vo: e.tensor_copy(out=vo[:, :, 2, :], in_=pv), reads=[PSK(7)], writes=["Vw"])

    def init_triples(buf):
        P.add("dve", lambda e: e.memset(buf, 0.0), writes=["Vw"])
        P.add("dve", lambda e: e.memset(buf.rearrange("p k (a c) -> p k a c", a=2)[:, :, :, 64:65], 1.0), writes=["Vw"])

    def normalize(M, Tn, ob, lo, l, h, chunk, sink, bcb=5):
        if 'norm' in DBG_SKIP:
            return
        sp = 64 if lo else 0
        mrows = 64 if lo else 128
        r0 = 0 if lo else 64
        if sink:
            P.add("dve", lambda e: e.tensor_scalar(out=M.rinv[sp:sp + 1, 0:Tn], in0=ps[ob][sp:sp + 1, 0:Tn], scalar1=esink[sp:sp + 1, l, h:h + 1],
                                                  scalar2=None, op0=ALU.add), reads=[PSK(ob), "esink"], writes=["rinv"])
            P.add("dve", lambda e: e.reciprocal(out=M.rinv[sp:sp + 1, 0:Tn], in_=M.rinv[sp:sp + 1, 0:Tn]), reads=["rinv"], writes=["rinv"])
        else:
            P.add("dve", lambda e: e.reciprocal(out=M.rinv[sp:sp + 1, 0:Tn], in_=ps[ob][sp:sp + 1, 0:Tn]), reads=[PSK(ob)], writes=["rinv"])
        P.add("pe", lambda e: e.matmul(ps[bcb][0:mrows, 0:Tn], lhsT=ones_f[sp:sp + 1, 0:mrows], rhs=M.rinv[sp:sp + 1, 0:Tn], start=True, stop=True),
              reads=["rinv", "ones_f"], writes=[PSK(bcb)])
        P.add("act", lambda e: e.activation(out=M.bc[r0:r0 + 64, 0:Tn], in_=ps[bcb][r0:r0 + 64, 0:Tn], func=AF.Copy), reads=[PSK(bcb)], writes=["bc"])
        P.add("dve", lambda e: e.tensor_tensor(out=M.mix[r0:r0 + 64, chunk, 0:Tn], in0=ps[ob][r0:r0 + 64, 0:Tn], in1=M.bc[r0:r0 + 64, 0:Tn], op=ALU.mult),
              reads=[PSK(ob), "bc"], writes=[("mix", chunk)])

    def mixers(K, M, l, is_ctx, t):
        Tn = CTX if is_ctx else T
        tok0 = 0 if is_ctx else t * T
        j = 1 if is_ctx else 0
        zv = (zcbuf if is_ctx else zbuf).rearrange("(s p) c -> p s c", p=128)
        c0 = 128 + tok0
        dma("sp", M.QT[0:96, :, 0:Tn], zv[0:96, 0:4, c0:c0 + Tn], [], ["QT"], "lq")
        dma("sp", M.scb[:, :, 0:Tn], zv[:, 4:6, c0:c0 + Tn], [], ["scb"], "lscb")
        dma("sp", M.sct[:, :, 0:Tn + 2], zv[:, 6:8, c0 - 1:c0 + Tn + 1], [], ["sct"], "lsct")
        dma("sp", M.waq[:, :, 0:Tn], zv[:, 8:10, c0:c0 + Tn], [], ["waq"], "lwaq")
        dma("sp", M.wak[:, 0:Tn + 256], zv[:, 10, c0 - 128:c0 + Tn + 128], [], ["wak"], "lwak")
        dma("sp", M.wav[:, 0:Tn + 256], zv[:, 11, c0 - 128:c0 + Tn + 128], [], ["wavsrc"], "lwav")
        dma("sp", M.cfu[:, :, 0:Tn + 30], zv[:, 12:14, c0 - 15:c0 + Tn + 15], [], ["cfu"], "lcfu")
        hsrc = hcbuf if is_ctx else hbuf
        dma("sp", M.ht[:, :, 0:Tn], hsrc.rearrange("(k p) c -> p k c", p=128)[:, :, tok0:tok0 + Tn], ["hdst"], [("ht", k) for k in range(8)], "hld")

        conv_ops = []

        def cadd(*a_, **k_):
            conv_ops.append((a_, k_))

        def wsc(c, k):
            return V("wsc", (l * 2 + c) * 3 + k, 1)

        def wcf(c, k):
            return V("wcf", (l * 2 + c) * 31 + k, 1)

        for c in (range(2) if 'sc' not in DBG_SKIP else []):
            acc = M.acc[c]
            cadd("dve", lambda e, c=c, acc=acc: e.tensor_scalar(out=acc[:, 0:Tn], in0=M.sct[:, c, 0:Tn], scalar1=wsc(c, 0), scalar2=None, op0=ALU.mult),
                  reads=["sct", "vecs"], writes=[("acc", c)])
            for k in (1, 2):
                cadd("dve", lambda e, c=c, k=k, acc=acc: e.scalar_tensor_tensor(out=acc[:, 0:Tn], in0=M.sct[:, c, k:k + Tn], scalar=wsc(c, k), in1=acc[:, 0:Tn],
                                                                                 op0=ALU.mult, op1=ALU.add), reads=["sct", ("acc", c), "vecs"], writes=[("acc", c)])
            cadd("dve", lambda e, c=c, acc=acc: e.tensor_tensor(out=M.mix[:, 2 + c, 0:Tn], in0=acc[:, 0:Tn], in1=M.scb[:, c, 0:Tn], op=ALU.mult),
                  reads=[("acc", c), "scb"], writes=[("mix", 2 + c)])
        for c in range(2):
            acc = M.acc[c]
            cadd("dve", lambda e, c=c, acc=acc: e.tensor_scalar(out=acc[:, 0:Tn], in0=M.cfu[:, c, 0:Tn], scalar1=wcf(c, 0), scalar2=V("bcf", l * 2 + c, 1),
                                                                 op0=ALU.mult, op1=ALU.add), reads=["cfu", "vecs", ("acc", c)], writes=[("acc", c)])
            for k in range(1, 31):
                cadd("dve", lambda e, c=c, k=k, acc=acc: e.scalar_tensor_tensor(out=acc[:, 0:Tn], in0=M.cfu[:, c, k:k + Tn], scalar=wcf(c, k), in1=acc[:, 0:Tn],
                                                                                 op0=ALU.mult, op1=ALU.add), reads=["cfu", ("acc", c), "vecs"], writes=[("acc", c)])
            cadd("dve", lambda e, c=c, acc=acc: e.tensor_tensor(out=M.f[c][:, 0:Tn], in0=acc[:, 0:Tn], in1=acc[:, 0:Tn], op=ALU.mult),
                  reads=[("acc", c)], writes=[("f", c)])

        kcs = [64, 65] if is_ctx else list(range(NKC))
        sc_a = 96.0 ** -0.5
        n = len(kcs)
        for h in (range(4) if 'mla' not in DBG_SKIP else []):
            ob = 4
            lo = (h % 2 == 0)
            vcol = (h // 2) * 192 + (0 if lo else 64)
            npair = n // 2
            SBK = [0, 1, 3]

            def S(p, h=h):
                w = SBK[p % 3]
                for half in range(2):
                    kc = kcs[2 * p + half]
                    P.add("pe", lambda e, kc=kc, half=half, w=w: e.matmul(psw[w][:, half * 512:half * 512 + Tn], lhsT=K.KT[0:96, h, kc * 128:(kc + 1) * 128],
                                                                         rhs=M.QT[0:96, h, 0:Tn], start=True, stop=True),
                          reads=[("KTn", h), ("KTr", h), "QT"], writes=[PSK(2 * w + half)])

            def E(p):
                w = SBK[p % 3]
                src = psw[w][:, :].rearrange("p (a c) -> p a c", a=2)[:, :, 0:Tn]
                dst = M.PT[p % 3].rearrange("p (a c) -> p a c", a=2)[:, :, 0:Tn]
                P.add("act", lambda e: e.activation(out=dst, in_=src, func=AF.Exp, scale=sc_a), reads=[PSK(2 * w), PSK(2 * w + 1)], writes=[("PT", p % 3)])

            def PV(p, ob=ob, vcol=vcol):
                for half in range(2):
                    kc = kcs[2 * p + half]
                    first = (p == 0 and half == 0)
                    last = (p == npair - 1 and half == 1)
                    P.add("pe", lambda e, kc=kc, half=half, first=first, last=last: e.matmul(
                        ps[ob][:, 0:Tn], lhsT=K.Vx[:, kc, vcol:vcol + 128], rhs=M.PT[p % 3][:, half * 512:half * 512 + Tn], start=first, stop=last),
                        reads=["Vx", ("PT", p % 3)], writes=[PSK(ob)])

            for p in range(min(3, npair)):
                S(p)
            for p in range(npair):
                E(p)
                PV(p)
                if p + 3 < npair:
                    S(p + 3)
            normalize(M, Tn, ob, lo, l, h, h // 2, False)
            per = (len(conv_ops) + 3) // 4
            for (a_, k_) in conv_ops[h * per:(h + 1) * per]:
                P.add(*a_, **k_)

        if 'mla' in DBG_SKIP:
            for (a_, k_) in conv_ops:
                P.add(*a_, **k_)
        for c in range(2):
            P.add("pe", lambda e, c=c: e.matmul(ps[6][:, 0:Tn], lhsT=ones_f, rhs=M.acc[c][:, 0:Tn], start=(c == 0), stop=(c == 1)),
                  reads=[("acc", c), "ones_f"], writes=[PSK(6)])
        for c in range(2):
            P.add("pe", lambda e, c=c: e.matmul(ps[7][:, 0:Tn], lhsT=ones_f, rhs=M.f[c][:, 0:Tn], start=(c == 0), stop=(c == 1)),
                  reads=[("f", c), "ones_f"], writes=[PSK(7)])
        P.add("act", lambda e: e.activation(out=M.f[2][:, 0:Tn], in_=ps[6][:, 0:Tn], func=AF.Identity, bias=V("zero"), scale=1.0 / 256),
              reads=[PSK(6), "vecs"], writes=[("f", 2)])
        P.add("dve", lambda e: e.tensor_tensor(out=M.f[0][:, 0:Tn], in0=M.f[2][:, 0:Tn], in1=M.f[2][:, 0:Tn], op=ALU.mult), reads=[("f", 2)], writes=[("f", 0)])
        P.add("dve", lambda e: e.scalar_tensor_tensor(out=M.f[0][:, 0:Tn], in0=ps[7][:, 0:Tn], scalar=1.0 / 256, in1=M.f[0][:, 0:Tn], op0=ALU.mult, op1=ALU.subtract),
              reads=[PSK(7), ("f", 0)], writes=[("f", 0)])
        P.add("act", lambda e: e.activation(out=M.f[0][:, 0:Tn], in_=M.f[0][:, 0:Tn], func=AF.Sqrt, bias=V("eps"), scale=1.0), reads=[("f", 0), "vecs"], writes=[("f", 0)])
        P.add("dve", lambda e: e.reciprocal(out=M.f[0][:, 0:Tn], in_=M.f[0][:, 0:Tn]), reads=[("f", 0)], writes=[("f", 0)])
        for c in range(2):
            acc = M.acc[c]
            P.add("dve", lambda e, acc=acc: e.tensor_tensor(out=acc[:, 0:Tn], in0=acc[:, 0:Tn], in1=M.f[2][:, 0:Tn], op=ALU.subtract),
                  reads=[("acc", c), ("f", 2)], writes=[("acc", c)])
            P.add("dve", lambda e, acc=acc: e.tensor_tensor(out=acc[:, 0:Tn], in0=acc[:, 0:Tn], in1=M.f[0][:, 0:Tn], op=ALU.mult),
                  reads=[("acc", c), ("f", 0)], writes=[("acc", c)])
            P.add("act", lambda e, c=c, acc=acc: e.activation(out=M.mix[:, 6 + c, 0:Tn], in_=acc[:, 0:Tn], func=AF.Silu, bias=V("bln", l * 2 + c, 1),
                                                             scale=V("gln", l * 2 + c, 1)), reads=[("acc", c), "vecs"], writes=[("mix", 6 + c)])

        sc_w = 64.0 ** -0.5
        nqb = Tn // 128
        if not is_ctx and 'tr' not in DBG_SKIP:
            transpose_to_triples(M, M.wav, nqb + 2, M.Vw)
        for g in (range(2) if 'win' not in DBG_SKIP else []):
            pr = slice(g * 64, (g + 1) * 64)
            for qb in range(nqb):
                setb = (g * nqb + qb) % 2
                sbanks = [0, 1, 2] if setb == 0 else [3, 6, 7]
                po = setb * 512
                if is_ctx:
                    kbl = [("ctx", 0, None), ("ctx", 1, None)]
                else:
                    mP = 2 if (t == 0 and qb == 0) else 0
                    mN = 3 if (t == NT - 1 and qb == nqb - 1) else 1
                    kbl = [("loc", qb, mP), ("loc", qb + 1, None), ("loc", qb + 2, mN), ("ctx", 0, None), ("ctx", 1, None)]
                for i, (kind, blk, mi) in enumerate(kbl):
                    b, off = sbanks[i // 2], (i % 2) * 256
                    ksrc = M.wakc if kind == "ctx" else M.wak
                    outv = ps[b][:, off:off + 256].rearrange("p (a c) -> p a c", a=2)
                    kl = ksrc[pr, blk * 128:(blk + 1) * 128]
                    qr = M.waq[pr, :, qb * 128:(qb + 1) * 128]
                    P.add("pe", lambda e, kl=kl, qr=qr, outv=outv, mi=mi: e.matmul(outv, lhsT=kl, rhs=qr, start=True, stop=(mi is None)),
                          reads=["wak", "wakc", "waq"], writes=[PSK(b)])
                    if mi is not None:
                        mo = ps[b][:, off:off + 256]
                        mr = masks_b[:, mi, :]
                        P.add("pe", lambda e, mo=mo, mr=mr: e.matmul(mo, lhsT=ident_b, rhs=mr, start=False, stop=True),
                              reads=["ident", "masks"], writes=[PSK(b)])
                ntile = (len(kbl) + 1) // 2
                for i in range(ntile):
                    w = 256 * min(2, len(kbl) - 2 * i)
                    eo = M.PT[i][:, po:po + w]
                    ei = ps[sbanks[i]][:, 0:w]
                    P.add("act", lambda e, eo=eo, ei=ei: e.activation(out=eo, in_=ei, func=AF.Exp, scale=sc_w),
                          reads=[PSK(sbanks[i])], writes=[("PTw", i, setb)])
                for c in range(2):
                    ob = 4 + c
                    for i, (kind, blk, mi) in enumerate(kbl):
                        vsrc = M.Vwc if kind == "ctx" else M.Vw
                        col = po + (i % 2) * 256 + c * 128
                        vl = vsrc[:, blk, g * 192 + c * 64:g * 192 + c * 64 + 128]
                        pr_ = M.PT[i // 2][:, col:col + 128]
                        oo = ps[ob][:, qb * 128:(qb + 1) * 128]
                        last = (i == len(kbl) - 1)
                        P.add("pe", lambda e, vl=vl, pr_=pr_, oo=oo, i=i, last=last: e.matmul(oo, lhsT=vl, rhs=pr_, start=(i == 0), stop=last),
                              reads=["Vw", ("PTw", i // 2, setb)], writes=[PSK(ob)])
            for c in range(2):
                normalize(M, Tn, 4 + c, c == 0, l, 2 * g + c, 4 + g, True, bcb=2)

        G2 = HG[:, l, j, 1, :]
        for c in range(8):
            bd = 1 + (c % 2)
            for k in range(8):
                P.add("pe", lambda e, c=c, k=k, bd=bd: e.matmul(ps[bd][:, 0:Tn], lhsT=M.wout[:, k, c * 128:(c + 1) * 128], rhs=M.mix[:, k, 0:Tn],
                                                               start=(k == 0), stop=(k == 7)), reads=["wout", ("mix", k)], writes=[PSK(bd)])
            P.add("dve", lambda e, c=c, bd=bd: e.scalar_tensor_tensor(out=M.ht[:, c, 0:Tn], in0=ps[bd][:, 0:Tn], scalar=G2[:, c:c + 1], in1=M.ht[:, c, 0:Tn],
                                                                     op0=ALU.mult, op1=ALU.add), reads=[PSK(bd), ("ht", c), "coef"], writes=[("ht", c)])
        dma("sp", hsrc.rearrange("(k p) c -> p k c", p=128)[:, :, tok0:tok0 + Tn], M.ht[:, :, 0:Tn], [("ht", k) for k in range(8)], ["hdst"], "hst")

    def phase_mix(l):
        base = bump[0]
        K = alloc_kv()
        kv_build(K, l)
        M = alloc_mix()
        dma("sp", M.wout, wb["w_out", l].rearrange("(k p) m -> p k m", p=128), WK("w_out", l), ["wout"], "wout")
        zc = zcbuf.rearrange("(s p) c -> p s c", p=128)
        dma("sp", M.wakc, zc[:, 10, 128:128 + CTX], [], ["wakc"], "lwakc")
        dma("sp", M.wavc, zc[:, 11, 128:128 + CTX], [], ["wavsrc"], "lwavc")
        init_triples(M.Vw)
        init_triples(M.Vwc)
        transpose_to_triples(M, M.wavc, 2, M.Vwc)
        for t in range(min(NT, DBG_MIXT[0])):
            mixers(K, M, l, False, t)
        if l == 0 and DBG_MIXT[0] >= NT:
            mixers(K, M, l, True, 0)
        P.barrier()
        bump[0] = base

    zt = alloc(NZ * 128, BF16, shape=[NZ, 128])
    P.add("dve", lambda e: e.memset(zt, 0.0), writes=["zt"])
    zcv = zcbuf.rearrange("(s p) c -> p s c", p=128)
    dma("sp", zcv[:, :, 0:128], zt, ["zt"], ["zc0"], "zc0")
    dma("sp", zcv[:, :, 128 + CTX:ZCW], zt, ["zt"], ["zc1"], "zc1")
    P.barrier()
    bump[0] = persist_end

    stage = STAGE[0]
    phase_ffn(None, 0, True)
    if stage >= 1.2:
        exchange(0)
    if stage >= 1.5:
        phase_mix(0)
    if stage >= 2.5:
        phase_ffn(0, 1, False)
    if stage >= 2.7:
        exchange(1)
        phase_mix(1)
    if stage >= 3:
        phase_ffn(1, None, False)
    if stage < 3:
        base = bump[0]
        Bd = alloc_common(T)
        for t in range(NT):
            load_h(Bd, hbuf, t * T, T)
            store_h(Bd, outT, t * T, T, key="out")
        if 'dumpc' in DBG_SKIP:
            load_h(Bd, hcbuf, 0, CTX)
            store_h(Bd, outT, 0, CTX, key="out")
        P.barrier()
        bump[0] = base
    P.barrier(final=True)
    P.emit(nc, st)
    st.close()
    return nc


STAGE = [3]
DBG_MIXT = [99]
DBG_SKIP = set()
RG_OVERRIDE = [None]
MOD_RANKS = [2]
MOD_GROUPS = [[[0, 1], [2, 3], [4, 5], [6, 7]]]


def _rope_tables(half):
    pos = half * S_OWN + np.arange(S_OWN)
    row = (pos // 64).astype(np.float32)
    col = (pos % 64).astype(np.float32)

    def tab(d_rot):
        d_ax = d_rot // 2
        inv = (10000.0 ** (-np.arange(0, d_ax, 2, dtype=np.float32) / d_ax)).astype(np.float32)
        ar = row[:, None] * inv[None, :]
        ac = col[:, None] * inv[None, :]
        cr, sr, cc_, sc_ = np.cos(ar), np.sin(ar), np.cos(ac), np.sin(ac)
        C = np.concatenate([cr, cr, cc_, cc_], axis=1).T.astype(np.float32)
        Sg = np.concatenate([-sr, sr, -sc_, sc_], axis=1).T.astype(np.float32)
        return C, Sg

    Cw, Sw = tab(64)
    ropeW = np.concatenate([np.tile(Cw, (2, 1)), np.tile(Sw, (2, 1))], axis=0)
    Cm, Sm = tab(32)
    ropeK = np.concatenate([Cm, Sm], axis=0)
    Cq = np.concatenate([np.ones((64, S_OWN), np.float32), Cm], axis=0)
    Sq = np.concatenate([np.zeros((64, S_OWN), np.float32), Sm], axis=0)
    ropeQ = np.concatenate([Cq, Sq], axis=0)
    return np.ascontiguousarray(ropeW), np.ascontiguousarray(ropeQ), np.ascontiguousarray(ropeK)


def _masks(half):
    kp = np.arange(128)[:, None]
    qp = np.arange(128)[None, :]
    mP = np.where(qp <= kp, 0.0, NEGM).astype(np.float32)
    mN = np.where(kp <= qp, 0.0, NEGM).astype(np.float32)
    neg = np.full((128, 128), NEGM, np.float32)
    kinds = [mP, mN, mP if half == 1 else neg, mN if half == 0 else neg]
    m = np.stack([np.concatenate([k, k], axis=1) for k in kinds], axis=1)
    return np.ascontiguousarray(m.reshape(128, 4 * 256))


def _fm(v):
    v = np.asarray(v, np.float32)
    lead = v.shape[:-1]
    n = v.shape[-1] // 128
    return np.moveaxis(v.reshape(lead + (n, 128)), -1, 0)


def _pack_vecs(inp, half, b):
    vec = np.zeros((128, NV), np.float32)

    def put(name, arr):
        arr = np.asarray(arr, np.float32).reshape(128, -1)
        w = dict(_VSPEC)[name]
        assert arr.shape[1] == w, (name, arr.shape, w)
        vec[:, VOFF[name]:VOFF[name] + w] = arr

    put("gf1", _fm(inp["g_ffn1"]))
    put("gmix", _fm(inp["g_mix"]))
    put("gf2", _fm(inp["g_ffn2"]))
    put("bmod", _fm(inp["b_mod"]))
    put("gfin", _fm(inp["g_final"]))
    gq = np.zeros((L, 256), np.float32)
    gq[:, :192] = inp["g_mla_q"]
    put("gq", _fm(gq))
    put("gkv", _fm(inp["g_mla_kv"]))
    wsc = np.asarray(inp["w_sc_conv"], np.float32)
    put("wsc", np.transpose(wsc.reshape(L, 3, 2, 128), (3, 0, 2, 1)))
    wcf = np.asarray(inp["w_cf_conv"], np.float32)
    put("wcf", np.transpose(wcf.reshape(L, 31, 2, 128), (3, 0, 2, 1)))
    put("bcf", _fm(inp["b_cf_conv"]))
    put("gln", _fm(inp["g_cf_ln"]))
    put("bln", _fm(inp["b_cf_ln"]))
    put("sink", np.broadcast_to(np.asarray(inp["wa_sink"], np.float32).reshape(1, L * 4), (128, L * 4)))
    put("lm", np.full((128, 1), 1.0 if half == 1 else 0.0, np.float32))
    put("rm", np.full((128, 1), 1.0 if half == 0 else 0.0, np.float32))
    put("eps", np.full((128, 1), EPS, np.float32))
    put("zero", np.zeros((128, 1), np.float32))
    bsel = np.zeros((128, 4), np.float32)
    bsel[:, b] = 1.0
    put("bsel", bsel)
    return vec


_NC_CACHE = {}


def kernel(**inputs):
    inp = {k: np.asarray(v) for k, v in inputs.items()}
    x = inp["x"].astype(np.float32, copy=False)
    ctx = inp["ctx"].astype(np.float32, copy=False)
    Bn = x.shape[0]
    key = STAGE[0]
    if key not in _NC_CACHE:
        _NC_CACHE[key] = build()
    nc = _NC_CACHE[key]
    shared = {n: np.ascontiguousarray(inp[n], dtype=np.float32) for n in
              ("w1_gate", "w1_up", "w1_down", "w2_gate", "w2_up", "w2_down", "w_in", "w_out", "w_mla_uq", "w_mla_ukv")}
    ident = np.eye(128, dtype=np.float32)
    in_maps = []
    for core in range(8):
        b, half = core // 2, core % 2
        rW, rQ, rK = _rope_tables(half)
        cc = np.stack([_fm(inp["c"][bb]) for bb in range(4)] + [_fm(inp["c_ctx"])], axis=-1).reshape(128, 40)
        nsh = MOD_RANKS[0]
        rk = core % nsh
        wsl = 9 * D // nsh
        m = {"xT": np.ascontiguousarray(x[b, half * S_OWN:(half + 1) * S_OWN, :].T),
             "ctxT": np.ascontiguousarray(ctx[b].T),
             "cc": np.ascontiguousarray(cc, dtype=np.float32),
             "vecs": _pack_vecs(inp, half, b),
             "ropeW": rW, "ropeQ": rQ, "ropeK": rK,
             "masks": _masks(half), "ident": ident}
        m["w_mod"] = np.ascontiguousarray(inp["w_mod"][:, :, rk * wsl:(rk + 1) * wsl], dtype=np.float32)
        m.update(shared)
        in_maps.append(m)
    res = run_bass_kernel_spmd(nc, in_maps, core_ids=list(range(8)))
    out = np.empty((Bn, 2 * S_OWN, D), np.float32)
    for core in range(8):
        b, half = core // 2, core % 2
        out[b, half * S_OWN:(half + 1) * S_OWN, :] = np.asarray(res.results[core]["outT"]).T
    return out
```

```python
import contextlib
import numpy as np
import concourse.bass as bass
import concourse.mybir as mybir
from concourse.bass_utils import run_bass_kernel_spmd

F32 = mybir.dt.float32
BF16 = mybir.dt.bfloat16
AF = mybir.ActivationFunctionType
ALU = mybir.AluOpType
ENGS = ("pe", "act", "dve", "pool", "sp")

L = 2
D = 1024
S_OWN = 4096
CTX = 256
T = 512
NT = S_OWN // T
DFF = 2816
NJ = DFF // 128
NWIN = 2560
ZW = 128 + S_OWN + 128
ZCW = 128 + CTX + 128
EPS = 1e-6
NZ = 14
ZQ, ZSCB, ZSCT, ZWAQ, ZWAK, ZWAV, ZCFU = 0, 4, 6, 8, 10, 11, 12
HALO_SLOTS = (6, 7, 10, 11, 12, 13)
NEGM = -30000.0

_VSPEC = [("gf1", L * 8), ("gmix", L * 8), ("gf2", L * 8), ("bmod", L * 72), ("gfin", 8), ("gq", L * 2),
          ("gkv", L), ("wsc", L * 6), ("wcf", L * 62), ("bcf", L * 2), ("gln", L * 2), ("bln", L * 2),
          ("sink", L * 4), ("lm", 1), ("rm", 1), ("eps", 1), ("zero", 1), ("bsel", 4)]
VOFF = {}
_o = 0
for _n, _w in _VSPEC:
    VOFF[_n] = _o
    _o += _w
NV = _o


class Op:
    __slots__ = ("eng", "fn", "deps", "signal", "count", "chan", "idx")

    def __init__(self, eng, fn, chan):
        self.idx = 0
        self.eng = eng
        self.fn = fn
        self.deps = set()
        self.signal = False
        self.count = 0
        self.chan = chan


class Chan:
    def __init__(self, name):
        self.name = name
        self.nobar = False
        self.sem = None
        self.n = 0
        self.last = None


class Prog:
    def __init__(self):
        self.ops = {e: [] for e in ENGS}
        self.last_w = {}
        self.readers = {}
        self.chans = []
        self.sticky = set()

    def chan(self, name):
        c = Chan(name)
        self.chans.append(c)
        return c

    def add(self, eng, fn, reads=(), writes=(), chan=None):
        op = Op(eng, fn, chan)
        deps = set()
        for k in reads:
            w = self.last_w.get(k)
            if w is not None:
                deps.add(w)
        for k in writes:
            w = self.last_w.get(k)
            if w is not None:
                deps.add(w)
            deps.update(self.readers.get(k, ()))
        if chan is not None:
            if chan.last is not None:
                deps.add(chan.last)
            chan.last = op
            chan.n += 1
            op.count = 16 * chan.n
            op.signal = True
        deps.discard(op)
        if eng == "pe":
            deps = {d for d in deps if not (d.eng == "pe" and d.chan is None)}
        best = {}
        for d in deps:
            k = ("c", id(d.chan)) if d.chan is not None else ("e", d.eng)
            if k not in best or best[k].idx < d.idx:
                best[k] = d
        deps = set(best.values())
        for d in deps:
            d.signal = True
        op.deps = deps
        op.idx = len(self.ops[eng]) if chan is None else chan.n
        for k in reads:
            self.readers.setdefault(k, []).append(op)
        for k in writes:
            self.last_w[k] = op
            self.readers[k] = []
        self.ops[eng].append(op)
        return op

    def barrier(self, final=False):
        lasts = []
        for e in ENGS:
            for op in reversed(self.ops[e]):
                if op.chan is None and op.fn is not None:
                    lasts.append(op)
                    break
        for c in self.chans:
            if c.last is not None and (final or not c.nobar):
                lasts.append(c.last)
        for e in ENGS:
            op = Op(e, None, None)
            op.deps = set(lasts)
            for d in op.deps:
                d.signal = True
            self.ops[e].append(op)
        self.last_w = {k: v for k, v in self.last_w.items() if k in self.sticky}
        self.readers = {}

    def emit(self, nc, stack):
        esem = {e: stack.enter_context(nc.semaphore("s_" + e)) for e in ENGS}
        for c in self.chans:
            if c.n > 0:
                c.sem = stack.enter_context(nc.semaphore("c_" + c.name))
        for e in ENGS:
            n = 0
            for op in self.ops[e]:
                if op.chan is None and op.signal:
                    n += 1
                    op.count = n
        block = stack.enter_context(nc.Block())

        def run(e, eng):
            waited = {}
            for op in self.ops[e]:
                need = {}
                for d in op.deps:
                    if d.chan is not None:
                        s, v = d.chan.sem, d.count
                    else:
                        s, v = esem[d.eng], d.count
                    key = id(s)
                    if key not in need or need[key][1] < v:
                        need[key] = (s, v)
                for key, (s, v) in need.items():
                    if waited.get(key, 0) < v:
                        eng.wait_ge(s, v)
                        waited[key] = v
                if op.fn is None:
                    continue
                ins = op.fn(eng)
                if op.chan is not None:
                    ins.then_inc(op.chan.sem, 16)
                elif op.signal:
                    ins.then_inc(esem[e], 1)

        @block.tensor
        def _(eng):
            run("pe", eng)

        @block.scalar
        def _(eng):
            run("act", eng)

        @block.vector
        def _(eng):
            run("dve", eng)

        @block.gpsimd
        def _(eng):
            run("pool", eng)

        @block.sync
        def _(eng):
            run("sp", eng)


def build(dbg=None):
    nc = bass.Bass("TRN2", target_bir_lowering=False)
    P = Prog()
    st = contextlib.ExitStack()

    def din(name, shape):
        return nc.dram_tensor(name, list(shape), F32, kind="ExternalInput").ap()

    def dscr(name, shape, dt):
        return nc.dram_tensor(name, list(shape), dt).ap()

    xT = din("xT", [D, S_OWN])
    ctxT = din("ctxT", [D, CTX])
    cc_in = din("cc", [128, 40])
    vecs_in = din("vecs", [128, NV])
    ropeW = din("ropeW", [256, S_OWN])
    ropeQ = din("ropeQ", [192, S_OWN])
    ropeK = din("ropeK", [64, S_OWN])
    masks_in = din("masks", [128, 4 * 256])
    ident_in = din("ident", [128, 128])
    w_mod = din("w_mod", [L, D, 9 * D // MOD_RANKS[0]])
    wsrc = {n: din(n, s) for n, s in [
        ("w1_gate", [L, D, DFF]), ("w1_up", [L, D, DFF]), ("w1_down", [L, DFF, D]),
        ("w2_gate", [L, D, DFF]), ("w2_up", [L, D, DFF]), ("w2_down", [L, DFF, D]),
        ("w_in", [L, D, 2144]), ("w_out", [L, D, D]), ("w_mla_uq", [L, 192, 384]), ("w_mla_ukv", [L, 128, 512])]}
    outT = nc.dram_tensor("outT", [D, S_OWN], F32, kind="ExternalOutput").ap()
    dbg_out = None
    if dbg:
        dbg_out = nc.dram_tensor("dbg", list(dbg), F32, kind="ExternalOutput").ap()

    hbuf = dscr("hbuf", [D, S_OWN], F32)
    hcbuf = dscr("hcbuf", [D, CTX], F32)
    zbuf = dscr("zbuf", [NZ * 128, ZW], BF16)
    zcbuf = dscr("zcbuf", [NZ * 128, ZCW], BF16)
    xin_mla = dscr("xin_mla", [160, S_OWN], BF16)
    xout_mla = dscr("xout_mla", [320, S_OWN], BF16)
    ckvc = dscr("ckvc", [160, CTX], BF16)
    xin_halo = dscr("xin_halo", [768, 256], BF16)
    xout_halo = dscr("xout_halo", [1536, 256], BF16)
    wb = {}
    for l in range(L):
        for n in ("w1_gate", "w1_up", "w2_gate", "w2_up"):
            wb[n, l] = dscr(f"b_{n}{l}", [D, DFF], BF16)
        for n in ("w1_down", "w2_down"):
            wb[n, l] = dscr(f"b_{n}{l}", [DFF, D], BF16)
        wb["w_in", l] = dscr(f"b_w_in{l}", [D, NWIN], BF16)
        wb["w_out", l] = dscr(f"b_w_out{l}", [D, D], BF16)
        wb["w_mla_uq", l] = dscr(f"b_wuq{l}", [192, 768], BF16)
        wb["w_mla_ukv", l] = dscr(f"b_wukv{l}", [128, 512], BF16)

    ARENA = 53000
    arena = st.enter_context(nc.sbuf_tensor("arena", [128, ARENA], F32))
    bump = [0]

    def alloc(cols, dt=F32, shape=None):
        words = cols if dt == F32 else (cols + 1) // 2
        a = bump[0]
        bump[0] += words
        assert bump[0] <= ARENA, ("SBUF arena overflow", bump[0])
        ap = arena[:, a:a + words]
        if dt != F32:
            ap = ap.bitcast(dt)[:, 0:cols]
        if shape is not None:
            names = " ".join(f"d{i}" for i in range(len(shape)))
            kw = {f"d{i}": s for i, s in enumerate(shape)}
            ap = ap.rearrange(f"p ({names}) -> p {names}", **kw)
        return ap

    psw = [st.enter_context(nc.psum_tensor(f"psw{i}", [128, 1024], F32)) for i in range(4)]
    ps = []
    for i in range(4):
        ps.append(psw[i][:, 0:512])
        ps.append(psw[i][:, 512:1024])

    def PSK(i):
        return ("ps", i)

    vecs = alloc(NV)
    ccs = alloc(40)
    modv = alloc(L * 144, shape=[L, 72, 2])
    Acoef = alloc(L * 2 * 3 * 8, shape=[L, 2, 3, 8])
    HG = alloc(L * 2 * 3 * 8, shape=[L, 2, 3, 8])
    esink = alloc(L * 4, shape=[L, 4])
    ones_b = alloc(128, BF16)
    ones_f = alloc(128)
    ident_b = alloc(128, BF16)
    masks_b = alloc(4 * 256, BF16, shape=[4, 256])
    persist_end = bump[0]

    def V(name, off=0, w=1):
        o = VOFF[name] + off
        return vecs[:, o:o + w]

    chn = {}

    def CH(name):
        if name not in chn:
            chn[name] = P.chan(name)
        return chn[name]

    def dma(q, out, in_, reads, writes, ch):
        return P.add(q, lambda e: e.dma_start(out=out, in_=in_), reads=reads, writes=writes, chan=CH(ch))

    dma("sp", vecs, vecs_in, [], ["vecs"], "ld0")
    dma("sp", ccs, cc_in, [], ["ccs"], "ld1")
    P.sticky.update(["ident", "masks"])
    dma("pool", ident_b, ident_in, [], ["ident"], "cv0")
    dma("pool", masks_b.rearrange("p a b -> p (a b)"), masks_in, [], ["masks"], "cv1")
    P.add("dve", lambda e: e.memset(ones_b, 1.0), writes=["ones_b"])
    P.add("dve", lambda e: e.memset(ones_f, 1.0), writes=["ones_f"])
    P.add("act", lambda e: e.activation(out=ccs, in_=ccs, func=AF.Silu), reads=["ccs"], writes=["ccs"])
    P.add("act", lambda e: e.activation(out=esink.rearrange("p a b -> p (a b)"), in_=V("sink", 0, L * 4), func=AF.Exp),
          reads=["vecs"], writes=["esink"])

    cvn = [0]
    CVKEYS = {}

    def conv(out, in_, key):
        cvn[0] += 1
        key = (key, cvn[0])
        CVKEYS.setdefault(key[0], []).append(key)
        P.sticky.add(key)
        op = dma("pool", out, in_, [], [key], f"cv{cvn[0] % 8}")
        op.chan.nobar = True

    def conv_rows(name, l, nrows, dst=None, c0=0, c1=None, d0=0):
        src = wsrc[name]
        c1 = c1 if c1 is not None else src.shape[2]
        dstap = wb[name, l] if dst is None else dst
        for r in range(0, nrows, 128):
            rr = min(128, nrows - r)
            conv(dstap[r:r + rr, d0:d0 + (c1 - c0)], src[l, r:r + rr, c0:c1], (name, l, r // 128))

    def conv_layer_ffn(l, which):
        conv_rows(f"w{which}_gate", l, D)
        conv_rows(f"w{which}_up", l, D)
        conv_rows(f"w{which}_down", l, DFF)

    def conv_layer_mix(l):
        src = wsrc["w_in"]
        dst = wb["w_in", l]
        segs = [(0, 1120, 0), (1120, 1184, 1120), (1248, 1312, 1184), (1184, 1248, 1248), (1312, 1376, 1312),
                (1376, 2144, 1376)]
        for b8, s8 in enumerate([8, 0, 24, 16]):
            segs.append((320 + s8, 320 + s8 + 8, 2144 + 8 * b8))
        for hi, hsrc in enumerate([1120, 1248, 1184, 1312, 1376, 1440]):
            for b16, s16 in enumerate([16, 0, 48, 32]):
                segs.append((hsrc + s16, hsrc + s16 + 16, 2176 + 64 * hi + 16 * b16))
        for (c0, c1, d0) in segs:
            for r in range(0, D, 512):
                conv(dst[r:r + 512, d0:d0 + (c1 - c0)], src[l, r:r + 512, c0:c1], ("w_in", l))
        conv_rows("w_out", l, D)
        srcq = wsrc["w_mla_uq"]
        dq = wb["w_mla_uq", l]
        conv(dq[0:128, 0:384], srcq[l, 0:128, :], ("w_mla_uq", l))
        conv(dq[128:192, 0:384], srcq[l, 128:192, :], ("w_mla_uq", l))
        for h in range(4):
            conv(dq[0:192, 384 + h * 96:384 + h * 96 + 64], srcq[l, :, h * 96:h * 96 + 64], ("w_mla_uq", l))
            for b8, s8 in enumerate([8, 0, 24, 16]):
                conv(dq[0:192, 384 + h * 96 + 64 + 8 * b8:384 + h * 96 + 64 + 8 * b8 + 8],
                     srcq[l, :, h * 96 + 64 + s8:h * 96 + 64 + s8 + 8], ("w_mla_uq", l))
        conv(wb["w_mla_ukv", l][:, :], wsrc["w_mla_ukv"][l, :, :], ("w_mla_ukv", l))

    conv_layer_ffn(0, 1)

    def WK(name, l):
        if name == "w_in" or name.startswith("w_mla"):
            base = [(name, l)]
        else:
            n = DFF if name.endswith("down") else D
            base = [(name, l, r) for r in range((n + 127) // 128)]
        out = []
        for bkey in base:
            out.extend(CVKEYS.get(bkey, []))
        return out

    def setup_mod():
        base = bump[0]
        nsh = MOD_RANKS[0]
        nq = 72 // nsh
        slab = alloc(8 * nq * 128, shape=[8, nq * 128])
        part = alloc(L * nq * 5)
        modall = alloc(nsh * L * nq * 5, shape=[nsh, L * nq, 5])
        modx_in = dscr("modx_in", [128, L * nq * 5], F32)
        modx_out = dscr("modx_out", [nsh * 128, L * nq * 5], F32)
        for l in range(L):
            for kc in range(8):
                dma("sp", slab[:, kc, :], w_mod[l, kc * 128:(kc + 1) * 128, :], [], [("wm", kc)], f"wm{kc % 2}")
            for oc in range(nq):
                col = (l * nq + oc) * 5
                for kc in range(8):
                    P.add("pe", lambda e, kc=kc, oc=oc, col=col: e.matmul(
                        ps[0][:, col:col + 5], lhsT=slab[:, kc, oc * 128:(oc + 1) * 128],
                        rhs=ccs[:, kc * 5:kc * 5 + 5], start=(kc == 0), stop=(kc == 7)),
                        reads=[("wm", kc), "ccs"], writes=[PSK(0)])
        P.add("act", lambda e: e.activation(out=part, in_=ps[0][:, 0:L * nq * 5], func=AF.Copy), reads=[PSK(0)], writes=["part"])
        dma("sp", modx_in, part, ["part"], ["modx_in"], "ld0")
        rg = MOD_GROUPS[0]
        P.add("pool", lambda e: e.collective_compute("AllGather", ALU.bypass, replica_groups=rg, ins=[modx_in.opt()], outs=[modx_out.opt()]),
              reads=["modx_in"], writes=["modx_out"]).signal = True
        dma("sp", modall.rearrange("p r q f -> p r (q f)"), modx_out.rearrange("(r p) c -> p r c", p=128), ["modx_out"], ["modall"], "ld1")
        for l in range(L):
            mv = modv[:, l, :, :].rearrange("p (r q) j -> p r q j", r=nsh)
            src = modall[:, :, l * nq:(l + 1) * nq, :]
            P.add("dve", lambda e, mv=mv, src=src: e.tensor_copy(out=mv[:, :, :, 1], in_=src[:, :, :, 4]), reads=["modall"], writes=["modv"])
            P.add("dve", lambda e, mv=mv, src=src: e.tensor_scalar(out=mv[:, :, :, 0], in0=src[:, :, :, 0], scalar1=V("bsel", 0, 1), scalar2=None, op0=ALU.mult),
                  reads=["modall", "vecs"], writes=["modv"])
            for bb in range(1, 4):
                P.add("dve", lambda e, mv=mv, src=src, bb=bb: e.scalar_tensor_tensor(out=mv[:, :, :, 0], in0=src[:, :, :, bb], scalar=V("bsel", bb, 1),
                                                                                   in1=mv[:, :, :, 0], op0=ALU.mult, op1=ALU.add),
                      reads=["modall", "vecs", "modv"], writes=["modv"])
            for j in range(2):
                P.add("dve", lambda e, l=l, j=j: e.tensor_tensor(out=modv[:, l, :, j], in0=modv[:, l, :, j], in1=V("bmod", l * 72, 72), op=ALU.add),
                      reads=["modv", "vecs"], writes=["modv"])
            for j in range(2):
                for s_, gname in enumerate(("gf1", "gmix", "gf2")):
                    i_scale = 3 * s_ + 1
                    P.add("dve", lambda e, l=l, j=j, s_=s_, gname=gname, i_scale=i_scale: e.scalar_tensor_tensor(
                        out=Acoef[:, l, j, s_, :], in0=modv[:, l, i_scale * 8:(i_scale + 1) * 8, j], scalar=1.0,
                        in1=V(gname, l * 8, 8), op0=ALU.add, op1=ALU.mult), reads=["modv", "vecs"], writes=["coef"])
                    i_gate = 3 * s_ + 2
                    P.add("dve", lambda e, l=l, j=j, s_=s_, i_gate=i_gate: e.tensor_scalar(
                        out=HG[:, l, j, s_, :], in0=modv[:, l, i_gate * 8:(i_gate + 1) * 8, j],
                        scalar1=(1.0 if s_ == 1 else 0.5), scalar2=None, op0=ALU.mult), reads=["modv"], writes=["coef"])
        P.barrier()
        bump[0] = base

    setup_mod()
    conv_layer_mix(0)
    conv_layer_ffn(0, 2)
    conv_layer_ffn(1, 1)
    conv_layer_mix(1)
    conv_layer_ffn(1, 2)

    def Bsh(l, j, s):
        return modv[:, l, (3 * s) * 8:(3 * s + 1) * 8, j]

    class FFNBufs:
        pass

    def alloc_common(Tm):
        B = FFNBufs()
        B.ht = alloc(8 * Tm, shape=[8, Tm])
        B.xn = alloc(8 * Tm, BF16, shape=[8, Tm])
        B.sq = [alloc(Tm, BF16) for _ in range(2)]
        B.rstd = alloc(Tm)
        B.tmp = [alloc(Tm) for _ in range(2)]
        return B

    def norm_mod(B, Tn, Avec, Bvec, tag):
        for kc in range(8):
            P.add("act", lambda e, kc=kc: e.activation(out=B.sq[kc % 2][:, 0:Tn], in_=B.ht[:, kc, 0:Tn], func=AF.Square),
                  reads=[("ht", kc)], writes=[("sq", kc % 2)])
            P.add("pe", lambda e, kc=kc: e.matmul(ps[0][:, 0:Tn], lhsT=ones_b, rhs=B.sq[kc % 2][:, 0:Tn],
                                                 start=(kc == 0), stop=(kc == 7)),
                  reads=[("sq", kc % 2), "ones_b"], writes=[PSK(0)])
        P.add("act", lambda e: e.activation(out=B.rstd[:, 0:Tn], in_=ps[0][:, 0:Tn], func=AF.Sqrt, bias=V("eps"), scale=1.0 / D),
              reads=[PSK(0), "vecs"], writes=["rstd"])
        P.add("dve", lambda e: e.reciprocal(out=B.rstd[:, 0:Tn], in_=B.rstd[:, 0:Tn]), reads=["rstd"], writes=["rstd"])
        for kc in range(8):
            P.add("dve", lambda e, kc=kc: e.tensor_tensor(out=B.tmp[kc % 2][:, 0:Tn], in0=B.ht[:, kc, 0:Tn], in1=B.rstd[:, 0:Tn],
                                                         op=ALU.mult), reads=[("ht", kc), "rstd"], writes=[("tmp", kc % 2)])
            bias = Bvec[:, kc:kc + 1] if Bvec is not None else V("zero")
            P.add("act", lambda e, kc=kc, bias=bias: e.activation(out=B.xn[:, kc, 0:Tn], in_=B.tmp[kc % 2][:, 0:Tn], func=AF.Identity,
                                                                  bias=bias, scale=Avec[:, kc:kc + 1]),
                  reads=[("tmp", kc % 2), "coef", "modv", "vecs"], writes=[("xn", kc)])

    JG = 2
    NG = (NJ + JG - 1) // JG

    def alloc_ffn(B, Tm):
        B.H = alloc(NJ * Tm, BF16, shape=[NJ, Tm])
        B.wg = [alloc(8 * JG * 128, BF16, shape=[8, JG * 128]) for _ in range(2)]
        B.wu = [alloc(8 * JG * 128, BF16, shape=[8, JG * 128]) for _ in range(2)]
        B.wd = alloc(NJ * D, BF16, shape=[NJ, D])
        B.sg = [alloc(Tm) for _ in range(2)]

    gcount = [0]

    def ffn(B, Tn, l, which, hg):
        wgd, wud, wdd = wb[f"w{which}_gate", l], wb[f"w{which}_up", l], wb[f"w{which}_down", l]
        kg, ku, kd = WK(f"w{which}_gate", l), WK(f"w{which}_up", l), WK(f"w{which}_down", l)

        def load_group(g):
            slot = (gcount[0] + g) % 2
            j0 = g * JG
            nj = min(JG, NJ - j0)
            dma("sp", B.wg[slot][:, :, 0:nj * 128], wgd[:, j0 * 128:(j0 + nj) * 128].rearrange("(k p) m -> p k m", p=128),
                kg, [("wg", slot)], f"wg{slot}")
            dma("sp", B.wu[slot][:, :, 0:nj * 128], wud[:, j0 * 128:(j0 + nj) * 128].rearrange("(k p) m -> p k m", p=128),
                ku, [("wu", slot)], f"wu{slot}")

        load_group(0)
        for g in range(NG):
            slot = (gcount[0] + g) % 2
            j0 = g * JG
            nj = min(JG, NJ - j0)
            if g + 1 < NG:
                load_group(g + 1)
            dma("sp", B.wd[:, j0:j0 + nj, :], wdd[j0 * 128:(j0 + nj) * 128, :].rearrange("(j p) m -> p j m", p=128),
                kd, [("wd", g)], "wd")
            for jj in range(nj):
                j = j0 + jj
                bg, bu = 1 + (j % 2), 3 + (j % 2)
                for kc in range(8):
                    P.add("pe", lambda e, kc=kc, jj=jj, slot=slot, bg=bg: e.matmul(
                        ps[bg][:, 0:Tn], lhsT=B.wg[slot][:, kc, jj * 128:(jj + 1) * 128], rhs=B.xn[:, kc, 0:Tn],
                        start=(kc == 0), stop=(kc == 7)), reads=[("wg", slot), ("xn", kc)], writes=[PSK(bg)])
                for kc in range(8):
                    P.add("pe", lambda e, kc=kc, jj=jj, slot=slot, bu=bu: e.matmul(
                        ps[bu][:, 0:Tn], lhsT=B.wu[slot][:, kc, jj * 128:(jj + 1) * 128], rhs=B.xn[:, kc, 0:Tn],
                        start=(kc == 0), stop=(kc == 7)), reads=[("wu", slot), ("xn", kc)], writes=[PSK(bu)])
                P.add("act", lambda e, j=j, bg=bg: e.activation(out=B.sg[j % 2][:, 0:Tn], in_=ps[bg][:, 0:Tn], func=AF.Silu),
                      reads=[PSK(bg)], writes=[("sg", j % 2)])
                P.add("dve", lambda e, j=j, bu=bu: e.tensor_tensor(out=B.H[:, j, 0:Tn], in0=ps[bu][:, 0:Tn], in1=B.sg[j % 2][:, 0:Tn],
                                                                  op=ALU.mult), reads=[PSK(bu), ("sg", j % 2)], writes=[("H", j)])
        gcount[0] += NG
        for c in range(8):
            bd = 5 + (c % 2)
            for j in range(NJ):
                P.add("pe", lambda e, c=c, j=j, bd=bd: e.matmul(ps[bd][:, 0:Tn], lhsT=B.wd[:, j, c * 128:(c + 1) * 128],
                                                               rhs=B.H[:, j, 0:Tn], start=(j == 0), stop=(j == NJ - 1)),
                      reads=[("wd", j // JG), ("H", j)], writes=[PSK(bd)])
            P.add("dve", lambda e, c=c, bd=bd: e.scalar_tensor_tensor(out=B.ht[:, c, 0:Tn], in0=ps[bd][:, 0:Tn], scalar=hg[:, c:c + 1],
                                                                     in1=B.ht[:, c, 0:Tn], op0=ALU.mult, op1=ALU.add),
                  reads=[PSK(bd), ("ht", c), "coef"], writes=[("ht", c)])

    def alloc_win(B, Tm):
        B.win = alloc(8 * NWIN, BF16, shape=[8, NWIN])
        B.wuq = alloc(2 * 768, BF16, shape=[2, 768])
        B.zst = alloc(NZ * Tm, BF16, shape=[NZ, Tm])
        P.add("dve", lambda e: e.memset(B.zst, 0.0), writes=[("zst", s_) for s_ in range(NZ)])
        B.mst = alloc(2 * Tm, BF16, shape=[2, Tm])
        B.cqn = alloc(2 * Tm, BF16, shape=[2, Tm])
        B.f1 = [alloc(Tm) for _ in range(3)]
        B.rW = alloc(2 * Tm, shape=[2, Tm])
        B.rQ = alloc(2 * Tm, shape=[2, Tm])
        B.rK = alloc(2 * Tm, shape=[2, Tm])


    rr = [0]

    def nb():
        rr[0] = rr[0] % 7 + 1
        return rr[0]

    def win_proj(B, Tn, l, rope, tok0, zdst, zcol0, mdst, mcol0, halo):
        if rope:
            dma("sp", B.rW[:, :, 0:Tn], ropeW.rearrange("(a p) c -> p a c", p=128)[:, :, tok0:tok0 + Tn], [], ["rW"], "rW")
            dma("sp", B.rQ[0:96, :, 0:Tn], ropeQ.rearrange("(a p) c -> p a c", p=96)[:, :, tok0:tok0 + Tn], [], ["rQ"], "rQ")
            dma("sp", B.rK[0:32, :, 0:Tn], ropeK.rearrange("(a p) c -> p a c", p=32)[:, :, tok0:tok0 + Tn], [], ["rK"], "rK")

        def group(M, col0):
            b = nb()
            for kc in range(8):
                P.add("pe", lambda e, kc=kc, b=b: e.matmul(ps[b][0:M, 0:Tn], lhsT=B.win[:, kc, col0:col0 + M], rhs=B.xn[:, kc, 0:Tn],
                                                          start=(kc == 0), stop=(kc == 7)),
                      reads=["win", ("xn", kc)], writes=[PSK(b)])
            return b

        fi = [0]

        def ftmp():
            fi[0] = (fi[0] + 1) % 3
            return fi[0]

        def copy_out(b, M, dst, key):
            P.add("act", lambda e: e.activation(out=dst, in_=ps[b][0:M, 0:Tn], func=AF.Copy), reads=[PSK(b)], writes=[key])

        def rope_out(bA, bB, M, tab, tabkey, dst, key):
            i0, i1 = ftmp(), ftmp()
            P.add("dve", lambda e: e.tensor_tensor(out=B.f1[i0][0:M, 0:Tn], in0=ps[bA][0:M, 0:Tn], in1=tab[0:M, 0, 0:Tn], op=ALU.mult),
                  reads=[PSK(bA), tabkey], writes=[("f1", i0)])
            P.add("dve", lambda e: e.tensor_tensor(out=B.f1[i1][0:M, 0:Tn], in0=ps[bB][0:M, 0:Tn], in1=tab[0:M, 1, 0:Tn], op=ALU.mult),
                  reads=[PSK(bB), tabkey], writes=[("f1", i1)])
            P.add("dve", lambda e: e.tensor_tensor(out=dst, in0=B.f1[i0][0:M, 0:Tn], in1=B.f1[i1][0:M, 0:Tn], op=ALU.add),
                  reads=[("f1", i0), ("f1", i1)], writes=[key])

        def rstd_from(bank_ss, n):
            P.add("act", lambda e: e.activation(out=B.rstd[:, 0:Tn], in_=ps[bank_ss][:, 0:Tn], func=AF.Sqrt, bias=V("eps"), scale=1.0 / n),
                  reads=[PSK(bank_ss), "vecs"], writes=["rstd"])
            P.add("dve", lambda e: e.reciprocal(out=B.rstd[:, 0:Tn], in_=B.rstd[:, 0:Tn]), reads=["rstd"], writes=["rstd"])

        b0 = group(128, 0)
        b1 = group(64, 128)
        P.add("act", lambda e: e.activation(out=B.sq[0][:, 0:Tn], in_=ps[b0][:, 0:Tn], func=AF.Square), reads=[PSK(b0)], writes=[("sq", 0)])
        P.add("act", lambda e: e.activation(out=B.sq[1][0:64, 0:Tn], in_=ps[b1][0:64, 0:Tn], func=AF.Square), reads=[PSK(b1)], writes=[("sq", 1)])
        P.add("pe", lambda e: e.matmul(ps[0][:, 0:Tn], lhsT=ones_b, rhs=B.sq[0][:, 0:Tn], start=True, stop=False),
              reads=[("sq", 0), "ones_b"], writes=[PSK(0)])
        P.add("pe", lambda e: e.matmul(ps[0][:, 0:Tn], lhsT=ones_b[0:64, :], rhs=B.sq[1][0:64, 0:Tn], start=False, stop=True),
              reads=[("sq", 1), "ones_b"], writes=[PSK(0)])
        rstd_from(0, 192)
        P.add("dve", lambda e: e.scalar_tensor_tensor(out=B.cqn[:, 0, 0:Tn], in0=ps[b0][:, 0:Tn], scalar=V("gq", l * 2, 1), in1=B.rstd[:, 0:Tn],
                                                     op0=ALU.mult, op1=ALU.mult), reads=[PSK(b0), "rstd", "vecs"], writes=[("cqn", 0)])
        P.add("dve", lambda e: e.scalar_tensor_tensor(out=B.cqn[0:64, 1, 0:Tn], in0=ps[b1][0:64, 0:Tn], scalar=V("gq", l * 2 + 1, 1)[0:64, :],
                                                     in1=B.rstd[0:64, 0:Tn], op0=ALU.mult, op1=ALU.mult),
              reads=[PSK(b1), "rstd", "vecs"], writes=[("cqn", 1)])
        for h in range(4):
            banks = []
            for rot in ([0, 1] if rope else [0]):
                b = nb()
                c0 = rot * 384 + h * 96
                P.add("pe", lambda e, b=b, c0=c0: e.matmul(ps[b][0:96, 0:Tn], lhsT=B.wuq[:, 0, c0:c0 + 96], rhs=B.cqn[:, 0, 0:Tn], start=True, stop=False),
                      reads=["wuq", ("cqn", 0)], writes=[PSK(b)])
                P.add("pe", lambda e, b=b, c0=c0: e.matmul(ps[b][0:96, 0:Tn], lhsT=B.wuq[0:64, 1, c0:c0 + 96], rhs=B.cqn[0:64, 1, 0:Tn], start=False, stop=True),
                      reads=["wuq", ("cqn", 1)], writes=[PSK(b)])
                banks.append(b)
            if rope:
                rope_out(banks[0], banks[1], 96, B.rQ, "rQ", B.zst[0:96, ZQ + h, 0:Tn], ("zst", ZQ + h))
            else:
                copy_out(banks[0], 96, B.zst[0:96, ZQ + h, 0:Tn], ("zst", ZQ + h))
        bkv = group(128, 192)
        P.add("act", lambda e: e.activation(out=B.sq[0][:, 0:Tn], in_=ps[bkv][:, 0:Tn], func=AF.Square), reads=[PSK(bkv)], writes=[("sq", 0)])
        P.add("pe", lambda e: e.matmul(ps[0][:, 0:Tn], lhsT=ones_b, rhs=B.sq[0][:, 0:Tn], start=True, stop=True),
              reads=[("sq", 0), "ones_b"], writes=[PSK(0)])
        rstd_from(0, 128)
        P.add("dve", lambda e: e.scalar_tensor_tensor(out=B.mst[:, 0, 0:Tn], in0=ps[bkv][:, 0:Tn], scalar=V("gkv", l, 1), in1=B.rstd[:, 0:Tn],
                                                     op0=ALU.mult, op1=ALU.mult), reads=[PSK(bkv), "rstd", "vecs"], writes=[("mst", 0)])
        bA = group(32, 320)
        if rope:
            bB = group(32, 2144)
            rope_out(bA, bB, 32, B.rK, "rK", B.mst[0:32, 1, 0:Tn], ("mst", 1))
        else:
            copy_out(bA, 32, B.mst[0:32, 1, 0:Tn], ("mst", 1))
        for c in range(2):
            b = group(128, 352 + c * 128)
            copy_out(b, 128, B.zst[:, ZSCB + c, 0:Tn], ("zst", ZSCB + c))
        for c in range(2):
            bc = group(128, 608 + c * 128)
            bx = group(128, 864 + c * 128)
            i = ftmp()
            P.add("act", lambda e, i=i, bx=bx: e.activation(out=B.f1[i][:, 0:Tn], in_=ps[bx][:, 0:Tn], func=AF.Copy), reads=[PSK(bx)], writes=[("f1", i)])
            P.add("dve", lambda e, i=i, bc=bc, c=c: e.tensor_tensor(out=B.zst[:, ZSCT + c, 0:Tn], in0=ps[bc][:, 0:Tn], in1=B.f1[i][:, 0:Tn], op=ALU.mult),
                  reads=[PSK(bc), ("f1", i)], writes=[("zst", ZSCT + c)])
        for c in range(2):
            bA = group(128, 1120 + c * 128)
            if rope:
                bB = group(128, 2176 + c * 128)
                rope_out(bA, bB, 128, B.rW, "rW", B.zst[:, ZWAQ + c, 0:Tn], ("zst", ZWAQ + c))
            else:
                copy_out(bA, 128, B.zst[:, ZWAQ + c, 0:Tn], ("zst", ZWAQ + c))
        bA = group(128, 1376)
        if rope:
            bB = group(128, 2432)
            rope_out(bA, bB, 128, B.rW, "rW", B.zst[:, ZWAK, 0:Tn], ("zst", ZWAK))
        else:
            copy_out(bA, 128, B.zst[:, ZWAK, 0:Tn], ("zst", ZWAK))
        b = group(128, 1504)
        copy_out(b, 128, B.zst[:, ZWAV, 0:Tn], ("zst", ZWAV))
        for c in range(2):
            ba = group(128, 1632 + c * 128)
            bg = group(128, 1888 + c * 128)
            i = ftmp()
            P.add("act", lambda e, i=i, bg=bg: e.activation(out=B.f1[i][:, 0:Tn], in_=ps[bg][:, 0:Tn], func=AF.Sigmoid), reads=[PSK(bg)], writes=[("f1", i)])
            P.add("dve", lambda e, i=i, ba=ba, c=c: e.tensor_tensor(out=B.zst[:, ZCFU + c, 0:Tn], in0=ps[ba][:, 0:Tn], in1=B.f1[i][:, 0:Tn], op=ALU.mult),
                  reads=[PSK(ba), ("f1", i)], writes=[("zst", ZCFU + c)])
        zk = [("zst", s) for s in range(NZ)]
        dma("sp", zdst.rearrange("(s p) c -> p s c", p=128)[:, :, zcol0:zcol0 + Tn], B.zst[:, :, 0:Tn], zk, ["zdst"], "zst")
        dma("sp", mdst[0:128, mcol0:mcol0 + Tn], B.mst[:, 0, 0:Tn], [("mst", 0)], ["mdst"], "mst0")
        dma("sp", mdst[128:160, mcol0:mcol0 + Tn], B.mst[0:32, 1, 0:Tn], [("mst", 1)], ["mdst"], "mst1")
        for (which, c0) in halo:
            xh = xin_halo.rearrange("(s p) c -> p s c", p=128)
            dma("sp", xh[:, 0:2, which * 128:(which + 1) * 128], B.zst[:, 6:8, c0:c0 + 128], zk, ["xin_halo"], "hal0")
            dma("sp", xh[:, 2:6, which * 128:(which + 1) * 128], B.zst[:, 10:14, c0:c0 + 128], zk, ["xin_halo"], "hal1")

    def load_win(B, l):
        dma("sp", B.win[:, 0:4, :], wb["w_in", l][0:512, :].rearrange("(k p) m -> p k m", p=128), WK("w_in", l), ["win"], "win0")
        dma("sp", B.win[:, 4:8, :], wb["w_in", l][512:1024, :].rearrange("(k p) m -> p k m", p=128), WK("w_in", l), ["win"], "win1")
        dma("sp", B.wuq[:, 0, :], wb["w_mla_uq", l][0:128, :], WK("w_mla_uq", l), ["wuq"], "wuq0")
        dma("sp", B.wuq[0:64, 1, :], wb["w_mla_uq", l][128:192, :], WK("w_mla_uq", l), ["wuq"], "wuq1")

    def load_h(B, src, col0, Tn):
        dma("sp", B.ht[:, :, 0:Tn], src.rearrange("(k p) c -> p k c", p=128)[:, :, col0:col0 + Tn], ["hsrc"], [("ht", k) for k in range(8)], "hld")

    def store_h(B, dst, col0, Tn, key="hdst"):
        dma("sp", dst.rearrange("(k p) c -> p k c", p=128)[:, :, col0:col0 + Tn], B.ht[:, :, 0:Tn], [("ht", k) for k in range(8)], [key], "hst")

    def tiles_lat_ctx():
        out = [(False, 0, T, t * T) for t in range(NT)]
        out.append((True, 1, CTX, 0))
        return out

    def phase_ffn(l_prev, l_next, first):
        base = bump[0]
        B = alloc_common(T)
        alloc_ffn(B, T)
        if l_next is not None:
            alloc_win(B, T)
        win_loaded = [False]
        for (is_ctx, j, Tn, tok0) in tiles_lat_ctx():
            if first:
                load_h(B, ctxT if is_ctx else xT, tok0, Tn)
            else:
                if is_ctx and l_next is None:
                    continue
                load_h(B, hcbuf if is_ctx else hbuf, tok0, Tn)
            if l_prev is not None:
                norm_mod(B, Tn, Acoef[:, l_prev, j, 2, :], Bsh(l_prev, j, 2), "f2")
                ffn(B, Tn, l_prev, 2, HG[:, l_prev, j, 2, :])
            if l_next is not None:
                norm_mod(B, Tn, Acoef[:, l_next, j, 0, :], Bsh(l_next, j, 0), "f1")
                ffn(B, Tn, l_next, 1, HG[:, l_next, j, 0, :])
                if not win_loaded[0]:
                    load_win(B, l_next)
                    win_loaded[0] = True
                store_h(B, hcbuf if is_ctx else hbuf, tok0, Tn)
                norm_mod(B, Tn, Acoef[:, l_next, j, 1, :], Bsh(l_next, j, 1), "mx")
                halo = []
                if not is_ctx and tok0 == 0:
                    halo.append((0, 0))
                if not is_ctx and tok0 == S_OWN - T:
                    halo.append((1, T - 128))
                win_proj(B, Tn, l_next, not is_ctx, tok0, zcbuf if is_ctx else zbuf, 128 + tok0,
                         ckvc if is_ctx else xin_mla, tok0, halo)
            else:
                norm_mod(B, Tn, V("gfin", 0, 8), None, "fin")
                for kc in range(8):
                    P.add("dve", lambda e, kc=kc, Tn=Tn: e.tensor_tensor(out=B.tmp[kc % 2][:, 0:Tn], in0=B.ht[:, kc, 0:Tn], in1=B.rstd[:, 0:Tn], op=ALU.mult),
                          reads=[("ht", kc), "rstd"], writes=[("tmp", kc % 2)])
                    P.add("dve", lambda e, kc=kc, Tn=Tn: e.tensor_scalar(out=B.ht[:, kc, 0:Tn], in0=B.tmp[kc % 2][:, 0:Tn], scalar1=V("gfin", kc, 1), scalar2=None,
                                                                 op0=ALU.mult), reads=[("tmp", kc % 2), "vecs"], writes=[("ht", kc)])
                store_h(B, outT, tok0, Tn, key="out")
        P.barrier()
        bump[0] = base

    RG = RG_OVERRIDE[0] or [[0, 1], [2, 3], [4, 5], [6, 7]]

    def exchange(l):
        base = bump[0]
        P.add("pool", lambda e: e.collective_compute("AllGather", ALU.bypass, replica_groups=RG, ins=[xin_mla.opt()], outs=[xout_mla.opt()]),
              reads=["mdst"], writes=["xout_mla"]).signal = True
        P.add("pool", lambda e: e.collective_compute("AllGather", ALU.bypass, replica_groups=RG, ins=[xin_halo.opt()], outs=[xout_halo.opt()]),
              reads=["xin_halo"], writes=["xout_halo"]).signal = True
        hl = alloc(6 * 128, BF16, shape=[6, 128])
        hr = alloc(6 * 128, BF16, shape=[6, 128])
        xo = xout_halo.rearrange("(r s p) c -> r p s c", r=2, p=128)
        dma("sp", hl, xo[0, :, :, 128:256], ["xout_halo"], ["hl"], "hl")
        dma("sp", hr, xo[1, :, :, 0:128], ["xout_halo"], ["hr"], "hr")
        P.add("dve", lambda e: e.tensor_scalar(out=hl, in0=hl, scalar1=V("lm"), scalar2=None, op0=ALU.mult), reads=["hl", "vecs"], writes=["hl"])
        P.add("dve", lambda e: e.tensor_scalar(out=hr, in0=hr, scalar1=V("rm"), scalar2=None, op0=ALU.mult), reads=["hr", "vecs"], writes=["hr"])
        zb = zbuf.rearrange("(s p) c -> p s c", p=128)
        dma("sp", zb[:, 6:8, 0:128], hl[:, 0:2, :], ["hl"], ["zdst"], "hl")
        dma("sp", zb[:, 10:14, 0:128], hl[:, 2:6, :], ["hl"], ["zdst"], "hl")
        dma("sp", zb[:, 6:8, 128 + S_OWN:ZW], hr[:, 0:2, :], ["hr"], ["zdst"], "hr")
        dma("sp", zb[:, 10:14, 128 + S_OWN:ZW], hr[:, 2:6, :], ["hr"], ["zdst"], "hr")
        P.barrier()
        bump[0] = base

    NKC = 66

    def alloc_kv():
        K = FFNBufs()
        K.KT = alloc(4 * NKC * 128, BF16, shape=[4, NKC * 128])
        K.Vx = alloc(NKC * 384, BF16, shape=[NKC, 384])
        K.wukv = alloc(512, BF16)
        K.ckt = [alloc(512, BF16) for _ in range(2)]
        return K

    def kv_build(K, l):
        dma("sp", K.wukv, wb["w_mla_ukv", l], WK("w_mla_ukv", l), ["wukv"], "wukv")
        P.add("dve", lambda e: e.memset(K.Vx, 0.0), writes=["Vx"])
        P.add("dve", lambda e: e.memset(K.Vx.rearrange("p k (a c) -> p k a c", a=2)[:, :, :, 64:65], 1.0), writes=["Vx"])
        srcs = [(xout_mla[r * 160:r * 160 + 128, t8 * 512:(t8 + 1) * 512], 512, r * S_OWN + t8 * 512) for r in range(2) for t8 in range(8)]
        srcs.append((ckvc[0:128, 0:CTX], CTX, 2 * S_OWN))
        for r in range(2):
            for h in range(4):
                dma("sp", K.KT[64:96, h, r * S_OWN:(r + 1) * S_OWN], xout_mla[r * 160 + 128:r * 160 + 160, :], ["xout_mla"], [("KTr", h)], f"ktr{h}")
        for h in range(4):
            dma("sp", K.KT[64:96, h, 2 * S_OWN:2 * S_OWN + CTX], ckvc[128:160, :], ["mdstc"], [("KTr", h)], f"ktr{h}")
        wv = K.wukv.rearrange("p (h c) -> p h c", h=4)[:, :, 64:128]
        for i, (src, n, key0) in enumerate(srcs):
            slot = i % 2
            dma("sp", K.ckt[slot][:, 0:n], src, ["xout_mla", "mdstc"], [("ckt", slot)], f"ckt{slot}")
            for h in range(4):
                b = 1 + (h % 2)
                P.add("pe", lambda e, h=h, b=b, slot=slot, n=n: e.matmul(ps[b][0:64, 0:n], lhsT=K.wukv[:, h * 128:h * 128 + 64], rhs=K.ckt[slot][:, 0:n],
                                                                        start=True, stop=True), reads=["wukv", ("ckt", slot)], writes=[PSK(b)])
                eng = "act" if h % 2 == 0 else "dve"
                if eng == "act":
                    P.add("act", lambda e, h=h, b=b, n=n, key0=key0: e.activation(out=K.KT[0:64, h, key0:key0 + n], in_=ps[b][0:64, 0:n], func=AF.Copy),
                          reads=[PSK(b)], writes=[("KTn", h)])
                else:
                    P.add("dve", lambda e, h=h, b=b, n=n, key0=key0: e.tensor_copy(out=K.KT[0:64, h, key0:key0 + n], in_=ps[b][0:64, 0:n]),
                          reads=[PSK(b)], writes=[("KTn", h)])
            for kb in range(n // 128):
                b = 3 + (kb % 2)
                kc = key0 // 128 + kb
                P.add("pe", lambda e, kb=kb, b=b, slot=slot: e.matmul(ps[b][:, 0:256], lhsT=K.ckt[slot][:, kb * 128:(kb + 1) * 128], rhs=wv,
                                                                     start=True, stop=True), reads=["wukv", ("ckt", slot)], writes=[PSK(b)])
                pv = ps[b][:, 0:256].rearrange("p (a b c) -> p a b c", a=2, b=2)
                vo = K.Vx[:, kc, :].rearrange("p (a b c) -> p a b c", a=2, b=3)
                P.add("act", lambda e, pv=pv, vo=vo: e.activation(out=vo[:, :, 0, :], in_=pv[:, :, 0, :], func=AF.Copy), reads=[PSK(b)], writes=["Vx"])
                P.add("dve", lambda e, pv=pv, vo=vo: e.tensor_copy(out=vo[:, :, 2, :], in_=pv[:, :, 1, :]), reads=[PSK(b)], writes=["Vx"])

    def alloc_mix():
        M = FFNBufs()
        M.ht = alloc(8 * T, shape=[8, T])
        M.mix = alloc(8 * T, BF16, shape=[8, T])
        M.wout = alloc(8 * D, BF16, shape=[8, D])
        M.QT = alloc(4 * T, BF16, shape=[4, T])
        M.PT = [alloc(2 * T, BF16) for _ in range(3)]
        M.scb = alloc(2 * T, BF16, shape=[2, T])
        M.sct = alloc(2 * (T + 2), BF16, shape=[2, T + 2])
        M.waq = alloc(2 * T, BF16, shape=[2, T])
        M.wak = alloc(T + 256, BF16)
        M.wav = alloc(T + 256, BF16)
        M.cfu = alloc(2 * (T + 32), BF16, shape=[2, T + 32])
        M.Vw = alloc(6 * 384, BF16, shape=[6, 384])
        M.wakc = alloc(CTX, BF16)
        M.Vwc = alloc(2 * 384, BF16, shape=[2, 384])
        M.acc = [alloc(T) for _ in range(2)]
        M.rinv = alloc(T)
        M.bc = alloc(T)
        M.wavc = M.bc.bitcast(BF16)[:, 0:CTX]
        M.f = [alloc(T) for _ in range(3)]
        return M

    def transpose_to_triples(M, src, nblk, dst):
        p7 = ps[7][:, :].bitcast(BF16)
        for blk in range(nblk):
            o = (blk % 4) * 128
            P.add("pe", lambda e, blk=blk, o=o: e.transpose(out=p7[:, o:o + 128], in_=src[:, blk * 128:(blk + 1) * 128], identity=ident_b),
                  reads=["wavsrc", "ident"], writes=[PSK(7)])
            pv = p7[:, o:o + 128].rearrange("p (a c) -> p a c", a=2)
            vo = dst[:, blk, :].rearrange("p (a b c) -> p a b c", a=2, b=3)
            P.add("act", lambda e, pv=pv, vo=vo: e.activation(out=vo[:, :, 0, :], in_=pv, func=AF.Copy), reads=[PSK(7)], writes=["Vw"])
            P.add("dve", lambda e, pv=pv, vo=vo: e.tensor_copy(out=vo[:, :, 2, :], in_=pv), reads=[PSK(7)], writes=["Vw"])

    def init_triples(buf):
        P.add("dve", lambda e: e.memset(buf, 0.0), writes=["Vw"])
        P.add("dve", lambda e: e.memset(buf.rearrange("p k (a c) -> p k a c", a=2)[:, :, :, 64:65], 1.0), writes=["Vw"])

    def normalize(M, Tn, ob, lo, l, h, chunk, sink, bcb=5):
        if 'norm' in DBG_SKIP:
            return
        sp = 64 if lo else 0
        mrows = 64 if lo else 128
        r0 = 0 if lo else 64
        if sink:
            P.add("dve", lambda e: e.tensor_scalar(out=M.rinv[sp:sp + 1, 0:Tn], in0=ps[ob][sp:sp + 1, 0:Tn], scalar1=esink[sp:sp + 1, l, h:h + 1],
                                                  scalar2=None, op0=ALU.add), reads=[PSK(ob), "esink"], writes=["rinv"])
            P.add("dve", lambda e: e.reciprocal(out=M.rinv[sp:sp + 1, 0:Tn], in_=M.rinv[sp:sp + 1, 0:Tn]), reads=["rinv"], writes=["rinv"])
        else:
            P.add("dve", lambda e: e.reciprocal(out=M.rinv[sp:sp + 1, 0:Tn], in_=ps[ob][sp:sp + 1, 0:Tn]), reads=[PSK(ob)], writes=["rinv"])
        P.add("pe", lambda e: e.matmul(ps[bcb][0:mrows, 0:Tn], lhsT=ones_f[sp:sp + 1, 0:mrows], rhs=M.rinv[sp:sp + 1, 0:Tn], start=True, stop=True),
              reads=["rinv", "ones_f"], writes=[PSK(bcb)])
        P.add("act", lambda e: e.activation(out=M.bc[r0:r0 + 64, 0:Tn], in_=ps[bcb][r0:r0 + 64, 0:Tn], func=AF.Copy), reads=[PSK(bcb)], writes=["bc"])
        P.add("dve", lambda e: e.tensor_tensor(out=M.mix[r0:r0 + 64, chunk, 0:Tn], in0=ps[ob][r0:r0 + 64, 0:Tn], in1=M.bc[r0:r0 + 64, 0:Tn], op=ALU.mult),
              reads=[PSK(ob), "bc"], writes=[("mix", chunk)])

    def mixers(K, M, l, is_ctx, t):
        Tn = CTX if is_ctx else T
        tok0 = 0 if is_ctx else t * T
        j = 1 if is_ctx else 0
        zv = (zcbuf if is_ctx else zbuf).rearrange("(s p) c -> p s c", p=128)
        c0 = 128 + tok0
        dma("sp", M.QT[0:96, :, 0:Tn], zv[0:96, 0:4, c0:c0 + Tn], [], ["QT"], "lq")
        dma("sp", M.scb[:, :, 0:Tn], zv[:, 4:6, c0:c0 + Tn], [], ["scb"], "lscb")
        dma("sp", M.sct[:, :, 0:Tn + 2], zv[:, 6:8, c0 - 1:c0 + Tn + 1], [], ["sct"], "lsct")
        dma("sp", M.waq[:, :, 0:Tn], zv[:, 8:10, c0:c0 + Tn], [], ["waq"], "lwaq")
        dma("sp", M.wak[:, 0:Tn + 256], zv[:, 10, c0 - 128:c0 + Tn + 128], [], ["wak"], "lwak")
        dma("sp", M.wav[:, 0:Tn + 256], zv[:, 11, c0 - 128:c0 + Tn + 128], [], ["wavsrc"], "lwav")
        dma("sp", M.cfu[:, :, 0:Tn + 30], zv[:, 12:14, c0 - 15:c0 + Tn + 15], [], ["cfu"], "lcfu")
        hsrc = hcbuf if is_ctx else hbuf
        dma("sp", M.ht[:, :, 0:Tn], hsrc.rearrange("(k p) c -> p k c", p=128)[:, :, tok0:tok0 + Tn], ["hdst"], [("ht", k) for k in range(8)], "hld")

        conv_ops = []

        def cadd(*a_, **k_):
            conv_ops.append((a_, k_))

        def wsc(c, k):
            return V("wsc", (l * 2 + c) * 3 + k, 1)

        def wcf(c, k):
            return V("wcf", (l * 2 + c) * 31 + k, 1)

        for c in (range(2) if 'sc' not in DBG_SKIP else []):
            acc = M.acc[c]
            cadd("dve", lambda e, c=c, acc=acc: e.tensor_scalar(out=acc[:, 0:Tn], in0=M.sct[:, c, 0:Tn], scalar1=wsc(c, 0), scalar2=None, op0=ALU.mult),
                  reads=["sct", "vecs"], writes=[("acc", c)])
            for k in (1, 2):
                cadd("dve", lambda e, c=c, k=k, acc=acc: e.scalar_tensor_tensor(out=acc[:, 0:Tn], in0=M.sct[:, c, k:k + Tn], scalar=wsc(c, k), in1=acc[:, 0:Tn],
                                                                                 op0=ALU.mult, op1=ALU.add), reads=["sct", ("acc", c), "vecs"], writes=[("acc", c)])
            cadd("dve", lambda e, c=c, acc=acc: e.tensor_tensor(out=M.mix[:, 2 + c, 0:Tn], in0=acc[:, 0:Tn], in1=M.scb[:, c, 0:Tn], op=ALU.mult),
                  reads=[("acc", c), "scb"], writes=[("mix", 2 + c)])
        for c in range(2):
            acc = M.acc[c]
            cadd("dve", lambda e, c=c, acc=acc: e.tensor_scalar(out=acc[:, 0:Tn], in0=M.cfu[:, c, 0:Tn], scalar1=wcf(c, 0), scalar2=V("bcf", l * 2 + c, 1),
                                                                 op0=ALU.mult, op1=ALU.add), reads=["cfu", "vecs", ("acc", c)], writes=[("acc", c)])
            for k in range(1, 31):
                cadd("dve", lambda e, c=c, k=k, acc=acc: e.scalar_tensor_tensor(out=acc[:, 0:Tn], in0=M.cfu[:, c, k:k + Tn], scalar=wcf(c, k), in1=acc[:, 0:Tn],
                                                                                 op0=ALU.mult, op1=ALU.add), reads=["cfu", ("acc", c), "vecs"], writes=[("acc", c)])
            cadd("dve", lambda e, c=c, acc=acc: e.tensor_tensor(out=M.f[c][:, 0:Tn], in0=acc[:, 0:Tn], in1=acc[:, 0:Tn], op=ALU.mult),
                  reads=[("acc", c)], writes=[("f", c)])

        kcs = [64, 65] if is_ctx else list(range(NKC))
        sc_a = 96.0 ** -0.5
        n = len(kcs)
        for h in (range(4) if 'mla' not in DBG_SKIP else []):
            ob = 4
            lo = (h % 2 == 0)
            vcol = (h // 2) * 192 + (0 if lo else 64)
            npair = n // 2
            SBK = [0, 1, 3]

            def S(p, h=h):
                w = SBK[p % 3]
                for half in range(2):
                    kc = kcs[2 * p + half]
                    P.add("pe", lambda e, kc=kc, half=half, w=w: e.matmul(psw[w][:, half * 512:half * 512 + Tn], lhsT=K.KT[0:96, h, kc * 128:(kc + 1) * 128],
                                                                         rhs=M.QT[0:96, h, 0:Tn], start=True, stop=True),
                          reads=[("KTn", h), ("KTr", h), "QT"], writes=[PSK(2 * w + half)])

            def E(p):
                w = SBK[p % 3]
                src = psw[w][:, :].rearrange("p (a c) -> p a c", a=2)[:, :, 0:Tn]
                dst = M.PT[p % 3].rearrange("p (a c) -> p a c", a=2)[:, :, 0:Tn]
                P.add("act", lambda e: e.activation(out=dst, in_=src, func=AF.Exp, scale=sc_a), reads=[PSK(2 * w), PSK(2 * w + 1)], writes=[("PT", p % 3)])

            def PV(p, ob=ob, vcol=vcol):
                for half in range(2):
                    kc = kcs[2 * p + half]
                    first = (p == 0 and half == 0)
                    last = (p == npair - 1 and half == 1)
                    P.add("pe", lambda e, kc=kc, half=half, first=first, last=last: e.matmul(
                        ps[ob][:, 0:Tn], lhsT=K.Vx[:, kc, vcol:vcol + 128], rhs=M.PT[p % 3][:, half * 512:half * 512 + Tn], start=first, stop=last),
                        reads=["Vx", ("PT", p % 3)], writes=[PSK(ob)])

            for p in range(min(3, npair)):
                S(p)
            for p in range(npair):
                E(p)
                PV(p)
                if p + 3 < npair:
                    S(p + 3)
            normalize(M, Tn, ob, lo, l, h, h // 2, False)
            per = (len(conv_ops) + 3) // 4
            for (a_, k_) in conv_ops[h * per:(h + 1) * per]:
                P.add(*a_, **k_)

        if 'mla' in DBG_SKIP:
            for (a_, k_) in conv_ops:
                P.add(*a_, **k_)
        for c in range(2):
            P.add("pe", lambda e, c=c: e.matmul(ps[6][:, 0:Tn], lhsT=ones_f, rhs=M.acc[c][:, 0:Tn], start=(c == 0), stop=(c == 1)),
                  reads=[("acc", c), "ones_f"], writes=[PSK(6)])
        for c in range(2):
            P.add("pe", lambda e, c=c: e.matmul(ps[7][:, 0:Tn], lhsT=ones_f, rhs=M.f[c][:, 0:Tn], start=(c == 0), stop=(c == 1)),
                  reads=[("f", c), "ones_f"], writes=[PSK(7)])
        P.add("act", lambda e: e.activation(out=M.f[2][:, 0:Tn], in_=ps[6][:, 0:Tn], func=AF.Identity, bias=V("zero"), scale=1.0 / 256),
              reads=[PSK(6), "vecs"], writes=[("f", 2)])
        P.add("dve", lambda e: e.tensor_tensor(out=M.f[0][:, 0:Tn], in0=M.f[2][:, 0:Tn], in1=M.f[2][:, 0:Tn], op=ALU.mult), reads=[("f", 2)], writes=[("f", 0)])
        P.add("dve", lambda e: e.scalar_tensor_tensor(out=M.f[0][:, 0:Tn], in0=ps[7][:, 0:Tn], scalar=1.0 / 256, in1=M.f[0][:, 0:Tn], op0=ALU.mult, op1=ALU.subtract),
              reads=[PSK(7), ("f", 0)], writes=[("f", 0)])
        P.add("act", lambda e: e.activation(out=M.f[0][:, 0:Tn], in_=M.f[0][:, 0:Tn], func=AF.Sqrt, bias=V("eps"), scale=1.0), reads=[("f", 0), "vecs"], writes=[("f", 0)])
        P.add("dve", lambda e: e.reciprocal(out=M.f[0][:, 0:Tn], in_=M.f[0][:, 0:Tn]), reads=[("f", 0)], writes=[("f", 0)])
        for c in range(2):
            acc = M.acc[c]
            P.add("dve", lambda e, acc=acc: e.tensor_tensor(out=acc[:, 0:Tn], in0=acc[:, 0:Tn], in1=M.f[2][:, 0:Tn], op=ALU.subtract),
                  reads=[("acc", c), ("f", 2)], writes=[("acc", c)])
            P.add("dve", lambda e, acc=acc: e.tensor_tensor(out=acc[:, 0:Tn], in0=acc[:, 0:Tn], in1=M.f[0][:, 0:Tn], op=ALU.mult),
                  reads=[("acc", c), ("f", 0)], writes=[("acc", c)])
            P.add("act", lambda e, c=c, acc=acc: e.activation(out=M.mix[:, 6 + c, 0:Tn], in_=acc[:, 0:Tn], func=AF.Silu, bias=V("bln", l * 2 + c, 1),
                                                             scale=V("gln", l * 2 + c, 1)), reads=[("acc", c), "vecs"], writes=[("mix", 6 + c)])

        sc_w = 64.0 ** -0.5
        nqb = Tn // 128
        if not is_ctx and 'tr' not in DBG_SKIP:
            transpose_to_triples(M, M.wav, nqb + 2, M.Vw)
        for g in (range(2) if 'win' not in DBG_SKIP else []):
            pr = slice(g * 64, (g + 1) * 64)
            for qb in range(nqb):
                setb = (g * nqb + qb) % 2
                sbanks = [0, 1, 2] if setb == 0 else [3, 6, 7]
                po = setb * 512
                if is_ctx:
                    kbl = [("ctx", 0, None), ("ctx", 1, None)]
                else:
                    mP = 2 if (t == 0 and qb == 0) else 0
                    mN = 3 if (t == NT - 1 and qb == nqb - 1) else 1
                    kbl = [("loc", qb, mP), ("loc", qb + 1, None), ("loc", qb + 2, mN), ("ctx", 0, None), ("ctx", 1, None)]
                for i, (kind, blk, mi) in enumerate(kbl):
                    b, off = sbanks[i // 2], (i % 2) * 256
                    ksrc = M.wakc if kind == "ctx" else M.wak
                    outv = ps[b][:, off:off + 256].rearrange("p (a c) -> p a c", a=2)
                    kl = ksrc[pr, blk * 128:(blk + 1) * 128]
                    qr = M.waq[pr, :, qb * 128:(qb + 1) * 128]
                    P.add("pe", lambda e, kl=kl, qr=qr, outv=outv, mi=mi: e.matmul(outv, lhsT=kl, rhs=qr, start=True, stop=(mi is None)),
                          reads=["wak", "wakc", "waq"], writes=[PSK(b)])
                    if mi is not None:
                        mo = ps[b][:, off:off + 256]
                        mr = masks_b[:, mi, :]
                        P.add("pe", lambda e, mo=mo, mr=mr: e.matmul(mo, lhsT=ident_b, rhs=mr, start=False, stop=True),
                              reads=["ident", "masks"], writes=[PSK(b)])
                ntile = (len(kbl) + 1) // 2
                for i in range(ntile):
                    w = 256 * min(2, len(kbl) - 2 * i)
                    eo = M.PT[i][:, po:po + w]
                    ei = ps[sbanks[i]][:, 0:w]
                    P.add("act", lambda e, eo=eo, ei=ei: e.activation(out=eo, in_=ei, func=AF.Exp, scale=sc_w),
                          reads=[PSK(sbanks[i])], writes=[("PTw", i, setb)])
                for c in range(2):
                    ob = 4 + c
                    for i, (kind, blk, mi) in enumerate(kbl):
                        vsrc = M.Vwc if kind == "ctx" else M.Vw
                        col = po + (i % 2) * 256 + c * 128
                        vl = vsrc[:, blk, g * 192 + c * 64:g * 192 + c * 64 + 128]
                        pr_ = M.PT[i // 2][:, col:col + 128]
                        oo = ps[ob][:, qb * 128:(qb + 1) * 128]
                        last = (i == len(kbl) - 1)
                        P.add("pe", lambda e, vl=vl, pr_=pr_, oo=oo, i=i, last=last: e.matmul(oo, lhsT=vl, rhs=pr_, start=(i == 0), stop=last),
                              reads=["Vw", ("PTw", i // 2, setb)], writes=[PSK(ob)])
            for c in range(2):
                normalize(M, Tn, 4 + c, c == 0, l, 2 * g + c, 4 + g, True, bcb=2)

        G2 = HG[:, l, j, 1, :]
        for c in range(8):
            bd = 1 + (c % 2)
            for k in range(8):
                P.add("pe", lambda e, c=c, k=k, bd=bd: e.matmul(ps[bd][:, 0:Tn], lhsT=M.wout[:, k, c * 128:(c + 1) * 128], rhs=M.mix[:, k, 0:Tn],
                                                               start=(k == 0), stop=(k == 7)), reads=["wout", ("mix", k)], writes=[PSK(bd)])
            P.add("dve", lambda e, c=c, bd=bd: e.scalar_tensor_tensor(out=M.ht[:, c, 0:Tn], in0=ps[bd][:, 0:Tn], scalar=G2[:, c:c + 1], in1=M.ht[:, c, 0:Tn],
                                                                     op0=ALU.mult, op1=ALU.add), reads=[PSK(bd), ("ht", c), "coef"], writes=[("ht", c)])
        dma("sp", hsrc.rearrange("(k p) c -> p k c", p=128)[:, :, tok0:tok0 + Tn], M.ht[:, :, 0:Tn], [("ht", k) for k in range(8)], ["hdst"], "hst")

    def phase_mix(l):
        base = bump[0]
        K = alloc_kv()
        kv_build(K, l)
        M = alloc_mix()
        dma("sp", M.wout, wb["w_out", l].rearrange("(k p) m -> p k m", p=128), WK("w_out", l), ["wout"], "wout")
        zc = zcbuf.rearrange("(s p) c -> p s c", p=128)
        dma("sp", M.wakc, zc[:, 10, 128:128 + CTX], [], ["wakc"], "lwakc")
        dma("sp", M.wavc, zc[:, 11, 128:128 + CTX], [], ["wavsrc"], "lwavc")
        init_triples(M.Vw)
        init_triples(M.Vwc)
        transpose_to_triples(M, M.wavc, 2, M.Vwc)
        for t in range(min(NT, DBG_MIXT[0])):
            mixers(K, M, l, False, t)
        if l == 0 and DBG_MIXT[0] >= NT:
            mixers(K, M, l, True, 0)
        P.barrier()
        bump[0] = base

    zt = alloc(NZ * 128, BF16, shape=[NZ, 128])
    P.add("dve", lambda e: e.memset(zt, 0.0), writes=["zt"])
    zcv = zcbuf.rearrange("(s p) c -> p s c", p=128)
    dma("sp", zcv[:, :, 0:128], zt, ["zt"], ["zc0"], "zc0")
    dma("sp", zcv[:, :, 128 + CTX:ZCW], zt, ["zt"], ["zc1"], "zc1")
    P.barrier()
    bump[0] = persist_end

    stage = STAGE[0]
    phase_ffn(None, 0, True)
    if stage >= 1.2:
        exchange(0)
    if stage >= 1.5:
        phase_mix(0)
    if stage >= 2.5:
        phase_ffn(0, 1, False)
    if stage >= 2.7:
        exchange(1)
        phase_mix(1)
    if stage >= 3:
        phase_ffn(1, None, False)
    if stage < 3:
        base = bump[0]
        Bd = alloc_common(T)
        for t in range(NT):
            load_h(Bd, hbuf, t * T, T)
            store_h(Bd, outT, t * T, T, key="out")
        if 'dumpc' in DBG_SKIP:
            load_h(Bd, hcbuf, 0, CTX)
            store_h(Bd, outT, 0, CTX, key="out")
        P.barrier()
        bump[0] = base
    P.barrier(final=True)
    P.emit(nc, st)
    st.close()
    return nc


STAGE = [3]
DBG_MIXT = [99]
DBG_SKIP = set()
RG_OVERRIDE = [None]
MOD_RANKS = [2]
MOD_GROUPS = [[[0, 1], [2, 3], [4, 5], [6, 7]]]


def _rope_tables(half):
    pos = half * S_OWN + np.arange(S_OWN)
    row = (pos // 64).astype(np.float32)
    col = (pos % 64).astype(np.float32)

    def tab(d_rot):
        d_ax = d_rot // 2
        inv = (10000.0 ** (-np.arange(0, d_ax, 2, dtype=np.float32) / d_ax)).astype(np.float32)
        ar = row[:, None] * inv[None, :]
        ac = col[:, None] * inv[None, :]
        cr, sr, cc_, sc_ = np.cos(ar), np.sin(ar), np.cos(ac), np.sin(ac)
        C = np.concatenate([cr, cr, cc_, cc_], axis=1).T.astype(np.float32)
        Sg = np.concatenate([-sr, sr, -sc_, sc_], axis=1).T.astype(np.float32)
        return C, Sg

    Cw, Sw = tab(64)
    ropeW = np.concatenate([np.tile(Cw, (2, 1)), np.tile(Sw, (2, 1))], axis=0)
    Cm, Sm = tab(32)
    ropeK = np.concatenate([Cm, Sm], axis=0)
    Cq = np.concatenate([np.ones((64, S_OWN), np.float32), Cm], axis=0)
    Sq = np.concatenate([np.zeros((64, S_OWN), np.float32), Sm], axis=0)
    ropeQ = np.concatenate([Cq, Sq], axis=0)
    return np.ascontiguousarray(ropeW), np.ascontiguousarray(ropeQ), np.ascontiguousarray(ropeK)


def _masks(half):
    kp = np.arange(128)[:, None]
    qp = np.arange(128)[None, :]
    mP = np.where(qp <= kp, 0.0, NEGM).astype(np.float32)
    mN = np.where(kp <= qp, 0.0, NEGM).astype(np.float32)
    neg = np.full((128, 128), NEGM, np.float32)
    kinds = [mP, mN, mP if half == 1 else neg, mN if half == 0 else neg]
    m = np.stack([np.concatenate([k, k], axis=1) for k in kinds], axis=1)
    return np.ascontiguousarray(m.reshape(128, 4 * 256))


def _fm(v):
    v = np.asarray(v, np.float32)
    lead = v.shape[:-1]
    n = v.shape[-1] // 128
    return np.moveaxis(v.reshape(lead + (n, 128)), -1, 0)


def _pack_vecs(inp, half, b):
    vec = np.zeros((128, NV), np.float32)

    def put(name, arr):
        arr = np.asarray(arr, np.float32).reshape(128, -1)
        w = dict(_VSPEC)[name]
        assert arr.shape[1] == w, (name, arr.shape, w)
        vec[:, VOFF[name]:VOFF[name] + w] = arr

    put("gf1", _fm(inp["g_ffn1"]))
    put("gmix", _fm(inp["g_mix"]))
    put("gf2", _fm(inp["g_ffn2"]))
    put("bmod", _fm(inp["b_mod"]))
    put("gfin", _fm(inp["g_final"]))
    gq = np.zeros((L, 256), np.float32)
    gq[:, :192] = inp["g_mla_q"]
    put("gq", _fm(gq))
    put("gkv", _fm(inp["g_mla_kv"]))
    wsc = np.asarray(inp["w_sc_conv"], np.float32)
    put("wsc", np.transpose(wsc.reshape(L, 3, 2, 128), (3, 0, 2, 1)))
    wcf = np.asarray(inp["w_cf_conv"], np.float32)
    put("wcf", np.transpose(wcf.reshape(L, 31, 2, 128), (3, 0, 2, 1)))
    put("bcf", _fm(inp["b_cf_conv"]))
    put("gln", _fm(inp["g_cf_ln"]))
    put("bln", _fm(inp["b_cf_ln"]))
    put("sink", np.broadcast_to(np.asarray(inp["wa_sink"], np.float32).reshape(1, L * 4), (128, L * 4)))
    put("lm", np.full((128, 1), 1.0 if half == 1 else 0.0, np.float32))
    put("rm", np.full((128, 1), 1.0 if half == 0 else 0.0, np.float32))
    put("eps", np.full((128, 1), EPS, np.float32))
    put("zero", np.zeros((128, 1), np.float32))
    bsel = np.zeros((128, 4), np.float32)
    bsel[:, b] = 1.0
    put("bsel", bsel)
    return vec


_NC_CACHE = {}


def kernel(**inputs):
    inp = {k: np.asarray(v) for k, v in inputs.items()}
    x = inp["x"].astype(np.float32, copy=False)
    ctx = inp["ctx"].astype(np.float32, copy=False)
    Bn = x.shape[0]
    key = STAGE[0]
    if key not in _NC_CACHE:
        _NC_CACHE[key] = build()
    nc = _NC_CACHE[key]
    shared = {n: np.ascontiguousarray(inp[n], dtype=np.float32) for n in
              ("w1_gate", "w1_up", "w1_down", "w2_gate", "w2_up", "w2_down", "w_in", "w_out", "w_mla_uq", "w_mla_ukv")}
    ident = np.eye(128, dtype=np.float32)
    in_maps = []
    for core in range(8):
        b, half = core // 2, core % 2
        rW, rQ, rK = _rope_tables(half)
        cc = np.stack([_fm(inp["c"][bb]) for bb in range(4)] + [_fm(inp["c_ctx"])], axis=-1).reshape(128, 40)
        nsh = MOD_RANKS[0]
        rk = core % nsh
        wsl = 9 * D // nsh
        m = {"xT": np.ascontiguousarray(x[b, half * S_OWN:(half + 1) * S_OWN, :].T),
             "ctxT": np.ascontiguousarray(ctx[b].T),
             "cc": np.ascontiguousarray(cc, dtype=np.float32),
             "vecs": _pack_vecs(inp, half, b),
             "ropeW": rW, "ropeQ": rQ, "ropeK": rK,
             "masks": _masks(half), "ident": ident}
        m["w_mod"] = np.ascontiguousarray(inp["w_mod"][:, :, rk * wsl:(rk + 1) * wsl], dtype=np.float32)
        m.update(shared)
        in_maps.append(m)
    res = run_bass_kernel_spmd(nc, in_maps, core_ids=list(range(8)))
    out = np.empty((Bn, 2 * S_OWN, D), np.float32)
    for core in range(8):
        b, half = core // 2, core % 2
        out[b, half * S_OWN:(half + 1) * S_OWN, :] = np.asarray(res.results[core]["outT"]).T
    return out
```
